# Optimizing a Trainium2 kernel written in Bass

```python
import math
import jax, jax.numpy as jnp
from jax import lax
import numpy as np

D_MODEL = 1024
BATCH = 16
SEQ = 256
DEPTH = 4
DEC_BATCH = 4
DEC_SEQ = 1024
PAST_LEN = 512

GRID_W = 64
N_MIXERS = 2
N_S5_LAYERS = (DEPTH + 1) // 2
N_ATTN_LAYERS = DEPTH // 2
GROUP_CH = 16
N_GROUPS = D_MODEL // GROUP_CH
STATE_DIM = 64
DT_MIN = 0.001
DT_MAX = 0.1
HEAD_DIM = 128
N_HEADS = D_MODEL // HEAD_DIM
N_KV_HEADS = 2
KV_REP = N_HEADS // N_KV_HEADS
D_Q = N_HEADS * HEAD_DIM
D_KV = N_KV_HEADS * HEAD_DIM
QKV_DIM = D_Q + 2 * D_KV
Q_BLOCK = 128
ROPE_THETA = 10000.0
AXIS_PAIRS = HEAD_DIM // 4
ATTN_SCALE = HEAD_DIM ** -0.5
D_FF = -(-8 * D_MODEL // (3 * 256)) * 256
DEEPNORM_ALPHA = (2.0 * DEPTH) ** 0.25
DEEPNORM_BETA = (8.0 * DEPTH) ** -0.25
LN_EPS = 1e-6
RMS_EPS = 1e-6

kernel_name = "hybrid_s5_gqa_prefix_diffusion_step"


def layer_norm(x, g, b):
    xf = x.astype(jnp.float32)
    mu = xf.mean(-1, keepdims=True)
    var = jnp.square(xf - mu).mean(-1, keepdims=True)
    return ((xf - mu) * lax.rsqrt(var + LN_EPS) * g + b).astype(x.dtype)


def rms_norm(x, g):
    xf = x.astype(jnp.float32)
    return (xf * lax.rsqrt(jnp.mean(xf * xf, -1, keepdims=True) + RMS_EPS) * g).astype(x.dtype)


def adaln(cvec, w_mod, b_mod):
    m = jax.nn.silu(cvec) @ w_mod + b_mod
    return jnp.split(m[:, None, :], 6, axis=-1)


def modulate(x, shift, scale):
    return x * (1.0 + scale) + shift


def swiglu(h, w_in, w_out):
    gate, up = jnp.split(h @ w_in, 2, axis=-1)
    return (jax.nn.silu(gate) * up) @ w_out


def s5_discretize(a_re, a_im, log_dt, b_re, b_im):
    dt = jnp.exp(log_dt)[:, None]
    mag = jnp.exp(dt * a_re)
    ab_re = mag * jnp.cos(dt * a_im)
    ab_im = mag * jnp.sin(dt * a_im)
    den = a_re * a_re + a_im * a_im
    nr = ab_re - 1.0
    k_re = (nr * a_re + ab_im * a_im) / den
    k_im = (ab_im * a_re - nr * a_im) / den
    bb_re = k_re[..., None] * b_re - k_im[..., None] * b_im
    bb_im = k_re[..., None] * b_im + k_im[..., None] * b_re
    return ab_re, ab_im, bb_re, bb_im


def _complex_affine_combine(e1, e2):
    a1r, a1i, b1r, b1i = e1
    a2r, a2i, b2r, b2i = e2
    return (a2r * a1r - a2i * a1i,
            a2r * a1i + a2i * a1r,
            a2r * b1r - a2i * b1i + b2r,
            a2r * b1i + a2i * b1r + b2i)


def s5_direction(u, h0_re, h0_im, a_re, a_im, log_dt, b_re, b_im, c_re, c_im, reverse):
    ab_re, ab_im, bb_re, bb_im = s5_discretize(a_re, a_im, log_dt, b_re, b_im)
    bu_re = jnp.einsum('blgh,gph->blgp', u, bb_re)
    bu_im = jnp.einsum('blgh,gph->blgp', u, bb_im)
    if reverse:
        bu_re, bu_im = jnp.flip(bu_re, 1), jnp.flip(bu_im, 1)
    bu_re = bu_re.at[:, 0].add(ab_re * h0_re - ab_im * h0_im)
    bu_im = bu_im.at[:, 0].add(ab_re * h0_im + ab_im * h0_re)
    ar = jnp.broadcast_to(ab_re, bu_re.shape)
    ai = jnp.broadcast_to(ab_im, bu_im.shape)
    _, _, s_re, s_im = lax.associative_scan(_complex_affine_combine, (ar, ai, bu_re, bu_im), axis=1)
    fin_re, fin_im = s_re[:, -1], s_im[:, -1]
    if reverse:
        s_re, s_im = jnp.flip(s_re, 1), jnp.flip(s_im, 1)
    y = jnp.einsum('blgp,ghp->blgh', s_re, c_re) - jnp.einsum('blgp,ghp->blgh', s_im, c_im)
    return y, fin_re, fin_im


def s5_mixer(h, h0, w_in, a_re, a_im, log_dt, b_re, b_im, c_re, c_im, d_skip, w_glu, w_out):
    b, l, _ = h.shape
    u = (h @ w_in).reshape(b, l, N_GROUPS, GROUP_CH)
    y = d_skip.reshape(N_GROUPS, GROUP_CH) * u
    finals = []
    for d in range(2):
        yd, fr, fi = s5_direction(u, h0[:, d, 0], h0[:, d, 1], a_re[d], a_im[d], log_dt[d],
                                  b_re[d], b_im[d], c_re[d], c_im[d], reverse=(d == 1))
        y = y + yd
        finals.append(jnp.stack([fr, fi], axis=1))
    z = jax.nn.gelu(y.reshape(b, l, D_MODEL))
    val, gate = jnp.split(z @ w_glu, 2, axis=-1)
    return (val * jax.nn.sigmoid(gate)) @ w_out, jnp.stack(finals, axis=1)


def attn_project(h, w_qkv, q_gain, k_gain):
    b, l, _ = h.shape
    qkv = h @ w_qkv
    q = qkv[..., :D_Q].reshape(b, l, N_HEADS, HEAD_DIM)
    k = qkv[..., D_Q:D_Q + D_KV].reshape(b, l, N_KV_HEADS, HEAD_DIM)
    v = qkv[..., D_Q + D_KV:].reshape(b, l, N_KV_HEADS, HEAD_DIM)
    return rms_norm(q, q_gain), rms_norm(k, k_gain), v


def _rotate(x, ang):
    cos = jnp.cos(ang)[:, None, :].astype(x.dtype)
    sin = jnp.sin(ang)[:, None, :].astype(x.dtype)
    x1, x2 = x[..., :AXIS_PAIRS], x[..., AXIS_PAIRS:]
    return jnp.concatenate([x1 * cos - x2 * sin, x2 * cos + x1 * sin], axis=-1)


def axial_rope(x):
    l = x.shape[1]
    rows = l // GRID_W
    row = jnp.repeat(jnp.arange(rows, dtype=jnp.float32), GRID_W)
    col = jnp.tile(jnp.arange(GRID_W, dtype=jnp.float32), rows)
    inv = ROPE_THETA ** (-jnp.arange(AXIS_PAIRS, dtype=jnp.float32) / AXIS_PAIRS)
    half = HEAD_DIM // 2
    return jnp.concatenate([_rotate(x[..., :half], row[:, None] * inv),
                            _rotate(x[..., half:], col[:, None] * inv)], axis=-1)


def blocked_attention(q, k, v):
    b, lq = q.shape[0], q.shape[1]
    nb = lq // Q_BLOCK
    qb = q.reshape(b, nb, Q_BLOCK, N_KV_HEADS, KV_REP, HEAD_DIM).transpose(1, 0, 2, 3, 4, 5)

    def one_block(qblk):
        s = jnp.einsum('bqgrd,bkgd->bgrqk', qblk, k).astype(jnp.float32) * ATTN_SCALE
        p = jax.nn.softmax(s, axis=-1).astype(v.dtype)
        return jnp.einsum('bgrqk,bkgd->bqgrd', p, v)

    o = lax.map(one_block, qb)
    return o.transpose(1, 0, 2, 3, 4, 5).reshape(b, lq, D_Q)


def setup_inputs(seed: int = 0) -> dict:
    key = jax.random.key(seed)
    ks = jax.random.split(key, 32)
    f32 = jnp.float32

    def nrm(k, shape, s=1.0):
        return s * jax.random.normal(k, shape, f32)

    s5_shape = (N_S5_LAYERS, 2, N_GROUPS, STATE_DIM)
    n_idx = jnp.arange(STATE_DIM, dtype=f32)
    return {
        "x_prompt": nrm(ks[0], (BATCH, SEQ, D_MODEL)),
        "x_sample": nrm(ks[1], (DEC_BATCH, DEC_SEQ, D_MODEL)),
        "c": nrm(ks[2], (DEC_BATCH, D_MODEL)),
        "cache_k": nrm(ks[3], (DEC_BATCH, N_ATTN_LAYERS, PAST_LEN, N_KV_HEADS, HEAD_DIM)),
        "cache_v": nrm(ks[4], (DEC_BATCH, N_ATTN_LAYERS, PAST_LEN, N_KV_HEADS, HEAD_DIM)),
        "state_s5": nrm(ks[5], (DEC_BATCH, N_S5_LAYERS, 2, 2, N_GROUPS, STATE_DIM), 0.1),
        "c_ctx": nrm(ks[6], (D_MODEL,)),
        "w_mod": nrm(ks[7], (DEPTH, D_MODEL, 6 * D_MODEL), 0.5 * D_MODEL ** -0.5),
        "b_mod": nrm(ks[8], (DEPTH, 6 * D_MODEL), 0.01),
        "ln_g": 1.0 + nrm(ks[9], (DEPTH, 2, D_MODEL), 0.02),
        "ln_b": nrm(ks[10], (DEPTH, 2, D_MODEL), 0.02),
        "w_s5_in": nrm(ks[11], (N_S5_LAYERS, D_MODEL, D_MODEL), D_MODEL ** -0.5),
        "s5_a_re": -0.5 * jnp.exp(nrm(ks[12], s5_shape, 0.05)),
        "s5_a_im": jnp.pi * n_idx + nrm(ks[13], s5_shape, 0.01),
        "s5_log_dt": jax.random.uniform(ks[14], (N_S5_LAYERS, 2, N_GROUPS), f32,
                                        math.log(DT_MIN), math.log(DT_MAX)),
        "s5_b_re": nrm(ks[15], (N_S5_LAYERS, 2, N_GROUPS, STATE_DIM, GROUP_CH), (2 * GROUP_CH) ** -0.5),
        "s5_b_im": nrm(ks[16], (N_S5_LAYERS, 2, N_GROUPS, STATE_DIM, GROUP_CH), (2 * GROUP_CH) ** -0.5),
        "s5_c_re": nrm(ks[17], (N_S5_LAYERS, 2, N_GROUPS, GROUP_CH, STATE_DIM), STATE_DIM ** -0.5),
        "s5_c_im": nrm(ks[18], (N_S5_LAYERS, 2, N_GROUPS, GROUP_CH, STATE_DIM), STATE_DIM ** -0.5),
        "s5_d": nrm(ks[19], (N_S5_LAYERS, D_MODEL)),
        "w_s5_glu": nrm(ks[20], (N_S5_LAYERS, D_MODEL, 2 * D_MODEL), D_MODEL ** -0.5),
        "w_s5_out": nrm(ks[21], (N_S5_LAYERS, D_MODEL, D_MODEL), DEEPNORM_BETA * D_MODEL ** -0.5),
        "w_qkv": nrm(ks[22], (N_ATTN_LAYERS, D_MODEL, QKV_DIM), D_MODEL ** -0.5),
        "q_norm_g": 1.0 + nrm(ks[23], (N_ATTN_LAYERS, HEAD_DIM), 0.02),
        "k_norm_g": 1.0 + nrm(ks[24], (N_ATTN_LAYERS, HEAD_DIM), 0.02),
        "w_o": nrm(ks[25], (N_ATTN_LAYERS, D_Q, D_MODEL), DEEPNORM_BETA * D_Q ** -0.5),
        "w_ffn_in": nrm(ks[26], (DEPTH, D_MODEL, 2 * D_FF), D_MODEL ** -0.5),
        "w_ffn_out": nrm(ks[27], (DEPTH, D_FF, D_MODEL), DEEPNORM_BETA * D_FF ** -0.5),
    }


def reference(x_prompt, x_sample, c, cache_k, cache_v, state_s5, c_ctx, w_mod, b_mod, ln_g, ln_b,
              w_s5_in, s5_a_re, s5_a_im, s5_log_dt, s5_b_re, s5_b_im, s5_c_re, s5_c_im, s5_d,
              w_s5_glu, w_s5_out, w_qkv, q_norm_g, k_norm_g, w_o, w_ffn_in, w_ffn_out):
    xp, xs = x_prompt, x_sample
    ctx_state0 = jnp.zeros((xp.shape[0], 2, 2, N_GROUPS, STATE_DIM), xp.dtype)
    new_k, new_v, new_s = [], [], []
    for layer in range(DEPTH):
        j = layer // N_MIXERS
        sh1p, sc1p, g1p, sh2p, sc2p, g2p = adaln(c_ctx[None, :], w_mod[layer], b_mod[layer])
        sh1s, sc1s, g1s, sh2s, sc2s, g2s = adaln(c, w_mod[layer], b_mod[layer])
        hp = modulate(xp, sh1p, sc1p)
        hs = modulate(xs, sh1s, sc1s)
        if layer % N_MIXERS == 0:
            prm = (w_s5_in[j], s5_a_re[j], s5_a_im[j], s5_log_dt[j], s5_b_re[j], s5_b_im[j],
                   s5_c_re[j], s5_c_im[j], s5_d[j], w_s5_glu[j], w_s5_out[j])
            mp, fin = s5_mixer(hp, ctx_state0, *prm)
            ms, _ = s5_mixer(hs, state_s5[:, j], *prm)
            new_s.append(fin)
        else:
            qp, kp, vp = attn_project(hp, w_qkv[j], q_norm_g[j], k_norm_g[j])
            mp = blocked_attention(qp, kp, vp) @ w_o[j]
            qs, ks_, vs = attn_project(hs, w_qkv[j], q_norm_g[j], k_norm_g[j])
            qs, ks_ = axial_rope(qs), axial_rope(ks_)
            k_all = jnp.concatenate([ks_, cache_k[:, j]], axis=1)
            v_all = jnp.concatenate([vs, cache_v[:, j]], axis=1)
            ms = blocked_attention(qs, k_all, v_all) @ w_o[j]
            new_k.append(kp)
            new_v.append(vp)
        xp = layer_norm(DEEPNORM_ALPHA * xp + g1p * mp, ln_g[layer, 0], ln_b[layer, 0])
        xs = layer_norm(DEEPNORM_ALPHA * xs + g1s * ms, ln_g[layer, 0], ln_b[layer, 0])
        fp = swiglu(modulate(xp, sh2p, sc2p), w_ffn_in[layer], w_ffn_out[layer])
        fs = swiglu(modulate(xs, sh2s, sc2s), w_ffn_in[layer], w_ffn_out[layer])
        xp = layer_norm(DEEPNORM_ALPHA * xp + g2p * fp, ln_g[layer, 1], ln_b[layer, 1])
        xs = layer_norm(DEEPNORM_ALPHA * xs + g2s * fs, ln_g[layer, 1], ln_b[layer, 1])
    y_prompt, y_sample = xp, xs
    new_cache_k = jnp.stack(new_k, axis=1)
    new_cache_v = jnp.stack(new_v, axis=1)
    new_state_s5 = jnp.stack(new_s, axis=1)
    return (y_prompt, y_sample, new_cache_k, new_cache_v, new_state_s5)
```

```python
import os
import numpy as np
import concourse.bass as bass
import concourse.mybir as mybir
from concourse.bass_utils import run_bass_kernel_spmd

F32 = mybir.dt.float32
BF16 = mybir.dt.bfloat16
AF = mybir.ActivationFunctionType
ALU = mybir.AluOpType

D = 1024
NT = 1024
DEPTH = 4
DFF = 2816
KFF = 22
T = 8
C = NT // T
NSLOT = 2 * (C + 1)
ALPHA = (2.0 * DEPTH) ** 0.25
LN_EPS = 1e-6
RMS_EPS = 1e-6
ATT_SCALE = 128 ** -0.5
WCOLS = 4096
NWSLOT = 3
EPOCH = 16000
NDMA = 8
MAGIC = 12582912.0
STAGE = int(os.environ.get("K_STAGE", "99"))
DBG = int(os.environ.get("K_DBG", "0"))
S5STOP = int(os.environ.get("K_S5STOP", "0"))
KVAR = int(os.environ.get("K_VAR", "0"))


class _Stop(Exception):
    pass


class Buf:
    __slots__ = ("w", "r", "name")

    def __init__(self, name=""):
        self.w = {}
        self.r = {}
        self.name = name


def merge(name, *olds):
    b = Buf(name)
    for o in olds:
        for k, v in list(o.w.items()) + list(o.r.items()):
            b.r[k] = max(b.r.get(k, 0), v)
            b.w[k] = max(b.w.get(k, 0), v)
    return b


class Prog:
    def __init__(self, nc):
        self.nc = nc
        self.eng = {"pe": nc.tensor, "act": nc.scalar, "dve": nc.vector, "pool": nc.gpsimd, "sp": nc.sync}
        self.cnt = {}
        self.semlist = {}
        self.known = {e: {} for e in self.eng}
        self.allsems = []
        self.rec = None
        for k in ["pe", "act", "dve", "pool"]:
            self.cnt[k] = 0
            self.semlist[k] = []
        for pre in ("q", "g"):
            for i in range(NDMA):
                k = "%s%d" % (pre, i)
                self.cnt[k] = 0
                self.semlist[k] = [self._newsem(k)]
        self.dma_rr = {"q": 0, "g": 0}

    def _newsem(self, name):
        cm = self.nc.semaphore("s_%s_%d" % (name, len(self.allsems)))
        s = cm.__enter__()
        self.allsems.append(s)
        return s

    def _semval(self, k, v):
        if k[0] in "qg":
            return self.semlist[k][0], v
        idx = (v - 1) // EPOCH
        while len(self.semlist[k]) <= idx:
            self.semlist[k].append(self._newsem(k))
        return self.semlist[k][idx], v - idx * EPOCH

    def op(self, e, fn, reads=(), writes=(), dma=False):
        if self.rec is not None:
            self.rec.append((e, fn, tuple(reads), tuple(writes), dma))
            return 0
        deps = {}
        for b in reads:
            for k, v in b.w.items():
                if v > deps.get(k, 0):
                    deps[k] = v
        for b in writes:
            for k, v in b.w.items():
                if v > deps.get(k, 0):
                    deps[k] = v
            for k, v in b.r.items():
                if v > deps.get(k, 0):
                    deps[k] = v
        kn = self.known[e]
        for k, v in deps.items():
            if k == e and e == "pe":
                continue
            if kn.get(k, 0) >= v:
                continue
            s, sv = self._semval(k, v)
            self.eng[e].wait_ge(s, sv)
            kn[k] = v
        inst = fn(self.eng[e])
        if dma:
            pre = "g" if e == "pool" else "q"
            key = "%s%d" % (pre, self.dma_rr[pre])
            self.dma_rr[pre] = (self.dma_rr[pre] + 1) % NDMA
            self.cnt[key] += 16
            val = self.cnt[key]
            inst.then_inc(self.semlist[key][0], 16)
        else:
            key = e
            self.cnt[e] += 1
            val = self.cnt[e]
            s, sv = self._semval(e, val)
            inst.then_inc(s, 1)
        for b in reads:
            if val > b.r.get(key, 0):
                b.r[key] = val
        for b in writes:
            b.w = {key: val}
            b.r = {}
        return val

    def finish(self):
        sp = self.eng["sp"]
        for k, v in self.cnt.items():
            if v > 0:
                s, sv = self._semval(k, v)
                sp.wait_ge(s, sv)
        for s in self.allsems:
            sp.sem_clear(s)


def _blk(W, cols):
    kin = W.shape[0]
    kc = kin // 128
    a = W[:, cols].reshape(kc, 128, len(cols)).transpose(1, 0, 2).reshape(128, kc * len(cols))
    out = np.zeros((128, WCOLS), np.float32)
    out[:, :a.shape[1]] = a
    return out


def _r(a, n):
    return np.arange(a, a + n)


def mod_blocks(w_mod_l):
    return [_blk(w_mod_l, _r(b * 512, 512)) for b in range(12)]


def ffn_blocks(w_in, w_out):
    bl = []
    for b in range(11):
        j0, j1 = 2 * b, 2 * b + 1
        cols = np.concatenate([_r(j0 * 128, 128), _r(DFF + j0 * 128, 128), _r(j1 * 128, 128), _r(DFF + j1 * 128, 128)])
        bl.append(_blk(w_in, cols))
    for c in range(8):
        bl.append(_blk(w_out, _r(c * 128, 128)))
    return bl


def s5_blocks(w_in, w_glu, w_out):
    bl = [_blk(w_in, _r(b * 512, 512)) for b in range(2)]
    for b in range(4):
        j0, j1 = 2 * b, 2 * b + 1
        cols = np.concatenate([_r(j0 * 128, 128), _r(D + j0 * 128, 128), _r(j1 * 128, 128), _r(D + j1 * 128, 128)])
        bl.append(_blk(w_glu, cols))
    bl += [_blk(w_out, _r(b * 512, 512)) for b in range(2)]
    return bl


def attn_blocks(w_qkv, w_o):
    bl = [_blk(w_qkv, _r(b * 512, 512)) for b in range(3)]
    bl += [_blk(w_o, _r(b * 512, 512)) for b in range(2)]
    return bl


def build_wall(inp):
    bl = []
    bl += mod_blocks(inp["w_mod"][0])
    for l in range(DEPTH):
        j = l // 2
        if l % 2 == 0:
            sb_ = s5_blocks(inp["w_s5_in"][j], inp["w_s5_glu"][j], inp["w_s5_out"][j])
            bl += sb_[0:2]
            if l + 1 < DEPTH:
                bl += mod_blocks(inp["w_mod"][l + 1])
            bl += sb_[2:]
        else:
            bl += attn_blocks(inp["w_qkv"][j], inp["w_o"][j])
            if l + 1 < DEPTH:
                bl += mod_blocks(inp["w_mod"][l + 1])
        bl += ffn_blocks(inp["w_ffn_in"][l], inp["w_ffn_out"][l])
    return np.ascontiguousarray(np.stack(bl, 0))


def s5_host_layout(inp):
    out = {}
    G, P, H = 64, 64, 16
    def c_lay(a):
        a5 = a.reshape(2, 2, 8, 8, P)
        a5 = np.broadcast_to(a5[:, :, :, :, None, :], (2, 2, 8, 8, H, P))
        return np.ascontiguousarray(a5.transpose(3, 4, 0, 1, 2, 5).reshape(128, 2, 2, 8, P))
    def b_lay(a):
        a5 = a.reshape(2, 2, 32, 2, P)
        return np.ascontiguousarray(a5.transpose(3, 4, 0, 1, 2).reshape(128, 2, 2, 32))
    ld = np.broadcast_to(inp["s5_log_dt"][:, :, :, None], (2, 2, G, P))
    nat = np.stack([inp["s5_a_re"], inp["s5_a_im"], ld], 0)
    out["s5n"] = np.ascontiguousarray(nat.transpose(3, 1, 0, 2, 4))
    selc = np.zeros((128, 8, 8, 16), np.float32)
    for g in range(64):
        selc[g, g // 8, g % 8, :] = 1.0
    out["selc"] = selc.reshape(128, 8, 128)
    selb = np.zeros((128, 34), np.float32)
    for g in range(64):
        selb[g, g // 2] = 1.0
        selb[g, 32 + (g % 2)] = 1.0
    out["selb"] = selb
    b = np.stack([inp["s5_b_re"], inp["s5_b_im"]], 2)
    b8 = b.reshape(2, 2, 2, 8, 4, 2, P, H)
    w0 = np.zeros((2, 8, 4, 2, H, 2, 2, 2, P), np.float32)
    bn = np.zeros((2, 8, 2, P, 2, 2, 4, 2, H), np.float32)
    for a in range(2):
        w0[:, :, :, a, :, :, :, a, :] = b8[:, :, :, :, :, a].transpose(0, 3, 4, 6, 1, 2, 5)
        bn[:, :, a, :, :, :, :, a, :] = b8[:, :, :, :, :, a].transpose(0, 3, 5, 1, 2, 4, 6)
    out["s5w0"] = np.ascontiguousarray(w0.reshape(2, 8, 128, 512))
    out["s5bn"] = np.ascontiguousarray(bn.reshape(2, 8, 128, 512))
    c = np.stack([inp["s5_c_re"], inp["s5_c_im"]], 2)
    c8 = c.reshape(2, 2, 2, 8, 4, 2, H, P)
    v0 = np.zeros((2, 8, 2, P, 2, 2, 4, 2, H), np.float32)
    for a in range(2):
        v0[:, :, a, :, :, :, :, a, :] = c8[:, :, :, :, :, a].transpose(0, 3, 6, 1, 2, 4, 5)
    out["s5v0"] = np.ascontiguousarray(v0.reshape(2, 8, 128, 512))
    out["s5d"] = np.ascontiguousarray(inp["s5_d"].reshape(2, 8, 128).transpose(2, 0, 1))
    return out


def h0_layout(st):
    s = st.reshape(2, 2, 2, 32, 2, 64)
    return np.ascontiguousarray(s.transpose(4, 5, 0, 1, 2, 3).reshape(128, 2, 2, 2, 32))


def rope_tables():
    l = np.arange(NT)
    row = (l // 64).astype(np.float32)
    col = (l % 64).astype(np.float32)
    inv = (np.float32(10000.0) ** (-np.arange(32, dtype=np.float32) / np.float32(32))).astype(np.float32)
    ar = row[None, :] * inv[:, None]
    ac = col[None, :] * inv[:, None]
    cos = np.concatenate([np.cos(ar), np.cos(ar), np.cos(ac), np.cos(ac)], 0).astype(np.float32)
    sin = np.concatenate([np.sin(ar), np.sin(ar), np.sin(ac), np.sin(ac)], 0).astype(np.float32)
    return cos, sin


def const_tables():
    ident = np.eye(128, dtype=np.float32)
    bd = np.kron(np.eye(8, dtype=np.float32), np.ones((16, 16), np.float32))
    rot = np.zeros((128, 128), np.float32)
    for base in (0, 64):
        for i in range(32):
            rot[base + 32 + i, base + i] = -1.0
            rot[base + i, base + 32 + i] = 1.0
    return np.ascontiguousarray(np.stack([ident, bd, rot], 1))


def build_program(nblk):
    nc = bass.Bass("TRN2", target_bir_lowering=False)
    P = Prog(nc)

    def din(name, shape):
        return nc.dram_tensor(name, list(shape), F32, kind="ExternalInput").ap()

    def dout(name, shape):
        return nc.dram_tensor(name, list(shape), F32, kind="ExternalOutput").ap()

    d_xT = din("xT", [D, NT])
    d_cT = din("cT", [128, 8])
    d_wall = din("wall", [nblk, 128, WCOLS])
    d_bmod = din("bmodT", [128, DEPTH * 48])
    d_ln = din("lnT", [128, 128])
    d_rope = din("rope", [128, 2, NT])
    d_mask = din("maskb", [128, 48])
    d_keep = din("keep", [128, 1])
    d_ckT = din("ckT", [2, 128, 2, 512])
    d_cv = din("cv", [2, 128, 4, 256])
    d_gain = din("gain", [128, 4])
    d_const = din("consts", [128, 3, 128])
    d_s5w0 = din("s5w0", [2, 8, 128, 512])
    d_s5bn = din("s5bn", [2, 8, 128, 512])
    d_s5v0 = din("s5v0", [2, 8, 128, 512])
    d_s5d = din("s5d", [128, 2, 8])
    d_h0 = din("h0", [128, 2, 2, 2, 32])
    d_s5n = din("s5n", [64, 2, 3, 2, 64])
    d_selc = din("selc", [128, 8, 128])
    d_selb = din("selb", [128, 34])
    o_yT = dout("yT", [D, NT])
    o_k = dout("kout", [2, 2, 128, NT])
    o_v = dout("vout", [2, NT, 256])
    o_st = dout("stout", [2, 128, 512])
    if DBG:
        o_dU = dout("dU", [D, NT]); o_dZ = dout("dZ", [D, NT]); o_dXA = dout("dXA", [D, NT]); o_dS = None
        o_dQ = dout("dQ", [D, NT]); o_dK = dout("dK", [128, 2, 1536]); o_dO = dout("dO", [D, NT]); o_dXA1 = dout("dXA1", [D, NT])

    def dbg_bf(dst, src, bufs):
        P.op("pool", lambda e: e.dma_start(out=dst, in_=src), reads=bufs, dma=True)

    def dbg_x(dst):
        for c in range(8):
            P.op("sp", lambda e, c=c: e.dma_start(out=dst[c * 128:(c + 1) * 128, :], in_=X[:, c, :]), reads=[Xb[c]], dma=True)

    def sb(name, shape, dt=F32):
        cm = nc.sbuf_tensor(name, list(shape), dt)
        return cm.__enter__()

    def ps(name, shape, dt=F32):
        cm = nc.psum_tensor(name, list(shape), dt)
        return cm.__enter__()

    X = sb("X", [128, 8, NT])
    Xb = [Buf("X%d" % i) for i in range(8)]
    H = sb("H", [128, 8, NT], BF16)
    Hb = [Buf("H%d" % i) for i in range(8)]
    BIG = sb("BIG", [128, KFF * NT], BF16)
    BIGb = [Buf("BIG%d" % i) for i in range(KFF)]
    WR = sb("WR", [128, NWSLOT, WCOLS], BF16)
    WRb = [Buf("WR%d" % i) for i in range(NWSLOT)]
    LNT = sb("LNT", [128, 4, NT], BF16)
    LNTb = [Buf("LNT%d" % i) for i in range(4)]
    TS = sb("TS", [128, 4, NT])
    TSb = [Buf("TS%d" % i) for i in range(4)]
    MOD = sb("MOD", [128, DEPTH, 48])
    MODb = [Buf("MOD%d" % i) for i in range(DEPTH)]
    BMOD = sb("BMOD", [128, DEPTH * 48]); BMODb = Buf("BMOD")
    LNP = sb("LNP", [128, 128]); LNPb = Buf("LNP")
    CT = sb("CT", [128, 8]); CTb = Buf("CT")
    CS = sb("CS", [128, 8], BF16); CSb = Buf("CS")
    ROW = sb("ROW", [1, 512]); ROWb = Buf("ROW")
    ONE1 = sb("ONE1", [1, 1]); ONE1b = Buf("ONE1")
    MASK = sb("MASK", [128, 48]); MASKb = Buf("MASK")
    KEEP = sb("KEEP", [128, 1]); KEEPb = Buf("KEEP")
    GAIN = sb("GAIN", [128, 4]); GAINb = Buf("GAIN")
    CONST = sb("CONST", [128, 3, 128]); CONSTb = Buf("CONST")
    CB = sb("CB", [128, 3, 128], BF16); CBb = Buf("CB")
    ONES = sb("ONES", [128, 128], BF16); ONESb = Buf("ONES")
    EPSC = sb("EPSC", [128, 2]); EPSb = Buf("EPSC")
    ARENA = sb("ARENA", [128, 8192])
    ARb = {"cur": [Buf("ARENA")]}

    def arena_switch(names):
        olds = ARb["cur"]
        news = [merge(n, *olds) for n in names]
        ARb["cur"] = news
        return news
    ROPE = ARENA[:, 0:2048].rearrange("p (a n) -> p a n", a=2)
    KTB = ARENA[:, 2048:3584].bitcast(BF16).rearrange("p (a n) -> p a n", a=2)
    VTB = ARENA[:, 3584:5120].bitcast(BF16).rearrange("p (a n) -> p a n", a=12)
    PR = ARENA[:, 5120:6144].bitcast(BF16).rearrange("p (a n) -> p a n", a=4)
    S5C = ARENA[:, 0:2048].rearrange("p (t d f x) -> p t d f x", t=2, d=2, f=8)
    S5K = ARENA[:, 2048:4096].rearrange("p (t d f x) -> p t d f x", t=2, d=2, f=8)
    NATQ = ARENA[0:64, 4096:4736].rearrange("p (t x) -> p t x", t=5)
    NATT = ARENA[0:64, 4736:5504].rearrange("p (t x) -> p t x", t=6)
    XPAD = ARENA[0:64, 5504:6528].rearrange("p (t x) -> p t x", t=8)
    SELC = ARENA[:, 6528:7552].rearrange("p (f x) -> p f x", f=8)
    SELB = ARENA[:, 7552:7586]
    NATQ128 = ARENA[:, 4096:4736].rearrange("p (t x) -> p t x", t=5)
    XPAD128 = ARENA[:, 5504:6528].rearrange("p (t x) -> p t x", t=8)
    VF1 = ARENA[:, 4096:4096 + 2304].bitcast(BF16).rearrange("p (n d r x) -> p n d r x", n=9, d=2, r=2)
    S5B = sb("S5B", [128, 2, 2, 32]); S5Bb = Buf("S5B")
    S5KB = sb("S5KB", [128, 2, 2, 32]); S5KBb = Buf("S5KB")
    S5TB = sb("S5TB", [128, 2, 2, 32]); S5TBb = Buf("S5TB")
    S5LT = sb("S5LT", [128, 2, 2, 32]); S5LTb = Buf("S5LT")
    S5LN = sb("S5LN", [128, 2, 32]); S5LNb = Buf("S5LN")
    S5D = sb("S5D", [128, 2, 8]); S5Db = Buf("S5D")
    H0 = sb("H0", [128, 2, 2, 2, 32]); H0b = Buf("H0")
    SELCb = Buf("SELC")
    SELBb = Buf("SELB")
    LDW = sb("LDW", [128, 2, 512]); LDWb = [Buf("LDW0"), Buf("LDW1")]
    LDN = sb("LDN", [128, 2, 512]); LDNb = [Buf("LDN0"), Buf("LDN1")]
    LDV = LDW; LDVb = LDWb
    WST = sb("WST", [128, 2, 2, 2, 128])
    WSTb = [[Buf("WST00"), Buf("WST01")], [Buf("WST10"), Buf("WST11")]]
    WTA = sb("WTA", [128, 2, 2, 2, 128])
    WTAb = [[Buf("WTA00"), Buf("WTA01")], [Buf("WTA10"), Buf("WTA11")]]
    BNB = sb("BNB", [128, 2, 2, 2, 128], BF16); BNBb = [Buf("BNB0"), Buf("BNB1")]
    KTMP = sb("KTMP", [128, 128]); KTMPb = Buf("KTMP")
    STG = sb("STG", [128, 4, 2, 2, 32]); STGb = Buf("STG")
    RST = sb("RST", [128, 2, 2, 2, 32]); RSTb = [Buf("RST0"), Buf("RST1")]
    RTM = sb("RTM", [128, 2, 2, 2, 32]); RTMb = [Buf("RTM0"), Buf("RTM1a"), Buf("RTM1b")]
    VST = sb("VST", [128, 2, 256]); VSTb = [Buf("VST0"), Buf("VST1")]

    PA = ps("PA", [128, NT]); PB = ps("PB", [128, NT]); PC = ps("PC", [128, NT]); PD = ps("PD", [128, NT])
    PAb, PBb, PCb, PDb = Buf("PA"), Buf("PB"), Buf("PC"), Buf("PD")
    PSUMS = [(PA, PAb), (PB, PBb), (PC, PCb), (PD, PDb)]

    wstate = {"issued": 0, "used": 0}

    def w_issue():
        i = wstate["issued"]
        if i >= nblk:
            return
        s = i % NWSLOT
        P.op("pool", lambda e: e.dma_start(out=WR[:, s, :], in_=d_wall[i]), writes=[WRb[s]], dma=True)
        wstate["issued"] += 1

    def w_get():
        i = wstate["used"]
        wstate["used"] += 1
        while wstate["issued"] < min(nblk, i + NWSLOT):
            w_issue()
        s = i % NWSLOT
        return WR[:, s, :], WRb[s]

    def sp_load(dst, src, buf):
        P.op("sp", lambda e: e.dma_start(out=dst, in_=src), writes=[buf], dma=True)

    for i in range(NWSLOT):
        w_issue()
    sp_load(CT[:], d_cT, CTb)
    sp_load(BMOD[:], d_bmod, BMODb)
    sp_load(LNP[:], d_ln, LNPb)
    sp_load(CONST[:], d_const, CONSTb)
    sp_load(GAIN[:], d_gain, GAINb)
    sp_load(MASK[:], d_mask, MASKb)
    sp_load(KEEP[:], d_keep, KEEPb)
    for c in range(8):
        sp_load(X[:, c, :], d_xT[c * 128:(c + 1) * 128, :], Xb[c])
    sp_load(S5D[:], d_s5d, S5Db)
    sp_load(H0[:], d_h0, H0b)
    sp_load(SELC, d_selc, SELCb)
    sp_load(SELB, d_selb, SELBb)

    P.op("dve", lambda e: e.memset(ONES[:], 1.0), writes=[ONESb])
    P.op("dve", lambda e: e.memset(ONE1[:], 1.0), writes=[ONE1b])
    P.op("dve", lambda e: e.memset(EPSC[:, 0:1], LN_EPS / (ALPHA * ALPHA)), writes=[EPSb])
    P.op("dve", lambda e: e.memset(EPSC[:, 1:2], RMS_EPS), writes=[EPSb])
    P.op("dve", lambda e: e.tensor_copy(out=CB[:], in_=CONST[:]), reads=[CONSTb], writes=[CBb])
    P.op("act", lambda e: e.activation(out=CS[:], in_=CT[:], func=AF.Silu), reads=[CTb], writes=[CSb])

    IDENT_F = CONST[:, 0, :]
    BDMASK = CONST[:, 1, :]
    ROTB = CB[:, 2, :]

    def compute_mod_gen(l):
        for b in range(12):
            wap, wb = w_get()
            w3 = wap.rearrange("p (k n) -> p k n", n=512)

            def mm(e):
                last = None
                for kc in range(8):
                    last = e.matmul(PC[0:1, 0:512], lhsT=CS[:, kc:kc + 1], rhs=w3[:, kc, :], start=(kc == 0), stop=(kc == 7))
                return last
            P.op("pe", mm, reads=[wb, CSb], writes=[PCb])
            P.op("act", lambda e: e.activation(out=ROW[:], in_=PC[0:1, 0:512], func=AF.Copy), reads=[PCb], writes=[ROWb])

            def tr(e):
                last = None
                for i in range(4):
                    col = b * 4 + i
                    last = e.matmul(PD[:, col:col + 1], lhsT=ROW[0:1, i * 128:(i + 1) * 128], rhs=ONE1[0:1, 0:1], start=True, stop=True)
                return last
            P.op("pe", tr, reads=[ROWb, ONE1b], writes=[PDb])
            yield b
        P.op("dve", lambda e: e.tensor_tensor(out=MOD[:, l, :], in0=PD[:, 0:48], in1=BMOD[:, l * 48:(l + 1) * 48], op=ALU.add),
             reads=[PDb, BMODb], writes=[MODb[l]])
        for base in (8, 32):
            P.op("dve", lambda e, base=base: e.tensor_scalar(out=MOD[:, l, base:base + 8], in0=MOD[:, l, base:base + 8], scalar1=1.0, scalar2=None, op0=ALU.add),
                 reads=[MODb[l]], writes=[MODb[l]])
        for base in (16, 40):
            P.op("dve", lambda e, base=base: e.tensor_scalar(out=MOD[:, l, base:base + 8], in0=MOD[:, l, base:base + 8], scalar1=1.0 / ALPHA, scalar2=None, op0=ALU.mult),
                 reads=[MODb[l]], writes=[MODb[l]])

    def compute_mod(l):
        for _ in compute_mod_gen(l):
            pass

    def modulate(l, which):
        so = 0 if which == 0 else 24
        for c in range(8):
            P.op("dve", lambda e, c=c: e.tensor_scalar(out=H[:, c, :], in0=X[:, c, :], scalar1=MOD[:, l, so + 8 + c:so + 9 + c],
                                                      scalar2=MOD[:, l, so + c:so + c + 1], op0=ALU.mult, op1=ALU.add),
                 reads=[Xb[c], MODb[l]], writes=[Hb[c]])

    def proj_chunk(pt, pbuf, w3, wb, oc, src, srcbufs, kcn):
        def mm(e):
            last = None
            for kc in range(kcn):
                for hf in range(2):
                    last = e.matmul(pt[:, hf * 512:(hf + 1) * 512], lhsT=w3[:, kc, oc * 128:(oc + 1) * 128],
                                    rhs=src[:, kc, hf * 512:(hf + 1) * 512], start=(kc == 0), stop=(kc == kcn - 1))
            return last
        P.op("pe", mm, reads=[wb] + list(srcbufs), writes=[pbuf])

    def outproj_ln(l, which, src, srcbufs, kcn, nblocks, cols_per_blk, next_mod):
        gcol = 16 if which == 0 else 40
        lni = (l * 2 + which) * 8
        oc_global = 0
        for b in range(nblocks):
            wap, wb = w_get()
            w3 = wap[:, 0:kcn * cols_per_blk].rearrange("p (k n) -> p k n", n=cols_per_blk)
            for oc in range(cols_per_blk // 128):
                c = oc_global
                pt, pbuf = PSUMS[c % 2]
                proj_chunk(pt, pbuf, w3, wb, oc, src, srcbufs, kcn)
                P.op("dve", lambda e, c=c, pt=pt: e.scalar_tensor_tensor(out=X[:, c, :], in0=pt[:], scalar=MOD[:, l, gcol + c:gcol + c + 1],
                                                                        in1=X[:, c, :], op0=ALU.mult, op1=ALU.add),
                     reads=[pbuf, MODb[l], Xb[c]], writes=[Xb[c]])
                s = (c % 2) * 2
                P.op("act", lambda e, c=c, s=s: e.activation(out=LNT[:, s, :], in_=X[:, c, :], func=AF.Copy), reads=[Xb[c]], writes=[LNTb[s]])
                P.op("act", lambda e, c=c, s=s: e.activation(out=LNT[:, s + 1, :], in_=X[:, c, :], func=AF.Square), reads=[Xb[c]], writes=[LNTb[s + 1]])

                def st(e, c=c, s=s):
                    last = None
                    for hf in range(2):
                        e.matmul(PC[:, hf * 512:(hf + 1) * 512], lhsT=ONES[:], rhs=LNT[:, s, hf * 512:(hf + 1) * 512], start=(c == 0), stop=(c == 7))
                        last = e.matmul(PD[:, hf * 512:(hf + 1) * 512], lhsT=ONES[:], rhs=LNT[:, s + 1, hf * 512:(hf + 1) * 512], start=(c == 0), stop=(c == 7))
                    return last
                P.op("pe", st, reads=[ONESb, LNTb[s], LNTb[s + 1]], writes=[PCb, PDb])
                oc_global += 1
        P.op("act", lambda e: e.activation(out=TS[:, 0, :], in_=PC[:], func=AF.Identity, scale=1.0 / D), reads=[PCb], writes=[TSb[0]])
        P.op("act", lambda e: e.activation(out=TS[:, 1, :], in_=TS[:, 0, :], func=AF.Square), reads=[TSb[0]], writes=[TSb[1]])
        P.op("dve", lambda e: e.scalar_tensor_tensor(out=TS[:, 1, :], in0=PD[:], scalar=1.0 / D, in1=TS[:, 1, :], op0=ALU.mult, op1=ALU.subtract),
             reads=[PDb, TSb[1]], writes=[TSb[1]])
        P.op("act", lambda e: e.activation(out=TS[:, 1, :], in_=TS[:, 1, :], func=AF.Ln, bias=EPSC[:, 0:1], scale=1.0), reads=[TSb[1], EPSb], writes=[TSb[1]])
        P.op("act", lambda e: e.activation(out=TS[:, 1, :], in_=TS[:, 1, :], func=AF.Exp, scale=-0.5), reads=[TSb[1]], writes=[TSb[1]])
        for c in range(8):
            t = 2 + (c % 2)
            P.op("pool" if c % 2 == 1 else "dve", lambda e, c=c, t=t: e.tensor_tensor(out=TS[:, t, :], in0=X[:, c, :], in1=TS[:, 0, :], op=ALU.subtract),
                 reads=[Xb[c], TSb[0]], writes=[TSb[t]])
            P.op("dve", lambda e, c=c, t=t: e.tensor_tensor(out=TS[:, t, :], in0=TS[:, t, :], in1=TS[:, 1, :], op=ALU.mult),
                 reads=[TSb[t], TSb[1]], writes=[TSb[t]])
            P.op("act", lambda e, c=c, t=t: e.activation(out=X[:, c, :], in_=TS[:, t, :], func=AF.Identity,
                                                         scale=LNP[:, lni + c:lni + c + 1], bias=LNP[:, 64 + lni + c:64 + lni + c + 1]),
                 reads=[TSb[t], LNPb], writes=[Xb[c]])
        if next_mod is not None:
            modulate(*next_mod)

    def ffn(l, bg=None):
        bg = bg if bg is not None else []
        for b in range(11):
            wap, wb = w_get()
            w3 = wap.rearrange("p (k n) -> p k n", n=512)
            for jj in range(2):
                j = 2 * b + jj
                pg, pgb = PSUMS[(jj * 2) % 4]
                pu, pub = PSUMS[(jj * 2 + 1) % 4]
                proj_chunk(pg, pgb, w3, wb, jj * 2, H, Hb, 8)
                proj_chunk(pu, pub, w3, wb, jj * 2 + 1, H, Hb, 8)
                t = jj
                P.op("act", lambda e, pg=pg, t=t: e.activation(out=TS[:, t, :], in_=pg[:], func=AF.Silu), reads=[pgb], writes=[TSb[t]])
                P.op("dve", lambda e, pu=pu, t=t, j=j: e.tensor_tensor(out=BIG[:, j * NT:(j + 1) * NT], in0=pu[:], in1=TS[:, t, :], op=ALU.mult),
                     reads=[pub, TSb[t]], writes=[BIGb[j]])
                replay(bg, 14)
        replay(bg, 10 ** 9)
        BIG3 = BIG[:].rearrange("p (k n) -> p k n", n=NT)
        nm = (l + 1, 0) if l + 1 < DEPTH else None
        outproj_ln(l, 1, BIG3, BIGb, KFF, 8, 128, nm)

    def attention(l):
        j = l // 2
        Q = BIG[:, 0:8 * NT].rearrange("p (k n) -> p k n", n=NT)
        nb = arena_switch(["ROPE", "KT0", "KT1"] + ["VT%d" % i for i in range(12)] + ["PR%d" % i for i in range(4)])
        ROPEb = nb[0]
        KTBb = nb[1:3]
        VTBb = nb[3:15]
        PRb = nb[15:19]
        sp_load(ROPE, d_rope, ROPEb)
        for kv in range(2):
            P.op("pool", lambda e, kv=kv: e.dma_start(out=KTB[:, kv, NT:NT + 512], in_=d_ckT[j, :, kv, :]), writes=[KTBb[kv]], dma=True)
        for t4 in range(4):
            P.op("pool", lambda e, t4=t4: e.dma_start(out=VTB[:, 8 + t4, :], in_=d_cv[j, :, t4, :]), writes=[VTBb[8 + t4]], dma=True)
        cur = None
        for hc in range(10):
            if hc % 4 == 0:
                cur = w_get()
            wap, wb = cur
            w3 = wap.rearrange("p (k n) -> p k n", n=512)
            pt, pbuf = PSUMS[hc % 2]
            proj_chunk(pt, pbuf, w3, wb, hc % 4, H, Hb, 8)
            isk = hc >= 8
            gcol = (2 + j) if isk else j
            P.op("act", lambda e, pt=pt: e.activation(out=TS[:, 2, :], in_=pt[:], func=AF.Copy), reads=[pbuf], writes=[TSb[2]])
            P.op("act", lambda e, pt=pt: e.activation(out=LNT[:, 0, :], in_=pt[:], func=AF.Square), reads=[pbuf], writes=[LNTb[0]])

            def st(e):
                last = None
                for hf in range(2):
                    last = e.matmul(PC[:, hf * 512:(hf + 1) * 512], lhsT=ONES[:], rhs=LNT[:, 0, hf * 512:(hf + 1) * 512], start=True, stop=True)
                return last
            P.op("pe", st, reads=[ONESb, LNTb[0]], writes=[PCb])
            P.op("act", lambda e: e.activation(out=TS[:, 3, :], in_=PC[:], func=AF.Ln, bias=EPSC[:, 1:2], scale=1.0 / 128), reads=[PCb, EPSb], writes=[TSb[3]])
            P.op("act", lambda e: e.activation(out=TS[:, 3, :], in_=TS[:, 3, :], func=AF.Exp, scale=-0.5), reads=[TSb[3]], writes=[TSb[3]])
            P.op("dve", lambda e, gcol=gcol: e.scalar_tensor_tensor(out=TS[:, 2, :], in0=TS[:, 2, :], scalar=GAIN[:, gcol:gcol + 1], in1=TS[:, 3, :],
                                                                    op0=ALU.mult, op1=ALU.mult), reads=[TSb[2], TSb[3], GAINb], writes=[TSb[2]])
            P.op("act", lambda e: e.activation(out=LNT[:, 1, :], in_=TS[:, 2, :], func=AF.Copy), reads=[TSb[2]], writes=[LNTb[1]])

            def rt(e):
                last = None
                for hf in range(2):
                    last = e.matmul(PD[:, hf * 512:(hf + 1) * 512], lhsT=ROTB, rhs=LNT[:, 1, hf * 512:(hf + 1) * 512], start=True, stop=True)
                return last
            P.op("pe", rt, reads=[CBb, LNTb[1]], writes=[PDb])
            P.op("dve", lambda e: e.tensor_tensor(out=TS[:, 0, :], in0=PD[:], in1=ROPE[:, 1, :], op=ALU.mult), reads=[PDb, ROPEb], writes=[TSb[0]])
            P.op("dve", lambda e: e.tensor_tensor(out=TS[:, 2, :], in0=TS[:, 2, :], in1=ROPE[:, 0, :], op=ALU.mult), reads=[TSb[2], ROPEb], writes=[TSb[2]])
            if not isk:
                P.op("dve", lambda e, hc=hc: e.tensor_tensor(out=Q[:, hc, :], in0=TS[:, 2, :], in1=TS[:, 0, :], op=ALU.add),
                     reads=[TSb[2], TSb[0]], writes=[BIGb[hc]])
            else:
                kv = hc - 8
                P.op("dve", lambda e: e.tensor_tensor(out=TS[:, 1, :], in0=TS[:, 2, :], in1=TS[:, 0, :], op=ALU.add), reads=[TSb[2], TSb[0]], writes=[TSb[1]])
                P.op("act", lambda e, kv=kv: e.activation(out=KTB[:, kv, 0:NT], in_=TS[:, 1, :], func=AF.Copy), reads=[TSb[1]], writes=[KTBb[kv]])
                P.op("sp", lambda e, kv=kv: e.dma_start(out=o_k[j, kv], in_=TS[:, 1, :]), reads=[TSb[1]], dma=True)
        wap, wb = cur
        w3 = wap.rearrange("p (k n) -> p k n", n=512)
        for tt in range(8):
            pt, pbuf = PSUMS[tt % 2]

            def mmv(e, tt=tt, pt=pt):
                last = None
                for kc in range(8):
                    last = e.matmul(pt[:, 0:256], lhsT=H[:, kc, tt * 128:(tt + 1) * 128], rhs=w3[:, kc, 256:512], start=(kc == 0), stop=(kc == 7))
                return last
            P.op("pe", mmv, reads=[wb] + Hb, writes=[pbuf])
            s = tt % 2
            P.op("act", lambda e, pt=pt, s=s: e.activation(out=VST[:, s, :], in_=pt[:, 0:256], func=AF.Copy), reads=[pbuf], writes=[VSTb[s]])
            P.op("dve", lambda e, tt=tt, s=s: e.tensor_copy(out=VTB[:, tt, :], in_=VST[:, s, :]), reads=[VSTb[s]], writes=[VTBb[tt]])
            P.op("sp", lambda e, tt=tt, s=s: e.dma_start(out=o_v[j, tt * 128:(tt + 1) * 128, :], in_=VST[:, s, :]), reads=[VSTb[s]], dma=True)
        if DBG and l == 1:
            for h in range(8):
                dbg_bf(o_dQ[h * 128:(h + 1) * 128, :], Q[:, h, :], [BIGb[h]])
            dbg_bf(o_dK, KTB, KTBb)
        PAh = [PA[:, 0:512], PA[:, 512:1024]]
        PAhb = [Buf("PAh0"), Buf("PAh1")]
        m0 = merge("x", PAb)
        for bb in PAhb:
            bb.w, bb.r = dict(m0.w), dict(m0.r)
        pri = 0
        for h in range(8):
            kv = h // 4
            for qh in range(2):
                qs = slice(qh * 512, (qh + 1) * 512)

                def smm(e, kt, h=h, kv=kv, qs=qs):
                    return e.matmul(PAh[kt % 2], lhsT=KTB[:, kv, kt * 128:(kt + 1) * 128], rhs=Q[:, h, qs], start=True, stop=True)
                P.op("pe", lambda e: smm(e, 0), reads=[KTBb[kv], BIGb[h]], writes=[PAhb[0]])
                for kt in range(12):
                    if kt + 1 < 12:
                        P.op("pe", lambda e, kt=kt: smm(e, kt + 1), reads=[KTBb[kv], BIGb[h]], writes=[PAhb[(kt + 1) % 2]])
                    pslot = pri % 4
                    pri += 1
                    for g2 in range(2):
                        qg = qh * 2 + g2
                        P.op("act", lambda e, kt=kt, g2=g2, qg=qg, pslot=pslot: e.activation(
                            out=PR[:, pslot, g2 * 256:(g2 + 1) * 256], in_=PAh[kt % 2][:, g2 * 256:(g2 + 1) * 256], func=AF.Exp,
                            bias=MASK[:, kt * 4 + qg:kt * 4 + qg + 1], scale=ATT_SCALE), reads=[PAhb[kt % 2], MASKb], writes=[PRb[pslot]])

                    def pv(e, kt=kt, kv=kv, qs=qs, pslot=pslot):
                        e.matmul(PB[:, qs], lhsT=VTB[:, kt, kv * 128:(kv + 1) * 128], rhs=PR[:, pslot, :], start=(kt == 0), stop=(kt == 11))
                        return e.matmul(PC[:, qs], lhsT=ONES[:], rhs=PR[:, pslot, :], start=(kt == 0), stop=(kt == 11))
                    P.op("pe", pv, reads=[VTBb[kt], PRb[pslot], ONESb], writes=[PBb, PCb])
                P.op("dve", lambda e, qs=qs: e.reciprocal(out=TS[:, 0, qs], in_=PC[:, qs]), reads=[PCb], writes=[TSb[0]])
                P.op("dve", lambda e, qs=qs, h=h: e.tensor_tensor(out=H[:, h, qs], in0=PB[:, qs], in1=TS[:, 0, qs], op=ALU.mult),
                     reads=[PBb, TSb[0]], writes=[Hb[h]])
        m1 = merge("PA", PAhb[0], PAhb[1])
        PAb.w, PAb.r = m1.w, m1.r
        if DBG and l == 1:
            for h in range(8):
                dbg_bf(o_dO[h * 128:(h + 1) * 128, :], H[:, h, :], [Hb[h]])
        outproj_ln(l, 0, H, Hb, 8, 2, 512, (l, 1))
        if DBG and l == 1:
            dbg_x(o_dXA1)

    def cmul_ops(e_name, out_r, out_i, in_r, in_i, lr, li, ta, tb, reads, writes, tbufs):
        return [
            lambda: P.op(e_name, lambda e: e.tensor_tensor(out=ta, in0=in_r, in1=lr, op=ALU.mult), reads=reads, writes=[tbufs[0]]),
            lambda: P.op(e_name, lambda e: e.tensor_tensor(out=tb, in0=in_i, in1=li, op=ALU.mult), reads=reads, writes=[tbufs[1]]),
            lambda: P.op(e_name, lambda e: e.tensor_tensor(out=ta, in0=ta, in1=tb, op=ALU.subtract), reads=[tbufs[0], tbufs[1]], writes=[tbufs[0]]),
            lambda: P.op(e_name, lambda e: e.tensor_tensor(out=tb, in0=in_r, in1=li, op=ALU.mult), reads=reads, writes=[tbufs[1]]),
            lambda: P.op(e_name, lambda e: e.tensor_tensor(out=out_i, in0=in_i, in1=lr, op=ALU.mult), reads=reads + [tbufs[0]], writes=writes),
            lambda: P.op(e_name, lambda e: e.tensor_tensor(out=out_i, in0=out_i, in1=tb, op=ALU.add), reads=[tbufs[1]] + writes, writes=writes),
            lambda: P.op(e_name, lambda e: e.tensor_copy(out=out_r, in_=ta), reads=[tbufs[0]], writes=writes),
        ]

    def cmul(*a):
        for t in cmul_ops(*a):
            t()

    def interleave(lists):
        n = max(len(L) for L in lists)
        for i in range(n):
            for L in lists:
                if i < len(L):
                    L[i]()

    def lam_compute(A, Ab, Kt, Kb, Tm, Tb_, n_t):
        bufs = [Ab, Kb, Tb_]
        A0, A1, A2 = A[:, 0], A[:, 1], A[:, 2]
        K0, K1 = Kt[:, 0], Kt[:, 1]
        T0, T1, T2, T3, T4, T5 = (Tm[:, i] for i in range(6))

        def tt(o, a, b, op):
            P.op("dve", lambda e: e.tensor_tensor(out=o, in0=a, in1=b, op=op), reads=bufs, writes=bufs)

        def tsc(o, a, s1, s2=None, op0=ALU.mult, op1=ALU.add):
            if s2 is None:
                P.op("dve", lambda e: e.tensor_scalar(out=o, in0=a, scalar1=s1, scalar2=None, op0=op0), reads=bufs, writes=bufs)
            else:
                P.op("dve", lambda e: e.tensor_scalar(out=o, in0=a, scalar1=s1, scalar2=s2, op0=op0, op1=op1), reads=bufs, writes=bufs)

        def stt(o, a, sc, b, op0, op1):
            P.op("dve", lambda e: e.scalar_tensor_tensor(out=o, in0=a, scalar=sc, in1=b, op0=op0, op1=op1), reads=bufs, writes=bufs)

        def horner(t, y, divs, sign):
            tsc(t, y, sign / divs[-1], 1.0)
            for dv in reversed(divs[:-1]):
                tt(t, t, y, ALU.mult)
                tsc(t, t, sign / dv, 1.0)
        tsc(K1, A2, 0.125)
        horner(K0, K1, [1.0, 2.0, 3.0, 4.0, 5.0, 6.0, 7.0, 8.0, 9.0, 10.0, 11.0], 1.0)
        for _ in range(3):
            tt(K0, K0, K0, ALU.mult)
        tt(T0, K0, A0, ALU.mult)
        tt(T1, K0, A1, ALU.mult)
        horner(K0, T0, [2.0, 3.0, 4.0, 5.0, 6.0, 7.0], 1.0)
        tt(T2, K0, T0, ALU.mult)
        tsc(K1, T1, 1.0 / 16.0)
        tt(T3, K1, K1, ALU.mult)
        horner(K0, T3, [6.0, 20.0, 42.0, 72.0, 110.0, 156.0, 210.0], -1.0)
        tt(T4, K0, K1, ALU.mult)
        horner(K0, T3, [12.0, 30.0, 56.0, 90.0, 132.0, 182.0, 240.0], -1.0)
        stt(T5, T3, -0.5, K0, ALU.mult, ALU.mult)
        for _ in range(4):
            tt(K1, T4, T4, ALU.mult)
            stt(K0, T5, 1.0, T4, ALU.add, ALU.mult)
            tsc(T4, K0, 2.0)
            tsc(T5, K1, -2.0)
        stt(K0, T2, 1.0, T5, ALU.add, ALU.mult)
        tt(T0, K0, T2, ALU.add)
        stt(T1, T2, 1.0, T4, ALU.add, ALU.mult)
        tt(T3, A0, A0, ALU.mult)
        tt(T2, A1, A1, ALU.mult)
        tt(T3, T3, T2, ALU.add)
        P.op("dve", lambda e: e.reciprocal(out=T3, in_=T3), reads=bufs, writes=bufs)
        tt(T2, T0, A0, ALU.mult)
        tt(T4, T1, A1, ALU.mult)
        tt(T2, T2, T4, ALU.add)
        tt(K0, T2, T3, ALU.mult)
        tt(T2, T1, A0, ALU.mult)
        tt(T4, T0, A1, ALU.mult)
        tt(T2, T2, T4, ALU.subtract)
        tt(K1, T2, T3, ALU.mult)
        tsc(A0, T0, 1.0, None, op0=ALU.add)
        P.op("dve", lambda e: e.tensor_copy(out=A1, in_=T1), reads=bufs, writes=bufs)

    lamctx = {}
    bgs = {}

    def s5_lambda(l):
        j = l // 2
        nb = arena_switch(["S5C", "S5K", "S5T"])
        S5Cb, S5Kb, S5Tb = nb
        lamctx[l] = nb
        P.op("dve", lambda e: e.memset(ARENA[64:128, 4096:6528], 0.0), writes=[S5Tb])
        sp_load(NATQ[:, 0:2, :], d_s5n[:, j, 0:2].rearrange("g t d p -> g t (d p)"), S5Tb)
        sp_load(NATQ[:, 2, :], d_s5n[:, j, 2].rearrange("g d p -> g (d p)"), S5Tb)
        lam_compute(NATQ[:, 0:3, :], S5Tb, NATQ[:, 3:5, :], S5Tb, NATT, S5Tb, 0)
        for fc in range(8):
            pt, pbuf = PSUMS[2 + fc % 2]

            def mmx(e, fc=fc, pt=pt):
                e.matmul(pt[:, 0:256], lhsT=SELC[:, fc, :], rhs=NATQ128[:, 0:2, :].rearrange("p t x -> p (t x)"), start=True, stop=True)
                return e.matmul(pt[:, 256:512], lhsT=SELC[:, fc, :], rhs=NATQ128[:, 3:5, :].rearrange("p t x -> p (t x)"), start=True, stop=True)
            P.op("pe", mmx, reads=[SELCb, S5Tb], writes=[pbuf])
            P.op("act", lambda e, fc=fc, pt=pt: e.activation(out=S5C[:, :, :, fc, :], in_=pt[:, 0:256].rearrange("p (t d x) -> p t d x", t=2, d=2), func=AF.Copy),
                 reads=[pbuf], writes=[S5Cb])
            P.op("act", lambda e, fc=fc, pt=pt: e.activation(out=S5K[:, :, :, fc, :], in_=pt[:, 256:512].rearrange("p (t d x) -> p t d x", t=2, d=2), func=AF.Copy),
                 reads=[pbuf], writes=[S5Kb])
        slots = (0, 1, 3, 4)
        for ti in range(4):
            for d in range(2):
                for a in range(2):
                    P.op("dve", lambda e, ti=ti, d=d, a=a: e.tensor_scalar(out=XPAD[:, ti * 2 + d, a * 64:(a + 1) * 64], in0=NATQ[:, slots[ti], d * 64:(d + 1) * 64],
                                                                        scalar1=SELB[0:64, 32 + a:33 + a], scalar2=None, op0=ALU.mult),
                         reads=[S5Tb, SELBb], writes=[S5Tb])

        def mmb(e):
            last = None
            for k in range(8):
                last = e.matmul(PA[:, k * 32:(k + 1) * 32], lhsT=XPAD128[:, k, :], rhs=SELB[:, 0:32], start=True, stop=True)
            return last
        P.op("pe", mmb, reads=[S5Tb, SELBb], writes=[PAb])
        P.op("act", lambda e: e.activation(out=S5B[:], in_=PA[:, 0:128].rearrange("p (t d q) -> p t d q", t=2, d=2), func=AF.Copy), reads=[PAb], writes=[S5Bb])
        P.op("act", lambda e: e.activation(out=S5KB[:], in_=PA[:, 128:256].rearrange("p (t d q) -> p t d q", t=2, d=2), func=AF.Copy), reads=[PAb], writes=[S5KBb])
        P.op("dve", lambda e: e.tensor_scalar(out=S5LN[:], in0=S5B[:, 1], scalar1=-1.0, scalar2=None, op0=ALU.mult), reads=[S5Bb], writes=[S5LNb])
        P.op("dve", lambda e: e.tensor_copy(out=S5LT[:], in_=S5B[:]), reads=[S5Bb], writes=[S5LTb])
        for _ in range(3):
            cmul("dve", S5LT[:, 0], S5LT[:, 1], S5LT[:, 0], S5LT[:, 1], S5LT[:, 0], S5LT[:, 1], S5TB[:, 0], S5TB[:, 1], [S5LTb], [S5LTb], [S5TBb, S5TBb])

    def record_lambda(l):
        P.rec = []
        s5_lambda(l)
        ops = P.rec
        P.rec = None
        return ops

    def replay(ops, n):
        k = 0
        while ops and k < n:
            P.op(*ops.pop(0))
            k += 1

    def s5_mixer(l):
        j = l // 2
        SALL = BIG[:, 0:NSLOT * 64].rearrange("p (s r q) -> p s r q", r=2, q=32)
        SBUFS = merge("S", *BIGb[0:17])
        VF0b = merge("VF0", *BIGb[17:22])
        for bb in BIGb:
            bb.w, bb.r = {}, {}
        S5Cb, S5Kb, S5Tb = lamctx[l]
        VF0 = BIG[:, 17 * NT:17 * NT + 9 * 512].rearrange("p (n d r x) -> p n d r x", n=9, d=2, r=2)
        VF = [VF0, VF1]
        VFb = [VF0b, S5Tb]
        KF = [LNT[:, 0:2, :].rearrange("p a (k x) -> p (a k) x", x=128), LNT[:, 2:4, :].rearrange("p a (k x) -> p (a k) x", x=128)]
        KFb = [merge("KF0", LNTb[0], LNTb[1]), merge("KF1", LNTb[2], LNTb[3])]
        UALL = TS[:].bitcast(BF16).rearrange("p a (h n) -> p (a h) n", n=NT)

        def Uap(fc):
            return UALL[:, fc, :], TSb[fc // 2]
        for b in range(2):
            wap, wb = w_get()
            w3 = wap.rearrange("p (k n) -> p k n", n=512)
            for oc in range(4):
                fc = b * 4 + oc
                pt, pbuf = PSUMS[fc % 2]
                proj_chunk(pt, pbuf, w3, wb, oc, H, Hb, 8)
                ua, ub = Uap(fc)
                P.op("act", lambda e, pt=pt, ua=ua: e.activation(out=ua, in_=pt[:], func=AF.Copy), reads=[pbuf], writes=[ub])
        if DBG and l == 0:
            for fc in range(8):
                dbg_bf(o_dU[fc * 128:(fc + 1) * 128, :], UALL[:, fc, :], [TSb[fc // 2]])
        replay(bgs.get(l, []), 10 ** 9)
        WF = [H[:, 0:4, :].rearrange("p k n -> p (k n)").rearrange("p (n d r x) -> p n d r x", n=8, d=2, r=2),
              H[:, 4:8, :].rearrange("p k n -> p (k n)").rearrange("p (n d r x) -> p n d r x", n=8, d=2, r=2)]
        WFb = [merge("WF0", *Hb[0:4]), merge("WF1", *Hb[4:8])]
        if S5STOP == 1:
            raise _Stop()
        P.op("dve", lambda e: e.tensor_copy(out=SALL[:, 0], in_=H0[:, j, 0]), reads=[H0b], writes=[SBUFS])
        P.op("dve", lambda e: e.tensor_copy(out=SALL[:, 2 * C + 1], in_=H0[:, j, 1]), reads=[H0b], writes=[SBUFS])
        PSA = [PA, PB, PC, PD]
        PSAb = [PAb, PBb, PCb, PDb]
        PSAh = [[PSA[q][:, 0:512], PSA[q][:, 512:1024]] for q in range(4)]
        PSAhb = [[merge("PSAh", PSAb[q]), merge("PSAh", PSAb[q])] for q in range(4)]
        ENG = ["dve", "dve"]

        def recur_ops(eng, d, n, views, lr, li, dst, dstbuf, lbufs):
            pi, po = (n - 1) % 2, n % 2
            a_, x_ = views
            sin_ = WST[:, pi, d].rearrange("p r (a x) -> p r a x", a=a_)
            ta4 = WTA[:, d, 0].rearrange("p r (a x) -> p r a x", a=a_)
            tb4 = WTA[:, d, 1].rearrange("p r (a x) -> p r a x", a=a_)
            return [
                lambda: P.op(eng, lambda e: e.tensor_tensor(out=ta4, in0=sin_, in1=lr, op=ALU.mult), reads=[WSTb[pi][d]] + lbufs, writes=[WTAb[d][0]]),
                lambda: P.op(eng, lambda e: e.tensor_tensor(out=tb4, in0=sin_, in1=li, op=ALU.mult), reads=[WSTb[pi][d]] + lbufs, writes=[WTAb[d][1]]),
                lambda: P.op(eng, lambda e: e.tensor_tensor(out=WST[:, po, d, 0, :], in0=WTA[:, d, 0, 0, :], in1=WTA[:, d, 1, 1, :], op=ALU.subtract),
                             reads=[WTAb[d][0], WTAb[d][1]], writes=[WSTb[po][d]]),
                lambda: P.op(eng, lambda e: e.tensor_tensor(out=WST[:, po, d, 1, :], in0=WTA[:, d, 1, 0, :], in1=WTA[:, d, 0, 1, :], op=ALU.add),
                             reads=[WTAb[d][0], WTAb[d][1]], writes=[WSTb[po][d]]),
                lambda: P.op("act", lambda e: e.activation(out=dst[:, n, d], in_=WST[:, po, d], func=AF.Copy), reads=[WSTb[po][d]], writes=[dstbuf]),
            ]

        pend = []
        for fc in range(8):
            s = fc % 2
            sp_load(LDW[:, s, :], d_s5w0[j, fc], LDWb[s])
            L4 = LDW[:, s, :].rearrange("p (d r x) -> p d r x", d=2, r=2)
            chains = []
            for d in range(2):
                eng = ENG[d]
                lr = S5C[:, 0, d, fc, :].unsqueeze(1).unsqueeze(1).broadcast_to([128, 2, 2, 64])
                li = S5C[:, 1, d, fc, :].unsqueeze(1).unsqueeze(1).broadcast_to([128, 2, 2, 64])
                kr = S5K[:, 0, d, fc, :].unsqueeze(1).broadcast_to([128, 2, 64])
                ki = S5K[:, 1, d, fc, :].unsqueeze(1).broadcast_to([128, 2, 64])
                v3 = lambda ap: ap.rearrange("p (a x) -> p a x", a=2)
                ch = cmul_ops(eng, v3(WST[:, 0, d, 0, :]), v3(WST[:, 0, d, 1, :]), v3(L4[:, d, 0, :]), v3(L4[:, d, 1, :]), kr, ki,
                              v3(WTA[:, d, 0, 0, :]), v3(WTA[:, d, 1, 0, :]), [LDWb[s], S5Kb, WSTb[0][d]], [WSTb[0][d]], [WTAb[d][0], WTAb[d][1]])
                ch.append(lambda d=d, s=s: P.op("act", lambda e: e.activation(out=WF[s][:, 0, d], in_=WST[:, 0, d], func=AF.Copy), reads=[WSTb[0][d]], writes=[WFb[s]]))
                for n in range(1, T):
                    ch += recur_ops(eng, d, n, (2, 64), lr, li, WF[s], WFb[s], [S5Cb])
                chains.append(ch)
            interleave(chains + [pend])
            ua, ub = Uap(fc)
            u3 = ua.rearrange("p (c t) -> p c t", t=T)
            hf = fc % 2

            def mmA(e, s=s, u3=u3, hf=hf):
                last = None
                for d in range(2):
                    for ri in range(2):
                        for jj in range(T):
                            n = (T - 1 - jj) if d == 0 else jj
                            for q in range(4):
                                last = e.matmul(PSAh[q][hf][:, (d * 2 + ri) * 128:(d * 2 + ri + 1) * 128], lhsT=WF[s][32 * q:32 * q + 32, n, d, ri, :],
                                                rhs=u3[32 * q:32 * q + 32, :, jj], start=(jj == 0), stop=(jj == T - 1), tile_position=(32 * q, 0))
                return last
            P.op("pe", mmA, reads=[WFb[s], ub], writes=[PSAhb[q][hf] for q in range(4)])
            pend = []
            for q in range(4):
                qq = fc * 4 + q
                for ri in range(2):
                    src = PSAh[q][hf].rearrange("p (d r c) -> p d r c", d=2, r=2)[:, :, ri, :]
                    dst = SALL[:, 1:2 * C + 1, ri, qq].rearrange("p (d c) -> p d c", d=2)
                    pend.append(lambda src=src, dst=dst, q=q, hf=hf: P.op("act", lambda e: e.activation(out=dst, in_=src, func=AF.Copy), reads=[PSAhb[q][hf]], writes=[SBUFS]))
        for t_ in pend:
            t_()
        for q in range(4):
            m_ = merge("PS", PSAhb[q][0], PSAhb[q][1])
            PSAb[q].w, PSAb[q].r = m_.w, m_.r
        for i in range(4):
            Hb[i].w, Hb[i].r = dict(WFb[0].w), dict(WFb[0].r)
            Hb[4 + i].w, Hb[4 + i].r = dict(WFb[1].w), dict(WFb[1].r)
        if S5STOP == 2:
            raise _Stop()
        LTr_b = S5LT[:, 0].unsqueeze(2).broadcast_to([128, 2, 2, 32])
        LTi = S5LT[:, 1]
        P.op("dve", lambda e: e.tensor_copy(out=RST[:, 1], in_=H0[:, j]), reads=[H0b], writes=[RSTb[1]])
        modgen = compute_mod_gen(l + 1) if l + 1 < DEPTH else iter(())
        SIN = merge("SIN", SBUFS)
        SOUT = merge("SOUT", SBUFS)
        for i in range(C):
            if i % 10 == 5:
                next(modgen, None)
            pp, pc = (i + 1) % 2, i % 2
            a0 = 1 + i
            stp = 2 * C - 1 - 2 * i
            sl = slice(a0, a0 + stp + 1, stp)
            if i > 0 and i % 32 == 0:
                P.op("dve", lambda e, pp=pp: e.tensor_scalar(out=RST[:, pp], in0=RST[:, pp], scalar1=KEEP[:, 0:1], scalar2=None, op0=ALU.mult),
                     reads=[RSTb[pp], KEEPb], writes=[RSTb[pp]])
            P.op("dve", lambda e, pp=pp: e.tensor_tensor(out=RTM[:, 0], in0=RST[:, pp], in1=LTr_b, op=ALU.mult), reads=[RSTb[pp], S5LTb], writes=[RTMb[0]])
            P.op("dve", lambda e, pp=pp: e.scalar_tensor_tensor(out=RTM[:, 1, :, 0, :], in0=RST[:, pp, :, 1, :], scalar=-1.0, in1=LTi, op0=ALU.mult, op1=ALU.mult),
                 reads=[RSTb[pp], S5LTb], writes=[RTMb[1]])
            P.op("dve", lambda e, pp=pp: e.tensor_tensor(out=RTM[:, 1, :, 1, :], in0=RST[:, pp, :, 0, :], in1=LTi, op=ALU.mult), reads=[RSTb[pp], S5LTb], writes=[RTMb[2]])
            P.op("dve", lambda e: e.tensor_tensor(out=RTM[:, 0], in0=RTM[:, 0], in1=RTM[:, 1], op=ALU.add), reads=[RTMb[0], RTMb[1], RTMb[2]], writes=[RTMb[0]])
            P.op("dve", lambda e, pc=pc, sl=sl: e.tensor_tensor(out=RST[:, pc], in0=RTM[:, 0], in1=SALL[:, sl], op=ALU.add), reads=[RTMb[0], SIN], writes=[RSTb[pc]])
            P.op("act", lambda e, pc=pc, sl=sl: e.activation(out=SALL[:, sl], in_=RST[:, pc], func=AF.Copy), reads=[RSTb[pc]], writes=[SOUT])
            if i % 32 == 31:
                k = i // 32
                P.op("act", lambda e, pc=pc, k=k: e.activation(out=STG[:, k], in_=RST[:, pc], func=AF.Copy), reads=[RSTb[pc]], writes=[STGb])
        P.op("sp", lambda e: e.dma_start(out=o_st[j], in_=STG[:].rearrange("p k d r q -> p (k d r q)")), reads=[STGb], dma=True)
        for _ in modgen:
            pass
        m_ = merge("S", SIN, SOUT)
        SBUFS.w, SBUFS.r = m_.w, m_.r
        for s0 in (32, C + 1 + 32):
            P.op("dve", lambda e, s0=s0: e.tensor_scalar(out=SALL[:, s0:s0 + 65:32], in0=SALL[:, s0:s0 + 65:32], scalar1=KEEP[:, 0:1], scalar2=None, op0=ALU.mult),
                 reads=[SBUFS, KEEPb], writes=[SBUFS])
        if S5STOP == 3:
            raise _Stop()
        Z = H
        kcnt = 0
        pend = []
        KPSb = [merge("KPS", PSUMS[2 + b_ // 2][1]) for b_ in range(4)]
        for fc in range(8):
            s = fc % 2
            sp_load(LDV[:, s, :], d_s5v0[j, fc], LDVb[s])
            sp_load(LDN[:, s, :], d_s5bn[j, fc], LDNb[s])
            V4 = LDV[:, s, :].rearrange("p (d r x) -> p d r x", d=2, r=2)
            N4 = LDN[:, s, :].rearrange("p (d r x) -> p d r x", d=2, r=2)
            q0 = fc * 4
            chains = []
            for d in range(2):
                eng = ENG[d]
                krb = S5KB[:, 0, d, q0:q0 + 4].unsqueeze(2).broadcast_to([128, 4, 32])
                kib = S5KB[:, 1, d, q0:q0 + 4].unsqueeze(2).broadcast_to([128, 4, 32])
                v4 = lambda ap: ap.rearrange("p (a x) -> p a x", a=4)
                ch = cmul_ops(eng, v4(WST[:, 0, d, 0, :]), v4(WST[:, 0, d, 1, :]), v4(N4[:, d, 0, :]), v4(N4[:, d, 1, :]), krb, kib,
                              v4(WTA[:, d, 0, 0, :]), v4(WTA[:, d, 1, 0, :]), [LDNb[s], S5KBb, WSTb[0][d]], [WSTb[0][d]], [WTAb[d][0], WTAb[d][1]])
                ch.append(lambda d=d, s=s: P.op("act", lambda e: e.activation(out=BNB[:, s, d], in_=WST[:, 0, d], func=AF.Copy), reads=[WSTb[0][d]], writes=[BNBb[s]]))
                ch.append(lambda d=d, eng=eng, V4=V4, s=s: P.op(eng, lambda e: e.tensor_copy(out=WST[:, 0, d, 0, :], in_=V4[:, d, 0, :]), reads=[LDVb[s]], writes=[WSTb[0][d]]))
                ch.append(lambda d=d, eng=eng, V4=V4, s=s: P.op(eng, lambda e: e.tensor_scalar(out=WST[:, 0, d, 1, :], in0=V4[:, d, 1, :], scalar1=-1.0, scalar2=None, op0=ALU.mult), reads=[LDVb[s]], writes=[WSTb[0][d]]))
                ch.append(lambda d=d, s=s: P.op("act", lambda e: e.activation(out=VF[s][:, 0, d], in_=WST[:, 0, d], func=AF.Copy), reads=[WSTb[0][d]], writes=[VFb[s]]))
                lrb = S5B[:, 0, d, q0:q0 + 4].unsqueeze(1).unsqueeze(3).broadcast_to([128, 2, 4, 32])
                lib = S5LN[:, d, q0:q0 + 4].unsqueeze(1).unsqueeze(3).broadcast_to([128, 2, 4, 32])
                for n in range(1, T + 1):
                    ch += recur_ops(eng, d, n, (4, 32), lrb, lib, VF[s], VFb[s], [S5Bb, S5LNb])
                chains.append(ch)
            interleave(chains + [pend])
            pend = []
            klist = [(dl, d) for dl in range(T) for d in range(2) if not (dl == 0 and d == 1)]
            for g0 in range(0, len(klist), 4):
                grp = klist[g0:g0 + 4]
                bank = kcnt % 4
                kcnt += 1
                pk = PSUMS[2 + bank // 2][0]
                pkb = KPSb[bank]
                cb = (bank % 2) * 512

                def mmK(e, grp=grp, pk=pk, cb=cb, s=s):
                    last = None
                    for gi, (dl, d) in enumerate(grp):
                        col = cb + gi * 128
                        dirs = (0, 1) if dl == 0 else (d,)
                        nmm = len(dirs) * 2
                        i_ = 0
                        for dd in dirs:
                            for ri in range(2):
                                last = e.matmul(pk[:, col:col + 128], lhsT=BNB[:, s, dd, ri, :], rhs=VF[s][:, dl, dd, ri, :], start=(i_ == 0), stop=(i_ == nmm - 1))
                                i_ += 1
                    return last
                P.op("pe", mmK, reads=[BNBb[s], VFb[s]], writes=[pkb])
                for gi, (dl, d) in enumerate(grp):
                    col = cb + gi * 128
                    idx = 7 + dl if d == 0 else 7 - dl
                    if dl == 0:
                        pend.append(lambda col=col, pk=pk, pkb=pkb: P.op("dve", lambda e: e.tensor_tensor(out=KTMP[:], in0=pk[:, col:col + 128], in1=BDMASK, op=ALU.mult), reads=[CONSTb], writes=[KTMPb, pkb]))
                        pend.append(lambda fc=fc, s=s: P.op("dve", lambda e: e.scalar_tensor_tensor(out=KF[s][:, 7, :], in0=IDENT_F, scalar=S5D[:, j, fc:fc + 1], in1=KTMP[:], op0=ALU.mult, op1=ALU.add),
                                                           reads=[CONSTb, S5Db, KTMPb], writes=[KFb[s]]))
                    else:
                        pend.append(lambda col=col, idx=idx, pk=pk, s=s, pkb=pkb: P.op("dve", lambda e: e.tensor_tensor(out=KF[s][:, idx, :], in0=pk[:, col:col + 128], in1=BDMASK, op=ALU.mult),
                                                                                    reads=[CONSTb], writes=[KFb[s], pkb]))
            ua, ub = Uap(fc)
            u3 = ua.rearrange("p (c t) -> p c t", t=T)
            pt, pbuf = PSUMS[fc % 2]

            def mmC(e, fc=fc, pt=pt, u3=u3, s=s):
                last = None
                for jj in range(T):
                    o = pt[:, jj * 128:(jj + 1) * 128]
                    for j2 in range(T):
                        last = e.matmul(o, lhsT=KF[s][:, 7 + jj - j2, :], rhs=u3[:, :, j2], start=(j2 == 0), stop=False)
                    for d in range(2):
                        n = jj + 1 if d == 0 else T - jj
                        s0 = 0 if d == 0 else C + 2
                        for ri in range(2):
                            lastq = (d == 1 and ri == 1)
                            for q in range(4):
                                qq = fc * 4 + q
                                oq = pt[32 * q:32 * q + 32, jj * 128:(jj + 1) * 128]
                                last = e.matmul(oq, lhsT=VF[s][:, n, d, ri, 32 * q:32 * q + 32], rhs=SALL[:, s0:s0 + C, ri, qq], start=False, stop=lastq, tile_position=(0, 32 * q))
                return last
            def fin(mmC=mmC, s=s, ub=ub, pbuf=pbuf, fc=fc, pt=pt):
                P.op("pe", mmC, reads=[KFb[s], VFb[s], ub, SBUFS], writes=[pbuf])
                P.op("act", lambda e: e.activation(out=Z[:, fc, :].rearrange("p (c t) -> p t c", t=T), in_=pt[:].rearrange("p (t c) -> p t c", t=T),
                                                   func=AF.Gelu_apprx_tanh), reads=[pbuf], writes=[Hb[fc]])
            pend.append(fin)
        for t_ in pend:
            t_()
        if DBG and l == 0:
            for fc in range(8):
                dbg_bf(o_dZ[fc * 128:(fc + 1) * 128, :], H[:, fc, :], [Hb[fc]])
        if S5STOP == 4:
            raise _Stop()
        for t_ in range(2):
            m_ = merge("PS", KPSb[2 * t_], KPSb[2 * t_ + 1])
            PSUMS[2 + t_][1].w, PSUMS[2 + t_][1].r = m_.w, m_.r
        for bb in BIGb[0:17]:
            bb.w, bb.r = dict(SBUFS.w), dict(SBUFS.r)
        for bb in BIGb[17:22]:
            bb.w, bb.r = dict(VFb[0].w), dict(VFb[0].r)
        for i in range(2):
            LNTb[i].w, LNTb[i].r = dict(KFb[0].w), dict(KFb[0].r)
            LNTb[2 + i].w, LNTb[2 + i].r = dict(KFb[1].w), dict(KFb[1].r)
        G3 = BIG[:, 0:8 * NT].rearrange("p (k n) -> p k n", n=NT)
        for b in range(4):
            wap, wb = w_get()
            w3 = wap.rearrange("p (k n) -> p k n", n=512)
            for jj in range(2):
                jc = 2 * b + jj
                pv_, pvb = PSUMS[(jj * 2) % 4]
                pg, pgb = PSUMS[(jj * 2 + 1) % 4]
                proj_chunk(pv_, pvb, w3, wb, jj * 2, H, Hb, 8)
                proj_chunk(pg, pgb, w3, wb, jj * 2 + 1, H, Hb, 8)
                t = jj
                P.op("act", lambda e, pg=pg, t=t: e.activation(out=TS[:, t, :], in_=pg[:], func=AF.Sigmoid), reads=[pgb], writes=[TSb[t]])
                P.op("dve", lambda e, pv_=pv_, t=t, jc=jc: e.tensor_tensor(out=G3[:, jc, :], in0=pv_[:], in1=TS[:, t, :], op=ALU.mult),
                     reads=[pvb, TSb[t]], writes=[BIGb[jc]])
        outproj_ln(l, 0, G3, BIGb[0:8], 8, 2, 512, (l, 1))
        if DBG and l == 0:
            dbg_x(o_dXA)


    bg0 = record_lambda(0)
    replay(bg0, 10 ** 9)
    compute_mod(0)
    modulate(0, 0)
    try:
        for l in range(DEPTH):
            if l >= STAGE:
                break
            if l % 2 == 0:
                s5_mixer(l)
            else:
                attention(l)
            bg = []
            if l + 1 < DEPTH and l % 2 == 1:
                compute_mod(l + 1)
                bg = record_lambda(l + 1)
            ffn(l, bg)
    except _Stop:
        pass
    for c in range(8):
        P.op("sp", lambda e, c=c: e.dma_start(out=o_yT[c * 128:(c + 1) * 128, :], in_=X[:, c, :]), reads=[Xb[c]], dma=True)
    P.finish()
    return nc


_CACHE = {}


def kernel(**inp):
    inp = {k: np.asarray(v) for k, v in inp.items()}
    f32 = np.float32
    wall = build_wall(inp)
    nblk = wall.shape[0]
    s5h = s5_host_layout(inp)
    cos, sin = rope_tables()
    rope_s = np.ascontiguousarray(np.stack([cos, sin], 1)).astype(f32)
    rope_p = np.ascontiguousarray(np.stack([np.ones_like(cos), np.zeros_like(sin)], 1)).astype(f32)
    consts = const_tables()
    bmodT = np.ascontiguousarray(inp["b_mod"].reshape(DEPTH, 48, 128).transpose(2, 0, 1).reshape(128, DEPTH * 48)).astype(f32)
    lng = inp["ln_g"].reshape(DEPTH * 2 * 8, 128).T
    lnb = inp["ln_b"].reshape(DEPTH * 2 * 8, 128).T
    lnT = np.ascontiguousarray(np.concatenate([lng, lnb], 1)).astype(f32)
    gain = np.ascontiguousarray(np.stack([inp["q_norm_g"][0], inp["q_norm_g"][1], inp["k_norm_g"][0], inp["k_norm_g"][1]], 1)).astype(f32)
    mask_s = np.zeros((128, 48), f32)
    mask_p = np.full((12, 4), -30000.0, f32)
    for kt in range(8):
        mask_p[kt, kt // 2] = 0.0
    mask_p = np.ascontiguousarray(np.broadcast_to(mask_p.reshape(1, 48), (128, 48))).astype(f32)
    in_maps = []
    for core in range(8):
        m = dict(wall=wall, bmodT=bmodT, lnT=lnT, consts=consts, gain=gain, **s5h)
        if core < 4:
            b = core
            m["xT"] = np.ascontiguousarray(inp["x_sample"][b].T)
            cvec = inp["c"][b]
            m["rope"] = rope_s
            m["maskb"] = mask_s
            m["keep"] = np.ones((128, 1), f32)
            m["ckT"] = np.ascontiguousarray(inp["cache_k"][b].transpose(0, 3, 2, 1))
            m["cv"] = np.ascontiguousarray(inp["cache_v"][b].reshape(2, 4, 128, 256).transpose(0, 2, 1, 3))
            m["h0"] = h0_layout(inp["state_s5"][b])
        else:
            s0 = (core - 4) * 4
            m["xT"] = np.ascontiguousarray(inp["x_prompt"][s0:s0 + 4].reshape(NT, D).T)
            cvec = inp["c_ctx"]
            m["rope"] = rope_p
            m["maskb"] = mask_p
            m["keep"] = np.zeros((128, 1), f32)
            m["ckT"] = np.zeros((2, 128, 2, 512), f32)
            m["cv"] = np.zeros((2, 128, 4, 256), f32)
            m["h0"] = np.zeros((128, 2, 2, 2, 32), f32)
        m["cT"] = np.ascontiguousarray(cvec.reshape(8, 128).T).astype(f32)
        in_maps.append({k: np.ascontiguousarray(v, dtype=f32) for k, v in m.items()})
    if "nc" not in _CACHE:
        _CACHE["nc"] = build_program(nblk)
    nc = _CACHE["nc"]
    _CACHE.pop("nc")
    res = run_bass_kernel_spmd(nc, in_maps, core_ids=list(range(8)))
    R = res.results
    if DBG:
        _CACHE["dbg"] = {"c%d_%s" % (ci, k): R[ci][k] for ci in (0, 4) for k in R[ci] if k.startswith("d")}
    y_sample = np.stack([R[b]["yT"].T for b in range(4)], 0).astype(f32)
    y_prompt = np.concatenate([R[4 + i]["yT"].T.reshape(4, 256, D) for i in range(4)], 0).astype(f32)
    nk = np.zeros((16, 2, 256, 2, 128), f32)
    nv = np.zeros((16, 2, 256, 2, 128), f32)
    ns = np.zeros((16, 2, 2, 2, 64, 64), f32)
    for i in range(4):
        r = R[4 + i]
        ko = r["kout"]
        vo = r["vout"]
        so = r["stout"].reshape(2, 2, 64, 4, 2, 2, 32)
        for s in range(4):
            bidx = i * 4 + s
            nk[bidx] = ko[:, :, :, s * 256:(s + 1) * 256].transpose(0, 3, 1, 2)
            nv[bidx] = vo[:, s * 256:(s + 1) * 256, :].reshape(2, 256, 2, 128)
            for d in range(2):
                k = s if d == 0 else 3 - s
                blk = so[:, :, :, k, d, :, :]
                ns[bidx, :, d] = blk.transpose(0, 3, 4, 1, 2).reshape(2, 2, 64, 64)
    return (y_prompt, y_sample, nk, nv, ns)
```

```python
import os
import numpy as np
import concourse.bass as bass
import concourse.mybir as mybir
from concourse.bass_utils import run_bass_kernel_spmd

F32 = mybir.dt.float32
BF16 = mybir.dt.bfloat16
AF = mybir.ActivationFunctionType
ALU = mybir.AluOpType

D = 1024
NT = 1024
DEPTH = 4
DFF = 2816
KFF = 22
T = 8
C = NT // T
NSLOT = 2 * (C + 1)
ALPHA = (2.0 * DEPTH) ** 0.25
LN_EPS = 1e-6
RMS_EPS = 1e-6
ATT_SCALE = 128 ** -0.5
WCOLS = 4096
NWSLOT = 3
EPOCH = 16000
NDMA = 8
MAGIC = 12582912.0
STAGE = int(os.environ.get("K_STAGE", "99"))
DBG = int(os.environ.get("K_DBG", "0"))
S5STOP = int(os.environ.get("K_S5STOP", "0"))
KVAR = int(os.environ.get("K_VAR", "0"))


class _Stop(Exception):
    pass


class Buf:
    __slots__ = ("w", "r", "name")

    def __init__(self, name=""):
        self.w = {}
        self.r = {}
        self.name = name


def merge(name, *olds):
    b = Buf(name)
    for o in olds:
        for k, v in list(o.w.items()) + list(o.r.items()):
            b.r[k] = max(b.r.get(k, 0), v)
            b.w[k] = max(b.w.get(k, 0), v)
    return b


class Prog:
    def __init__(self, nc):
        self.nc = nc
        self.eng = {"pe": nc.tensor, "act": nc.scalar, "dve": nc.vector, "pool": nc.gpsimd, "sp": nc.sync}
        self.cnt = {}
        self.semlist = {}
        self.known = {e: {} for e in self.eng}
        self.allsems = []
        self.rec = None
        for k in ["pe", "act", "dve", "pool"]:
            self.cnt[k] = 0
            self.semlist[k] = []
        for pre in ("q", "g"):
            for i in range(NDMA):
                k = "%s%d" % (pre, i)
                self.cnt[k] = 0
                self.semlist[k] = [self._newsem(k)]
        self.dma_rr = {"q": 0, "g": 0}

    def _newsem(self, name):
        cm = self.nc.semaphore("s_%s_%d" % (name, len(self.allsems)))
        s = cm.__enter__()
        self.allsems.append(s)
        return s

    def _semval(self, k, v):
        if k[0] in "qg":
            return self.semlist[k][0], v
        idx = (v - 1) // EPOCH
        while len(self.semlist[k]) <= idx:
            self.semlist[k].append(self._newsem(k))
        return self.semlist[k][idx], v - idx * EPOCH

    def op(self, e, fn, reads=(), writes=(), dma=False):
        if self.rec is not None:
            self.rec.append((e, fn, tuple(reads), tuple(writes), dma))
            return 0
        deps = {}
        for b in reads:
            for k, v in b.w.items():
                if v > deps.get(k, 0):
                    deps[k] = v
        for b in writes:
            for k, v in b.w.items():
                if v > deps.get(k, 0):
                    deps[k] = v
            for k, v in b.r.items():
                if v > deps.get(k, 0):
                    deps[k] = v
        kn = self.known[e]
        for k, v in deps.items():
            if k == e and e == "pe":
                continue
            if kn.get(k, 0) >= v:
                continue
            s, sv = self._semval(k, v)
            self.eng[e].wait_ge(s, sv)
            kn[k] = v
        inst = fn(self.eng[e])
        if dma:
            pre = "g" if e == "pool" else "q"
            key = "%s%d" % (pre, self.dma_rr[pre])
            self.dma_rr[pre] = (self.dma_rr[pre] + 1) % NDMA
            self.cnt[key] += 16
            val = self.cnt[key]
            inst.then_inc(self.semlist[key][0], 16)
        else:
            key = e
            self.cnt[e] += 1
            val = self.cnt[e]
            s, sv = self._semval(e, val)
            inst.then_inc(s, 1)
        for b in reads:
            if val > b.r.get(key, 0):
                b.r[key] = val
        for b in writes:
            b.w = {key: val}
            b.r = {}
        return val

    def finish(self):
        sp = self.eng["sp"]
        for k, v in self.cnt.items():
            if v > 0:
                s, sv = self._semval(k, v)
                sp.wait_ge(s, sv)
        for s in self.allsems:
            sp.sem_clear(s)


def _blk(W, cols):
    kin = W.shape[0]
    kc = kin // 128
    a = W[:, cols].reshape(kc, 128, len(cols)).transpose(1, 0, 2).reshape(128, kc * len(cols))
    out = np.zeros((128, WCOLS), np.float32)
    out[:, :a.shape[1]] = a
    return out


def _r(a, n):
    return np.arange(a, a + n)


def mod_blocks(w_mod_l):
    return [_blk(w_mod_l, _r(b * 512, 512)) for b in range(12)]


def ffn_blocks(w_in, w_out):
    bl = []
    for b in range(11):
        j0, j1 = 2 * b, 2 * b + 1
        cols = np.concatenate([_r(j0 * 128, 128), _r(DFF + j0 * 128, 128), _r(j1 * 128, 128), _r(DFF + j1 * 128, 128)])
        bl.append(_blk(w_in, cols))
    for c in range(8):
        bl.append(_blk(w_out, _r(c * 128, 128)))
    return bl


def s5_blocks(w_in, w_glu, w_out):
    bl = [_blk(w_in, _r(b * 512, 512)) for b in range(2)]
    for b in range(4):
        j0, j1 = 2 * b, 2 * b + 1
        cols = np.concatenate([_r(j0 * 128, 128), _r(D + j0 * 128, 128), _r(j1 * 128, 128), _r(D + j1 * 128, 128)])
        bl.append(_blk(w_glu, cols))
    bl += [_blk(w_out, _r(b * 512, 512)) for b in range(2)]
    return bl


def attn_blocks(w_qkv, w_o):
    bl = [_blk(w_qkv, _r(b * 512, 512)) for b in range(3)]
    bl += [_blk(w_o, _r(b * 512, 512)) for b in range(2)]
    return bl


def build_wall(inp):
    bl = []
    bl += mod_blocks(inp["w_mod"][0])
    for l in range(DEPTH):
        j = l // 2
        if l % 2 == 0:
            sb_ = s5_blocks(inp["w_s5_in"][j], inp["w_s5_glu"][j], inp["w_s5_out"][j])
            bl += sb_[0:2]
            if l + 1 < DEPTH:
                bl += mod_blocks(inp["w_mod"][l + 1])
            bl += sb_[2:]
        else:
            bl += attn_blocks(inp["w_qkv"][j], inp["w_o"][j])
            if l + 1 < DEPTH:
                bl += mod_blocks(inp["w_mod"][l + 1])
        bl += ffn_blocks(inp["w_ffn_in"][l], inp["w_ffn_out"][l])
    return np.ascontiguousarray(np.stack(bl, 0))


def s5_host_layout(inp):
    out = {}
    G, P, H = 64, 64, 16
    def c_lay(a):
        a5 = a.reshape(2, 2, 8, 8, P)
        a5 = np.broadcast_to(a5[:, :, :, :, None, :], (2, 2, 8, 8, H, P))
        return np.ascontiguousarray(a5.transpose(3, 4, 0, 1, 2, 5).reshape(128, 2, 2, 8, P))
    def b_lay(a):
        a5 = a.reshape(2, 2, 32, 2, P)
        return np.ascontiguousarray(a5.transpose(3, 4, 0, 1, 2).reshape(128, 2, 2, 32))
    ld = np.broadcast_to(inp["s5_log_dt"][:, :, :, None], (2, 2, G, P))
    nat = np.stack([inp["s5_a_re"], inp["s5_a_im"], ld], 0)
    out["s5n"] = np.ascontiguousarray(nat.transpose(3, 1, 0, 2, 4))
    selc = np.zeros((128, 8, 8, 16), np.float32)
    for g in range(64):
        selc[g, g // 8, g % 8, :] = 1.0
    out["selc"] = selc.reshape(128, 8, 128)
    selb = np.zeros((128, 34), np.float32)
    for g in range(64):
        selb[g, g // 2] = 1.0
        selb[g, 32 + (g % 2)] = 1.0
    out["selb"] = selb
    b = np.stack([inp["s5_b_re"], inp["s5_b_im"]], 2)
    b8 = b.reshape(2, 2, 2, 8, 4, 2, P, H)
    w0 = np.zeros((2, 8, 4, 2, H, 2, 2, 2, P), np.float32)
    bn = np.zeros((2, 8, 2, P, 2, 2, 4, 2, H), np.float32)
    for a in range(2):
        w0[:, :, :, a, :, :, :, a, :] = b8[:, :, :, :, :, a].transpose(0, 3, 4, 6, 1, 2, 5)
        bn[:, :, a, :, :, :, :, a, :] = b8[:, :, :, :, :, a].transpose(0, 3, 5, 1, 2, 4, 6)
    out["s5w0"] = np.ascontiguousarray(w0.reshape(2, 8, 128, 512))
    out["s5bn"] = np.ascontiguousarray(bn.reshape(2, 8, 128, 512))
    c = np.stack([inp["s5_c_re"], inp["s5_c_im"]], 2)
    c8 = c.reshape(2, 2, 2, 8, 4, 2, H, P)
    v0 = np.zeros((2, 8, 2, P, 2, 2, 4, 2, H), np.float32)
    for a in range(2):
        v0[:, :, a, :, :, :, :, a, :] = c8[:, :, :, :, :, a].transpose(0, 3, 6, 1, 2, 4, 5)
    out["s5v0"] = np.ascontiguousarray(v0.reshape(2, 8, 128, 512))
    out["s5d"] = np.ascontiguousarray(inp["s5_d"].reshape(2, 8, 128).transpose(2, 0, 1))
    return out


def h0_layout(st):
    s = st.reshape(2, 2, 2, 32, 2, 64)
    return np.ascontiguousarray(s.transpose(4, 5, 0, 1, 2, 3).reshape(128, 2, 2, 2, 32))


def rope_tables():
    l = np.arange(NT)
    row = (l // 64).astype(np.float32)
    col = (l % 64).astype(np.float32)
    inv = (np.float32(10000.0) ** (-np.arange(32, dtype=np.float32) / np.float32(32))).astype(np.float32)
    ar = row[None, :] * inv[:, None]
    ac = col[None, :] * inv[:, None]
    cos = np.concatenate([np.cos(ar), np.cos(ar), np.cos(ac), np.cos(ac)], 0).astype(np.float32)
    sin = np.concatenate([np.sin(ar), np.sin(ar), np.sin(ac), np.sin(ac)], 0).astype(np.float32)
    return cos, sin


def const_tables():
    ident = np.eye(128, dtype=np.float32)
    bd = np.kron(np.eye(8, dtype=np.float32), np.ones((16, 16), np.float32))
    rot = np.zeros((128, 128), np.float32)
    for base in (0, 64):
        for i in range(32):
            rot[base + 32 + i, base + i] = -1.0
            rot[base + i, base + 32 + i] = 1.0
    return np.ascontiguousarray(np.stack([ident, bd, rot], 1))


def build_program(nblk):
    nc = bass.Bass("TRN2", target_bir_lowering=False)
    P = Prog(nc)

    def din(name, shape):
        return nc.dram_tensor(name, list(shape), F32, kind="ExternalInput").ap()

    def dout(name, shape):
        return nc.dram_tensor(name, list(shape), F32, kind="ExternalOutput").ap()

    d_xT = din("xT", [D, NT])
    d_cT = din("cT", [128, 8])
    d_wall = din("wall", [nblk, 128, WCOLS])
    d_bmod = din("bmodT", [128, DEPTH * 48])
    d_ln = din("lnT", [128, 128])
    d_rope = din("rope", [128, 2, NT])
    d_mask = din("maskb", [128, 48])
    d_keep = din("keep", [128, 1])
    d_ckT = din("ckT", [2, 128, 2, 512])
    d_cv = din("cv", [2, 128, 4, 256])
    d_gain = din("gain", [128, 4])
    d_const = din("consts", [128, 3, 128])
    d_s5w0 = din("s5w0", [2, 8, 128, 512])
    d_s5bn = din("s5bn", [2, 8, 128, 512])
    d_s5v0 = din("s5v0", [2, 8, 128, 512])
    d_s5d = din("s5d", [128, 2, 8])
    d_h0 = din("h0", [128, 2, 2, 2, 32])
    d_s5n = din("s5n", [64, 2, 3, 2, 64])
    d_selc = din("selc", [128, 8, 128])
    d_selb = din("selb", [128, 34])
    o_yT = dout("yT", [D, NT])
    o_k = dout("kout", [2, 2, 128, NT])
    o_v = dout("vout", [2, NT, 256])
    o_st = dout("stout", [2, 128, 512])
    if DBG:
        o_dU = dout("dU", [D, NT]); o_dZ = dout("dZ", [D, NT]); o_dXA = dout("dXA", [D, NT]); o_dS = None
        o_dQ = dout("dQ", [D, NT]); o_dK = dout("dK", [128, 2, 1536]); o_dO = dout("dO", [D, NT]); o_dXA1 = dout("dXA1", [D, NT])

    def dbg_bf(dst, src, bufs):
        P.op("pool", lambda e: e.dma_start(out=dst, in_=src), reads=bufs, dma=True)

    def dbg_x(dst):
        for c in range(8):
            P.op("sp", lambda e, c=c: e.dma_start(out=dst[c * 128:(c + 1) * 128, :], in_=X[:, c, :]), reads=[Xb[c]], dma=True)

    def sb(name, shape, dt=F32):
        cm = nc.sbuf_tensor(name, list(shape), dt)
        return cm.__enter__()

    def ps(name, shape, dt=F32):
        cm = nc.psum_tensor(name, list(shape), dt)
        return cm.__enter__()

    X = sb("X", [128, 8, NT])
    Xb = [Buf("X%d" % i) for i in range(8)]
    H = sb("H", [128, 8, NT], BF16)
    Hb = [Buf("H%d" % i) for i in range(8)]
    BIG = sb("BIG", [128, KFF * NT], BF16)
    BIGb = [Buf("BIG%d" % i) for i in range(KFF)]
    WR = sb("WR", [128, NWSLOT, WCOLS], BF16)
    WRb = [Buf("WR%d" % i) for i in range(NWSLOT)]
    LNT = sb("LNT", [128, 4, NT], BF16)
    LNTb = [Buf("LNT%d" % i) for i in range(4)]
    TS = sb("TS", [128, 4, NT])
    TSb = [Buf("TS%d" % i) for i in range(4)]
    MOD = sb("MOD", [128, DEPTH, 48])
    MODb = [Buf("MOD%d" % i) for i in range(DEPTH)]
    BMOD = sb("BMOD", [128, DEPTH * 48]); BMODb = Buf("BMOD")
    LNP = sb("LNP", [128, 128]); LNPb = Buf("LNP")
    CT = sb("CT", [128, 8]); CTb = Buf("CT")
    CS = sb("CS", [128, 8], BF16); CSb = Buf("CS")
    ROW = sb("ROW", [1, 512]); ROWb = Buf("ROW")
    ONE1 = sb("ONE1", [1, 1]); ONE1b = Buf("ONE1")
    MASK = sb("MASK", [128, 48]); MASKb = Buf("MASK")
    KEEP = sb("KEEP", [128, 1]); KEEPb = Buf("KEEP")
    GAIN = sb("GAIN", [128, 4]); GAINb = Buf("GAIN")
    CONST = sb("CONST", [128, 3, 128]); CONSTb = Buf("CONST")
    CB = sb("CB", [128, 3, 128], BF16); CBb = Buf("CB")
    ONES = sb("ONES", [128, 128], BF16); ONESb = Buf("ONES")
    EPSC = sb("EPSC", [128, 2]); EPSb = Buf("EPSC")
    ARENA = sb("ARENA", [128, 8192])
    ARb = {"cur": [Buf("ARENA")]}

    def arena_switch(names):
        olds = ARb["cur"]
        news = [merge(n, *olds) for n in names]
        ARb["cur"] = news
        return news
    ROPE = ARENA[:, 0:2048].rearrange("p (a n) -> p a n", a=2)
    KTB = ARENA[:, 2048:3584].bitcast(BF16).rearrange("p (a n) -> p a n", a=2)
    VTB = ARENA[:, 3584:5120].bitcast(BF16).rearrange("p (a n) -> p a n", a=12)
    PR = ARENA[:, 5120:6144].bitcast(BF16).rearrange("p (a n) -> p a n", a=4)
    S5C = ARENA[:, 0:2048].rearrange("p (t d f x) -> p t d f x", t=2, d=2, f=8)
    S5K = ARENA[:, 2048:4096].rearrange("p (t d f x) -> p t d f x", t=2, d=2, f=8)
    NATQ = ARENA[0:64, 4096:4736].rearrange("p (t x) -> p t x", t=5)
    NATT = ARENA[0:64, 4736:5504].rearrange("p (t x) -> p t x", t=6)
    XPAD = ARENA[0:64, 5504:6528].rearrange("p (t x) -> p t x", t=8)
    SELC = ARENA[:, 6528:7552].rearrange("p (f x) -> p f x", f=8)
    SELB = ARENA[:, 7552:7586]
    NATQ128 = ARENA[:, 4096:4736].rearrange("p (t x) -> p t x", t=5)
    XPAD128 = ARENA[:, 5504:6528].rearrange("p (t x) -> p t x", t=8)
    VF1 = ARENA[:, 4096:4096 + 2304].bitcast(BF16).rearrange("p (n d r x) -> p n d r x", n=9, d=2, r=2)
    S5B = sb("S5B", [128, 2, 2, 32]); S5Bb = Buf("S5B")
    S5KB = sb("S5KB", [128, 2, 2, 32]); S5KBb = Buf("S5KB")
    S5TB = sb("S5TB", [128, 2, 2, 32]); S5TBb = Buf("S5TB")
    S5LT = sb("S5LT", [128, 2, 2, 32]); S5LTb = Buf("S5LT")
    S5LN = sb("S5LN", [128, 2, 32]); S5LNb = Buf("S5LN")
    S5D = sb("S5D", [128, 2, 8]); S5Db = Buf("S5D")
    H0 = sb("H0", [128, 2, 2, 2, 32]); H0b = Buf("H0")
    SELCb = Buf("SELC")
    SELBb = Buf("SELB")
    LDW = sb("LDW", [128, 2, 512]); LDWb = [Buf("LDW0"), Buf("LDW1")]
    LDN = sb("LDN", [128, 2, 512]); LDNb = [Buf("LDN0"), Buf("LDN1")]
    LDV = LDW; LDVb = LDWb
    WST = sb("WST", [128, 2, 2, 2, 128])
    WSTb = [[Buf("WST00"), Buf("WST01")], [Buf("WST10"), Buf("WST11")]]
    WTA = sb("WTA", [128, 2, 2, 2, 128])
    WTAb = [[Buf("WTA00"), Buf("WTA01")], [Buf("WTA10"), Buf("WTA11")]]
    BNB = sb("BNB", [128, 2, 2, 2, 128], BF16); BNBb = [Buf("BNB0"), Buf("BNB1")]
    KTMP = sb("KTMP", [128, 128]); KTMPb = Buf("KTMP")
    STG = sb("STG", [128, 4, 2, 2, 32]); STGb = Buf("STG")
    RST = sb("RST", [128, 2, 2, 2, 32]); RSTb = [Buf("RST0"), Buf("RST1")]
    RTM = sb("RTM", [128, 2, 2, 2, 32]); RTMb = [Buf("RTM0"), Buf("RTM1a"), Buf("RTM1b")]
    VST = sb("VST", [128, 2, 256]); VSTb = [Buf("VST0"), Buf("VST1")]

    PA = ps("PA", [128, NT]); PB = ps("PB", [128, NT]); PC = ps("PC", [128, NT]); PD = ps("PD", [128, NT])
    PAb, PBb, PCb, PDb = Buf("PA"), Buf("PB"), Buf("PC"), Buf("PD")
    PSUMS = [(PA, PAb), (PB, PBb), (PC, PCb), (PD, PDb)]

    wstate = {"issued": 0, "used": 0}

    def w_issue():
        i = wstate["issued"]
        if i >= nblk:
            return
        s = i % NWSLOT
        P.op("pool", lambda e: e.dma_start(out=WR[:, s, :], in_=d_wall[i]), writes=[WRb[s]], dma=True)
        wstate["issued"] += 1

    def w_get():
        i = wstate["used"]
        wstate["used"] += 1
        while wstate["issued"] < min(nblk, i + NWSLOT):
            w_issue()
        s = i % NWSLOT
        return WR[:, s, :], WRb[s]

    def sp_load(dst, src, buf):
        P.op("sp", lambda e: e.dma_start(out=dst, in_=src), writes=[buf], dma=True)

    for i in range(NWSLOT):
        w_issue()
    sp_load(CT[:], d_cT, CTb)
    sp_load(BMOD[:], d_bmod, BMODb)
    sp_load(LNP[:], d_ln, LNPb)
    sp_load(CONST[:], d_const, CONSTb)
    sp_load(GAIN[:], d_gain, GAINb)
    sp_load(MASK[:], d_mask, MASKb)
    sp_load(KEEP[:], d_keep, KEEPb)
    for c in range(8):
        sp_load(X[:, c, :], d_xT[c * 128:(c + 1) * 128, :], Xb[c])
    sp_load(S5D[:], d_s5d, S5Db)
    sp_load(H0[:], d_h0, H0b)
    sp_load(SELC, d_selc, SELCb)
    sp_load(SELB, d_selb, SELBb)

    P.op("dve", lambda e: e.memset(ONES[:], 1.0), writes=[ONESb])
    P.op("dve", lambda e: e.memset(ONE1[:], 1.0), writes=[ONE1b])
    P.op("dve", lambda e: e.memset(EPSC[:, 0:1], LN_EPS / (ALPHA * ALPHA)), writes=[EPSb])
    P.op("dve", lambda e: e.memset(EPSC[:, 1:2], RMS_EPS), writes=[EPSb])
    P.op("dve", lambda e: e.tensor_copy(out=CB[:], in_=CONST[:]), reads=[CONSTb], writes=[CBb])
    P.op("act", lambda e: e.activation(out=CS[:], in_=CT[:], func=AF.Silu), reads=[CTb], writes=[CSb])

    IDENT_F = CONST[:, 0, :]
    BDMASK = CONST[:, 1, :]
    ROTB = CB[:, 2, :]

    def compute_mod_gen(l):
        for b in range(12):
            wap, wb = w_get()
            w3 = wap.rearrange("p (k n) -> p k n", n=512)

            def mm(e):
                last = None
                for kc in range(8):
                    last = e.matmul(PC[0:1, 0:512], lhsT=CS[:, kc:kc + 1], rhs=w3[:, kc, :], start=(kc == 0), stop=(kc == 7))
                return last
            P.op("pe", mm, reads=[wb, CSb], writes=[PCb])
            P.op("act", lambda e: e.activation(out=ROW[:], in_=PC[0:1, 0:512], func=AF.Copy), reads=[PCb], writes=[ROWb])

            def tr(e):
                last = None
                for i in range(4):
                    col = b * 4 + i
                    last = e.matmul(PD[:, col:col + 1], lhsT=ROW[0:1, i * 128:(i + 1) * 128], rhs=ONE1[0:1, 0:1], start=True, stop=True)
                return last
            P.op("pe", tr, reads=[ROWb, ONE1b], writes=[PDb])
            yield b
        P.op("dve", lambda e: e.tensor_tensor(out=MOD[:, l, :], in0=PD[:, 0:48], in1=BMOD[:, l * 48:(l + 1) * 48], op=ALU.add),
             reads=[PDb, BMODb], writes=[MODb[l]])
        for base in (8, 32):
            P.op("dve", lambda e, base=base: e.tensor_scalar(out=MOD[:, l, base:base + 8], in0=MOD[:, l, base:base + 8], scalar1=1.0, scalar2=None, op0=ALU.add),
                 reads=[MODb[l]], writes=[MODb[l]])
        for base in (16, 40):
            P.op("dve", lambda e, base=base: e.tensor_scalar(out=MOD[:, l, base:base + 8], in0=MOD[:, l, base:base + 8], scalar1=1.0 / ALPHA, scalar2=None, op0=ALU.mult),
                 reads=[MODb[l]], writes=[MODb[l]])

    def compute_mod(l):
        for _ in compute_mod_gen(l):
            pass

    def modulate(l, which):
        so = 0 if which == 0 else 24
        for c in range(8):
            P.op("dve", lambda e, c=c: e.tensor_scalar(out=H[:, c, :], in0=X[:, c, :], scalar1=MOD[:, l, so + 8 + c:so + 9 + c],
                                                      scalar2=MOD[:, l, so + c:so + c + 1], op0=ALU.mult, op1=ALU.add),
                 reads=[Xb[c], MODb[l]], writes=[Hb[c]])

    def proj_chunk(pt, pbuf, w3, wb, oc, src, srcbufs, kcn):
        def mm(e):
            last = None
            for kc in range(kcn):
                for hf in range(2):
                    last = e.matmul(pt[:, hf * 512:(hf + 1) * 512], lhsT=w3[:, kc, oc * 128:(oc + 1) * 128],
                                    rhs=src[:, kc, hf * 512:(hf + 1) * 512], start=(kc == 0), stop=(kc == kcn - 1))
            return last
        P.op("pe", mm, reads=[wb] + list(srcbufs), writes=[pbuf])

    def outproj_ln(l, which, src, srcbufs, kcn, nblocks, cols_per_blk, next_mod):
        gcol = 16 if which == 0 else 40
        lni = (l * 2 + which) * 8
        oc_global = 0
        for b in range(nblocks):
            wap, wb = w_get()
            w3 = wap[:, 0:kcn * cols_per_blk].rearrange("p (k n) -> p k n", n=cols_per_blk)
            for oc in range(cols_per_blk // 128):
                c = oc_global
                pt, pbuf = PSUMS[c % 2]
                proj_chunk(pt, pbuf, w3, wb, oc, src, srcbufs, kcn)
                P.op("dve", lambda e, c=c, pt=pt: e.scalar_tensor_tensor(out=X[:, c, :], in0=pt[:], scalar=MOD[:, l, gcol + c:gcol + c + 1],
                                                                        in1=X[:, c, :], op0=ALU.mult, op1=ALU.add),
                     reads=[pbuf, MODb[l], Xb[c]], writes=[Xb[c]])
                s = (c % 2) * 2
                P.op("act", lambda e, c=c, s=s: e.activation(out=LNT[:, s, :], in_=X[:, c, :], func=AF.Copy), reads=[Xb[c]], writes=[LNTb[s]])
                P.op("act", lambda e, c=c, s=s: e.activation(out=LNT[:, s + 1, :], in_=X[:, c, :], func=AF.Square), reads=[Xb[c]], writes=[LNTb[s + 1]])

                def st(e, c=c, s=s):
                    last = None
                    for hf in range(2):
                        e.matmul(PC[:, hf * 512:(hf + 1) * 512], lhsT=ONES[:], rhs=LNT[:, s, hf * 512:(hf + 1) * 512], start=(c == 0), stop=(c == 7))
                        last = e.matmul(PD[:, hf * 512:(hf + 1) * 512], lhsT=ONES[:], rhs=LNT[:, s + 1, hf * 512:(hf + 1) * 512], start=(c == 0), stop=(c == 7))
                    return last
                P.op("pe", st, reads=[ONESb, LNTb[s], LNTb[s + 1]], writes=[PCb, PDb])
                oc_global += 1
        P.op("act", lambda e: e.activation(out=TS[:, 0, :], in_=PC[:], func=AF.Identity, scale=1.0 / D), reads=[PCb], writes=[TSb[0]])
        P.op("act", lambda e: e.activation(out=TS[:, 1, :], in_=TS[:, 0, :], func=AF.Square), reads=[TSb[0]], writes=[TSb[1]])
        P.op("dve", lambda e: e.scalar_tensor_tensor(out=TS[:, 1, :], in0=PD[:], scalar=1.0 / D, in1=TS[:, 1, :], op0=ALU.mult, op1=ALU.subtract),
             reads=[PDb, TSb[1]], writes=[TSb[1]])
        P.op("act", lambda e: e.activation(out=TS[:, 1, :], in_=TS[:, 1, :], func=AF.Ln, bias=EPSC[:, 0:1], scale=1.0), reads=[TSb[1], EPSb], writes=[TSb[1]])
        P.op("act", lambda e: e.activation(out=TS[:, 1, :], in_=TS[:, 1, :], func=AF.Exp, scale=-0.5), reads=[TSb[1]], writes=[TSb[1]])
        for c in range(8):
            t = 2 + (c % 2)
            P.op("pool" if c % 2 == 1 else "dve", lambda e, c=c, t=t: e.tensor_tensor(out=TS[:, t, :], in0=X[:, c, :], in1=TS[:, 0, :], op=ALU.subtract),
                 reads=[Xb[c], TSb[0]], writes=[TSb[t]])
            P.op("dve", lambda e, c=c, t=t: e.tensor_tensor(out=TS[:, t, :], in0=TS[:, t, :], in1=TS[:, 1, :], op=ALU.mult),
                 reads=[TSb[t], TSb[1]], writes=[TSb[t]])
            P.op("act", lambda e, c=c, t=t: e.activation(out=X[:, c, :], in_=TS[:, t, :], func=AF.Identity,
                                                         scale=LNP[:, lni + c:lni + c + 1], bias=LNP[:, 64 + lni + c:64 + lni + c + 1]),
                 reads=[TSb[t], LNPb], writes=[Xb[c]])
        if next_mod is not None:
            modulate(*next_mod)

    def ffn(l, bg=None):
        bg = bg if bg is not None else []
        for b in range(11):
            wap, wb = w_get()
            w3 = wap.rearrange("p (k n) -> p k n", n=512)
            for jj in range(2):
                j = 2 * b + jj
                pg, pgb = PSUMS[(jj * 2) % 4]
                pu, pub = PSUMS[(jj * 2 + 1) % 4]
                proj_chunk(pg, pgb, w3, wb, jj * 2, H, Hb, 8)
                proj_chunk(pu, pub, w3, wb, jj * 2 + 1, H, Hb, 8)
                t = jj
                P.op("act", lambda e, pg=pg, t=t: e.activation(out=TS[:, t, :], in_=pg[:], func=AF.Silu), reads=[pgb], writes=[TSb[t]])
                P.op("dve", lambda e, pu=pu, t=t, j=j: e.tensor_tensor(out=BIG[:, j * NT:(j + 1) * NT], in0=pu[:], in1=TS[:, t, :], op=ALU.mult),
                     reads=[pub, TSb[t]], writes=[BIGb[j]])
                replay(bg, 14)
        replay(bg, 10 ** 9)
        BIG3 = BIG[:].rearrange("p (k n) -> p k n", n=NT)
        nm = (l + 1, 0) if l + 1 < DEPTH else None
        outproj_ln(l, 1, BIG3, BIGb, KFF, 8, 128, nm)

    def attention(l):
        j = l // 2
        Q = BIG[:, 0:8 * NT].rearrange("p (k n) -> p k n", n=NT)
        nb = arena_switch(["ROPE", "KT0", "KT1"] + ["VT%d" % i for i in range(12)] + ["PR%d" % i for i in range(4)])
        ROPEb = nb[0]
        KTBb = nb[1:3]
        VTBb = nb[3:15]
        PRb = nb[15:19]
        sp_load(ROPE, d_rope, ROPEb)
        for kv in range(2):
            P.op("pool", lambda e, kv=kv: e.dma_start(out=KTB[:, kv, NT:NT + 512], in_=d_ckT[j, :, kv, :]), writes=[KTBb[kv]], dma=True)
        for t4 in range(4):
            P.op("pool", lambda e, t4=t4: e.dma_start(out=VTB[:, 8 + t4, :], in_=d_cv[j, :, t4, :]), writes=[VTBb[8 + t4]], dma=True)
        cur = None
        for hc in range(10):
            if hc % 4 == 0:
                cur = w_get()
            wap, wb = cur
            w3 = wap.rearrange("p (k n) -> p k n", n=512)
            pt, pbuf = PSUMS[hc % 2]
            proj_chunk(pt, pbuf, w3, wb, hc % 4, H, Hb, 8)
            isk = hc >= 8
            gcol = (2 + j) if isk else j
            P.op("act", lambda e, pt=pt: e.activation(out=TS[:, 2, :], in_=pt[:], func=AF.Copy), reads=[pbuf], writes=[TSb[2]])
            P.op("act", lambda e, pt=pt: e.activation(out=LNT[:, 0, :], in_=pt[:], func=AF.Square), reads=[pbuf], writes=[LNTb[0]])

            def st(e):
                last = None
                for hf in range(2):
                    last = e.matmul(PC[:, hf * 512:(hf + 1) * 512], lhsT=ONES[:], rhs=LNT[:, 0, hf * 512:(hf + 1) * 512], start=True, stop=True)
                return last
            P.op("pe", st, reads=[ONESb, LNTb[0]], writes=[PCb])
            P.op("act", lambda e: e.activation(out=TS[:, 3, :], in_=PC[:], func=AF.Ln, bias=EPSC[:, 1:2], scale=1.0 / 128), reads=[PCb, EPSb], writes=[TSb[3]])
            P.op("act", lambda e: e.activation(out=TS[:, 3, :], in_=TS[:, 3, :], func=AF.Exp, scale=-0.5), reads=[TSb[3]], writes=[TSb[3]])
            P.op("dve", lambda e, gcol=gcol: e.scalar_tensor_tensor(out=TS[:, 2, :], in0=TS[:, 2, :], scalar=GAIN[:, gcol:gcol + 1], in1=TS[:, 3, :],
                                                                    op0=ALU.mult, op1=ALU.mult), reads=[TSb[2], TSb[3], GAINb], writes=[TSb[2]])
            P.op("act", lambda e: e.activation(out=LNT[:, 1, :], in_=TS[:, 2, :], func=AF.Copy), reads=[TSb[2]], writes=[LNTb[1]])

            def rt(e):
                last = None
                for hf in range(2):
                    last = e.matmul(PD[:, hf * 512:(hf + 1) * 512], lhsT=ROTB, rhs=LNT[:, 1, hf * 512:(hf + 1) * 512], start=True, stop=True)
                return last
            P.op("pe", rt, reads=[CBb, LNTb[1]], writes=[PDb])
            P.op("dve", lambda e: e.tensor_tensor(out=TS[:, 0, :], in0=PD[:], in1=ROPE[:, 1, :], op=ALU.mult), reads=[PDb, ROPEb], writes=[TSb[0]])
            P.op("dve", lambda e: e.tensor_tensor(out=TS[:, 2, :], in0=TS[:, 2, :], in1=ROPE[:, 0, :], op=ALU.mult), reads=[TSb[2], ROPEb], writes=[TSb[2]])
            if not isk:
                P.op("dve", lambda e, hc=hc: e.tensor_tensor(out=Q[:, hc, :], in0=TS[:, 2, :], in1=TS[:, 0, :], op=ALU.add),
                     reads=[TSb[2], TSb[0]], writes=[BIGb[hc]])
            else:
                kv = hc - 8
                P.op("dve", lambda e: e.tensor_tensor(out=TS[:, 1, :], in0=TS[:, 2, :], in1=TS[:, 0, :], op=ALU.add), reads=[TSb[2], TSb[0]], writes=[TSb[1]])
                P.op("act", lambda e, kv=kv: e.activation(out=KTB[:, kv, 0:NT], in_=TS[:, 1, :], func=AF.Copy), reads=[TSb[1]], writes=[KTBb[kv]])
                P.op("sp", lambda e, kv=kv: e.dma_start(out=o_k[j, kv], in_=TS[:, 1, :]), reads=[TSb[1]], dma=True)
        wap, wb = cur
        w3 = wap.rearrange("p (k n) -> p k n", n=512)
        for tt in range(8):
            pt, pbuf = PSUMS[tt % 2]

            def mmv(e, tt=tt, pt=pt):
                last = None
                for kc in range(8):
                    last = e.matmul(pt[:, 0:256], lhsT=H[:, kc, tt * 128:(tt + 1) * 128], rhs=w3[:, kc, 256:512], start=(kc == 0), stop=(kc == 7))
                return last
            P.op("pe", mmv, reads=[wb] + Hb, writes=[pbuf])
            s = tt % 2
            P.op("act", lambda e, pt=pt, s=s: e.activation(out=VST[:, s, :], in_=pt[:, 0:256], func=AF.Copy), reads=[pbuf], writes=[VSTb[s]])
            P.op("dve", lambda e, tt=tt, s=s: e.tensor_copy(out=VTB[:, tt, :], in_=VST[:, s, :]), reads=[VSTb[s]], writes=[VTBb[tt]])
            P.op("sp", lambda e, tt=tt, s=s: e.dma_start(out=o_v[j, tt * 128:(tt + 1) * 128, :], in_=VST[:, s, :]), reads=[VSTb[s]], dma=True)
        if DBG and l == 1:
            for h in range(8):
                dbg_bf(o_dQ[h * 128:(h + 1) * 128, :], Q[:, h, :], [BIGb[h]])
            dbg_bf(o_dK, KTB, KTBb)
        NSB = 4
        PAh = [PA[:, 0:512], PA[:, 512:1024], PD[:, 0:512], PD[:, 512:1024]]
        PAhb = [merge("PAh", PAb), merge("PAh", PAb), merge("PDh", PDb), merge("PDh", PDb)]
        pri = 0
        for h in range(8):
            kv = h // 4
            for qh in range(2):
                qs = slice(qh * 512, (qh + 1) * 512)

                def smm(e, kt, h=h, kv=kv, qs=qs):
                    return e.matmul(PAh[kt % NSB], lhsT=KTB[:, kv, kt * 128:(kt + 1) * 128], rhs=Q[:, h, qs], start=True, stop=True)
                for k0 in range(NSB - 1):
                    P.op("pe", lambda e, k0=k0: smm(e, k0), reads=[KTBb[kv], BIGb[h]], writes=[PAhb[k0]])
                for kt in range(12):
                    if kt + NSB - 1 < 12:
                        P.op("pe", lambda e, kt=kt: smm(e, kt + NSB - 1), reads=[KTBb[kv], BIGb[h]], writes=[PAhb[(kt + NSB - 1) % NSB]])
                    pslot = pri % 4
                    pri += 1
                    for g2 in range(2):
                        qg = qh * 2 + g2
                        P.op("act", lambda e, kt=kt, g2=g2, qg=qg, pslot=pslot: e.activation(
                            out=PR[:, pslot, g2 * 256:(g2 + 1) * 256], in_=PAh[kt % NSB][:, g2 * 256:(g2 + 1) * 256], func=AF.Exp,
                            bias=MASK[:, kt * 4 + qg:kt * 4 + qg + 1], scale=ATT_SCALE), reads=[PAhb[kt % NSB], MASKb], writes=[PRb[pslot]])

                    def pv(e, kt=kt, kv=kv, qs=qs, pslot=pslot):
                        e.matmul(PB[:, qs], lhsT=VTB[:, kt, kv * 128:(kv + 1) * 128], rhs=PR[:, pslot, :], start=(kt == 0), stop=(kt == 11))
                        return e.matmul(PC[:, qs], lhsT=ONES[:], rhs=PR[:, pslot, :], start=(kt == 0), stop=(kt == 11))
                    P.op("pe", pv, reads=[VTBb[kt], PRb[pslot], ONESb], writes=[PBb, PCb])
                P.op("dve", lambda e, qs=qs: e.reciprocal(out=TS[:, 0, qs], in_=PC[:, qs]), reads=[PCb], writes=[TSb[0]])
                P.op("dve", lambda e, qs=qs, h=h: e.tensor_tensor(out=H[:, h, qs], in0=PB[:, qs], in1=TS[:, 0, qs], op=ALU.mult),
                     reads=[PBb, TSb[0]], writes=[Hb[h]])
        m1 = merge("PA", PAhb[0], PAhb[1])
        PAb.w, PAb.r = m1.w, m1.r
        m1 = merge("PD", PAhb[2], PAhb[3])
        PDb.w, PDb.r = m1.w, m1.r
        if DBG and l == 1:
            for h in range(8):
                dbg_bf(o_dO[h * 128:(h + 1) * 128, :], H[:, h, :], [Hb[h]])
        outproj_ln(l, 0, H, Hb, 8, 2, 512, (l, 1))
        if DBG and l == 1:
            dbg_x(o_dXA1)

    def cmul_ops(e_name, out_r, out_i, in_r, in_i, lr, li, ta, tb, reads, writes, tbufs):
        return [
            lambda: P.op(e_name, lambda e: e.tensor_tensor(out=ta, in0=in_r, in1=lr, op=ALU.mult), reads=reads, writes=[tbufs[0]]),
            lambda: P.op(e_name, lambda e: e.tensor_tensor(out=tb, in0=in_i, in1=li, op=ALU.mult), reads=reads, writes=[tbufs[1]]),
            lambda: P.op(e_name, lambda e: e.tensor_tensor(out=ta, in0=ta, in1=tb, op=ALU.subtract), reads=[tbufs[0], tbufs[1]], writes=[tbufs[0]]),
            lambda: P.op(e_name, lambda e: e.tensor_tensor(out=tb, in0=in_r, in1=li, op=ALU.mult), reads=reads, writes=[tbufs[1]]),
            lambda: P.op(e_name, lambda e: e.tensor_tensor(out=out_i, in0=in_i, in1=lr, op=ALU.mult), reads=reads + [tbufs[0]], writes=writes),
            lambda: P.op(e_name, lambda e: e.tensor_tensor(out=out_i, in0=out_i, in1=tb, op=ALU.add), reads=[tbufs[1]] + writes, writes=writes),
            lambda: P.op(e_name, lambda e: e.tensor_copy(out=out_r, in_=ta), reads=[tbufs[0]], writes=writes),
        ]

    def cmul(*a):
        for t in cmul_ops(*a):
            t()

    def interleave(lists):
        n = max(len(L) for L in lists)
        for i in range(n):
            for L in lists:
                if i < len(L):
                    L[i]()

    def lam_compute(A, Ab, Kt, Kb, Tm, Tb_, n_t):
        bufs = [Ab, Kb, Tb_]
        A0, A1, A2 = A[:, 0], A[:, 1], A[:, 2]
        K0, K1 = Kt[:, 0], Kt[:, 1]
        T0, T1, T2, T3, T4, T5 = (Tm[:, i] for i in range(6))

        def tt(o, a, b, op):
            P.op("dve", lambda e: e.tensor_tensor(out=o, in0=a, in1=b, op=op), reads=bufs, writes=bufs)

        def tsc(o, a, s1, s2=None, op0=ALU.mult, op1=ALU.add):
            if s2 is None:
                P.op("dve", lambda e: e.tensor_scalar(out=o, in0=a, scalar1=s1, scalar2=None, op0=op0), reads=bufs, writes=bufs)
            else:
                P.op("dve", lambda e: e.tensor_scalar(out=o, in0=a, scalar1=s1, scalar2=s2, op0=op0, op1=op1), reads=bufs, writes=bufs)

        def stt(o, a, sc, b, op0, op1):
            P.op("dve", lambda e: e.scalar_tensor_tensor(out=o, in0=a, scalar=sc, in1=b, op0=op0, op1=op1), reads=bufs, writes=bufs)

        def horner(t, y, divs, sign):
            tsc(t, y, sign / divs[-1], 1.0)
            for dv in reversed(divs[:-1]):
                tt(t, t, y, ALU.mult)
                tsc(t, t, sign / dv, 1.0)
        tsc(K1, A2, 0.125)
        horner(K0, K1, [1.0, 2.0, 3.0, 4.0, 5.0, 6.0, 7.0, 8.0, 9.0, 10.0, 11.0], 1.0)
        for _ in range(3):
            tt(K0, K0, K0, ALU.mult)
        tt(T0, K0, A0, ALU.mult)
        tt(T1, K0, A1, ALU.mult)
        horner(K0, T0, [2.0, 3.0, 4.0, 5.0, 6.0, 7.0], 1.0)
        tt(T2, K0, T0, ALU.mult)
        tsc(K1, T1, 1.0 / 16.0)
        tt(T3, K1, K1, ALU.mult)
        horner(K0, T3, [6.0, 20.0, 42.0, 72.0, 110.0, 156.0, 210.0], -1.0)
        tt(T4, K0, K1, ALU.mult)
        horner(K0, T3, [12.0, 30.0, 56.0, 90.0, 132.0, 182.0, 240.0], -1.0)
        stt(T5, T3, -0.5, K0, ALU.mult, ALU.mult)
        for _ in range(4):
            tt(K1, T4, T4, ALU.mult)
            stt(K0, T5, 1.0, T4, ALU.add, ALU.mult)
            tsc(T4, K0, 2.0)
            tsc(T5, K1, -2.0)
        stt(K0, T2, 1.0, T5, ALU.add, ALU.mult)
        tt(T0, K0, T2, ALU.add)
        stt(T1, T2, 1.0, T4, ALU.add, ALU.mult)
        tt(T3, A0, A0, ALU.mult)
        tt(T2, A1, A1, ALU.mult)
        tt(T3, T3, T2, ALU.add)
        P.op("dve", lambda e: e.reciprocal(out=T3, in_=T3), reads=bufs, writes=bufs)
        tt(T2, T0, A0, ALU.mult)
        tt(T4, T1, A1, ALU.mult)
        tt(T2, T2, T4, ALU.add)
        tt(K0, T2, T3, ALU.mult)
        tt(T2, T1, A0, ALU.mult)
        tt(T4, T0, A1, ALU.mult)
        tt(T2, T2, T4, ALU.subtract)
        tt(K1, T2, T3, ALU.mult)
        tsc(A0, T0, 1.0, None, op0=ALU.add)
        P.op("dve", lambda e: e.tensor_copy(out=A1, in_=T1), reads=bufs, writes=bufs)

    lamctx = {}
    bgs = {}

    def s5_lambda(l):
        j = l // 2
        nb = arena_switch(["S5C", "S5K", "S5T"])
        S5Cb, S5Kb, S5Tb = nb
        lamctx[l] = nb
        P.op("dve", lambda e: e.memset(ARENA[64:128, 4096:6528], 0.0), writes=[S5Tb])
        sp_load(NATQ[:, 0:2, :], d_s5n[:, j, 0:2].rearrange("g t d p -> g t (d p)"), S5Tb)
        sp_load(NATQ[:, 2, :], d_s5n[:, j, 2].rearrange("g d p -> g (d p)"), S5Tb)
        lam_compute(NATQ[:, 0:3, :], S5Tb, NATQ[:, 3:5, :], S5Tb, NATT, S5Tb, 0)
        for fc in range(8):
            pt, pbuf = PSUMS[2 + fc % 2]

            def mmx(e, fc=fc, pt=pt):
                e.matmul(pt[:, 0:256], lhsT=SELC[:, fc, :], rhs=NATQ128[:, 0:2, :].rearrange("p t x -> p (t x)"), start=True, stop=True)
                return e.matmul(pt[:, 256:512], lhsT=SELC[:, fc, :], rhs=NATQ128[:, 3:5, :].rearrange("p t x -> p (t x)"), start=True, stop=True)
            P.op("pe", mmx, reads=[SELCb, S5Tb], writes=[pbuf])
            P.op("act", lambda e, fc=fc, pt=pt: e.activation(out=S5C[:, :, :, fc, :], in_=pt[:, 0:256].rearrange("p (t d x) -> p t d x", t=2, d=2), func=AF.Copy),
                 reads=[pbuf], writes=[S5Cb])
            P.op("act", lambda e, fc=fc, pt=pt: e.activation(out=S5K[:, :, :, fc, :], in_=pt[:, 256:512].rearrange("p (t d x) -> p t d x", t=2, d=2), func=AF.Copy),
                 reads=[pbuf], writes=[S5Kb])
        slots = (0, 1, 3, 4)
        for ti in range(4):
            for d in range(2):
                for a in range(2):
                    P.op("dve", lambda e, ti=ti, d=d, a=a: e.tensor_scalar(out=XPAD[:, ti * 2 + d, a * 64:(a + 1) * 64], in0=NATQ[:, slots[ti], d * 64:(d + 1) * 64],
                                                                        scalar1=SELB[0:64, 32 + a:33 + a], scalar2=None, op0=ALU.mult),
                         reads=[S5Tb, SELBb], writes=[S5Tb])

        def mmb(e):
            last = None
            for k in range(8):
                last = e.matmul(PA[:, k * 32:(k + 1) * 32], lhsT=XPAD128[:, k, :], rhs=SELB[:, 0:32], start=True, stop=True)
            return last
        P.op("pe", mmb, reads=[S5Tb, SELBb], writes=[PAb])
        P.op("act", lambda e: e.activation(out=S5B[:], in_=PA[:, 0:128].rearrange("p (t d q) -> p t d q", t=2, d=2), func=AF.Copy), reads=[PAb], writes=[S5Bb])
        P.op("act", lambda e: e.activation(out=S5KB[:], in_=PA[:, 128:256].rearrange("p (t d q) -> p t d q", t=2, d=2), func=AF.Copy), reads=[PAb], writes=[S5KBb])
        P.op("dve", lambda e: e.tensor_scalar(out=S5LN[:], in0=S5B[:, 1], scalar1=-1.0, scalar2=None, op0=ALU.mult), reads=[S5Bb], writes=[S5LNb])
        P.op("dve", lambda e: e.tensor_copy(out=S5LT[:], in_=S5B[:]), reads=[S5Bb], writes=[S5LTb])
        for _ in range(3):
            cmul("dve", S5LT[:, 0], S5LT[:, 1], S5LT[:, 0], S5LT[:, 1], S5LT[:, 0], S5LT[:, 1], S5TB[:, 0], S5TB[:, 1], [S5LTb], [S5LTb], [S5TBb, S5TBb])

    def record_lambda(l):
        P.rec = []
        s5_lambda(l)
        ops = P.rec
        P.rec = None
        return ops

    def replay(ops, n):
        k = 0
        while ops and k < n:
            P.op(*ops.pop(0))
            k += 1

    def s5_mixer(l):
        j = l // 2
        SALL = BIG[:, 0:NSLOT * 64].rearrange("p (s r q) -> p s r q", r=2, q=32)
        SBUFS = merge("S", *BIGb[0:17])
        VF0b = merge("VF0", *BIGb[17:22])
        for bb in BIGb:
            bb.w, bb.r = {}, {}
        S5Cb, S5Kb, S5Tb = lamctx[l]
        VF0 = BIG[:, 17 * NT:17 * NT + 9 * 512].rearrange("p (n d r x) -> p n d r x", n=9, d=2, r=2)
        VF = [VF0, VF1]
        VFb = [VF0b, S5Tb]
        KF = [LNT[:, 0:2, :].rearrange("p a (k x) -> p (a k) x", x=128), LNT[:, 2:4, :].rearrange("p a (k x) -> p (a k) x", x=128)]
        KFb = [merge("KF0", LNTb[0], LNTb[1]), merge("KF1", LNTb[2], LNTb[3])]
        UALL = TS[:].bitcast(BF16).rearrange("p a (h n) -> p (a h) n", n=NT)

        def Uap(fc):
            return UALL[:, fc, :], TSb[fc // 2]
        for b in range(2):
            wap, wb = w_get()
            w3 = wap.rearrange("p (k n) -> p k n", n=512)
            for oc in range(4):
                fc = b * 4 + oc
                pt, pbuf = PSUMS[fc % 2]
                proj_chunk(pt, pbuf, w3, wb, oc, H, Hb, 8)
                ua, ub = Uap(fc)
                P.op("act", lambda e, pt=pt, ua=ua: e.activation(out=ua, in_=pt[:], func=AF.Copy), reads=[pbuf], writes=[ub])
        if DBG and l == 0:
            for fc in range(8):
                dbg_bf(o_dU[fc * 128:(fc + 1) * 128, :], UALL[:, fc, :], [TSb[fc // 2]])
        replay(bgs.get(l, []), 10 ** 9)
        WF = [H[:, 0:4, :].rearrange("p k n -> p (k n)").rearrange("p (n d r x) -> p n d r x", n=8, d=2, r=2),
              H[:, 4:8, :].rearrange("p k n -> p (k n)").rearrange("p (n d r x) -> p n d r x", n=8, d=2, r=2)]
        WFb = [merge("WF0", *Hb[0:4]), merge("WF1", *Hb[4:8])]
        if S5STOP == 1:
            raise _Stop()
        P.op("dve", lambda e: e.tensor_copy(out=SALL[:, 0], in_=H0[:, j, 0]), reads=[H0b], writes=[SBUFS])
        P.op("dve", lambda e: e.tensor_copy(out=SALL[:, 2 * C + 1], in_=H0[:, j, 1]), reads=[H0b], writes=[SBUFS])
        PSA = [PA, PB, PC, PD]
        PSAb = [PAb, PBb, PCb, PDb]
        PSAh = [[PSA[q][:, 0:512], PSA[q][:, 512:1024]] for q in range(4)]
        PSAhb = [[merge("PSAh", PSAb[q]), merge("PSAh", PSAb[q])] for q in range(4)]
        ENG = ["dve", "dve"]

        def recur_ops(eng, d, n, views, lr, li, dst, dstbuf, lbufs):
            pi, po = (n - 1) % 2, n % 2
            a_, x_ = views
            sin_ = WST[:, pi, d].rearrange("p r (a x) -> p r a x", a=a_)
            ta4 = WTA[:, d, 0].rearrange("p r (a x) -> p r a x", a=a_)
            tb4 = WTA[:, d, 1].rearrange("p r (a x) -> p r a x", a=a_)
            return [
                lambda: P.op(eng, lambda e: e.tensor_tensor(out=ta4, in0=sin_, in1=lr, op=ALU.mult), reads=[WSTb[pi][d]] + lbufs, writes=[WTAb[d][0]]),
                lambda: P.op(eng, lambda e: e.tensor_tensor(out=tb4, in0=sin_, in1=li, op=ALU.mult), reads=[WSTb[pi][d]] + lbufs, writes=[WTAb[d][1]]),
                lambda: P.op(eng, lambda e: e.tensor_tensor(out=WST[:, po, d, 0, :], in0=WTA[:, d, 0, 0, :], in1=WTA[:, d, 1, 1, :], op=ALU.subtract),
                             reads=[WTAb[d][0], WTAb[d][1]], writes=[WSTb[po][d]]),
                lambda: P.op(eng, lambda e: e.tensor_tensor(out=WST[:, po, d, 1, :], in0=WTA[:, d, 1, 0, :], in1=WTA[:, d, 0, 1, :], op=ALU.add),
                             reads=[WTAb[d][0], WTAb[d][1]], writes=[WSTb[po][d]]),
                lambda: P.op("act", lambda e: e.activation(out=dst[:, n, d], in_=WST[:, po, d], func=AF.Copy), reads=[WSTb[po][d]], writes=[dstbuf]),
            ]

        pend = []
        for fc in range(8):
            s = fc % 2
            sp_load(LDW[:, s, :], d_s5w0[j, fc], LDWb[s])
            L4 = LDW[:, s, :].rearrange("p (d r x) -> p d r x", d=2, r=2)
            chains = []
            for d in range(2):
                eng = ENG[d]
                lr = S5C[:, 0, d, fc, :].unsqueeze(1).unsqueeze(1).broadcast_to([128, 2, 2, 64])
                li = S5C[:, 1, d, fc, :].unsqueeze(1).unsqueeze(1).broadcast_to([128, 2, 2, 64])
                kr = S5K[:, 0, d, fc, :].unsqueeze(1).broadcast_to([128, 2, 64])
                ki = S5K[:, 1, d, fc, :].unsqueeze(1).broadcast_to([128, 2, 64])
                v3 = lambda ap: ap.rearrange("p (a x) -> p a x", a=2)
                ch = cmul_ops(eng, v3(WST[:, 0, d, 0, :]), v3(WST[:, 0, d, 1, :]), v3(L4[:, d, 0, :]), v3(L4[:, d, 1, :]), kr, ki,
                              v3(WTA[:, d, 0, 0, :]), v3(WTA[:, d, 1, 0, :]), [LDWb[s], S5Kb, WSTb[0][d]], [WSTb[0][d]], [WTAb[d][0], WTAb[d][1]])
                ch.append(lambda d=d, s=s: P.op("act", lambda e: e.activation(out=WF[s][:, 0, d], in_=WST[:, 0, d], func=AF.Copy), reads=[WSTb[0][d]], writes=[WFb[s]]))
                for n in range(1, T):
                    ch += recur_ops(eng, d, n, (2, 64), lr, li, WF[s], WFb[s], [S5Cb])
                chains.append(ch)
            interleave(chains + [pend])
            ua, ub = Uap(fc)
            u3 = ua.rearrange("p (c t) -> p c t", t=T)
            hf = fc % 2

            def mmA(e, s=s, u3=u3, hf=hf):
                last = None
                for d in range(2):
                    for ri in range(2):
                        for jj in range(T):
                            n = (T - 1 - jj) if d == 0 else jj
                            for q in range(4):
                                last = e.matmul(PSAh[q][hf][:, (d * 2 + ri) * 128:(d * 2 + ri + 1) * 128], lhsT=WF[s][32 * q:32 * q + 32, n, d, ri, :],
                                                rhs=u3[32 * q:32 * q + 32, :, jj], start=(jj == 0), stop=(jj == T - 1), tile_position=(32 * q, 0))
                return last
            P.op("pe", mmA, reads=[WFb[s], ub], writes=[PSAhb[q][hf] for q in range(4)])
            pend = []
            for q in range(4):
                qq = fc * 4 + q
                for ri in range(2):
                    src = PSAh[q][hf].rearrange("p (d r c) -> p d r c", d=2, r=2)[:, :, ri, :]
                    dst = SALL[:, 1:2 * C + 1, ri, qq].rearrange("p (d c) -> p d c", d=2)
                    pend.append(lambda src=src, dst=dst, q=q, hf=hf: P.op("act", lambda e: e.activation(out=dst, in_=src, func=AF.Copy), reads=[PSAhb[q][hf]], writes=[SBUFS]))
        for t_ in pend:
            t_()
        for q in range(4):
            m_ = merge("PS", PSAhb[q][0], PSAhb[q][1])
            PSAb[q].w, PSAb[q].r = m_.w, m_.r
        for i in range(4):
            Hb[i].w, Hb[i].r = dict(WFb[0].w), dict(WFb[0].r)
            Hb[4 + i].w, Hb[4 + i].r = dict(WFb[1].w), dict(WFb[1].r)
        if S5STOP == 2:
            raise _Stop()
        LTr_b = S5LT[:, 0].unsqueeze(2).broadcast_to([128, 2, 2, 32])
        LTi = S5LT[:, 1]
        P.op("dve", lambda e: e.tensor_copy(out=RST[:, 1], in_=H0[:, j]), reads=[H0b], writes=[RSTb[1]])
        modgen = compute_mod_gen(l + 1) if l + 1 < DEPTH else iter(())
        SIN = merge("SIN", SBUFS)
        SOUT = merge("SOUT", SBUFS)
        for i in range(C):
            if i % 10 == 5:
                next(modgen, None)
            pp, pc = (i + 1) % 2, i % 2
            a0 = 1 + i
            stp = 2 * C - 1 - 2 * i
            sl = slice(a0, a0 + stp + 1, stp)
            if i > 0 and i % 32 == 0:
                P.op("dve", lambda e, pp=pp: e.tensor_scalar(out=RST[:, pp], in0=RST[:, pp], scalar1=KEEP[:, 0:1], scalar2=None, op0=ALU.mult),
                     reads=[RSTb[pp], KEEPb], writes=[RSTb[pp]])
            P.op("dve", lambda e, pp=pp: e.tensor_tensor(out=RTM[:, 0], in0=RST[:, pp], in1=LTr_b, op=ALU.mult), reads=[RSTb[pp], S5LTb], writes=[RTMb[0]])
            P.op("dve", lambda e, pp=pp: e.scalar_tensor_tensor(out=RTM[:, 1, :, 0, :], in0=RST[:, pp, :, 1, :], scalar=-1.0, in1=LTi, op0=ALU.mult, op1=ALU.mult),
                 reads=[RSTb[pp], S5LTb], writes=[RTMb[1]])
            P.op("dve", lambda e, pp=pp: e.tensor_tensor(out=RTM[:, 1, :, 1, :], in0=RST[:, pp, :, 0, :], in1=LTi, op=ALU.mult), reads=[RSTb[pp], S5LTb], writes=[RTMb[2]])
            P.op("dve", lambda e: e.tensor_tensor(out=RTM[:, 0], in0=RTM[:, 0], in1=RTM[:, 1], op=ALU.add), reads=[RTMb[0], RTMb[1], RTMb[2]], writes=[RTMb[0]])
            P.op("dve", lambda e, pc=pc, sl=sl: e.tensor_tensor(out=RST[:, pc], in0=RTM[:, 0], in1=SALL[:, sl], op=ALU.add), reads=[RTMb[0], SIN], writes=[RSTb[pc]])
            P.op("act", lambda e, pc=pc, sl=sl: e.activation(out=SALL[:, sl], in_=RST[:, pc], func=AF.Copy), reads=[RSTb[pc]], writes=[SOUT])
            if i % 32 == 31:
                k = i // 32
                P.op("act", lambda e, pc=pc, k=k: e.activation(out=STG[:, k], in_=RST[:, pc], func=AF.Copy), reads=[RSTb[pc]], writes=[STGb])
        P.op("sp", lambda e: e.dma_start(out=o_st[j], in_=STG[:].rearrange("p k d r q -> p (k d r q)")), reads=[STGb], dma=True)
        for _ in modgen:
            pass
        m_ = merge("S", SIN, SOUT)
        SBUFS.w, SBUFS.r = m_.w, m_.r
        for s0 in (32, C + 1 + 32):
            P.op("dve", lambda e, s0=s0: e.tensor_scalar(out=SALL[:, s0:s0 + 65:32], in0=SALL[:, s0:s0 + 65:32], scalar1=KEEP[:, 0:1], scalar2=None, op0=ALU.mult),
                 reads=[SBUFS, KEEPb], writes=[SBUFS])
        if S5STOP == 3:
            raise _Stop()
        Z = H
        kcnt = 0
        pend = []
        KPSb = [merge("KPS", PSUMS[2 + b_ // 2][1]) for b_ in range(4)]
        for fc in range(8):
            s = fc % 2
            sp_load(LDV[:, s, :], d_s5v0[j, fc], LDVb[s])
            sp_load(LDN[:, s, :], d_s5bn[j, fc], LDNb[s])
            V4 = LDV[:, s, :].rearrange("p (d r x) -> p d r x", d=2, r=2)
            N4 = LDN[:, s, :].rearrange("p (d r x) -> p d r x", d=2, r=2)
            q0 = fc * 4
            chains = []
            for d in range(2):
                eng = ENG[d]
                krb = S5KB[:, 0, d, q0:q0 + 4].unsqueeze(2).broadcast_to([128, 4, 32])
                kib = S5KB[:, 1, d, q0:q0 + 4].unsqueeze(2).broadcast_to([128, 4, 32])
                v4 = lambda ap: ap.rearrange("p (a x) -> p a x", a=4)
                ch = cmul_ops(eng, v4(WST[:, 0, d, 0, :]), v4(WST[:, 0, d, 1, :]), v4(N4[:, d, 0, :]), v4(N4[:, d, 1, :]), krb, kib,
                              v4(WTA[:, d, 0, 0, :]), v4(WTA[:, d, 1, 0, :]), [LDNb[s], S5KBb, WSTb[0][d]], [WSTb[0][d]], [WTAb[d][0], WTAb[d][1]])
                ch.append(lambda d=d, s=s: P.op("act", lambda e: e.activation(out=BNB[:, s, d], in_=WST[:, 0, d], func=AF.Copy), reads=[WSTb[0][d]], writes=[BNBb[s]]))
                ch.append(lambda d=d, eng=eng, V4=V4, s=s: P.op(eng, lambda e: e.tensor_copy(out=WST[:, 0, d, 0, :], in_=V4[:, d, 0, :]), reads=[LDVb[s]], writes=[WSTb[0][d]]))
                ch.append(lambda d=d, eng=eng, V4=V4, s=s: P.op(eng, lambda e: e.tensor_scalar(out=WST[:, 0, d, 1, :], in0=V4[:, d, 1, :], scalar1=-1.0, scalar2=None, op0=ALU.mult), reads=[LDVb[s]], writes=[WSTb[0][d]]))
                ch.append(lambda d=d, s=s: P.op("act", lambda e: e.activation(out=VF[s][:, 0, d], in_=WST[:, 0, d], func=AF.Copy), reads=[WSTb[0][d]], writes=[VFb[s]]))
                lrb = S5B[:, 0, d, q0:q0 + 4].unsqueeze(1).unsqueeze(3).broadcast_to([128, 2, 4, 32])
                lib = S5LN[:, d, q0:q0 + 4].unsqueeze(1).unsqueeze(3).broadcast_to([128, 2, 4, 32])
                for n in range(1, T + 1):
                    ch += recur_ops(eng, d, n, (4, 32), lrb, lib, VF[s], VFb[s], [S5Bb, S5LNb])
                chains.append(ch)
            interleave(chains + [pend])
            pend = []
            klist = [(dl, d) for dl in range(T) for d in range(2) if not (dl == 0 and d == 1)]
            for g0 in range(0, len(klist), 4):
                grp = klist[g0:g0 + 4]
                bank = kcnt % 4
                kcnt += 1
                pk = PSUMS[2 + bank // 2][0]
                pkb = KPSb[bank]
                cb = (bank % 2) * 512

                def mmK(e, grp=grp, pk=pk, cb=cb, s=s):
                    last = None
                    for gi, (dl, d) in enumerate(grp):
                        col = cb + gi * 128
                        dirs = (0, 1) if dl == 0 else (d,)
                        nmm = len(dirs) * 2
                        i_ = 0
                        for dd in dirs:
                            for ri in range(2):
                                last = e.matmul(pk[:, col:col + 128], lhsT=BNB[:, s, dd, ri, :], rhs=VF[s][:, dl, dd, ri, :], start=(i_ == 0), stop=(i_ == nmm - 1))
                                i_ += 1
                    return last
                P.op("pe", mmK, reads=[BNBb[s], VFb[s]], writes=[pkb])
                for gi, (dl, d) in enumerate(grp):
                    col = cb + gi * 128
                    idx = 7 + dl if d == 0 else 7 - dl
                    if dl == 0:
                        pend.append(lambda col=col, pk=pk, pkb=pkb: P.op("dve", lambda e: e.tensor_tensor(out=KTMP[:], in0=pk[:, col:col + 128], in1=BDMASK, op=ALU.mult), reads=[CONSTb], writes=[KTMPb, pkb]))
                        pend.append(lambda fc=fc, s=s: P.op("dve", lambda e: e.scalar_tensor_tensor(out=KF[s][:, 7, :], in0=IDENT_F, scalar=S5D[:, j, fc:fc + 1], in1=KTMP[:], op0=ALU.mult, op1=ALU.add),
                                                           reads=[CONSTb, S5Db, KTMPb], writes=[KFb[s]]))
                    else:
                        pend.append(lambda col=col, idx=idx, pk=pk, s=s, pkb=pkb: P.op("dve", lambda e: e.tensor_tensor(out=KF[s][:, idx, :], in0=pk[:, col:col + 128], in1=BDMASK, op=ALU.mult),
                                                                                    reads=[CONSTb], writes=[KFb[s], pkb]))
            ua, ub = Uap(fc)
            u3 = ua.rearrange("p (c t) -> p c t", t=T)
            pt, pbuf = PSUMS[fc % 2]

            def mmC(e, fc=fc, pt=pt, u3=u3, s=s):
                last = None
                for jj in range(T):
                    o = pt[:, jj * 128:(jj + 1) * 128]
                    for j2 in range(T):
                        last = e.matmul(o, lhsT=KF[s][:, 7 + jj - j2, :], rhs=u3[:, :, j2], start=(j2 == 0), stop=False)
                    for d in range(2):
                        n = jj + 1 if d == 0 else T - jj
                        s0 = 0 if d == 0 else C + 2
                        for ri in range(2):
                            lastq = (d == 1 and ri == 1)
                            for q in range(4):
                                qq = fc * 4 + q
                                oq = pt[32 * q:32 * q + 32, jj * 128:(jj + 1) * 128]
                                last = e.matmul(oq, lhsT=VF[s][:, n, d, ri, 32 * q:32 * q + 32], rhs=SALL[:, s0:s0 + C, ri, qq], start=False, stop=lastq, tile_position=(0, 32 * q))
                return last
            def fin(mmC=mmC, s=s, ub=ub, pbuf=pbuf, fc=fc, pt=pt):
                P.op("pe", mmC, reads=[KFb[s], VFb[s], ub, SBUFS], writes=[pbuf])
                P.op("act", lambda e: e.activation(out=Z[:, fc, :].rearrange("p (c t) -> p t c", t=T), in_=pt[:].rearrange("p (t c) -> p t c", t=T),
                                                   func=AF.Gelu_apprx_tanh), reads=[pbuf], writes=[Hb[fc]])
            pend.append(fin)
        for t_ in pend:
            t_()
        if DBG and l == 0:
            for fc in range(8):
                dbg_bf(o_dZ[fc * 128:(fc + 1) * 128, :], H[:, fc, :], [Hb[fc]])
        if S5STOP == 4:
            raise _Stop()
        for t_ in range(2):
            m_ = merge("PS", KPSb[2 * t_], KPSb[2 * t_ + 1])
            PSUMS[2 + t_][1].w, PSUMS[2 + t_][1].r = m_.w, m_.r
        for bb in BIGb[0:17]:
            bb.w, bb.r = dict(SBUFS.w), dict(SBUFS.r)
        for bb in BIGb[17:22]:
            bb.w, bb.r = dict(VFb[0].w), dict(VFb[0].r)
        for i in range(2):
            LNTb[i].w, LNTb[i].r = dict(KFb[0].w), dict(KFb[0].r)
            LNTb[2 + i].w, LNTb[2 + i].r = dict(KFb[1].w), dict(KFb[1].r)
        G3 = BIG[:, 0:8 * NT].rearrange("p (k n) -> p k n", n=NT)
        for b in range(4):
            wap, wb = w_get()
            w3 = wap.rearrange("p (k n) -> p k n", n=512)
            for jj in range(2):
                jc = 2 * b + jj
                pv_, pvb = PSUMS[(jj * 2) % 4]
                pg, pgb = PSUMS[(jj * 2 + 1) % 4]
                proj_chunk(pv_, pvb, w3, wb, jj * 2, H, Hb, 8)
                proj_chunk(pg, pgb, w3, wb, jj * 2 + 1, H, Hb, 8)
                t = jj
                P.op("act", lambda e, pg=pg, t=t: e.activation(out=TS[:, t, :], in_=pg[:], func=AF.Sigmoid), reads=[pgb], writes=[TSb[t]])
                P.op("dve", lambda e, pv_=pv_, t=t, jc=jc: e.tensor_tensor(out=G3[:, jc, :], in0=pv_[:], in1=TS[:, t, :], op=ALU.mult),
                     reads=[pvb, TSb[t]], writes=[BIGb[jc]])
        outproj_ln(l, 0, G3, BIGb[0:8], 8, 2, 512, (l, 1))
        if DBG and l == 0:
            dbg_x(o_dXA)


    compute_mod(0)
    bgs[0] = record_lambda(0)
    modulate(0, 0)
    try:
        for l in range(DEPTH):
            if l >= STAGE:
                break
            if l % 2 == 0:
                s5_mixer(l)
            else:
                attention(l)
            bg = []
            if l + 1 < DEPTH and l % 2 == 1:
                compute_mod(l + 1)
                bg = record_lambda(l + 1)
            ffn(l, bg)
    except _Stop:
        pass
    for c in range(8):
        P.op("sp", lambda e, c=c: e.dma_start(out=o_yT[c * 128:(c + 1) * 128, :], in_=X[:, c, :]), reads=[Xb[c]], dma=True)
    P.finish()
    return nc


_CACHE = {}


def kernel(**inp):
    inp = {k: np.asarray(v) for k, v in inp.items()}
    f32 = np.float32
    wall = build_wall(inp)
    nblk = wall.shape[0]
    s5h = s5_host_layout(inp)
    cos, sin = rope_tables()
    rope_s = np.ascontiguousarray(np.stack([cos, sin], 1)).astype(f32)
    rope_p = np.ascontiguousarray(np.stack([np.ones_like(cos), np.zeros_like(sin)], 1)).astype(f32)
    consts = const_tables()
    bmodT = np.ascontiguousarray(inp["b_mod"].reshape(DEPTH, 48, 128).transpose(2, 0, 1).reshape(128, DEPTH * 48)).astype(f32)
    lng = inp["ln_g"].reshape(DEPTH * 2 * 8, 128).T
    lnb = inp["ln_b"].reshape(DEPTH * 2 * 8, 128).T
    lnT = np.ascontiguousarray(np.concatenate([lng, lnb], 1)).astype(f32)
    gain = np.ascontiguousarray(np.stack([inp["q_norm_g"][0], inp["q_norm_g"][1], inp["k_norm_g"][0], inp["k_norm_g"][1]], 1)).astype(f32)
    mask_s = np.zeros((128, 48), f32)
    mask_p = np.full((12, 4), -30000.0, f32)
    for kt in range(8):
        mask_p[kt, kt // 2] = 0.0
    mask_p = np.ascontiguousarray(np.broadcast_to(mask_p.reshape(1, 48), (128, 48))).astype(f32)
    in_maps = []
    for core in range(8):
        m = dict(wall=wall, bmodT=bmodT, lnT=lnT, consts=consts, gain=gain, **s5h)
        if core < 4:
            b = core
            m["xT"] = np.ascontiguousarray(inp["x_sample"][b].T)
            cvec = inp["c"][b]
            m["rope"] = rope_s
            m["maskb"] = mask_s
            m["keep"] = np.ones((128, 1), f32)
            m["ckT"] = np.ascontiguousarray(inp["cache_k"][b].transpose(0, 3, 2, 1))
            m["cv"] = np.ascontiguousarray(inp["cache_v"][b].reshape(2, 4, 128, 256).transpose(0, 2, 1, 3))
            m["h0"] = h0_layout(inp["state_s5"][b])
        else:
            s0 = (core - 4) * 4
            m["xT"] = np.ascontiguousarray(inp["x_prompt"][s0:s0 + 4].reshape(NT, D).T)
            cvec = inp["c_ctx"]
            m["rope"] = rope_p
            m["maskb"] = mask_p
            m["keep"] = np.zeros((128, 1), f32)
            m["ckT"] = np.zeros((2, 128, 2, 512), f32)
            m["cv"] = np.zeros((2, 128, 4, 256), f32)
            m["h0"] = np.zeros((128, 2, 2, 2, 32), f32)
        m["cT"] = np.ascontiguousarray(cvec.reshape(8, 128).T).astype(f32)
        in_maps.append({k: np.ascontiguousarray(v, dtype=f32) for k, v in m.items()})
    if "nc" not in _CACHE:
        _CACHE["nc"] = build_program(nblk)
    nc = _CACHE["nc"]
    _CACHE.pop("nc")
    res = run_bass_kernel_spmd(nc, in_maps, core_ids=list(range(8)))
    R = res.results
    if DBG:
        _CACHE["dbg"] = {"c%d_%s" % (ci, k): R[ci][k] for ci in (0, 4) for k in R[ci] if k.startswith("d")}
    y_sample = np.stack([R[b]["yT"].T for b in range(4)], 0).astype(f32)
    y_prompt = np.concatenate([R[4 + i]["yT"].T.reshape(4, 256, D) for i in range(4)], 0).astype(f32)
    nk = np.zeros((16, 2, 256, 2, 128), f32)
    nv = np.zeros((16, 2, 256, 2, 128), f32)
    ns = np.zeros((16, 2, 2, 2, 64, 64), f32)
    for i in range(4):
        r = R[4 + i]
        ko = r["kout"]
        vo = r["vout"]
        so = r["stout"].reshape(2, 2, 64, 4, 2, 2, 32)
        for s in range(4):
            bidx = i * 4 + s
            nk[bidx] = ko[:, :, :, s * 256:(s + 1) * 256].transpose(0, 3, 1, 2)
            nv[bidx] = vo[:, s * 256:(s + 1) * 256, :].reshape(2, 256, 2, 128)
            for d in range(2):
                k = s if d == 0 else 3 - s
                blk = so[:, :, :, k, d, :, :]
                ns[bidx, :, d] = blk.transpose(0, 3, 4, 1, 2).reshape(2, 2, 64, 64)
    return (y_prompt, y_sample, nk, nv, ns)
```

```python
import os
import numpy as np
import concourse.bass as bass
import concourse.mybir as mybir
from concourse.bass_utils import run_bass_kernel_spmd

F32 = mybir.dt.float32
BF16 = mybir.dt.bfloat16
AF = mybir.ActivationFunctionType
ALU = mybir.AluOpType

D = 1024
NT = 1024
DEPTH = 4
DFF = 2816
KFF = 22
T = 8
C = NT // T
NSLOT = 2 * (C + 1)
ALPHA = (2.0 * DEPTH) ** 0.25
LN_EPS = 1e-6
RMS_EPS = 1e-6
ATT_SCALE = 128 ** -0.5
WCOLS = 4096
NWSLOT = 3
EPOCH = 16000
NDMA = 8
MAGIC = 12582912.0
STAGE = int(os.environ.get("K_STAGE", "99"))
DBG = int(os.environ.get("K_DBG", "0"))
S5STOP = int(os.environ.get("K_S5STOP", "0"))
KVAR = int(os.environ.get("K_VAR", "0"))


class _Stop(Exception):
    pass


class Buf:
    __slots__ = ("w", "r", "name")

    def __init__(self, name=""):
        self.w = {}
        self.r = {}
        self.name = name


def merge(name, *olds):
    b = Buf(name)
    for o in olds:
        for k, v in list(o.w.items()) + list(o.r.items()):
            b.r[k] = max(b.r.get(k, 0), v)
            b.w[k] = max(b.w.get(k, 0), v)
    return b


class Prog:
    def __init__(self, nc):
        self.nc = nc
        self.eng = {"pe": nc.tensor, "act": nc.scalar, "dve": nc.vector, "pool": nc.gpsimd, "sp": nc.sync}
        self.cnt = {}
        self.semlist = {}
        self.known = {e: {} for e in self.eng}
        self.allsems = []
        self.rec = None
        for k in ["pe", "act", "dve", "pool"]:
            self.cnt[k] = 0
            self.semlist[k] = []
        for pre in ("q", "g"):
            for i in range(NDMA):
                k = "%s%d" % (pre, i)
                self.cnt[k] = 0
                self.semlist[k] = [self._newsem(k)]
        self.dma_rr = {"q": 0, "g": 0}

    def _newsem(self, name):
        cm = self.nc.semaphore("s_%s_%d" % (name, len(self.allsems)))
        s = cm.__enter__()
        self.allsems.append(s)
        return s

    def _semval(self, k, v):
        if k[0] in "qg":
            return self.semlist[k][0], v
        idx = (v - 1) // EPOCH
        while len(self.semlist[k]) <= idx:
            self.semlist[k].append(self._newsem(k))
        return self.semlist[k][idx], v - idx * EPOCH

    def op(self, e, fn, reads=(), writes=(), dma=False):
        if self.rec is not None:
            self.rec.append((e, fn, tuple(reads), tuple(writes), dma))
            return 0
        deps = {}
        for b in reads:
            for k, v in b.w.items():
                if v > deps.get(k, 0):
                    deps[k] = v
        for b in writes:
            for k, v in b.w.items():
                if v > deps.get(k, 0):
                    deps[k] = v
            for k, v in b.r.items():
                if v > deps.get(k, 0):
                    deps[k] = v
        kn = self.known[e]
        for k, v in deps.items():
            if k == e and e == "pe":
                continue
            if kn.get(k, 0) >= v:
                continue
            s, sv = self._semval(k, v)
            self.eng[e].wait_ge(s, sv)
            kn[k] = v
        inst = fn(self.eng[e])
        if dma:
            pre = "g" if e == "pool" else "q"
            key = "%s%d" % (pre, self.dma_rr[pre])
            self.dma_rr[pre] = (self.dma_rr[pre] + 1) % NDMA
            self.cnt[key] += 16
            val = self.cnt[key]
            inst.then_inc(self.semlist[key][0], 16)
        else:
            key = e
            self.cnt[e] += 1
            val = self.cnt[e]
            s, sv = self._semval(e, val)
            inst.then_inc(s, 1)
        for b in reads:
            if val > b.r.get(key, 0):
                b.r[key] = val
        for b in writes:
            b.w = {key: val}
            b.r = {}
        return val

    def finish(self):
        sp = self.eng["sp"]
        for k, v in self.cnt.items():
            if v > 0:
                s, sv = self._semval(k, v)
                sp.wait_ge(s, sv)
        for s in self.allsems:
            sp.sem_clear(s)


def _blk(W, cols):
    kin = W.shape[0]
    kc = kin // 128
    a = W[:, cols].reshape(kc, 128, len(cols)).transpose(1, 0, 2).reshape(128, kc * len(cols))
    out = np.zeros((128, WCOLS), np.float32)
    out[:, :a.shape[1]] = a
    return out


def _r(a, n):
    return np.arange(a, a + n)


def mod_blocks(w_mod_l):
    return [_blk(w_mod_l, _r(b * 512, 512)) for b in range(12)]


def ffn_blocks(w_in, w_out):
    bl = []
    for b in range(11):
        j0, j1 = 2 * b, 2 * b + 1
        cols = np.concatenate([_r(j0 * 128, 128), _r(DFF + j0 * 128, 128), _r(j1 * 128, 128), _r(DFF + j1 * 128, 128)])
        bl.append(_blk(w_in, cols))
    for c in range(8):
        bl.append(_blk(w_out, _r(c * 128, 128)))
    return bl


def s5_blocks(w_in, w_glu, w_out):
    bl = [_blk(w_in, _r(b * 512, 512)) for b in range(2)]
    for b in range(4):
        j0, j1 = 2 * b, 2 * b + 1
        cols = np.concatenate([_r(j0 * 128, 128), _r(D + j0 * 128, 128), _r(j1 * 128, 128), _r(D + j1 * 128, 128)])
        bl.append(_blk(w_glu, cols))
    bl += [_blk(w_out, _r(b * 512, 512)) for b in range(2)]
    return bl


def attn_blocks(w_qkv, w_o):
    bl = [_blk(w_qkv, _r(b * 512, 512)) for b in range(3)]
    bl += [_blk(w_o, _r(b * 512, 512)) for b in range(2)]
    return bl


def build_wall(inp):
    bl = []
    bl += mod_blocks(inp["w_mod"][0])
    for l in range(DEPTH):
        j = l // 2
        if l % 2 == 0:
            sb_ = s5_blocks(inp["w_s5_in"][j], inp["w_s5_glu"][j], inp["w_s5_out"][j])
            bl += sb_[0:2]
            if l + 1 < DEPTH:
                bl += mod_blocks(inp["w_mod"][l + 1])
            bl += sb_[2:]
        else:
            bl += attn_blocks(inp["w_qkv"][j], inp["w_o"][j])
            if l + 1 < DEPTH:
                bl += mod_blocks(inp["w_mod"][l + 1])
        bl += ffn_blocks(inp["w_ffn_in"][l], inp["w_ffn_out"][l])
    return np.ascontiguousarray(np.stack(bl, 0))


def s5_host_layout(inp):
    out = {}
    G, P, H = 64, 64, 16
    def c_lay(a):
        a5 = a.reshape(2, 2, 8, 8, P)
        a5 = np.broadcast_to(a5[:, :, :, :, None, :], (2, 2, 8, 8, H, P))
        return np.ascontiguousarray(a5.transpose(3, 4, 0, 1, 2, 5).reshape(128, 2, 2, 8, P))
    def b_lay(a):
        a5 = a.reshape(2, 2, 32, 2, P)
        return np.ascontiguousarray(a5.transpose(3, 4, 0, 1, 2).reshape(128, 2, 2, 32))
    ld = np.broadcast_to(inp["s5_log_dt"][:, :, :, None], (2, 2, G, P))
    nat = np.stack([inp["s5_a_re"], inp["s5_a_im"], ld], 0)
    out["s5n"] = np.ascontiguousarray(nat.transpose(3, 1, 0, 2, 4))
    selc = np.zeros((128, 8, 8, 16), np.float32)
    for g in range(64):
        selc[g, g // 8, g % 8, :] = 1.0
    out["selc"] = selc.reshape(128, 8, 128)
    selb = np.zeros((128, 34), np.float32)
    for g in range(64):
        selb[g, g // 2] = 1.0
        selb[g, 32 + (g % 2)] = 1.0
    out["selb"] = selb
    b = np.stack([inp["s5_b_re"], inp["s5_b_im"]], 2)
    b8 = b.reshape(2, 2, 2, 8, 4, 2, P, H)
    w0 = np.zeros((2, 8, 4, 2, H, 2, 2, 2, P), np.float32)
    bn = np.zeros((2, 8, 2, P, 2, 2, 4, 2, H), np.float32)
    for a in range(2):
        w0[:, :, :, a, :, :, :, a, :] = b8[:, :, :, :, :, a].transpose(0, 3, 4, 6, 1, 2, 5)
        bn[:, :, a, :, :, :, :, a, :] = b8[:, :, :, :, :, a].transpose(0, 3, 5, 1, 2, 4, 6)
    out["s5w0"] = np.ascontiguousarray(w0.reshape(2, 8, 128, 512))
    out["s5bn"] = np.ascontiguousarray(bn.reshape(2, 8, 128, 512))
    c = np.stack([inp["s5_c_re"], inp["s5_c_im"]], 2)
    c8 = c.reshape(2, 2, 2, 8, 4, 2, H, P)
    v0 = np.zeros((2, 8, 2, P, 2, 2, 4, 2, H), np.float32)
    for a in range(2):
        v0[:, :, a, :, :, :, :, a, :] = c8[:, :, :, :, :, a].transpose(0, 3, 6, 1, 2, 4, 5)
    out["s5v0"] = np.ascontiguousarray(v0.reshape(2, 8, 128, 512))
    out["s5d"] = np.ascontiguousarray(inp["s5_d"].reshape(2, 8, 128).transpose(2, 0, 1))
    return out


def h0_layout(st):
    s = st.reshape(2, 2, 2, 32, 2, 64)
    return np.ascontiguousarray(s.transpose(4, 5, 0, 1, 2, 3).reshape(128, 2, 2, 2, 32))


def rope_tables():
    l = np.arange(NT)
    row = (l // 64).astype(np.float32)
    col = (l % 64).astype(np.float32)
    inv = (np.float32(10000.0) ** (-np.arange(32, dtype=np.float32) / np.float32(32))).astype(np.float32)
    ar = row[None, :] * inv[:, None]
    ac = col[None, :] * inv[:, None]
    cos = np.concatenate([np.cos(ar), np.cos(ar), np.cos(ac), np.cos(ac)], 0).astype(np.float32)
    sin = np.concatenate([np.sin(ar), np.sin(ar), np.sin(ac), np.sin(ac)], 0).astype(np.float32)
    return cos, sin


def const_tables():
    ident = np.eye(128, dtype=np.float32)
    bd = np.kron(np.eye(8, dtype=np.float32), np.ones((16, 16), np.float32))
    rot = np.zeros((128, 128), np.float32)
    for base in (0, 64):
        for i in range(32):
            rot[base + 32 + i, base + i] = -1.0
            rot[base + i, base + 32 + i] = 1.0
    return np.ascontiguousarray(np.stack([ident, bd, rot], 1))


def build_program(nblk):
    nc = bass.Bass("TRN2", target_bir_lowering=False)
    P = Prog(nc)

    def din(name, shape):
        return nc.dram_tensor(name, list(shape), F32, kind="ExternalInput").ap()

    def dout(name, shape):
        return nc.dram_tensor(name, list(shape), F32, kind="ExternalOutput").ap()

    d_xT = din("xT", [D, NT])
    d_cT = din("cT", [128, 8])
    d_wall = din("wall", [nblk, 128, WCOLS])
    d_bmod = din("bmodT", [128, DEPTH * 48])
    d_ln = din("lnT", [128, 128])
    d_rope = din("rope", [128, 2, NT])
    d_mask = din("maskb", [128, 48])
    d_keep = din("keep", [128, 1])
    d_ckT = din("ckT", [2, 128, 2, 512])
    d_cv = din("cv", [2, 128, 4, 256])
    d_gain = din("gain", [128, 4])
    d_const = din("consts", [128, 3, 128])
    d_s5w0 = din("s5w0", [2, 8, 128, 512])
    d_s5bn = din("s5bn", [2, 8, 128, 512])
    d_s5v0 = din("s5v0", [2, 8, 128, 512])
    d_s5d = din("s5d", [128, 2, 8])
    d_h0 = din("h0", [128, 2, 2, 2, 32])
    d_s5n = din("s5n", [64, 2, 3, 2, 64])
    d_selc = din("selc", [128, 8, 128])
    d_selb = din("selb", [128, 34])
    o_yT = dout("yT", [D, NT])
    o_k = dout("kout", [2, 2, 128, NT])
    o_v = dout("vout", [2, NT, 256])
    o_st = dout("stout", [2, 128, 512])
    if DBG:
        o_dU = dout("dU", [D, NT]); o_dZ = dout("dZ", [D, NT]); o_dXA = dout("dXA", [D, NT]); o_dS = None
        o_dQ = dout("dQ", [D, NT]); o_dK = dout("dK", [128, 2, 1536]); o_dO = dout("dO", [D, NT]); o_dXA1 = dout("dXA1", [D, NT])

    def dbg_bf(dst, src, bufs):
        P.op("pool", lambda e: e.dma_start(out=dst, in_=src), reads=bufs, dma=True)

    def dbg_x(dst):
        for c in range(8):
            P.op("sp", lambda e, c=c: e.dma_start(out=dst[c * 128:(c + 1) * 128, :], in_=X[:, c, :]), reads=[Xb[c]], dma=True)

    def sb(name, shape, dt=F32):
        cm = nc.sbuf_tensor(name, list(shape), dt)
        return cm.__enter__()

    def ps(name, shape, dt=F32):
        cm = nc.psum_tensor(name, list(shape), dt)
        return cm.__enter__()

    X = sb("X", [128, 8, NT])
    Xb = [Buf("X%d" % i) for i in range(8)]
    H = sb("H", [128, 8, NT], BF16)
    Hb = [Buf("H%d" % i) for i in range(8)]
    BIG = sb("BIG", [128, KFF * NT], BF16)
    BIGb = [Buf("BIG%d" % i) for i in range(KFF)]
    WR = sb("WR", [128, NWSLOT, WCOLS], BF16)
    WRb = [Buf("WR%d" % i) for i in range(NWSLOT)]
    LNT = sb("LNT", [128, 4, NT], BF16)
    LNTb = [Buf("LNT%d" % i) for i in range(4)]
    TS = sb("TS", [128, 4, NT])
    TSb = [Buf("TS%d" % i) for i in range(4)]
    MOD = sb("MOD", [128, DEPTH, 48])
    MODb = [Buf("MOD%d" % i) for i in range(DEPTH)]
    BMOD = sb("BMOD", [128, DEPTH * 48]); BMODb = Buf("BMOD")
    LNP = sb("LNP", [128, 128]); LNPb = Buf("LNP")
    CT = sb("CT", [128, 8]); CTb = Buf("CT")
    CS = sb("CS", [128, 8], BF16); CSb = Buf("CS")
    ROW = sb("ROW", [1, 512]); ROWb = Buf("ROW")
    ONE1 = sb("ONE1", [1, 1]); ONE1b = Buf("ONE1")
    MASK = sb("MASK", [128, 48]); MASKb = Buf("MASK")
    KEEP = sb("KEEP", [128, 1]); KEEPb = Buf("KEEP")
    GAIN = sb("GAIN", [128, 4]); GAINb = Buf("GAIN")
    CONST = sb("CONST", [128, 3, 128]); CONSTb = Buf("CONST")
    CB = sb("CB", [128, 3, 128], BF16); CBb = Buf("CB")
    ONES = sb("ONES", [128, 128], BF16); ONESb = Buf("ONES")
    EPSC = sb("EPSC", [128, 2]); EPSb = Buf("EPSC")
    ARENA = sb("ARENA", [128, 8192])
    ARb = {"cur": [Buf("ARENA")]}

    def arena_switch(names):
        olds = ARb["cur"]
        news = [merge(n, *olds) for n in names]
        ARb["cur"] = news
        return news
    ROPE = ARENA[:, 0:2048].rearrange("p (a n) -> p a n", a=2)
    KTB = ARENA[:, 2048:3584].bitcast(BF16).rearrange("p (a n) -> p a n", a=2)
    VTB = ARENA[:, 3584:5120].bitcast(BF16).rearrange("p (a n) -> p a n", a=12)
    PR = ARENA[:, 5120:6144].bitcast(BF16).rearrange("p (a n) -> p a n", a=4)
    S5C = ARENA[:, 0:2048].rearrange("p (t d f x) -> p t d f x", t=2, d=2, f=8)
    S5K = ARENA[:, 2048:4096].rearrange("p (t d f x) -> p t d f x", t=2, d=2, f=8)
    NATQ = ARENA[0:64, 4096:4736].rearrange("p (t x) -> p t x", t=5)
    NATT = ARENA[0:64, 4736:5504].rearrange("p (t x) -> p t x", t=6)
    XPAD = ARENA[0:64, 5504:6528].rearrange("p (t x) -> p t x", t=8)
    SELC = ARENA[:, 6528:7552].rearrange("p (f x) -> p f x", f=8)
    SELB = ARENA[:, 7552:7586]
    NATQ128 = ARENA[:, 4096:4736].rearrange("p (t x) -> p t x", t=5)
    XPAD128 = ARENA[:, 5504:6528].rearrange("p (t x) -> p t x", t=8)
    VF1 = ARENA[:, 4096:4096 + 2304].bitcast(BF16).rearrange("p (n d r x) -> p n d r x", n=9, d=2, r=2)
    S5B = sb("S5B", [128, 2, 2, 32]); S5Bb = Buf("S5B")
    S5KB = sb("S5KB", [128, 2, 2, 32]); S5KBb = Buf("S5KB")
    S5TB = sb("S5TB", [128, 2, 2, 32]); S5TBb = Buf("S5TB")
    S5LT = sb("S5LT", [128, 2, 2, 32]); S5LTb = Buf("S5LT")
    S5LN = sb("S5LN", [128, 2, 32]); S5LNb = Buf("S5LN")
    S5D = sb("S5D", [128, 2, 8]); S5Db = Buf("S5D")
    H0 = sb("H0", [128, 2, 2, 2, 32]); H0b = Buf("H0")
    SELCb = Buf("SELC")
    SELBb = Buf("SELB")
    LDW = sb("LDW", [128, 2, 512]); LDWb = [Buf("LDW0"), Buf("LDW1")]
    LDN = sb("LDN", [128, 2, 512]); LDNb = [Buf("LDN0"), Buf("LDN1")]
    LDV = LDW; LDVb = LDWb
    WST = sb("WST", [128, 2, 2, 2, 128])
    WSTb = [[Buf("WST00"), Buf("WST01")], [Buf("WST10"), Buf("WST11")]]
    WTA = sb("WTA", [128, 2, 2, 2, 128])
    WTAb = [[Buf("WTA00"), Buf("WTA01")], [Buf("WTA10"), Buf("WTA11")]]
    BNB = sb("BNB", [128, 2, 2, 2, 128], BF16); BNBb = [Buf("BNB0"), Buf("BNB1")]
    KTMP = sb("KTMP", [128, 128]); KTMPb = Buf("KTMP")
    STG = sb("STG", [128, 4, 2, 2, 32]); STGb = Buf("STG")
    RST = sb("RST", [128, 2, 2, 2, 32]); RSTb = [Buf("RST0"), Buf("RST1")]
    RTM = sb("RTM", [128, 2, 2, 2, 32]); RTMb = [Buf("RTM0"), Buf("RTM1a"), Buf("RTM1b")]
    VST = sb("VST", [128, 2, 256]); VSTb = [Buf("VST0"), Buf("VST1")]

    PA = ps("PA", [128, NT]); PB = ps("PB", [128, NT]); PC = ps("PC", [128, NT]); PD = ps("PD", [128, NT])
    PAb, PBb, PCb, PDb = Buf("PA"), Buf("PB"), Buf("PC"), Buf("PD")
    PSUMS = [(PA, PAb), (PB, PBb), (PC, PCb), (PD, PDb)]

    wstate = {"issued": 0, "used": 0}

    def w_issue():
        i = wstate["issued"]
        if i >= nblk:
            return
        s = i % NWSLOT
        P.op("pool", lambda e: e.dma_start(out=WR[:, s, :], in_=d_wall[i]), writes=[WRb[s]], dma=True)
        wstate["issued"] += 1

    def w_get():
        i = wstate["used"]
        wstate["used"] += 1
        while wstate["issued"] < min(nblk, i + NWSLOT):
            w_issue()
        s = i % NWSLOT
        return WR[:, s, :], WRb[s]

    def sp_load(dst, src, buf):
        P.op("sp", lambda e: e.dma_start(out=dst, in_=src), writes=[buf], dma=True)

    for i in range(NWSLOT):
        w_issue()
    sp_load(CT[:], d_cT, CTb)
    sp_load(BMOD[:], d_bmod, BMODb)
    sp_load(LNP[:], d_ln, LNPb)
    sp_load(CONST[:], d_const, CONSTb)
    sp_load(GAIN[:], d_gain, GAINb)
    sp_load(MASK[:], d_mask, MASKb)
    sp_load(KEEP[:], d_keep, KEEPb)
    for c in range(8):
        sp_load(X[:, c, :], d_xT[c * 128:(c + 1) * 128, :], Xb[c])
    sp_load(S5D[:], d_s5d, S5Db)
    sp_load(H0[:], d_h0, H0b)
    sp_load(SELC, d_selc, SELCb)
    sp_load(SELB, d_selb, SELBb)

    P.op("dve", lambda e: e.memset(ONES[:], 1.0), writes=[ONESb])
    P.op("dve", lambda e: e.memset(ONE1[:], 1.0), writes=[ONE1b])
    P.op("dve", lambda e: e.memset(EPSC[:, 0:1], LN_EPS / (ALPHA * ALPHA)), writes=[EPSb])
    P.op("dve", lambda e: e.memset(EPSC[:, 1:2], RMS_EPS), writes=[EPSb])
    P.op("dve", lambda e: e.tensor_copy(out=CB[:], in_=CONST[:]), reads=[CONSTb], writes=[CBb])
    P.op("act", lambda e: e.activation(out=CS[:], in_=CT[:], func=AF.Silu), reads=[CTb], writes=[CSb])

    IDENT_F = CONST[:, 0, :]
    BDMASK = CONST[:, 1, :]
    ROTB = CB[:, 2, :]

    def compute_mod_gen(l):
        for b in range(12):
            wap, wb = w_get()
            w3 = wap.rearrange("p (k n) -> p k n", n=512)

            def mm(e):
                last = None
                for kc in range(8):
                    last = e.matmul(PC[0:1, 0:512], lhsT=CS[:, kc:kc + 1], rhs=w3[:, kc, :], start=(kc == 0), stop=(kc == 7))
                return last
            P.op("pe", mm, reads=[wb, CSb], writes=[PCb])
            P.op("act", lambda e: e.activation(out=ROW[:], in_=PC[0:1, 0:512], func=AF.Copy), reads=[PCb], writes=[ROWb])

            def tr(e):
                last = None
                for i in range(4):
                    col = b * 4 + i
                    last = e.matmul(PD[:, col:col + 1], lhsT=ROW[0:1, i * 128:(i + 1) * 128], rhs=ONE1[0:1, 0:1], start=True, stop=True)
                return last
            P.op("pe", tr, reads=[ROWb, ONE1b], writes=[PDb])
            yield b
        P.op("dve", lambda e: e.tensor_tensor(out=MOD[:, l, :], in0=PD[:, 0:48], in1=BMOD[:, l * 48:(l + 1) * 48], op=ALU.add),
             reads=[PDb, BMODb], writes=[MODb[l]])
        for base in (8, 32):
            P.op("dve", lambda e, base=base: e.tensor_scalar(out=MOD[:, l, base:base + 8], in0=MOD[:, l, base:base + 8], scalar1=1.0, scalar2=None, op0=ALU.add),
                 reads=[MODb[l]], writes=[MODb[l]])
        for base in (16, 40):
            P.op("dve", lambda e, base=base: e.tensor_scalar(out=MOD[:, l, base:base + 8], in0=MOD[:, l, base:base + 8], scalar1=1.0 / ALPHA, scalar2=None, op0=ALU.mult),
                 reads=[MODb[l]], writes=[MODb[l]])

    def compute_mod(l):
        for _ in compute_mod_gen(l):
            pass

    def modulate(l, which):
        so = 0 if which == 0 else 24
        for c in range(8):
            P.op("dve", lambda e, c=c: e.tensor_scalar(out=H[:, c, :], in0=X[:, c, :], scalar1=MOD[:, l, so + 8 + c:so + 9 + c],
                                                      scalar2=MOD[:, l, so + c:so + c + 1], op0=ALU.mult, op1=ALU.add),
                 reads=[Xb[c], MODb[l]], writes=[Hb[c]])

    def proj_chunk(pt, pbuf, w3, wb, oc, src, srcbufs, kcn):
        def mm(e):
            last = None
            for kc in range(kcn):
                for hf in range(2):
                    last = e.matmul(pt[:, hf * 512:(hf + 1) * 512], lhsT=w3[:, kc, oc * 128:(oc + 1) * 128],
                                    rhs=src[:, kc, hf * 512:(hf + 1) * 512], start=(kc == 0), stop=(kc == kcn - 1))
            return last
        P.op("pe", mm, reads=[wb] + list(srcbufs), writes=[pbuf])

    def outproj_ln(l, which, src, srcbufs, kcn, nblocks, cols_per_blk, next_mod):
        gcol = 16 if which == 0 else 40
        lni = (l * 2 + which) * 8
        oc_global = 0
        pend_st = None
        for b in range(nblocks):
            wap, wb = w_get()
            w3 = wap[:, 0:kcn * cols_per_blk].rearrange("p (k n) -> p k n", n=cols_per_blk)
            for oc in range(cols_per_blk // 128):
                c = oc_global
                pt, pbuf = PSUMS[c % 2]
                proj_chunk(pt, pbuf, w3, wb, oc, src, srcbufs, kcn)
                P.op("dve", lambda e, c=c, pt=pt: e.scalar_tensor_tensor(out=X[:, c, :], in0=pt[:], scalar=MOD[:, l, gcol + c:gcol + c + 1],
                                                                        in1=X[:, c, :], op0=ALU.mult, op1=ALU.add),
                     reads=[pbuf, MODb[l], Xb[c]], writes=[Xb[c]])
                s = (c % 2) * 2
                P.op("act", lambda e, c=c, s=s: e.activation(out=LNT[:, s, :], in_=X[:, c, :], func=AF.Copy), reads=[Xb[c]], writes=[LNTb[s]])
                P.op("act", lambda e, c=c, s=s: e.activation(out=LNT[:, s + 1, :], in_=X[:, c, :], func=AF.Square), reads=[Xb[c]], writes=[LNTb[s + 1]])

                def st(e, c=c, s=s):
                    last = None
                    for hf in range(2):
                        e.matmul(PC[:, hf * 512:(hf + 1) * 512], lhsT=ONES[:], rhs=LNT[:, s, hf * 512:(hf + 1) * 512], start=(c == 0), stop=(c == 7))
                        last = e.matmul(PD[:, hf * 512:(hf + 1) * 512], lhsT=ONES[:], rhs=LNT[:, s + 1, hf * 512:(hf + 1) * 512], start=(c == 0), stop=(c == 7))
                    return last
                if pend_st is not None:
                    pend_st()
                pend_st = (lambda st=st, s=s: P.op("pe", st, reads=[ONESb, LNTb[s], LNTb[s + 1]], writes=[PCb, PDb]))
                oc_global += 1
        pend_st()
        P.op("act", lambda e: e.activation(out=TS[:, 0, :], in_=PC[:], func=AF.Identity, scale=1.0 / D), reads=[PCb], writes=[TSb[0]])
        P.op("act", lambda e: e.activation(out=TS[:, 1, :], in_=TS[:, 0, :], func=AF.Square), reads=[TSb[0]], writes=[TSb[1]])
        P.op("dve", lambda e: e.scalar_tensor_tensor(out=TS[:, 1, :], in0=PD[:], scalar=1.0 / D, in1=TS[:, 1, :], op0=ALU.mult, op1=ALU.subtract),
             reads=[PDb, TSb[1]], writes=[TSb[1]])
        P.op("act", lambda e: e.activation(out=TS[:, 1, :], in_=TS[:, 1, :], func=AF.Ln, bias=EPSC[:, 0:1], scale=1.0), reads=[TSb[1], EPSb], writes=[TSb[1]])
        P.op("act", lambda e: e.activation(out=TS[:, 1, :], in_=TS[:, 1, :], func=AF.Exp, scale=-0.5), reads=[TSb[1]], writes=[TSb[1]])
        for c in range(8):
            t = 2 + (c % 2)
            P.op("pool" if c % 2 == 1 else "dve", lambda e, c=c, t=t: e.tensor_tensor(out=TS[:, t, :], in0=X[:, c, :], in1=TS[:, 0, :], op=ALU.subtract),
                 reads=[Xb[c], TSb[0]], writes=[TSb[t]])
            P.op("dve", lambda e, c=c, t=t: e.tensor_tensor(out=TS[:, t, :], in0=TS[:, t, :], in1=TS[:, 1, :], op=ALU.mult),
                 reads=[TSb[t], TSb[1]], writes=[TSb[t]])
            P.op("act", lambda e, c=c, t=t: e.activation(out=X[:, c, :], in_=TS[:, t, :], func=AF.Identity,
                                                         scale=LNP[:, lni + c:lni + c + 1], bias=LNP[:, 64 + lni + c:64 + lni + c + 1]),
                 reads=[TSb[t], LNPb], writes=[Xb[c]])
        if next_mod is not None:
            modulate(*next_mod)

    def ffn(l, bg=None):
        bg = bg if bg is not None else []
        for b in range(11):
            wap, wb = w_get()
            w3 = wap.rearrange("p (k n) -> p k n", n=512)
            for jj in range(2):
                j = 2 * b + jj
                pg, pgb = PSUMS[(jj * 2) % 4]
                pu, pub = PSUMS[(jj * 2 + 1) % 4]
                proj_chunk(pg, pgb, w3, wb, jj * 2, H, Hb, 8)
                proj_chunk(pu, pub, w3, wb, jj * 2 + 1, H, Hb, 8)
                t = jj
                P.op("act", lambda e, pg=pg, t=t: e.activation(out=TS[:, t, :], in_=pg[:], func=AF.Silu), reads=[pgb], writes=[TSb[t]])
                P.op("dve", lambda e, pu=pu, t=t, j=j: e.tensor_tensor(out=BIG[:, j * NT:(j + 1) * NT], in0=pu[:], in1=TS[:, t, :], op=ALU.mult),
                     reads=[pub, TSb[t]], writes=[BIGb[j]])
                replay(bg, 14)
        replay(bg, 10 ** 9)
        BIG3 = BIG[:].rearrange("p (k n) -> p k n", n=NT)
        nm = (l + 1, 0) if l + 1 < DEPTH else None
        outproj_ln(l, 1, BIG3, BIGb, KFF, 8, 128, nm)

    def attention(l):
        j = l // 2
        Q = BIG[:, 0:8 * NT].rearrange("p (k n) -> p k n", n=NT)
        nb = arena_switch(["ROPE", "KT0", "KT1"] + ["VT%d" % i for i in range(12)] + ["PR%d" % i for i in range(4)])
        ROPEb = nb[0]
        KTBb = nb[1:3]
        VTBb = nb[3:15]
        PRb = nb[15:19]
        sp_load(ROPE, d_rope, ROPEb)
        for kv in range(2):
            P.op("pool", lambda e, kv=kv: e.dma_start(out=KTB[:, kv, NT:NT + 512], in_=d_ckT[j, :, kv, :]), writes=[KTBb[kv]], dma=True)
        for t4 in range(4):
            P.op("pool", lambda e, t4=t4: e.dma_start(out=VTB[:, 8 + t4, :], in_=d_cv[j, :, t4, :]), writes=[VTBb[8 + t4]], dma=True)
        cur = None

        def emit_proj(hc_):
            nonlocal cur
            if hc_ % 4 == 0:
                cur = w_get()
            wap_, wb_ = cur
            w3_ = wap_.rearrange("p (k n) -> p k n", n=512)
            pt_, pbuf_ = PSUMS[hc_ % 2]
            proj_chunk(pt_, pbuf_, w3_, wb_, hc_ % 4, H, Hb, 8)
        emit_proj(0)
        for hc in range(10):
            pt, pbuf = PSUMS[hc % 2]
            if hc + 1 < 10:
                emit_proj(hc + 1)
            isk = hc >= 8
            gcol = (2 + j) if isk else j
            P.op("act", lambda e, pt=pt: e.activation(out=TS[:, 2, :], in_=pt[:], func=AF.Copy), reads=[pbuf], writes=[TSb[2]])
            P.op("act", lambda e, pt=pt: e.activation(out=LNT[:, 0, :], in_=pt[:], func=AF.Square), reads=[pbuf], writes=[LNTb[0]])

            def st(e):
                last = None
                for hf in range(2):
                    last = e.matmul(PC[:, hf * 512:(hf + 1) * 512], lhsT=ONES[:], rhs=LNT[:, 0, hf * 512:(hf + 1) * 512], start=True, stop=True)
                return last
            P.op("pe", st, reads=[ONESb, LNTb[0]], writes=[PCb])
            P.op("act", lambda e: e.activation(out=TS[:, 3, :], in_=PC[:], func=AF.Ln, bias=EPSC[:, 1:2], scale=1.0 / 128), reads=[PCb, EPSb], writes=[TSb[3]])
            P.op("act", lambda e: e.activation(out=TS[:, 3, :], in_=TS[:, 3, :], func=AF.Exp, scale=-0.5), reads=[TSb[3]], writes=[TSb[3]])
            P.op("dve", lambda e, gcol=gcol: e.scalar_tensor_tensor(out=TS[:, 2, :], in0=TS[:, 2, :], scalar=GAIN[:, gcol:gcol + 1], in1=TS[:, 3, :],
                                                                    op0=ALU.mult, op1=ALU.mult), reads=[TSb[2], TSb[3], GAINb], writes=[TSb[2]])
            P.op("act", lambda e: e.activation(out=LNT[:, 1, :], in_=TS[:, 2, :], func=AF.Copy), reads=[TSb[2]], writes=[LNTb[1]])

            def rt(e):
                last = None
                for hf in range(2):
                    last = e.matmul(PD[:, hf * 512:(hf + 1) * 512], lhsT=ROTB, rhs=LNT[:, 1, hf * 512:(hf + 1) * 512], start=True, stop=True)
                return last
            P.op("pe", rt, reads=[CBb, LNTb[1]], writes=[PDb])
            P.op("dve", lambda e: e.tensor_tensor(out=TS[:, 0, :], in0=PD[:], in1=ROPE[:, 1, :], op=ALU.mult), reads=[PDb, ROPEb], writes=[TSb[0]])
            P.op("dve", lambda e: e.tensor_tensor(out=TS[:, 2, :], in0=TS[:, 2, :], in1=ROPE[:, 0, :], op=ALU.mult), reads=[TSb[2], ROPEb], writes=[TSb[2]])
            if not isk:
                P.op("dve", lambda e, hc=hc: e.tensor_tensor(out=Q[:, hc, :], in0=TS[:, 2, :], in1=TS[:, 0, :], op=ALU.add),
                     reads=[TSb[2], TSb[0]], writes=[BIGb[hc]])
            else:
                kv = hc - 8
                P.op("dve", lambda e: e.tensor_tensor(out=TS[:, 1, :], in0=TS[:, 2, :], in1=TS[:, 0, :], op=ALU.add), reads=[TSb[2], TSb[0]], writes=[TSb[1]])
                P.op("act", lambda e, kv=kv: e.activation(out=KTB[:, kv, 0:NT], in_=TS[:, 1, :], func=AF.Copy), reads=[TSb[1]], writes=[KTBb[kv]])
                P.op("sp", lambda e, kv=kv: e.dma_start(out=o_k[j, kv], in_=TS[:, 1, :]), reads=[TSb[1]], dma=True)
        wap, wb = cur
        w3 = wap.rearrange("p (k n) -> p k n", n=512)
        for tt in range(8):
            pt, pbuf = PSUMS[tt % 2]

            def mmv(e, tt=tt, pt=pt):
                last = None
                for kc in range(8):
                    last = e.matmul(pt[:, 0:256], lhsT=H[:, kc, tt * 128:(tt + 1) * 128], rhs=w3[:, kc, 256:512], start=(kc == 0), stop=(kc == 7))
                return last
            P.op("pe", mmv, reads=[wb] + Hb, writes=[pbuf])
            s = tt % 2
            P.op("act", lambda e, pt=pt, s=s: e.activation(out=VST[:, s, :], in_=pt[:, 0:256], func=AF.Copy), reads=[pbuf], writes=[VSTb[s]])
            P.op("dve", lambda e, tt=tt, s=s: e.tensor_copy(out=VTB[:, tt, :], in_=VST[:, s, :]), reads=[VSTb[s]], writes=[VTBb[tt]])
            P.op("sp", lambda e, tt=tt, s=s: e.dma_start(out=o_v[j, tt * 128:(tt + 1) * 128, :], in_=VST[:, s, :]), reads=[VSTb[s]], dma=True)
        if DBG and l == 1:
            for h in range(8):
                dbg_bf(o_dQ[h * 128:(h + 1) * 128, :], Q[:, h, :], [BIGb[h]])
            dbg_bf(o_dK, KTB, KTBb)
        NSB = 4
        PAh = [PA[:, 0:512], PA[:, 512:1024], PD[:, 0:512], PD[:, 512:1024]]
        PAhb = [merge("PAh", PAb), merge("PAh", PAb), merge("PDh", PDb), merge("PDh", PDb)]
        pri = 0
        for h in range(8):
            kv = h // 4
            for qh in range(2):
                qs = slice(qh * 512, (qh + 1) * 512)

                def smm(e, kt, h=h, kv=kv, qs=qs):
                    return e.matmul(PAh[kt % NSB], lhsT=KTB[:, kv, kt * 128:(kt + 1) * 128], rhs=Q[:, h, qs], start=True, stop=True)
                for k0 in range(NSB - 1):
                    P.op("pe", lambda e, k0=k0: smm(e, k0), reads=[KTBb[kv], BIGb[h]], writes=[PAhb[k0]])
                for kt in range(12):
                    if kt + NSB - 1 < 12:
                        P.op("pe", lambda e, kt=kt: smm(e, kt + NSB - 1), reads=[KTBb[kv], BIGb[h]], writes=[PAhb[(kt + NSB - 1) % NSB]])
                    pslot = pri % 4
                    pri += 1
                    for g2 in range(2):
                        qg = qh * 2 + g2
                        P.op("act", lambda e, kt=kt, g2=g2, qg=qg, pslot=pslot: e.activation(
                            out=PR[:, pslot, g2 * 256:(g2 + 1) * 256], in_=PAh[kt % NSB][:, g2 * 256:(g2 + 1) * 256], func=AF.Exp,
                            bias=MASK[:, kt * 4 + qg:kt * 4 + qg + 1], scale=ATT_SCALE), reads=[PAhb[kt % NSB], MASKb], writes=[PRb[pslot]])

                    def pv(e, kt=kt, kv=kv, qs=qs, pslot=pslot):
                        e.matmul(PB[:, qs], lhsT=VTB[:, kt, kv * 128:(kv + 1) * 128], rhs=PR[:, pslot, :], start=(kt == 0), stop=(kt == 11))
                        return e.matmul(PC[:, qs], lhsT=ONES[:], rhs=PR[:, pslot, :], start=(kt == 0), stop=(kt == 11))
                    P.op("pe", pv, reads=[VTBb[kt], PRb[pslot], ONESb], writes=[PBb, PCb])
                P.op("dve", lambda e, qs=qs: e.reciprocal(out=TS[:, 0, qs], in_=PC[:, qs]), reads=[PCb], writes=[TSb[0]])
                P.op("dve", lambda e, qs=qs, h=h: e.tensor_tensor(out=H[:, h, qs], in0=PB[:, qs], in1=TS[:, 0, qs], op=ALU.mult),
                     reads=[PBb, TSb[0]], writes=[Hb[h]])
        m1 = merge("PA", PAhb[0], PAhb[1])
        PAb.w, PAb.r = m1.w, m1.r
        m1 = merge("PD", PAhb[2], PAhb[3])
        PDb.w, PDb.r = m1.w, m1.r
        if DBG and l == 1:
            for h in range(8):
                dbg_bf(o_dO[h * 128:(h + 1) * 128, :], H[:, h, :], [Hb[h]])
        outproj_ln(l, 0, H, Hb, 8, 2, 512, (l, 1))
        if DBG and l == 1:
            dbg_x(o_dXA1)

    def cmul_ops(e_name, out_r, out_i, in_r, in_i, lr, li, ta, tb, reads, writes, tbufs):
        return [
            lambda: P.op(e_name, lambda e: e.tensor_tensor(out=ta, in0=in_r, in1=lr, op=ALU.mult), reads=reads, writes=[tbufs[0]]),
            lambda: P.op(e_name, lambda e: e.tensor_tensor(out=tb, in0=in_i, in1=li, op=ALU.mult), reads=reads, writes=[tbufs[1]]),
            lambda: P.op(e_name, lambda e: e.tensor_tensor(out=ta, in0=ta, in1=tb, op=ALU.subtract), reads=[tbufs[0], tbufs[1]], writes=[tbufs[0]]),
            lambda: P.op(e_name, lambda e: e.tensor_tensor(out=tb, in0=in_r, in1=li, op=ALU.mult), reads=reads, writes=[tbufs[1]]),
            lambda: P.op(e_name, lambda e: e.tensor_tensor(out=out_i, in0=in_i, in1=lr, op=ALU.mult), reads=reads + [tbufs[0]], writes=writes),
            lambda: P.op(e_name, lambda e: e.tensor_tensor(out=out_i, in0=out_i, in1=tb, op=ALU.add), reads=[tbufs[1]] + writes, writes=writes),
            lambda: P.op(e_name, lambda e: e.tensor_copy(out=out_r, in_=ta), reads=[tbufs[0]], writes=writes),
        ]

    def cmul(*a):
        for t in cmul_ops(*a):
            t()

    def interleave(lists):
        n = max(len(L) for L in lists)
        for i in range(n):
            for L in lists:
                if i < len(L):
                    L[i]()

    def lam_compute(A, Ab, Kt, Kb, Tm, Tb_, n_t):
        bufs = [Ab, Kb, Tb_]
        A0, A1, A2 = A[:, 0], A[:, 1], A[:, 2]
        K0, K1 = Kt[:, 0], Kt[:, 1]
        T0, T1, T2, T3, T4, T5 = (Tm[:, i] for i in range(6))

        def tt(o, a, b, op):
            P.op("dve", lambda e: e.tensor_tensor(out=o, in0=a, in1=b, op=op), reads=bufs, writes=bufs)

        def tsc(o, a, s1, s2=None, op0=ALU.mult, op1=ALU.add):
            if s2 is None:
                P.op("dve", lambda e: e.tensor_scalar(out=o, in0=a, scalar1=s1, scalar2=None, op0=op0), reads=bufs, writes=bufs)
            else:
                P.op("dve", lambda e: e.tensor_scalar(out=o, in0=a, scalar1=s1, scalar2=s2, op0=op0, op1=op1), reads=bufs, writes=bufs)

        def stt(o, a, sc, b, op0, op1):
            P.op("dve", lambda e: e.scalar_tensor_tensor(out=o, in0=a, scalar=sc, in1=b, op0=op0, op1=op1), reads=bufs, writes=bufs)

        def horner(t, y, divs, sign):
            tsc(t, y, sign / divs[-1], 1.0)
            for dv in reversed(divs[:-1]):
                tt(t, t, y, ALU.mult)
                tsc(t, t, sign / dv, 1.0)
        tsc(K1, A2, 0.125)
        horner(K0, K1, [1.0, 2.0, 3.0, 4.0, 5.0, 6.0, 7.0, 8.0, 9.0, 10.0, 11.0], 1.0)
        for _ in range(3):
            tt(K0, K0, K0, ALU.mult)
        tt(T0, K0, A0, ALU.mult)
        tt(T1, K0, A1, ALU.mult)
        horner(K0, T0, [2.0, 3.0, 4.0, 5.0, 6.0, 7.0], 1.0)
        tt(T2, K0, T0, ALU.mult)
        tsc(K1, T1, 1.0 / 16.0)
        tt(T3, K1, K1, ALU.mult)
        horner(K0, T3, [6.0, 20.0, 42.0, 72.0, 110.0, 156.0, 210.0], -1.0)
        tt(T4, K0, K1, ALU.mult)
        horner(K0, T3, [12.0, 30.0, 56.0, 90.0, 132.0, 182.0, 240.0], -1.0)
        stt(T5, T3, -0.5, K0, ALU.mult, ALU.mult)
        for _ in range(4):
            tt(K1, T4, T4, ALU.mult)
            stt(K0, T5, 1.0, T4, ALU.add, ALU.mult)
            tsc(T4, K0, 2.0)
            tsc(T5, K1, -2.0)
        stt(K0, T2, 1.0, T5, ALU.add, ALU.mult)
        tt(T0, K0, T2, ALU.add)
        stt(T1, T2, 1.0, T4, ALU.add, ALU.mult)
        tt(T3, A0, A0, ALU.mult)
        tt(T2, A1, A1, ALU.mult)
        tt(T3, T3, T2, ALU.add)
        P.op("dve", lambda e: e.reciprocal(out=T3, in_=T3), reads=bufs, writes=bufs)
        tt(T2, T0, A0, ALU.mult)
        tt(T4, T1, A1, ALU.mult)
        tt(T2, T2, T4, ALU.add)
        tt(K0, T2, T3, ALU.mult)
        tt(T2, T1, A0, ALU.mult)
        tt(T4, T0, A1, ALU.mult)
        tt(T2, T2, T4, ALU.subtract)
        tt(K1, T2, T3, ALU.mult)
        tsc(A0, T0, 1.0, None, op0=ALU.add)
        P.op("dve", lambda e: e.tensor_copy(out=A1, in_=T1), reads=bufs, writes=bufs)

    lamctx = {}
    bgs = {}

    def s5_lambda(l):
        j = l // 2
        nb = arena_switch(["S5C", "S5K", "S5T"])
        S5Cb, S5Kb, S5Tb = nb
        lamctx[l] = nb
        P.op("dve", lambda e: e.memset(ARENA[64:128, 4096:6528], 0.0), writes=[S5Tb])
        sp_load(NATQ[:, 0:2, :], d_s5n[:, j, 0:2].rearrange("g t d p -> g t (d p)"), S5Tb)
        sp_load(NATQ[:, 2, :], d_s5n[:, j, 2].rearrange("g d p -> g (d p)"), S5Tb)
        lam_compute(NATQ[:, 0:3, :], S5Tb, NATQ[:, 3:5, :], S5Tb, NATT, S5Tb, 0)
        for fc in range(8):
            pt, pbuf = PSUMS[2 + fc % 2]

            def mmx(e, fc=fc, pt=pt):
                e.matmul(pt[:, 0:256], lhsT=SELC[:, fc, :], rhs=NATQ128[:, 0:2, :].rearrange("p t x -> p (t x)"), start=True, stop=True)
                return e.matmul(pt[:, 256:512], lhsT=SELC[:, fc, :], rhs=NATQ128[:, 3:5, :].rearrange("p t x -> p (t x)"), start=True, stop=True)
            P.op("pe", mmx, reads=[SELCb, S5Tb], writes=[pbuf])
            P.op("act", lambda e, fc=fc, pt=pt: e.activation(out=S5C[:, :, :, fc, :], in_=pt[:, 0:256].rearrange("p (t d x) -> p t d x", t=2, d=2), func=AF.Copy),
                 reads=[pbuf], writes=[S5Cb])
            P.op("act", lambda e, fc=fc, pt=pt: e.activation(out=S5K[:, :, :, fc, :], in_=pt[:, 256:512].rearrange("p (t d x) -> p t d x", t=2, d=2), func=AF.Copy),
                 reads=[pbuf], writes=[S5Kb])
        slots = (0, 1, 3, 4)
        for ti in range(4):
            for d in range(2):
                for a in range(2):
                    P.op("dve", lambda e, ti=ti, d=d, a=a: e.tensor_scalar(out=XPAD[:, ti * 2 + d, a * 64:(a + 1) * 64], in0=NATQ[:, slots[ti], d * 64:(d + 1) * 64],
                                                                        scalar1=SELB[0:64, 32 + a:33 + a], scalar2=None, op0=ALU.mult),
                         reads=[S5Tb, SELBb], writes=[S5Tb])

        def mmb(e):
            last = None
            for k in range(8):
                last = e.matmul(PA[:, k * 32:(k + 1) * 32], lhsT=XPAD128[:, k, :], rhs=SELB[:, 0:32], start=True, stop=True)
            return last
        P.op("pe", mmb, reads=[S5Tb, SELBb], writes=[PAb])
        P.op("act", lambda e: e.activation(out=S5B[:], in_=PA[:, 0:128].rearrange("p (t d q) -> p t d q", t=2, d=2), func=AF.Copy), reads=[PAb], writes=[S5Bb])
        P.op("act", lambda e: e.activation(out=S5KB[:], in_=PA[:, 128:256].rearrange("p (t d q) -> p t d q", t=2, d=2), func=AF.Copy), reads=[PAb], writes=[S5KBb])
        P.op("dve", lambda e: e.tensor_scalar(out=S5LN[:], in0=S5B[:, 1], scalar1=-1.0, scalar2=None, op0=ALU.mult), reads=[S5Bb], writes=[S5LNb])
        P.op("dve", lambda e: e.tensor_copy(out=S5LT[:], in_=S5B[:]), reads=[S5Bb], writes=[S5LTb])
        for _ in range(3):
            cmul("dve", S5LT[:, 0], S5LT[:, 1], S5LT[:, 0], S5LT[:, 1], S5LT[:, 0], S5LT[:, 1], S5TB[:, 0], S5TB[:, 1], [S5LTb], [S5LTb], [S5TBb, S5TBb])

    def record_lambda(l):
        P.rec = []
        s5_lambda(l)
        ops = P.rec
        P.rec = None
        return ops

    def replay(ops, n):
        k = 0
        while ops and k < n:
            P.op(*ops.pop(0))
            k += 1

    def s5_mixer(l):
        j = l // 2
        SALL = BIG[:, 0:NSLOT * 64].rearrange("p (s r q) -> p s r q", r=2, q=32)
        SBUFS = merge("S", *BIGb[0:17])
        VF0b = merge("VF0", *BIGb[17:22])
        for bb in BIGb:
            bb.w, bb.r = {}, {}
        S5Cb, S5Kb, S5Tb = lamctx[l]
        VF0 = BIG[:, 17 * NT:17 * NT + 9 * 512].rearrange("p (n d r x) -> p n d r x", n=9, d=2, r=2)
        VF = [VF0, VF1]
        VFb = [VF0b, S5Tb]
        KF = [LNT[:, 0:2, :].rearrange("p a (k x) -> p (a k) x", x=128), LNT[:, 2:4, :].rearrange("p a (k x) -> p (a k) x", x=128)]
        KFb = [merge("KF0", LNTb[0], LNTb[1]), merge("KF1", LNTb[2], LNTb[3])]
        UALL = TS[:].bitcast(BF16).rearrange("p a (h n) -> p (a h) n", n=NT)

        def Uap(fc):
            return UALL[:, fc, :], TSb[fc // 2]
        for b in range(2):
            wap, wb = w_get()
            w3 = wap.rearrange("p (k n) -> p k n", n=512)
            for oc in range(4):
                fc = b * 4 + oc
                pt, pbuf = PSUMS[fc % 2]
                proj_chunk(pt, pbuf, w3, wb, oc, H, Hb, 8)
                ua, ub = Uap(fc)
                P.op("act", lambda e, pt=pt, ua=ua: e.activation(out=ua, in_=pt[:], func=AF.Copy), reads=[pbuf], writes=[ub])
        if DBG and l == 0:
            for fc in range(8):
                dbg_bf(o_dU[fc * 128:(fc + 1) * 128, :], UALL[:, fc, :], [TSb[fc // 2]])
        replay(bgs.get(l, []), 10 ** 9)
        WF = [H[:, 0:4, :].rearrange("p k n -> p (k n)").rearrange("p (n d r x) -> p n d r x", n=8, d=2, r=2),
              H[:, 4:8, :].rearrange("p k n -> p (k n)").rearrange("p (n d r x) -> p n d r x", n=8, d=2, r=2)]
        WFb = [merge("WF0", *Hb[0:4]), merge("WF1", *Hb[4:8])]
        if S5STOP == 1:
            raise _Stop()
        P.op("dve", lambda e: e.tensor_copy(out=SALL[:, 0], in_=H0[:, j, 0]), reads=[H0b], writes=[SBUFS])
        P.op("dve", lambda e: e.tensor_copy(out=SALL[:, 2 * C + 1], in_=H0[:, j, 1]), reads=[H0b], writes=[SBUFS])
        PSA = [PA, PB, PC, PD]
        PSAb = [PAb, PBb, PCb, PDb]
        PSAh = [[PSA[q][:, 0:512], PSA[q][:, 512:1024]] for q in range(4)]
        PSAhb = [[merge("PSAh", PSAb[q]), merge("PSAh", PSAb[q])] for q in range(4)]
        ENG = ["dve", "dve"]

        def recur_ops(eng, d, n, views, lr, li, dst, dstbuf, lbufs):
            pi, po = (n - 1) % 2, n % 2
            a_, x_ = views
            sin_ = WST[:, pi, d].rearrange("p r (a x) -> p r a x", a=a_)
            ta4 = WTA[:, d, 0].rearrange("p r (a x) -> p r a x", a=a_)
            tb4 = WTA[:, d, 1].rearrange("p r (a x) -> p r a x", a=a_)
            return [
                lambda: P.op(eng, lambda e: e.tensor_tensor(out=ta4, in0=sin_, in1=lr, op=ALU.mult), reads=[WSTb[pi][d]] + lbufs, writes=[WTAb[d][0]]),
                lambda: P.op(eng, lambda e: e.tensor_tensor(out=tb4, in0=sin_, in1=li, op=ALU.mult), reads=[WSTb[pi][d]] + lbufs, writes=[WTAb[d][1]]),
                lambda: P.op(eng, lambda e: e.tensor_tensor(out=WST[:, po, d, 0, :], in0=WTA[:, d, 0, 0, :], in1=WTA[:, d, 1, 1, :], op=ALU.subtract),
                             reads=[WTAb[d][0], WTAb[d][1]], writes=[WSTb[po][d]]),
                lambda: P.op(eng, lambda e: e.tensor_tensor(out=WST[:, po, d, 1, :], in0=WTA[:, d, 1, 0, :], in1=WTA[:, d, 0, 1, :], op=ALU.add),
                             reads=[WTAb[d][0], WTAb[d][1]], writes=[WSTb[po][d]]),
                lambda: P.op("act", lambda e: e.activation(out=dst[:, n, d], in_=WST[:, po, d], func=AF.Copy), reads=[WSTb[po][d]], writes=[dstbuf]),
            ]

        pend = []
        for fc in range(8):
            s = fc % 2
            sp_load(LDW[:, s, :], d_s5w0[j, fc], LDWb[s])
            L4 = LDW[:, s, :].rearrange("p (d r x) -> p d r x", d=2, r=2)
            chains = []
            for d in range(2):
                eng = ENG[d]
                lr = S5C[:, 0, d, fc, :].unsqueeze(1).unsqueeze(1).broadcast_to([128, 2, 2, 64])
                li = S5C[:, 1, d, fc, :].unsqueeze(1).unsqueeze(1).broadcast_to([128, 2, 2, 64])
                kr = S5K[:, 0, d, fc, :].unsqueeze(1).broadcast_to([128, 2, 64])
                ki = S5K[:, 1, d, fc, :].unsqueeze(1).broadcast_to([128, 2, 64])
                v3 = lambda ap: ap.rearrange("p (a x) -> p a x", a=2)
                ch = cmul_ops(eng, v3(WST[:, 0, d, 0, :]), v3(WST[:, 0, d, 1, :]), v3(L4[:, d, 0, :]), v3(L4[:, d, 1, :]), kr, ki,
                              v3(WTA[:, d, 0, 0, :]), v3(WTA[:, d, 1, 0, :]), [LDWb[s], S5Kb, WSTb[0][d]], [WSTb[0][d]], [WTAb[d][0], WTAb[d][1]])
                ch.append(lambda d=d, s=s: P.op("act", lambda e: e.activation(out=WF[s][:, 0, d], in_=WST[:, 0, d], func=AF.Copy), reads=[WSTb[0][d]], writes=[WFb[s]]))
                for n in range(1, T):
                    ch += recur_ops(eng, d, n, (2, 64), lr, li, WF[s], WFb[s], [S5Cb])
                chains.append(ch)
            interleave(chains + [pend])
            ua, ub = Uap(fc)
            u3 = ua.rearrange("p (c t) -> p c t", t=T)
            hf = fc % 2

            def mmA(e, s=s, u3=u3, hf=hf):
                last = None
                for d in range(2):
                    for ri in range(2):
                        for jj in range(T):
                            n = (T - 1 - jj) if d == 0 else jj
                            for q in range(4):
                                last = e.matmul(PSAh[q][hf][:, (d * 2 + ri) * 128:(d * 2 + ri + 1) * 128], lhsT=WF[s][32 * q:32 * q + 32, n, d, ri, :],
                                                rhs=u3[32 * q:32 * q + 32, :, jj], start=(jj == 0), stop=(jj == T - 1), tile_position=(32 * q, 0))
                return last
            P.op("pe", mmA, reads=[WFb[s], ub], writes=[PSAhb[q][hf] for q in range(4)])
            pend = []
            for q in range(4):
                qq = fc * 4 + q
                for ri in range(2):
                    src = PSAh[q][hf].rearrange("p (d r c) -> p d r c", d=2, r=2)[:, :, ri, :]
                    dst = SALL[:, 1:2 * C + 1, ri, qq].rearrange("p (d c) -> p d c", d=2)
                    pend.append(lambda src=src, dst=dst, q=q, hf=hf: P.op("act", lambda e: e.activation(out=dst, in_=src, func=AF.Copy), reads=[PSAhb[q][hf]], writes=[SBUFS]))
        for t_ in pend:
            t_()
        for q in range(4):
            m_ = merge("PS", PSAhb[q][0], PSAhb[q][1])
            PSAb[q].w, PSAb[q].r = m_.w, m_.r
        for i in range(4):
            Hb[i].w, Hb[i].r = dict(WFb[0].w), dict(WFb[0].r)
            Hb[4 + i].w, Hb[4 + i].r = dict(WFb[1].w), dict(WFb[1].r)
        if S5STOP == 2:
            raise _Stop()
        LTr_b = S5LT[:, 0].unsqueeze(2).broadcast_to([128, 2, 2, 32])
        LTi = S5LT[:, 1]
        P.op("dve", lambda e: e.tensor_copy(out=RST[:, 1], in_=H0[:, j]), reads=[H0b], writes=[RSTb[1]])
        modgen = compute_mod_gen(l + 1) if l + 1 < DEPTH else iter(())
        SIN = merge("SIN", SBUFS)
        SOUT = merge("SOUT", SBUFS)
        for i in range(C):
            if i % 10 == 5:
                next(modgen, None)
            pp, pc = (i + 1) % 2, i % 2
            a0 = 1 + i
            stp = 2 * C - 1 - 2 * i
            sl = slice(a0, a0 + stp + 1, stp)
            if i > 0 and i % 32 == 0:
                P.op("dve", lambda e, pp=pp: e.tensor_scalar(out=RST[:, pp], in0=RST[:, pp], scalar1=KEEP[:, 0:1], scalar2=None, op0=ALU.mult),
                     reads=[RSTb[pp], KEEPb], writes=[RSTb[pp]])
            P.op("dve", lambda e, pp=pp: e.tensor_tensor(out=RTM[:, 0], in0=RST[:, pp], in1=LTr_b, op=ALU.mult), reads=[RSTb[pp], S5LTb], writes=[RTMb[0]])
            P.op("dve", lambda e, pp=pp: e.scalar_tensor_tensor(out=RTM[:, 1, :, 0, :], in0=RST[:, pp, :, 1, :], scalar=-1.0, in1=LTi, op0=ALU.mult, op1=ALU.mult),
                 reads=[RSTb[pp], S5LTb], writes=[RTMb[1]])
            P.op("dve", lambda e, pp=pp: e.tensor_tensor(out=RTM[:, 1, :, 1, :], in0=RST[:, pp, :, 0, :], in1=LTi, op=ALU.mult), reads=[RSTb[pp], S5LTb], writes=[RTMb[2]])
            P.op("dve", lambda e: e.tensor_tensor(out=RTM[:, 0], in0=RTM[:, 0], in1=RTM[:, 1], op=ALU.add), reads=[RTMb[0], RTMb[1], RTMb[2]], writes=[RTMb[0]])
            P.op("dve", lambda e, pc=pc, sl=sl: e.tensor_tensor(out=RST[:, pc], in0=RTM[:, 0], in1=SALL[:, sl], op=ALU.add), reads=[RTMb[0], SIN], writes=[RSTb[pc]])
            P.op("act", lambda e, pc=pc, sl=sl: e.activation(out=SALL[:, sl], in_=RST[:, pc], func=AF.Copy), reads=[RSTb[pc]], writes=[SOUT])
            if i % 32 == 31:
                k = i // 32
                P.op("act", lambda e, pc=pc, k=k: e.activation(out=STG[:, k], in_=RST[:, pc], func=AF.Copy), reads=[RSTb[pc]], writes=[STGb])
        P.op("sp", lambda e: e.dma_start(out=o_st[j], in_=STG[:].rearrange("p k d r q -> p (k d r q)")), reads=[STGb], dma=True)
        for _ in modgen:
            pass
        m_ = merge("S", SIN, SOUT)
        SBUFS.w, SBUFS.r = m_.w, m_.r
        for s0 in (32, C + 1 + 32):
            P.op("dve", lambda e, s0=s0: e.tensor_scalar(out=SALL[:, s0:s0 + 65:32], in0=SALL[:, s0:s0 + 65:32], scalar1=KEEP[:, 0:1], scalar2=None, op0=ALU.mult),
                 reads=[SBUFS, KEEPb], writes=[SBUFS])
        if S5STOP == 3:
            raise _Stop()
        Z = H
        kcnt = 0
        pend = []
        KPSb = [merge("KPS", PSUMS[2 + b_ // 2][1]) for b_ in range(4)]
        for fc in range(8):
            s = fc % 2
            sp_load(LDV[:, s, :], d_s5v0[j, fc], LDVb[s])
            sp_load(LDN[:, s, :], d_s5bn[j, fc], LDNb[s])
            V4 = LDV[:, s, :].rearrange("p (d r x) -> p d r x", d=2, r=2)
            N4 = LDN[:, s, :].rearrange("p (d r x) -> p d r x", d=2, r=2)
            q0 = fc * 4
            chains = []
            for d in range(2):
                eng = ENG[d]
                krb = S5KB[:, 0, d, q0:q0 + 4].unsqueeze(2).broadcast_to([128, 4, 32])
                kib = S5KB[:, 1, d, q0:q0 + 4].unsqueeze(2).broadcast_to([128, 4, 32])
                v4 = lambda ap: ap.rearrange("p (a x) -> p a x", a=4)
                ch = cmul_ops(eng, v4(WST[:, 0, d, 0, :]), v4(WST[:, 0, d, 1, :]), v4(N4[:, d, 0, :]), v4(N4[:, d, 1, :]), krb, kib,
                              v4(WTA[:, d, 0, 0, :]), v4(WTA[:, d, 1, 0, :]), [LDNb[s], S5KBb, WSTb[0][d]], [WSTb[0][d]], [WTAb[d][0], WTAb[d][1]])
                ch.append(lambda d=d, s=s: P.op("act", lambda e: e.activation(out=BNB[:, s, d], in_=WST[:, 0, d], func=AF.Copy), reads=[WSTb[0][d]], writes=[BNBb[s]]))
                ch.append(lambda d=d, eng=eng, V4=V4, s=s: P.op(eng, lambda e: e.tensor_copy(out=WST[:, 0, d, 0, :], in_=V4[:, d, 0, :]), reads=[LDVb[s]], writes=[WSTb[0][d]]))
                ch.append(lambda d=d, eng=eng, V4=V4, s=s: P.op(eng, lambda e: e.tensor_scalar(out=WST[:, 0, d, 1, :], in0=V4[:, d, 1, :], scalar1=-1.0, scalar2=None, op0=ALU.mult), reads=[LDVb[s]], writes=[WSTb[0][d]]))
                ch.append(lambda d=d, s=s: P.op("act", lambda e: e.activation(out=VF[s][:, 0, d], in_=WST[:, 0, d], func=AF.Copy), reads=[WSTb[0][d]], writes=[VFb[s]]))
                lrb = S5B[:, 0, d, q0:q0 + 4].unsqueeze(1).unsqueeze(3).broadcast_to([128, 2, 4, 32])
                lib = S5LN[:, d, q0:q0 + 4].unsqueeze(1).unsqueeze(3).broadcast_to([128, 2, 4, 32])
                for n in range(1, T + 1):
                    ch += recur_ops(eng, d, n, (4, 32), lrb, lib, VF[s], VFb[s], [S5Bb, S5LNb])
                chains.append(ch)
            interleave(chains + [pend])
            pend = []
            klist = [(dl, d) for dl in range(T) for d in range(2) if not (dl == 0 and d == 1)]
            for g0 in range(0, len(klist), 4):
                grp = klist[g0:g0 + 4]
                bank = kcnt % 4
                kcnt += 1
                pk = PSUMS[2 + bank // 2][0]
                pkb = KPSb[bank]
                cb = (bank % 2) * 512

                def mmK(e, grp=grp, pk=pk, cb=cb, s=s):
                    last = None
                    for gi, (dl, d) in enumerate(grp):
                        col = cb + gi * 128
                        dirs = (0, 1) if dl == 0 else (d,)
                        nmm = len(dirs) * 2
                        i_ = 0
                        for dd in dirs:
                            for ri in range(2):
                                last = e.matmul(pk[:, col:col + 128], lhsT=BNB[:, s, dd, ri, :], rhs=VF[s][:, dl, dd, ri, :], start=(i_ == 0), stop=(i_ == nmm - 1))
                                i_ += 1
                    return last
                P.op("pe", mmK, reads=[BNBb[s], VFb[s]], writes=[pkb])
                for gi, (dl, d) in enumerate(grp):
                    col = cb + gi * 128
                    idx = 7 + dl if d == 0 else 7 - dl
                    if dl == 0:
                        pend.append(lambda col=col, pk=pk, pkb=pkb: P.op("dve", lambda e: e.tensor_tensor(out=KTMP[:], in0=pk[:, col:col + 128], in1=BDMASK, op=ALU.mult), reads=[CONSTb], writes=[KTMPb, pkb]))
                        pend.append(lambda fc=fc, s=s: P.op("dve", lambda e: e.scalar_tensor_tensor(out=KF[s][:, 7, :], in0=IDENT_F, scalar=S5D[:, j, fc:fc + 1], in1=KTMP[:], op0=ALU.mult, op1=ALU.add),
                                                           reads=[CONSTb, S5Db, KTMPb], writes=[KFb[s]]))
                    else:
                        pend.append(lambda col=col, idx=idx, pk=pk, s=s, pkb=pkb: P.op("dve", lambda e: e.tensor_tensor(out=KF[s][:, idx, :], in0=pk[:, col:col + 128], in1=BDMASK, op=ALU.mult),
                                                                                    reads=[CONSTb], writes=[KFb[s], pkb]))
            ua, ub = Uap(fc)
            u3 = ua.rearrange("p (c t) -> p c t", t=T)
            pt, pbuf = PSUMS[fc % 2]

            def mmC(e, fc=fc, pt=pt, u3=u3, s=s):
                last = None
                for jj in range(T):
                    o = pt[:, jj * 128:(jj + 1) * 128]
                    for j2 in range(T):
                        last = e.matmul(o, lhsT=KF[s][:, 7 + jj - j2, :], rhs=u3[:, :, j2], start=(j2 == 0), stop=False)
                    for d in range(2):
                        n = jj + 1 if d == 0 else T - jj
                        s0 = 0 if d == 0 else C + 2
                        for ri in range(2):
                            lastq = (d == 1 and ri == 1)
                            for q in range(4):
                                qq = fc * 4 + q
                                oq = pt[32 * q:32 * q + 32, jj * 128:(jj + 1) * 128]
                                last = e.matmul(oq, lhsT=VF[s][:, n, d, ri, 32 * q:32 * q + 32], rhs=SALL[:, s0:s0 + C, ri, qq], start=False, stop=lastq, tile_position=(0, 32 * q))
                return last
            def fin(mmC=mmC, s=s, ub=ub, pbuf=pbuf, fc=fc, pt=pt):
                P.op("pe", mmC, reads=[KFb[s], VFb[s], ub, SBUFS], writes=[pbuf])
                P.op("act", lambda e: e.activation(out=Z[:, fc, :].rearrange("p (c t) -> p t c", t=T), in_=pt[:].rearrange("p (t c) -> p t c", t=T),
                                                   func=AF.Gelu_apprx_tanh), reads=[pbuf], writes=[Hb[fc]])
            pend.append(fin)
        for t_ in pend:
            t_()
        if DBG and l == 0:
            for fc in range(8):
                dbg_bf(o_dZ[fc * 128:(fc + 1) * 128, :], H[:, fc, :], [Hb[fc]])
        if S5STOP == 4:
            raise _Stop()
        for t_ in range(2):
            m_ = merge("PS", KPSb[2 * t_], KPSb[2 * t_ + 1])
            PSUMS[2 + t_][1].w, PSUMS[2 + t_][1].r = m_.w, m_.r
        for bb in BIGb[0:17]:
            bb.w, bb.r = dict(SBUFS.w), dict(SBUFS.r)
        for bb in BIGb[17:22]:
            bb.w, bb.r = dict(VFb[0].w), dict(VFb[0].r)
        for i in range(2):
            LNTb[i].w, LNTb[i].r = dict(KFb[0].w), dict(KFb[0].r)
            LNTb[2 + i].w, LNTb[2 + i].r = dict(KFb[1].w), dict(KFb[1].r)
        G3 = BIG[:, 0:8 * NT].rearrange("p (k n) -> p k n", n=NT)
        for b in range(4):
            wap, wb = w_get()
            w3 = wap.rearrange("p (k n) -> p k n", n=512)
            for jj in range(2):
                jc = 2 * b + jj
                pv_, pvb = PSUMS[(jj * 2) % 4]
                pg, pgb = PSUMS[(jj * 2 + 1) % 4]
                proj_chunk(pv_, pvb, w3, wb, jj * 2, H, Hb, 8)
                proj_chunk(pg, pgb, w3, wb, jj * 2 + 1, H, Hb, 8)
                t = jj
                P.op("act", lambda e, pg=pg, t=t: e.activation(out=TS[:, t, :], in_=pg[:], func=AF.Sigmoid), reads=[pgb], writes=[TSb[t]])
                P.op("dve", lambda e, pv_=pv_, t=t, jc=jc: e.tensor_tensor(out=G3[:, jc, :], in0=pv_[:], in1=TS[:, t, :], op=ALU.mult),
                     reads=[pvb, TSb[t]], writes=[BIGb[jc]])
        outproj_ln(l, 0, G3, BIGb[0:8], 8, 2, 512, (l, 1))
        if DBG and l == 0:
            dbg_x(o_dXA)


    compute_mod(0)
    bgs[0] = record_lambda(0)
    modulate(0, 0)
    try:
        for l in range(DEPTH):
            if l >= STAGE:
                break
            if l % 2 == 0:
                s5_mixer(l)
            else:
                attention(l)
            bg = []
            if l + 1 < DEPTH and l % 2 == 1:
                compute_mod(l + 1)
                bg = record_lambda(l + 1)
            ffn(l, bg)
    except _Stop:
        pass
    for c in range(8):
        P.op("sp", lambda e, c=c: e.dma_start(out=o_yT[c * 128:(c + 1) * 128, :], in_=X[:, c, :]), reads=[Xb[c]], dma=True)
    P.finish()
    return nc


_CACHE = {}


def kernel(**inp):
    inp = {k: np.asarray(v) for k, v in inp.items()}
    f32 = np.float32
    wall = build_wall(inp)
    nblk = wall.shape[0]
    s5h = s5_host_layout(inp)
    cos, sin = rope_tables()
    rope_s = np.ascontiguousarray(np.stack([cos, sin], 1)).astype(f32)
    rope_p = np.ascontiguousarray(np.stack([np.ones_like(cos), np.zeros_like(sin)], 1)).astype(f32)
    consts = const_tables()
    bmodT = np.ascontiguousarray(inp["b_mod"].reshape(DEPTH, 48, 128).transpose(2, 0, 1).reshape(128, DEPTH * 48)).astype(f32)
    lng = inp["ln_g"].reshape(DEPTH * 2 * 8, 128).T
    lnb = inp["ln_b"].reshape(DEPTH * 2 * 8, 128).T
    lnT = np.ascontiguousarray(np.concatenate([lng, lnb], 1)).astype(f32)
    gain = np.ascontiguousarray(np.stack([inp["q_norm_g"][0], inp["q_norm_g"][1], inp["k_norm_g"][0], inp["k_norm_g"][1]], 1)).astype(f32)
    mask_s = np.zeros((128, 48), f32)
    mask_p = np.full((12, 4), -30000.0, f32)
    for kt in range(8):
        mask_p[kt, kt // 2] = 0.0
    mask_p = np.ascontiguousarray(np.broadcast_to(mask_p.reshape(1, 48), (128, 48))).astype(f32)
    in_maps = []
    for core in range(8):
        m = dict(wall=wall, bmodT=bmodT, lnT=lnT, consts=consts, gain=gain, **s5h)
        if core < 4:
            b = core
            m["xT"] = np.ascontiguousarray(inp["x_sample"][b].T)
            cvec = inp["c"][b]
            m["rope"] = rope_s
            m["maskb"] = mask_s
            m["keep"] = np.ones((128, 1), f32)
            m["ckT"] = np.ascontiguousarray(inp["cache_k"][b].transpose(0, 3, 2, 1))
            m["cv"] = np.ascontiguousarray(inp["cache_v"][b].reshape(2, 4, 128, 256).transpose(0, 2, 1, 3))
            m["h0"] = h0_layout(inp["state_s5"][b])
        else:
            s0 = (core - 4) * 4
            m["xT"] = np.ascontiguousarray(inp["x_prompt"][s0:s0 + 4].reshape(NT, D).T)
            cvec = inp["c_ctx"]
            m["rope"] = rope_p
            m["maskb"] = mask_p
            m["keep"] = np.zeros((128, 1), f32)
            m["ckT"] = np.zeros((2, 128, 2, 512), f32)
            m["cv"] = np.zeros((2, 128, 4, 256), f32)
            m["h0"] = np.zeros((128, 2, 2, 2, 32), f32)
        m["cT"] = np.ascontiguousarray(cvec.reshape(8, 128).T).astype(f32)
        in_maps.append({k: np.ascontiguousarray(v, dtype=f32) for k, v in m.items()})
    if "nc" not in _CACHE:
        _CACHE["nc"] = build_program(nblk)
    nc = _CACHE["nc"]
    _CACHE.pop("nc")
    res = run_bass_kernel_spmd(nc, in_maps, core_ids=list(range(8)))
    R = res.results
    if DBG:
        _CACHE["dbg"] = {"c%d_%s" % (ci, k): R[ci][k] for ci in (0, 4) for k in R[ci] if k.startswith("d")}
    y_sample = np.stack([R[b]["yT"].T for b in range(4)], 0).astype(f32)
    y_prompt = np.concatenate([R[4 + i]["yT"].T.reshape(4, 256, D) for i in range(4)], 0).astype(f32)
    nk = np.zeros((16, 2, 256, 2, 128), f32)
    nv = np.zeros((16, 2, 256, 2, 128), f32)
    ns = np.zeros((16, 2, 2, 2, 64, 64), f32)
    for i in range(4):
        r = R[4 + i]
        ko = r["kout"]
        vo = r["vout"]
        so = r["stout"].reshape(2, 2, 64, 4, 2, 2, 32)
        for s in range(4):
            bidx = i * 4 + s
            nk[bidx] = ko[:, :, :, s * 256:(s + 1) * 256].transpose(0, 3, 1, 2)
            nv[bidx] = vo[:, s * 256:(s + 1) * 256, :].reshape(2, 256, 2, 128)
            for d in range(2):
                k = s if d == 0 else 3 - s
                blk = so[:, :, :, k, d, :, :]
                ns[bidx, :, d] = blk.transpose(0, 3, 4, 1, 2).reshape(2, 2, 64, 64)
    return (y_prompt, y_sample, nk, nv, ns)
```

```python
import os
import numpy as np
import concourse.bass as bass
import concourse.mybir as mybir
from concourse.bass_utils import run_bass_kernel_spmd

F32 = mybir.dt.float32
BF16 = mybir.dt.bfloat16
AF = mybir.ActivationFunctionType
ALU = mybir.AluOpType

D = 1024
NT = 1024
DEPTH = 4
DFF = 2816
KFF = 22
T = 8
C = NT // T
NSLOT = 2 * (C + 1)
ALPHA = (2.0 * DEPTH) ** 0.25
LN_EPS = 1e-6
RMS_EPS = 1e-6
ATT_SCALE = 128 ** -0.5
WCOLS = 4096
NWSLOT = 3
EPOCH = 16000
NDMA = 8
MAGIC = 12582912.0
STAGE = int(os.environ.get("K_STAGE", "99"))
DBG = int(os.environ.get("K_DBG", "0"))
S5STOP = int(os.environ.get("K_S5STOP", "0"))
KVAR = int(os.environ.get("K_VAR", "0"))


class _Stop(Exception):
    pass


class Buf:
    __slots__ = ("w", "r", "name")

    def __init__(self, name=""):
        self.w = {}
        self.r = {}
        self.name = name


def merge(name, *olds):
    b = Buf(name)
    for o in olds:
        for k, v in list(o.w.items()) + list(o.r.items()):
            b.r[k] = max(b.r.get(k, 0), v)
            b.w[k] = max(b.w.get(k, 0), v)
    return b


class Prog:
    def __init__(self, nc):
        self.nc = nc
        self.eng = {"pe": nc.tensor, "act": nc.scalar, "dve": nc.vector, "pool": nc.gpsimd, "sp": nc.sync}
        self.cnt = {}
        self.semlist = {}
        self.known = {e: {} for e in self.eng}
        self.allsems = []
        self.rec = None
        for k in ["pe", "act", "dve", "pool"]:
            self.cnt[k] = 0
            self.semlist[k] = []
        for pre in ("q", "g"):
            for i in range(NDMA):
                k = "%s%d" % (pre, i)
                self.cnt[k] = 0
                self.semlist[k] = [self._newsem(k)]
        self.dma_rr = {"q": 0, "g": 0}

    def _newsem(self, name):
        cm = self.nc.semaphore("s_%s_%d" % (name, len(self.allsems)))
        s = cm.__enter__()
        self.allsems.append(s)
        return s

    def _semval(self, k, v):
        if k[0] in "qg":
            return self.semlist[k][0], v
        idx = (v - 1) // EPOCH
        while len(self.semlist[k]) <= idx:
            self.semlist[k].append(self._newsem(k))
        return self.semlist[k][idx], v - idx * EPOCH

    def op(self, e, fn, reads=(), writes=(), dma=False):
        if self.rec is not None:
            self.rec.append((e, fn, tuple(reads), tuple(writes), dma))
            return 0
        deps = {}
        for b in reads:
            for k, v in b.w.items():
                if v > deps.get(k, 0):
                    deps[k] = v
        for b in writes:
            for k, v in b.w.items():
                if v > deps.get(k, 0):
                    deps[k] = v
            for k, v in b.r.items():
                if v > deps.get(k, 0):
                    deps[k] = v
        kn = self.known[e]
        for k, v in deps.items():
            if k == e and e == "pe":
                continue
            if kn.get(k, 0) >= v:
                continue
            s, sv = self._semval(k, v)
            self.eng[e].wait_ge(s, sv)
            kn[k] = v
        inst = fn(self.eng[e])
        if dma:
            pre = "g" if e == "pool" else "q"
            key = "%s%d" % (pre, self.dma_rr[pre])
            self.dma_rr[pre] = (self.dma_rr[pre] + 1) % NDMA
            self.cnt[key] += 16
            val = self.cnt[key]
            inst.then_inc(self.semlist[key][0], 16)
        else:
            key = e
            self.cnt[e] += 1
            val = self.cnt[e]
            s, sv = self._semval(e, val)
            inst.then_inc(s, 1)
        for b in reads:
            if val > b.r.get(key, 0):
                b.r[key] = val
        for b in writes:
            b.w = {key: val}
            b.r = {}
        return val

    def finish(self):
        sp = self.eng["sp"]
        for k, v in self.cnt.items():
            if v > 0:
                s, sv = self._semval(k, v)
                sp.wait_ge(s, sv)
        for s in self.allsems:
            sp.sem_clear(s)


def _blk(W, cols):
    kin = W.shape[0]
    kc = kin // 128
    a = W[:, cols].reshape(kc, 128, len(cols)).transpose(1, 0, 2).reshape(128, kc * len(cols))
    out = np.zeros((128, WCOLS), np.float32)
    out[:, :a.shape[1]] = a
    return out


def _r(a, n):
    return np.arange(a, a + n)


def mod_blocks(w_mod_l):
    return [_blk(w_mod_l, _r(b * 512, 512)) for b in range(12)]


def ffn_blocks(w_in, w_out):
    bl = []
    for b in range(11):
        j0, j1 = 2 * b, 2 * b + 1
        cols = np.concatenate([_r(j0 * 128, 128), _r(DFF + j0 * 128, 128), _r(j1 * 128, 128), _r(DFF + j1 * 128, 128)])
        bl.append(_blk(w_in, cols))
    for c in range(8):
        bl.append(_blk(w_out, _r(c * 128, 128)))
    return bl


def s5_blocks(w_in, w_glu, w_out):
    bl = [_blk(w_in, _r(b * 512, 512)) for b in range(2)]
    for b in range(4):
        j0, j1 = 2 * b, 2 * b + 1
        cols = np.concatenate([_r(j0 * 128, 128), _r(D + j0 * 128, 128), _r(j1 * 128, 128), _r(D + j1 * 128, 128)])
        bl.append(_blk(w_glu, cols))
    bl += [_blk(w_out, _r(b * 512, 512)) for b in range(2)]
    return bl


def attn_blocks(w_qkv, w_o):
    bl = [_blk(w_qkv, _r(b * 512, 512)) for b in range(3)]
    bl += [_blk(w_o, _r(b * 512, 512)) for b in range(2)]
    return bl


def build_wall(inp):
    bl = []
    bl += mod_blocks(inp["w_mod"][0])
    for l in range(DEPTH):
        j = l // 2
        if l % 2 == 0:
            sb_ = s5_blocks(inp["w_s5_in"][j], inp["w_s5_glu"][j], inp["w_s5_out"][j])
            bl += sb_[0:2]
            if l + 1 < DEPTH:
                bl += mod_blocks(inp["w_mod"][l + 1])
            bl += sb_[2:]
        else:
            bl += attn_blocks(inp["w_qkv"][j], inp["w_o"][j])
            if l + 1 < DEPTH:
                bl += mod_blocks(inp["w_mod"][l + 1])
        bl += ffn_blocks(inp["w_ffn_in"][l], inp["w_ffn_out"][l])
    return np.ascontiguousarray(np.stack(bl, 0))


def s5_host_layout(inp):
    out = {}
    G, P, H = 64, 64, 16
    def c_lay(a):
        a5 = a.reshape(2, 2, 8, 8, P)
        a5 = np.broadcast_to(a5[:, :, :, :, None, :], (2, 2, 8, 8, H, P))
        return np.ascontiguousarray(a5.transpose(3, 4, 0, 1, 2, 5).reshape(128, 2, 2, 8, P))
    def b_lay(a):
        a5 = a.reshape(2, 2, 32, 2, P)
        return np.ascontiguousarray(a5.transpose(3, 4, 0, 1, 2).reshape(128, 2, 2, 32))
    ld = np.broadcast_to(inp["s5_log_dt"][:, :, :, None], (2, 2, G, P))
    nat = np.stack([inp["s5_a_re"], inp["s5_a_im"], ld], 0)
    out["s5n"] = np.ascontiguousarray(nat.transpose(3, 1, 0, 2, 4))
    selc = np.zeros((128, 8, 8, 16), np.float32)
    for g in range(64):
        selc[g, g // 8, g % 8, :] = 1.0
    out["selc"] = selc.reshape(128, 8, 128)
    selb = np.zeros((128, 34), np.float32)
    for g in range(64):
        selb[g, g // 2] = 1.0
        selb[g, 32 + (g % 2)] = 1.0
    out["selb"] = selb
    b = np.stack([inp["s5_b_re"], inp["s5_b_im"]], 2)
    b8 = b.reshape(2, 2, 2, 8, 4, 2, P, H)
    w0 = np.zeros((2, 8, 4, 2, H, 2, 2, 2, P), np.float32)
    bn = np.zeros((2, 8, 2, P, 2, 2, 4, 2, H), np.float32)
    for a in range(2):
        w0[:, :, :, a, :, :, :, a, :] = b8[:, :, :, :, :, a].transpose(0, 3, 4, 6, 1, 2, 5)
        bn[:, :, a, :, :, :, :, a, :] = b8[:, :, :, :, :, a].transpose(0, 3, 5, 1, 2, 4, 6)
    out["s5w0"] = np.ascontiguousarray(w0.reshape(2, 8, 128, 512))
    out["s5bn"] = np.ascontiguousarray(bn.reshape(2, 8, 128, 512))
    c = np.stack([inp["s5_c_re"], inp["s5_c_im"]], 2)
    c8 = c.reshape(2, 2, 2, 8, 4, 2, H, P)
    v0 = np.zeros((2, 8, 2, P, 2, 2, 4, 2, H), np.float32)
    for a in range(2):
        v0[:, :, a, :, :, :, :, a, :] = c8[:, :, :, :, :, a].transpose(0, 3, 6, 1, 2, 4, 5)
    out["s5v0"] = np.ascontiguousarray(v0.reshape(2, 8, 128, 512))
    out["s5d"] = np.ascontiguousarray(inp["s5_d"].reshape(2, 8, 128).transpose(2, 0, 1))
    return out


def h0_layout(st):
    s = st.reshape(2, 2, 2, 32, 2, 64)
    return np.ascontiguousarray(s.transpose(4, 5, 0, 1, 2, 3).reshape(128, 2, 2, 2, 32))


def rope_tables():
    l = np.arange(NT)
    row = (l // 64).astype(np.float32)
    col = (l % 64).astype(np.float32)
    inv = (np.float32(10000.0) ** (-np.arange(32, dtype=np.float32) / np.float32(32))).astype(np.float32)
    ar = row[None, :] * inv[:, None]
    ac = col[None, :] * inv[:, None]
    cos = np.concatenate([np.cos(ar), np.cos(ar), np.cos(ac), np.cos(ac)], 0).astype(np.float32)
    sin = np.concatenate([np.sin(ar), np.sin(ar), np.sin(ac), np.sin(ac)], 0).astype(np.float32)
    return cos, sin


def const_tables():
    ident = np.eye(128, dtype=np.float32)
    bd = np.kron(np.eye(8, dtype=np.float32), np.ones((16, 16), np.float32))
    rot = np.zeros((128, 128), np.float32)
    for base in (0, 64):
        for i in range(32):
            rot[base + 32 + i, base + i] = -1.0
            rot[base + i, base + 32 + i] = 1.0
    return np.ascontiguousarray(np.stack([ident, bd, rot], 1))


def build_program(nblk):
    nc = bass.Bass("TRN2", target_bir_lowering=False)
    P = Prog(nc)

    def din(name, shape):
        return nc.dram_tensor(name, list(shape), F32, kind="ExternalInput").ap()

    def dout(name, shape):
        return nc.dram_tensor(name, list(shape), F32, kind="ExternalOutput").ap()

    d_xT = din("xT", [D, NT])
    d_cT = din("cT", [128, 8])
    d_wall = din("wall", [nblk, 128, WCOLS])
    d_bmod = din("bmodT", [128, DEPTH * 48])
    d_ln = din("lnT", [128, 128])
    d_rope = din("rope", [128, 2, NT])
    d_mask = din("maskb", [128, 48])
    d_keep = din("keep", [128, 1])
    d_ckT = din("ckT", [2, 128, 2, 512])
    d_cv = din("cv", [2, 128, 4, 256])
    d_gain = din("gain", [128, 4])
    d_const = din("consts", [128, 3, 128])
    d_s5w0 = din("s5w0", [2, 8, 128, 512])
    d_s5bn = din("s5bn", [2, 8, 128, 512])
    d_s5v0 = din("s5v0", [2, 8, 128, 512])
    d_s5d = din("s5d", [128, 2, 8])
    d_h0 = din("h0", [128, 2, 2, 2, 32])
    d_s5n = din("s5n", [64, 2, 3, 2, 64])
    d_selc = din("selc", [128, 8, 128])
    d_selb = din("selb", [128, 34])
    o_yT = dout("yT", [D, NT])
    o_k = dout("kout", [2, 2, 128, NT])
    o_v = dout("vout", [2, NT, 256])
    o_st = dout("stout", [2, 128, 512])
    if DBG:
        o_dU = dout("dU", [D, NT]); o_dZ = dout("dZ", [D, NT]); o_dXA = dout("dXA", [D, NT]); o_dS = None
        o_dQ = dout("dQ", [D, NT]); o_dK = dout("dK", [128, 2, 1536]); o_dO = dout("dO", [D, NT]); o_dXA1 = dout("dXA1", [D, NT])

    def dbg_bf(dst, src, bufs):
        P.op("pool", lambda e: e.dma_start(out=dst, in_=src), reads=bufs, dma=True)

    def dbg_x(dst):
        for c in range(8):
            P.op("sp", lambda e, c=c: e.dma_start(out=dst[c * 128:(c + 1) * 128, :], in_=X[:, c, :]), reads=[Xb[c]], dma=True)

    def sb(name, shape, dt=F32):
        cm = nc.sbuf_tensor(name, list(shape), dt)
        return cm.__enter__()

    def ps(name, shape, dt=F32):
        cm = nc.psum_tensor(name, list(shape), dt)
        return cm.__enter__()

    X = sb("X", [128, 8, NT])
    Xb = [Buf("X%d" % i) for i in range(8)]
    H = sb("H", [128, 8, NT], BF16)
    Hb = [Buf("H%d" % i) for i in range(8)]
    BIG = sb("BIG", [128, KFF * NT], BF16)
    BIGb = [Buf("BIG%d" % i) for i in range(KFF)]
    WR = sb("WR", [128, NWSLOT, WCOLS], BF16)
    WRb = [Buf("WR%d" % i) for i in range(NWSLOT)]
    LNT = sb("LNT", [128, 4, NT], BF16)
    LNTb = [Buf("LNT%d" % i) for i in range(4)]
    TS = sb("TS", [128, 4, NT])
    TSb = [Buf("TS%d" % i) for i in range(4)]
    MOD = sb("MOD", [128, DEPTH, 48])
    MODb = [Buf("MOD%d" % i) for i in range(DEPTH)]
    BMOD = sb("BMOD", [128, DEPTH * 48]); BMODb = Buf("BMOD")
    LNP = sb("LNP", [128, 128]); LNPb = Buf("LNP")
    CT = sb("CT", [128, 8]); CTb = Buf("CT")
    CS = sb("CS", [128, 8], BF16); CSb = Buf("CS")
    ROW = sb("ROW", [1, 512]); ROWb = Buf("ROW")
    ONE1 = sb("ONE1", [1, 1]); ONE1b = Buf("ONE1")
    MASK = sb("MASK", [128, 48]); MASKb = Buf("MASK")
    KEEP = sb("KEEP", [128, 1]); KEEPb = Buf("KEEP")
    GAIN = sb("GAIN", [128, 4]); GAINb = Buf("GAIN")
    CONST = sb("CONST", [128, 3, 128]); CONSTb = Buf("CONST")
    CB = sb("CB", [128, 3, 128], BF16); CBb = Buf("CB")
    ONES = sb("ONES", [128, 128], BF16); ONESb = Buf("ONES")
    EPSC = sb("EPSC", [128, 2]); EPSb = Buf("EPSC")
    ARENA = sb("ARENA", [128, 8192])
    ARb = {"cur": [Buf("ARENA")]}

    def arena_switch(names):
        olds = ARb["cur"]
        news = [merge(n, *olds) for n in names]
        ARb["cur"] = news
        return news
    ROPE = ARENA[:, 0:2048].rearrange("p (a n) -> p a n", a=2)
    KTB = ARENA[:, 2048:3584].bitcast(BF16).rearrange("p (a n) -> p a n", a=2)
    VTB = ARENA[:, 3584:5120].bitcast(BF16).rearrange("p (a n) -> p a n", a=12)
    PR = ARENA[:, 5120:6144].bitcast(BF16).rearrange("p (a n) -> p a n", a=4)
    S5C = ARENA[:, 0:2048].rearrange("p (t d f x) -> p t d f x", t=2, d=2, f=8)
    S5K = ARENA[:, 2048:4096].rearrange("p (t d f x) -> p t d f x", t=2, d=2, f=8)
    NATQ = ARENA[0:64, 4096:4736].rearrange("p (t x) -> p t x", t=5)
    NATT = ARENA[0:64, 4736:5504].rearrange("p (t x) -> p t x", t=6)
    XPAD = ARENA[0:64, 5504:6528].rearrange("p (t x) -> p t x", t=8)
    SELC = ARENA[:, 6528:7552].rearrange("p (f x) -> p f x", f=8)
    SELB = ARENA[:, 7552:7586]
    NATQ128 = ARENA[:, 4096:4736].rearrange("p (t x) -> p t x", t=5)
    XPAD128 = ARENA[:, 5504:6528].rearrange("p (t x) -> p t x", t=8)
    VF1 = ARENA[:, 4096:4096 + 2304].bitcast(BF16).rearrange("p (n d r x) -> p n d r x", n=9, d=2, r=2)
    S5B = sb("S5B", [128, 2, 2, 32]); S5Bb = Buf("S5B")
    S5KB = sb("S5KB", [128, 2, 2, 32]); S5KBb = Buf("S5KB")
    S5TB = sb("S5TB", [128, 2, 2, 32]); S5TBb = Buf("S5TB")
    S5LT = sb("S5LT", [128, 2, 2, 32]); S5LTb = Buf("S5LT")
    S5LN = sb("S5LN", [128, 2, 32]); S5LNb = Buf("S5LN")
    S5D = sb("S5D", [128, 2, 8]); S5Db = Buf("S5D")
    H0 = sb("H0", [128, 2, 2, 2, 32]); H0b = Buf("H0")
    SELCb = Buf("SELC")
    SELBb = Buf("SELB")
    LDW = sb("LDW", [128, 2, 512]); LDWb = [Buf("LDW0"), Buf("LDW1")]
    LDN = sb("LDN", [128, 2, 512]); LDNb = [Buf("LDN0"), Buf("LDN1")]
    LDV = LDW; LDVb = LDWb
    WST = sb("WST", [128, 2, 2, 2, 128])
    WSTb = [[Buf("WST00"), Buf("WST01")], [Buf("WST10"), Buf("WST11")]]
    WTA = sb("WTA", [128, 2, 2, 2, 128])
    WTAb = [[Buf("WTA00"), Buf("WTA01")], [Buf("WTA10"), Buf("WTA11")]]
    BNB = sb("BNB", [128, 2, 2, 2, 128], BF16); BNBb = [Buf("BNB0"), Buf("BNB1")]
    KTMP = sb("KTMP", [128, 128]); KTMPb = Buf("KTMP")
    STG = sb("STG", [128, 4, 2, 2, 32]); STGb = Buf("STG")
    RST = sb("RST", [128, 2, 2, 2, 32]); RSTb = [Buf("RST0"), Buf("RST1")]
    RTM = sb("RTM", [128, 2, 2, 2, 32]); RTMb = [Buf("RTM0"), Buf("RTM1a"), Buf("RTM1b")]
    VST = sb("VST", [128, 2, 256]); VSTb = [Buf("VST0"), Buf("VST1")]

    PA = ps("PA", [128, NT]); PB = ps("PB", [128, NT]); PC = ps("PC", [128, NT]); PD = ps("PD", [128, NT])
    PAb, PBb, PCb, PDb = Buf("PA"), Buf("PB"), Buf("PC"), Buf("PD")
    PSUMS = [(PA, PAb), (PB, PBb), (PC, PCb), (PD, PDb)]

    wstate = {"issued": 0, "used": 0}

    def w_issue():
        i = wstate["issued"]
        if i >= nblk:
            return
        s = i % NWSLOT
        P.op("pool", lambda e: e.dma_start(out=WR[:, s, :], in_=d_wall[i]), writes=[WRb[s]], dma=True)
        wstate["issued"] += 1

    def w_get():
        i = wstate["used"]
        wstate["used"] += 1
        while wstate["issued"] < min(nblk, i + NWSLOT):
            w_issue()
        s = i % NWSLOT
        return WR[:, s, :], WRb[s]

    def sp_load(dst, src, buf):
        P.op("sp", lambda e: e.dma_start(out=dst, in_=src), writes=[buf], dma=True)

    for i in range(NWSLOT):
        w_issue()
    sp_load(CT[:], d_cT, CTb)
    sp_load(BMOD[:], d_bmod, BMODb)
    sp_load(LNP[:], d_ln, LNPb)
    sp_load(CONST[:], d_const, CONSTb)
    sp_load(GAIN[:], d_gain, GAINb)
    sp_load(MASK[:], d_mask, MASKb)
    sp_load(KEEP[:], d_keep, KEEPb)
    for c in range(8):
        sp_load(X[:, c, :], d_xT[c * 128:(c + 1) * 128, :], Xb[c])
    sp_load(S5D[:], d_s5d, S5Db)
    sp_load(H0[:], d_h0, H0b)
    sp_load(SELC, d_selc, SELCb)
    sp_load(SELB, d_selb, SELBb)

    P.op("dve", lambda e: e.memset(ONES[:], 1.0), writes=[ONESb])
    P.op("dve", lambda e: e.memset(ONE1[:], 1.0), writes=[ONE1b])
    P.op("dve", lambda e: e.memset(EPSC[:, 0:1], LN_EPS / (ALPHA * ALPHA)), writes=[EPSb])
    P.op("dve", lambda e: e.memset(EPSC[:, 1:2], RMS_EPS), writes=[EPSb])
    P.op("dve", lambda e: e.tensor_copy(out=CB[:], in_=CONST[:]), reads=[CONSTb], writes=[CBb])
    P.op("act", lambda e: e.activation(out=CS[:], in_=CT[:], func=AF.Silu), reads=[CTb], writes=[CSb])

    IDENT_F = CONST[:, 0, :]
    BDMASK = CONST[:, 1, :]
    ROTB = CB[:, 2, :]

    def compute_mod_gen(l):
        for b in range(12):
            wap, wb = w_get()
            w3 = wap.rearrange("p (k n) -> p k n", n=512)

            def mm(e):
                last = None
                for kc in range(8):
                    last = e.matmul(PC[0:1, 0:512], lhsT=CS[:, kc:kc + 1], rhs=w3[:, kc, :], start=(kc == 0), stop=(kc == 7))
                return last
            P.op("pe", mm, reads=[wb, CSb], writes=[PCb])
            P.op("act", lambda e: e.activation(out=ROW[:], in_=PC[0:1, 0:512], func=AF.Copy), reads=[PCb], writes=[ROWb])

            def tr(e):
                last = None
                for i in range(4):
                    col = b * 4 + i
                    last = e.matmul(PD[:, col:col + 1], lhsT=ROW[0:1, i * 128:(i + 1) * 128], rhs=ONE1[0:1, 0:1], start=True, stop=True)
                return last
            P.op("pe", tr, reads=[ROWb, ONE1b], writes=[PDb])
            yield b
        P.op("dve", lambda e: e.tensor_tensor(out=MOD[:, l, :], in0=PD[:, 0:48], in1=BMOD[:, l * 48:(l + 1) * 48], op=ALU.add),
             reads=[PDb, BMODb], writes=[MODb[l]])
        for base in (8, 32):
            P.op("dve", lambda e, base=base: e.tensor_scalar(out=MOD[:, l, base:base + 8], in0=MOD[:, l, base:base + 8], scalar1=1.0, scalar2=None, op0=ALU.add),
                 reads=[MODb[l]], writes=[MODb[l]])
        for base in (16, 40):
            P.op("dve", lambda e, base=base: e.tensor_scalar(out=MOD[:, l, base:base + 8], in0=MOD[:, l, base:base + 8], scalar1=1.0 / ALPHA, scalar2=None, op0=ALU.mult),
                 reads=[MODb[l]], writes=[MODb[l]])

    def compute_mod(l):
        for _ in compute_mod_gen(l):
            pass

    def modulate(l, which):
        so = 0 if which == 0 else 24
        for c in range(8):
            P.op("dve", lambda e, c=c: e.tensor_scalar(out=H[:, c, :], in0=X[:, c, :], scalar1=MOD[:, l, so + 8 + c:so + 9 + c],
                                                      scalar2=MOD[:, l, so + c:so + c + 1], op0=ALU.mult, op1=ALU.add),
                 reads=[Xb[c], MODb[l]], writes=[Hb[c]])

    def proj_chunk(pt, pbuf, w3, wb, oc, src, srcbufs, kcn):
        def mm(e):
            last = None
            for kc in range(kcn):
                for hf in range(2):
                    last = e.matmul(pt[:, hf * 512:(hf + 1) * 512], lhsT=w3[:, kc, oc * 128:(oc + 1) * 128],
                                    rhs=src[:, kc, hf * 512:(hf + 1) * 512], start=(kc == 0), stop=(kc == kcn - 1))
            return last
        P.op("pe", mm, reads=[wb] + list(srcbufs), writes=[pbuf])

    def outproj_ln(l, which, src, srcbufs, kcn, nblocks, cols_per_blk, next_mod):
        gcol = 16 if which == 0 else 40
        lni = (l * 2 + which) * 8
        oc_global = 0
        pend_st = None
        for b in range(nblocks):
            wap, wb = w_get()
            w3 = wap[:, 0:kcn * cols_per_blk].rearrange("p (k n) -> p k n", n=cols_per_blk)
            for oc in range(cols_per_blk // 128):
                c = oc_global
                pt, pbuf = PSUMS[c % 2]
                proj_chunk(pt, pbuf, w3, wb, oc, src, srcbufs, kcn)
                P.op("dve", lambda e, c=c, pt=pt: e.scalar_tensor_tensor(out=X[:, c, :], in0=pt[:], scalar=MOD[:, l, gcol + c:gcol + c + 1],
                                                                        in1=X[:, c, :], op0=ALU.mult, op1=ALU.add),
                     reads=[pbuf, MODb[l], Xb[c]], writes=[Xb[c]])
                s = (c % 2) * 2
                P.op("act", lambda e, c=c, s=s: e.activation(out=LNT[:, s, :], in_=X[:, c, :], func=AF.Copy), reads=[Xb[c]], writes=[LNTb[s]])
                P.op("act", lambda e, c=c, s=s: e.activation(out=LNT[:, s + 1, :], in_=X[:, c, :], func=AF.Square), reads=[Xb[c]], writes=[LNTb[s + 1]])

                def st(e, c=c, s=s):
                    last = None
                    for hf in range(2):
                        e.matmul(PC[:, hf * 512:(hf + 1) * 512], lhsT=ONES[:], rhs=LNT[:, s, hf * 512:(hf + 1) * 512], start=(c == 0), stop=(c == 7))
                        last = e.matmul(PD[:, hf * 512:(hf + 1) * 512], lhsT=ONES[:], rhs=LNT[:, s + 1, hf * 512:(hf + 1) * 512], start=(c == 0), stop=(c == 7))
                    return last
                if pend_st is not None:
                    pend_st()
                pend_st = (lambda st=st, s=s: P.op("pe", st, reads=[ONESb, LNTb[s], LNTb[s + 1]], writes=[PCb, PDb]))
                oc_global += 1
        pend_st()
        P.op("act", lambda e: e.activation(out=TS[:, 0, :], in_=PC[:], func=AF.Identity, scale=1.0 / D), reads=[PCb], writes=[TSb[0]])
        P.op("act", lambda e: e.activation(out=TS[:, 1, :], in_=TS[:, 0, :], func=AF.Square), reads=[TSb[0]], writes=[TSb[1]])
        P.op("dve", lambda e: e.scalar_tensor_tensor(out=TS[:, 1, :], in0=PD[:], scalar=1.0 / D, in1=TS[:, 1, :], op0=ALU.mult, op1=ALU.subtract),
             reads=[PDb, TSb[1]], writes=[TSb[1]])
        P.op("act", lambda e: e.activation(out=TS[:, 1, :], in_=TS[:, 1, :], func=AF.Ln, bias=EPSC[:, 0:1], scale=1.0), reads=[TSb[1], EPSb], writes=[TSb[1]])
        P.op("act", lambda e: e.activation(out=TS[:, 1, :], in_=TS[:, 1, :], func=AF.Exp, scale=-0.5), reads=[TSb[1]], writes=[TSb[1]])
        for c in range(8):
            t = 2 + (c % 2)
            P.op("pool" if c % 2 == 1 else "dve", lambda e, c=c, t=t: e.tensor_tensor(out=TS[:, t, :], in0=X[:, c, :], in1=TS[:, 0, :], op=ALU.subtract),
                 reads=[Xb[c], TSb[0]], writes=[TSb[t]])
            P.op("dve", lambda e, c=c, t=t: e.tensor_tensor(out=TS[:, t, :], in0=TS[:, t, :], in1=TS[:, 1, :], op=ALU.mult),
                 reads=[TSb[t], TSb[1]], writes=[TSb[t]])
            P.op("act", lambda e, c=c, t=t: e.activation(out=X[:, c, :], in_=TS[:, t, :], func=AF.Identity,
                                                         scale=LNP[:, lni + c:lni + c + 1], bias=LNP[:, 64 + lni + c:64 + lni + c + 1]),
                 reads=[TSb[t], LNPb], writes=[Xb[c]])
        if next_mod is not None:
            modulate(*next_mod)

    def ffn(l, bg=None):
        bg = bg if bg is not None else []
        for b in range(11):
            wap, wb = w_get()
            w3 = wap.rearrange("p (k n) -> p k n", n=512)
            for jj in range(2):
                j = 2 * b + jj
                pg, pgb = PSUMS[(jj * 2) % 4]
                pu, pub = PSUMS[(jj * 2 + 1) % 4]
                proj_chunk(pg, pgb, w3, wb, jj * 2, H, Hb, 8)
                proj_chunk(pu, pub, w3, wb, jj * 2 + 1, H, Hb, 8)
                t = jj
                P.op("act", lambda e, pg=pg, t=t: e.activation(out=TS[:, t, :], in_=pg[:], func=AF.Silu), reads=[pgb], writes=[TSb[t]])
                P.op("dve", lambda e, pu=pu, t=t, j=j: e.tensor_tensor(out=BIG[:, j * NT:(j + 1) * NT], in0=pu[:], in1=TS[:, t, :], op=ALU.mult),
                     reads=[pub, TSb[t]], writes=[BIGb[j]])
                replay(bg, 14)
        replay(bg, 10 ** 9)
        BIG3 = BIG[:].rearrange("p (k n) -> p k n", n=NT)
        nm = (l + 1, 0) if l + 1 < DEPTH else None
        outproj_ln(l, 1, BIG3, BIGb, KFF, 8, 128, nm)

    def attention(l):
        j = l // 2
        Q = BIG[:, 0:8 * NT].rearrange("p (k n) -> p k n", n=NT)
        nb = arena_switch(["ROPE", "KT0", "KT1"] + ["VT%d" % i for i in range(12)] + ["PR%d" % i for i in range(4)])
        ROPEb = nb[0]
        KTBb = nb[1:3]
        VTBb = nb[3:15]
        PRb = nb[15:19]
        sp_load(ROPE, d_rope, ROPEb)
        for kv in range(2):
            P.op("pool", lambda e, kv=kv: e.dma_start(out=KTB[:, kv, NT:NT + 512], in_=d_ckT[j, :, kv, :]), writes=[KTBb[kv]], dma=True)
        for t4 in range(4):
            P.op("pool", lambda e, t4=t4: e.dma_start(out=VTB[:, 8 + t4, :], in_=d_cv[j, :, t4, :]), writes=[VTBb[8 + t4]], dma=True)
        cur = None

        def emit_proj(hc_):
            nonlocal cur
            if hc_ % 4 == 0:
                cur = w_get()
            wap_, wb_ = cur
            w3_ = wap_.rearrange("p (k n) -> p k n", n=512)
            pt_, pbuf_ = PSUMS[hc_ % 2]
            proj_chunk(pt_, pbuf_, w3_, wb_, hc_ % 4, H, Hb, 8)
        emit_proj(0)
        for hc in range(10):
            pt, pbuf = PSUMS[hc % 2]
            if hc + 1 < 10:
                emit_proj(hc + 1)
            isk = hc >= 8
            gcol = (2 + j) if isk else j
            P.op("act", lambda e, pt=pt: e.activation(out=TS[:, 2, :], in_=pt[:], func=AF.Copy), reads=[pbuf], writes=[TSb[2]])
            P.op("act", lambda e, pt=pt: e.activation(out=LNT[:, 0, :], in_=pt[:], func=AF.Square), reads=[pbuf], writes=[LNTb[0]])

            def st(e):
                last = None
                for hf in range(2):
                    last = e.matmul(PC[:, hf * 512:(hf + 1) * 512], lhsT=ONES[:], rhs=LNT[:, 0, hf * 512:(hf + 1) * 512], start=True, stop=True)
                return last
            P.op("pe", st, reads=[ONESb, LNTb[0]], writes=[PCb])
            P.op("act", lambda e: e.activation(out=TS[:, 3, :], in_=PC[:], func=AF.Ln, bias=EPSC[:, 1:2], scale=1.0 / 128), reads=[PCb, EPSb], writes=[TSb[3]])
            P.op("act", lambda e: e.activation(out=TS[:, 3, :], in_=TS[:, 3, :], func=AF.Exp, scale=-0.5), reads=[TSb[3]], writes=[TSb[3]])
            P.op("dve", lambda e, gcol=gcol: e.scalar_tensor_tensor(out=TS[:, 2, :], in0=TS[:, 2, :], scalar=GAIN[:, gcol:gcol + 1], in1=TS[:, 3, :],
                                                                    op0=ALU.mult, op1=ALU.mult), reads=[TSb[2], TSb[3], GAINb], writes=[TSb[2]])
            P.op("act", lambda e: e.activation(out=LNT[:, 1, :], in_=TS[:, 2, :], func=AF.Copy), reads=[TSb[2]], writes=[LNTb[1]])

            def rt(e):
                last = None
                for hf in range(2):
                    last = e.matmul(PD[:, hf * 512:(hf + 1) * 512], lhsT=ROTB, rhs=LNT[:, 1, hf * 512:(hf + 1) * 512], start=True, stop=True)
                return last
            P.op("pe", rt, reads=[CBb, LNTb[1]], writes=[PDb])
            P.op("dve", lambda e: e.tensor_tensor(out=TS[:, 0, :], in0=PD[:], in1=ROPE[:, 1, :], op=ALU.mult), reads=[PDb, ROPEb], writes=[TSb[0]])
            P.op("dve", lambda e: e.tensor_tensor(out=TS[:, 2, :], in0=TS[:, 2, :], in1=ROPE[:, 0, :], op=ALU.mult), reads=[TSb[2], ROPEb], writes=[TSb[2]])
            if not isk:
                P.op("dve", lambda e, hc=hc: e.tensor_tensor(out=Q[:, hc, :], in0=TS[:, 2, :], in1=TS[:, 0, :], op=ALU.add),
                     reads=[TSb[2], TSb[0]], writes=[BIGb[hc]])
            else:
                kv = hc - 8
                P.op("dve", lambda e: e.tensor_tensor(out=TS[:, 1, :], in0=TS[:, 2, :], in1=TS[:, 0, :], op=ALU.add), reads=[TSb[2], TSb[0]], writes=[TSb[1]])
                P.op("act", lambda e, kv=kv: e.activation(out=KTB[:, kv, 0:NT], in_=TS[:, 1, :], func=AF.Copy), reads=[TSb[1]], writes=[KTBb[kv]])
                P.op("sp", lambda e, kv=kv: e.dma_start(out=o_k[j, kv], in_=TS[:, 1, :]), reads=[TSb[1]], dma=True)
        wap, wb = cur
        w3 = wap.rearrange("p (k n) -> p k n", n=512)
        for tt in range(8):
            pt, pbuf = PSUMS[tt % 2]

            def mmv(e, tt=tt, pt=pt):
                last = None
                for kc in range(8):
                    last = e.matmul(pt[:, 0:256], lhsT=H[:, kc, tt * 128:(tt + 1) * 128], rhs=w3[:, kc, 256:512], start=(kc == 0), stop=(kc == 7))
                return last
            P.op("pe", mmv, reads=[wb] + Hb, writes=[pbuf])
            s = tt % 2
            P.op("act", lambda e, pt=pt, s=s: e.activation(out=VST[:, s, :], in_=pt[:, 0:256], func=AF.Copy), reads=[pbuf], writes=[VSTb[s]])
            P.op("dve", lambda e, tt=tt, s=s: e.tensor_copy(out=VTB[:, tt, :], in_=VST[:, s, :]), reads=[VSTb[s]], writes=[VTBb[tt]])
            P.op("sp", lambda e, tt=tt, s=s: e.dma_start(out=o_v[j, tt * 128:(tt + 1) * 128, :], in_=VST[:, s, :]), reads=[VSTb[s]], dma=True)
        if DBG and l == 1:
            for h in range(8):
                dbg_bf(o_dQ[h * 128:(h + 1) * 128, :], Q[:, h, :], [BIGb[h]])
            dbg_bf(o_dK, KTB, KTBb)
        NSB = 4
        PAh = [PA[:, 0:512], PA[:, 512:1024], PD[:, 0:512], PD[:, 512:1024]]
        PAhb = [merge("PAh", PAb), merge("PAh", PAb), merge("PDh", PDb), merge("PDh", PDb)]
        pri = 0
        PRh = [[merge("PRh", PRb[i]), merge("PRh", PRb[i])] for i in range(4)]
        for h in range(8):
            kv = h // 4
            for qh in range(2):
                qs = slice(qh * 512, (qh + 1) * 512)

                def smm(e, kt, h=h, kv=kv, qs=qs):
                    return e.matmul(PAh[kt % NSB], lhsT=KTB[:, kv, kt * 128:(kt + 1) * 128], rhs=Q[:, h, qs], start=True, stop=True)
                for k0 in range(NSB - 1):
                    P.op("pe", lambda e, k0=k0: smm(e, k0), reads=[KTBb[kv], BIGb[h]], writes=[PAhb[k0]])
                for kt in range(12):
                    if kt + NSB - 1 < 12:
                        P.op("pe", lambda e, kt=kt: smm(e, kt + NSB - 1), reads=[KTBb[kv], BIGb[h]], writes=[PAhb[(kt + NSB - 1) % NSB]])
                    pslot = pri % 4
                    pri += 1
                    for g2 in range(2):
                        qg = qh * 2 + g2
                        P.op("act", lambda e, kt=kt, g2=g2, qg=qg, pslot=pslot: e.activation(
                            out=PR[:, pslot, g2 * 256:(g2 + 1) * 256], in_=PAh[kt % NSB][:, g2 * 256:(g2 + 1) * 256], func=AF.Exp,
                            bias=MASK[:, kt * 4 + qg:kt * 4 + qg + 1], scale=ATT_SCALE), reads=[PAhb[kt % NSB], MASKb], writes=[PRh[pslot][g2]])

                    def pv(e, kt=kt, kv=kv, qs=qs, pslot=pslot):
                        e.matmul(PB[:, qs], lhsT=VTB[:, kt, kv * 128:(kv + 1) * 128], rhs=PR[:, pslot, :], start=(kt == 0), stop=(kt == 11))
                        return e.matmul(PC[:, qs], lhsT=ONES[:], rhs=PR[:, pslot, :], start=(kt == 0), stop=(kt == 11))
                    P.op("pe", pv, reads=[VTBb[kt], PRh[pslot][0], PRh[pslot][1], ONESb], writes=[PBb, PCb])
                P.op("dve", lambda e, qs=qs: e.reciprocal(out=TS[:, 0, qs], in_=PC[:, qs]), reads=[PCb], writes=[TSb[0]])
                P.op("dve", lambda e, qs=qs, h=h: e.tensor_tensor(out=H[:, h, qs], in0=PB[:, qs], in1=TS[:, 0, qs], op=ALU.mult),
                     reads=[PBb, TSb[0]], writes=[Hb[h]])
        m1 = merge("PA", PAhb[0], PAhb[1])
        PAb.w, PAb.r = m1.w, m1.r
        m1 = merge("PD", PAhb[2], PAhb[3])
        PDb.w, PDb.r = m1.w, m1.r
        if DBG and l == 1:
            for h in range(8):
                dbg_bf(o_dO[h * 128:(h + 1) * 128, :], H[:, h, :], [Hb[h]])
        outproj_ln(l, 0, H, Hb, 8, 2, 512, (l, 1))
        if DBG and l == 1:
            dbg_x(o_dXA1)

    def cmul_ops(e_name, out_r, out_i, in_r, in_i, lr, li, ta, tb, reads, writes, tbufs):
        return [
            lambda: P.op(e_name, lambda e: e.tensor_tensor(out=ta, in0=in_r, in1=lr, op=ALU.mult), reads=reads, writes=[tbufs[0]]),
            lambda: P.op(e_name, lambda e: e.tensor_tensor(out=tb, in0=in_i, in1=li, op=ALU.mult), reads=reads, writes=[tbufs[1]]),
            lambda: P.op(e_name, lambda e: e.tensor_tensor(out=ta, in0=ta, in1=tb, op=ALU.subtract), reads=[tbufs[0], tbufs[1]], writes=[tbufs[0]]),
            lambda: P.op(e_name, lambda e: e.tensor_tensor(out=tb, in0=in_r, in1=li, op=ALU.mult), reads=reads, writes=[tbufs[1]]),
            lambda: P.op(e_name, lambda e: e.tensor_tensor(out=out_i, in0=in_i, in1=lr, op=ALU.mult), reads=reads + [tbufs[0]], writes=writes),
            lambda: P.op(e_name, lambda e: e.tensor_tensor(out=out_i, in0=out_i, in1=tb, op=ALU.add), reads=[tbufs[1]] + writes, writes=writes),
            lambda: P.op(e_name, lambda e: e.tensor_copy(out=out_r, in_=ta), reads=[tbufs[0]], writes=writes),
        ]

    def cmul(*a):
        for t in cmul_ops(*a):
            t()

    def interleave(lists):
        n = max(len(L) for L in lists)
        for i in range(n):
            for L in lists:
                if i < len(L):
                    L[i]()

    def lam_compute(A, Ab, Kt, Kb, Tm, Tb_, n_t):
        bufs = [Ab, Kb, Tb_]
        A0, A1, A2 = A[:, 0], A[:, 1], A[:, 2]
        K0, K1 = Kt[:, 0], Kt[:, 1]
        T0, T1, T2, T3, T4, T5 = (Tm[:, i] for i in range(6))

        def tt(o, a, b, op):
            P.op("dve", lambda e: e.tensor_tensor(out=o, in0=a, in1=b, op=op), reads=bufs, writes=bufs)

        def tsc(o, a, s1, s2=None, op0=ALU.mult, op1=ALU.add):
            if s2 is None:
                P.op("dve", lambda e: e.tensor_scalar(out=o, in0=a, scalar1=s1, scalar2=None, op0=op0), reads=bufs, writes=bufs)
            else:
                P.op("dve", lambda e: e.tensor_scalar(out=o, in0=a, scalar1=s1, scalar2=s2, op0=op0, op1=op1), reads=bufs, writes=bufs)

        def stt(o, a, sc, b, op0, op1):
            P.op("dve", lambda e: e.scalar_tensor_tensor(out=o, in0=a, scalar=sc, in1=b, op0=op0, op1=op1), reads=bufs, writes=bufs)

        def horner(t, y, divs, sign):
            tsc(t, y, sign / divs[-1], 1.0)
            for dv in reversed(divs[:-1]):
                tt(t, t, y, ALU.mult)
                tsc(t, t, sign / dv, 1.0)
        tsc(K1, A2, 0.125)
        horner(K0, K1, [1.0, 2.0, 3.0, 4.0, 5.0, 6.0, 7.0, 8.0, 9.0, 10.0, 11.0], 1.0)
        for _ in range(3):
            tt(K0, K0, K0, ALU.mult)
        tt(T0, K0, A0, ALU.mult)
        tt(T1, K0, A1, ALU.mult)
        horner(K0, T0, [2.0, 3.0, 4.0, 5.0, 6.0, 7.0], 1.0)
        tt(T2, K0, T0, ALU.mult)
        tsc(K1, T1, 1.0 / 16.0)
        tt(T3, K1, K1, ALU.mult)
        horner(K0, T3, [6.0, 20.0, 42.0, 72.0, 110.0, 156.0, 210.0], -1.0)
        tt(T4, K0, K1, ALU.mult)
        horner(K0, T3, [12.0, 30.0, 56.0, 90.0, 132.0, 182.0, 240.0], -1.0)
        stt(T5, T3, -0.5, K0, ALU.mult, ALU.mult)
        for _ in range(4):
            tt(K1, T4, T4, ALU.mult)
            stt(K0, T5, 1.0, T4, ALU.add, ALU.mult)
            tsc(T4, K0, 2.0)
            tsc(T5, K1, -2.0)
        stt(K0, T2, 1.0, T5, ALU.add, ALU.mult)
        tt(T0, K0, T2, ALU.add)
        stt(T1, T2, 1.0, T4, ALU.add, ALU.mult)
        tt(T3, A0, A0, ALU.mult)
        tt(T2, A1, A1, ALU.mult)
        tt(T3, T3, T2, ALU.add)
        P.op("dve", lambda e: e.reciprocal(out=T3, in_=T3), reads=bufs, writes=bufs)
        tt(T2, T0, A0, ALU.mult)
        tt(T4, T1, A1, ALU.mult)
        tt(T2, T2, T4, ALU.add)
        tt(K0, T2, T3, ALU.mult)
        tt(T2, T1, A0, ALU.mult)
        tt(T4, T0, A1, ALU.mult)
        tt(T2, T2, T4, ALU.subtract)
        tt(K1, T2, T3, ALU.mult)
        tsc(A0, T0, 1.0, None, op0=ALU.add)
        P.op("dve", lambda e: e.tensor_copy(out=A1, in_=T1), reads=bufs, writes=bufs)

    lamctx = {}
    bgs = {}

    def s5_lambda(l):
        j = l // 2
        nb = arena_switch(["S5C", "S5K", "S5T"])
        S5Cb, S5Kb, S5Tb = nb
        lamctx[l] = nb
        P.op("dve", lambda e: e.memset(ARENA[64:128, 4096:6528], 0.0), writes=[S5Tb])
        sp_load(NATQ[:, 0:2, :], d_s5n[:, j, 0:2].rearrange("g t d p -> g t (d p)"), S5Tb)
        sp_load(NATQ[:, 2, :], d_s5n[:, j, 2].rearrange("g d p -> g (d p)"), S5Tb)
        lam_compute(NATQ[:, 0:3, :], S5Tb, NATQ[:, 3:5, :], S5Tb, NATT, S5Tb, 0)
        for fc in range(8):
            pt, pbuf = PSUMS[2 + fc % 2]

            def mmx(e, fc=fc, pt=pt):
                e.matmul(pt[:, 0:256], lhsT=SELC[:, fc, :], rhs=NATQ128[:, 0:2, :].rearrange("p t x -> p (t x)"), start=True, stop=True)
                return e.matmul(pt[:, 256:512], lhsT=SELC[:, fc, :], rhs=NATQ128[:, 3:5, :].rearrange("p t x -> p (t x)"), start=True, stop=True)
            P.op("pe", mmx, reads=[SELCb, S5Tb], writes=[pbuf])
            P.op("act", lambda e, fc=fc, pt=pt: e.activation(out=S5C[:, :, :, fc, :], in_=pt[:, 0:256].rearrange("p (t d x) -> p t d x", t=2, d=2), func=AF.Copy),
                 reads=[pbuf], writes=[S5Cb])
            P.op("act", lambda e, fc=fc, pt=pt: e.activation(out=S5K[:, :, :, fc, :], in_=pt[:, 256:512].rearrange("p (t d x) -> p t d x", t=2, d=2), func=AF.Copy),
                 reads=[pbuf], writes=[S5Kb])
        slots = (0, 1, 3, 4)
        for ti in range(4):
            for d in range(2):
                for a in range(2):
                    P.op("dve", lambda e, ti=ti, d=d, a=a: e.tensor_scalar(out=XPAD[:, ti * 2 + d, a * 64:(a + 1) * 64], in0=NATQ[:, slots[ti], d * 64:(d + 1) * 64],
                                                                        scalar1=SELB[0:64, 32 + a:33 + a], scalar2=None, op0=ALU.mult),
                         reads=[S5Tb, SELBb], writes=[S5Tb])

        def mmb(e):
            last = None
            for k in range(8):
                last = e.matmul(PA[:, k * 32:(k + 1) * 32], lhsT=XPAD128[:, k, :], rhs=SELB[:, 0:32], start=True, stop=True)
            return last
        P.op("pe", mmb, reads=[S5Tb, SELBb], writes=[PAb])
        P.op("act", lambda e: e.activation(out=S5B[:], in_=PA[:, 0:128].rearrange("p (t d q) -> p t d q", t=2, d=2), func=AF.Copy), reads=[PAb], writes=[S5Bb])
        P.op("act", lambda e: e.activation(out=S5KB[:], in_=PA[:, 128:256].rearrange("p (t d q) -> p t d q", t=2, d=2), func=AF.Copy), reads=[PAb], writes=[S5KBb])
        P.op("dve", lambda e: e.tensor_scalar(out=S5LN[:], in0=S5B[:, 1], scalar1=-1.0, scalar2=None, op0=ALU.mult), reads=[S5Bb], writes=[S5LNb])
        P.op("dve", lambda e: e.tensor_copy(out=S5LT[:], in_=S5B[:]), reads=[S5Bb], writes=[S5LTb])
        for _ in range(3):
            cmul("dve", S5LT[:, 0], S5LT[:, 1], S5LT[:, 0], S5LT[:, 1], S5LT[:, 0], S5LT[:, 1], S5TB[:, 0], S5TB[:, 1], [S5LTb], [S5LTb], [S5TBb, S5TBb])

    def record_lambda(l):
        P.rec = []
        s5_lambda(l)
        ops = P.rec
        P.rec = None
        return ops

    def replay(ops, n):
        k = 0
        while ops and k < n:
            P.op(*ops.pop(0))
            k += 1

    def s5_mixer(l):
        j = l // 2
        SALL = BIG[:, 0:NSLOT * 64].rearrange("p (s r q) -> p s r q", r=2, q=32)
        SBUFS = merge("S", *BIGb[0:17])
        VF0b = merge("VF0", *BIGb[17:22])
        for bb in BIGb:
            bb.w, bb.r = {}, {}
        S5Cb, S5Kb, S5Tb = lamctx[l]
        VF0 = BIG[:, 17 * NT:17 * NT + 9 * 512].rearrange("p (n d r x) -> p n d r x", n=9, d=2, r=2)
        VF = [VF0, VF1]
        VFb = [VF0b, S5Tb]
        KF = [LNT[:, 0:2, :].rearrange("p a (k x) -> p (a k) x", x=128), LNT[:, 2:4, :].rearrange("p a (k x) -> p (a k) x", x=128)]
        KFb = [merge("KF0", LNTb[0], LNTb[1]), merge("KF1", LNTb[2], LNTb[3])]
        UALL = TS[:].bitcast(BF16).rearrange("p a (h n) -> p (a h) n", n=NT)

        def Uap(fc):
            return UALL[:, fc, :], TSb[fc // 2]
        for b in range(2):
            wap, wb = w_get()
            w3 = wap.rearrange("p (k n) -> p k n", n=512)
            for oc in range(4):
                fc = b * 4 + oc
                pt, pbuf = PSUMS[fc % 2]
                proj_chunk(pt, pbuf, w3, wb, oc, H, Hb, 8)
                ua, ub = Uap(fc)
                P.op("act", lambda e, pt=pt, ua=ua: e.activation(out=ua, in_=pt[:], func=AF.Copy), reads=[pbuf], writes=[ub])
        if DBG and l == 0:
            for fc in range(8):
                dbg_bf(o_dU[fc * 128:(fc + 1) * 128, :], UALL[:, fc, :], [TSb[fc // 2]])
        replay(bgs.get(l, []), 10 ** 9)
        WF = [H[:, 0:4, :].rearrange("p k n -> p (k n)").rearrange("p (n d r x) -> p n d r x", n=8, d=2, r=2),
              H[:, 4:8, :].rearrange("p k n -> p (k n)").rearrange("p (n d r x) -> p n d r x", n=8, d=2, r=2)]
        WFb = [merge("WF0", *Hb[0:4]), merge("WF1", *Hb[4:8])]
        if S5STOP == 1:
            raise _Stop()
        P.op("dve", lambda e: e.tensor_copy(out=SALL[:, 0], in_=H0[:, j, 0]), reads=[H0b], writes=[SBUFS])
        P.op("dve", lambda e: e.tensor_copy(out=SALL[:, 2 * C + 1], in_=H0[:, j, 1]), reads=[H0b], writes=[SBUFS])
        PSA = [PA, PB, PC, PD]
        PSAb = [PAb, PBb, PCb, PDb]
        PSAh = [[PSA[q][:, 0:512], PSA[q][:, 512:1024]] for q in range(4)]
        PSAhb = [[merge("PSAh", PSAb[q]), merge("PSAh", PSAb[q])] for q in range(4)]
        ENG = ["dve", "dve"]

        def recur_ops(eng, d, n, views, lr, li, dst, dstbuf, lbufs):
            pi, po = (n - 1) % 2, n % 2
            a_, x_ = views
            sin_ = WST[:, pi, d].rearrange("p r (a x) -> p r a x", a=a_)
            ta4 = WTA[:, d, 0].rearrange("p r (a x) -> p r a x", a=a_)
            tb4 = WTA[:, d, 1].rearrange("p r (a x) -> p r a x", a=a_)
            return [
                lambda: P.op(eng, lambda e: e.tensor_tensor(out=ta4, in0=sin_, in1=lr, op=ALU.mult), reads=[WSTb[pi][d]] + lbufs, writes=[WTAb[d][0]]),
                lambda: P.op(eng, lambda e: e.tensor_tensor(out=tb4, in0=sin_, in1=li, op=ALU.mult), reads=[WSTb[pi][d]] + lbufs, writes=[WTAb[d][1]]),
                lambda: P.op(eng, lambda e: e.tensor_tensor(out=WST[:, po, d, 0, :], in0=WTA[:, d, 0, 0, :], in1=WTA[:, d, 1, 1, :], op=ALU.subtract),
                             reads=[WTAb[d][0], WTAb[d][1]], writes=[WSTb[po][d]]),
                lambda: P.op(eng, lambda e: e.tensor_tensor(out=WST[:, po, d, 1, :], in0=WTA[:, d, 1, 0, :], in1=WTA[:, d, 0, 1, :], op=ALU.add),
                             reads=[WTAb[d][0], WTAb[d][1]], writes=[WSTb[po][d]]),
                lambda: P.op("act", lambda e: e.activation(out=dst[:, n, d], in_=WST[:, po, d], func=AF.Copy), reads=[WSTb[po][d]], writes=[dstbuf]),
            ]

        pend = []
        for fc in range(8):
            s = fc % 2
            sp_load(LDW[:, s, :], d_s5w0[j, fc], LDWb[s])
            L4 = LDW[:, s, :].rearrange("p (d r x) -> p d r x", d=2, r=2)
            chains = []
            for d in range(2):
                eng = ENG[d]
                lr = S5C[:, 0, d, fc, :].unsqueeze(1).unsqueeze(1).broadcast_to([128, 2, 2, 64])
                li = S5C[:, 1, d, fc, :].unsqueeze(1).unsqueeze(1).broadcast_to([128, 2, 2, 64])
                kr = S5K[:, 0, d, fc, :].unsqueeze(1).broadcast_to([128, 2, 64])
                ki = S5K[:, 1, d, fc, :].unsqueeze(1).broadcast_to([128, 2, 64])
                v3 = lambda ap: ap.rearrange("p (a x) -> p a x", a=2)
                ch = cmul_ops(eng, v3(WST[:, 0, d, 0, :]), v3(WST[:, 0, d, 1, :]), v3(L4[:, d, 0, :]), v3(L4[:, d, 1, :]), kr, ki,
                              v3(WTA[:, d, 0, 0, :]), v3(WTA[:, d, 1, 0, :]), [LDWb[s], S5Kb, WSTb[0][d]], [WSTb[0][d]], [WTAb[d][0], WTAb[d][1]])
                ch.append(lambda d=d, s=s: P.op("act", lambda e: e.activation(out=WF[s][:, 0, d], in_=WST[:, 0, d], func=AF.Copy), reads=[WSTb[0][d]], writes=[WFb[s]]))
                for n in range(1, T):
                    ch += recur_ops(eng, d, n, (2, 64), lr, li, WF[s], WFb[s], [S5Cb])
                chains.append(ch)
            interleave(chains + [pend])
            ua, ub = Uap(fc)
            u3 = ua.rearrange("p (c t) -> p c t", t=T)
            hf = fc % 2

            def mmA(e, s=s, u3=u3, hf=hf):
                last = None
                for d in range(2):
                    for ri in range(2):
                        for jj in range(T):
                            n = (T - 1 - jj) if d == 0 else jj
                            for q in range(4):
                                last = e.matmul(PSAh[q][hf][:, (d * 2 + ri) * 128:(d * 2 + ri + 1) * 128], lhsT=WF[s][32 * q:32 * q + 32, n, d, ri, :],
                                                rhs=u3[32 * q:32 * q + 32, :, jj], start=(jj == 0), stop=(jj == T - 1), tile_position=(32 * q, 0))
                return last
            P.op("pe", mmA, reads=[WFb[s], ub], writes=[PSAhb[q][hf] for q in range(4)])
            pend = []
            for q in range(4):
                qq = fc * 4 + q
                for ri in range(2):
                    src = PSAh[q][hf].rearrange("p (d r c) -> p d r c", d=2, r=2)[:, :, ri, :]
                    dst = SALL[:, 1:2 * C + 1, ri, qq].rearrange("p (d c) -> p d c", d=2)
                    pend.append(lambda src=src, dst=dst, q=q, hf=hf: P.op("act", lambda e: e.activation(out=dst, in_=src, func=AF.Copy), reads=[PSAhb[q][hf]], writes=[SBUFS]))
        for t_ in pend:
            t_()
        for q in range(4):
            m_ = merge("PS", PSAhb[q][0], PSAhb[q][1])
            PSAb[q].w, PSAb[q].r = m_.w, m_.r
        for i in range(4):
            Hb[i].w, Hb[i].r = dict(WFb[0].w), dict(WFb[0].r)
            Hb[4 + i].w, Hb[4 + i].r = dict(WFb[1].w), dict(WFb[1].r)
        if S5STOP == 2:
            raise _Stop()
        LTr_b = S5LT[:, 0].unsqueeze(2).broadcast_to([128, 2, 2, 32])
        LTi = S5LT[:, 1]
        P.op("dve", lambda e: e.tensor_copy(out=RST[:, 1], in_=H0[:, j]), reads=[H0b], writes=[RSTb[1]])
        modgen = compute_mod_gen(l + 1) if l + 1 < DEPTH else iter(())
        SIN = merge("SIN", SBUFS)
        SOUT = merge("SOUT", SBUFS)
        for i in range(C):
            if i % 10 == 5:
                next(modgen, None)
            pp, pc = (i + 1) % 2, i % 2
            a0 = 1 + i
            stp = 2 * C - 1 - 2 * i
            sl = slice(a0, a0 + stp + 1, stp)
            if i > 0 and i % 32 == 0:
                P.op("dve", lambda e, pp=pp: e.tensor_scalar(out=RST[:, pp], in0=RST[:, pp], scalar1=KEEP[:, 0:1], scalar2=None, op0=ALU.mult),
                     reads=[RSTb[pp], KEEPb], writes=[RSTb[pp]])
            P.op("dve", lambda e, pp=pp: e.tensor_tensor(out=RTM[:, 0], in0=RST[:, pp], in1=LTr_b, op=ALU.mult), reads=[RSTb[pp], S5LTb], writes=[RTMb[0]])
            P.op("dve", lambda e, pp=pp: e.scalar_tensor_tensor(out=RTM[:, 1, :, 0, :], in0=RST[:, pp, :, 1, :], scalar=-1.0, in1=LTi, op0=ALU.mult, op1=ALU.mult),
                 reads=[RSTb[pp], S5LTb], writes=[RTMb[1]])
            P.op("dve", lambda e, pp=pp: e.tensor_tensor(out=RTM[:, 1, :, 1, :], in0=RST[:, pp, :, 0, :], in1=LTi, op=ALU.mult), reads=[RSTb[pp], S5LTb], writes=[RTMb[2]])
            P.op("dve", lambda e: e.tensor_tensor(out=RTM[:, 0], in0=RTM[:, 0], in1=RTM[:, 1], op=ALU.add), reads=[RTMb[0], RTMb[1], RTMb[2]], writes=[RTMb[0]])
            P.op("dve", lambda e, pc=pc, sl=sl: e.tensor_tensor(out=RST[:, pc], in0=RTM[:, 0], in1=SALL[:, sl], op=ALU.add), reads=[RTMb[0], SIN], writes=[RSTb[pc]])
            P.op("act", lambda e, pc=pc, sl=sl: e.activation(out=SALL[:, sl], in_=RST[:, pc], func=AF.Copy), reads=[RSTb[pc]], writes=[SOUT])
            if i % 32 == 31:
                k = i // 32
                P.op("act", lambda e, pc=pc, k=k: e.activation(out=STG[:, k], in_=RST[:, pc], func=AF.Copy), reads=[RSTb[pc]], writes=[STGb])
        P.op("sp", lambda e: e.dma_start(out=o_st[j], in_=STG[:].rearrange("p k d r q -> p (k d r q)")), reads=[STGb], dma=True)
        for _ in modgen:
            pass
        m_ = merge("S", SIN, SOUT)
        SBUFS.w, SBUFS.r = m_.w, m_.r
        for s0 in (32, C + 1 + 32):
            P.op("dve", lambda e, s0=s0: e.tensor_scalar(out=SALL[:, s0:s0 + 65:32], in0=SALL[:, s0:s0 + 65:32], scalar1=KEEP[:, 0:1], scalar2=None, op0=ALU.mult),
                 reads=[SBUFS, KEEPb], writes=[SBUFS])
        if S5STOP == 3:
            raise _Stop()
        Z = H
        kcnt = 0
        pend = []
        KPSb = [merge("KPS", PSUMS[2 + b_ // 2][1]) for b_ in range(4)]
        for fc in range(8):
            s = fc % 2
            sp_load(LDV[:, s, :], d_s5v0[j, fc], LDVb[s])
            sp_load(LDN[:, s, :], d_s5bn[j, fc], LDNb[s])
            V4 = LDV[:, s, :].rearrange("p (d r x) -> p d r x", d=2, r=2)
            N4 = LDN[:, s, :].rearrange("p (d r x) -> p d r x", d=2, r=2)
            q0 = fc * 4
            chains = []
            for d in range(2):
                eng = ENG[d]
                krb = S5KB[:, 0, d, q0:q0 + 4].unsqueeze(2).broadcast_to([128, 4, 32])
                kib = S5KB[:, 1, d, q0:q0 + 4].unsqueeze(2).broadcast_to([128, 4, 32])
                v4 = lambda ap: ap.rearrange("p (a x) -> p a x", a=4)
                ch = cmul_ops(eng, v4(WST[:, 0, d, 0, :]), v4(WST[:, 0, d, 1, :]), v4(N4[:, d, 0, :]), v4(N4[:, d, 1, :]), krb, kib,
                              v4(WTA[:, d, 0, 0, :]), v4(WTA[:, d, 1, 0, :]), [LDNb[s], S5KBb, WSTb[0][d]], [WSTb[0][d]], [WTAb[d][0], WTAb[d][1]])
                ch.append(lambda d=d, s=s: P.op("act", lambda e: e.activation(out=BNB[:, s, d], in_=WST[:, 0, d], func=AF.Copy), reads=[WSTb[0][d]], writes=[BNBb[s]]))
                ch.append(lambda d=d, eng=eng, V4=V4, s=s: P.op(eng, lambda e: e.tensor_copy(out=WST[:, 0, d, 0, :], in_=V4[:, d, 0, :]), reads=[LDVb[s]], writes=[WSTb[0][d]]))
                ch.append(lambda d=d, eng=eng, V4=V4, s=s: P.op(eng, lambda e: e.tensor_scalar(out=WST[:, 0, d, 1, :], in0=V4[:, d, 1, :], scalar1=-1.0, scalar2=None, op0=ALU.mult), reads=[LDVb[s]], writes=[WSTb[0][d]]))
                ch.append(lambda d=d, s=s: P.op("act", lambda e: e.activation(out=VF[s][:, 0, d], in_=WST[:, 0, d], func=AF.Copy), reads=[WSTb[0][d]], writes=[VFb[s]]))
                lrb = S5B[:, 0, d, q0:q0 + 4].unsqueeze(1).unsqueeze(3).broadcast_to([128, 2, 4, 32])
                lib = S5LN[:, d, q0:q0 + 4].unsqueeze(1).unsqueeze(3).broadcast_to([128, 2, 4, 32])
                for n in range(1, T + 1):
                    ch += recur_ops(eng, d, n, (4, 32), lrb, lib, VF[s], VFb[s], [S5Bb, S5LNb])
                chains.append(ch)
            interleave(chains + [pend])
            pend = []
            klist = [(dl, d) for dl in range(T) for d in range(2) if not (dl == 0 and d == 1)]
            for g0 in range(0, len(klist), 4):
                grp = klist[g0:g0 + 4]
                bank = kcnt % 4
                kcnt += 1
                pk = PSUMS[2 + bank // 2][0]
                pkb = KPSb[bank]
                cb = (bank % 2) * 512

                def mmK(e, grp=grp, pk=pk, cb=cb, s=s):
                    last = None
                    for gi, (dl, d) in enumerate(grp):
                        col = cb + gi * 128
                        dirs = (0, 1) if dl == 0 else (d,)
                        nmm = len(dirs) * 2
                        i_ = 0
                        for dd in dirs:
                            for ri in range(2):
                                last = e.matmul(pk[:, col:col + 128], lhsT=BNB[:, s, dd, ri, :], rhs=VF[s][:, dl, dd, ri, :], start=(i_ == 0), stop=(i_ == nmm - 1))
                                i_ += 1
                    return last
                P.op("pe", mmK, reads=[BNBb[s], VFb[s]], writes=[pkb])
                for gi, (dl, d) in enumerate(grp):
                    col = cb + gi * 128
                    idx = 7 + dl if d == 0 else 7 - dl
                    if dl == 0:
                        pend.append(lambda col=col, pk=pk, pkb=pkb: P.op("dve", lambda e: e.tensor_tensor(out=KTMP[:], in0=pk[:, col:col + 128], in1=BDMASK, op=ALU.mult), reads=[CONSTb], writes=[KTMPb, pkb]))
                        pend.append(lambda fc=fc, s=s: P.op("dve", lambda e: e.scalar_tensor_tensor(out=KF[s][:, 7, :], in0=IDENT_F, scalar=S5D[:, j, fc:fc + 1], in1=KTMP[:], op0=ALU.mult, op1=ALU.add),
                                                           reads=[CONSTb, S5Db, KTMPb], writes=[KFb[s]]))
                    else:
                        pend.append(lambda col=col, idx=idx, pk=pk, s=s, pkb=pkb: P.op("dve", lambda e: e.tensor_tensor(out=KF[s][:, idx, :], in0=pk[:, col:col + 128], in1=BDMASK, op=ALU.mult),
                                                                                    reads=[CONSTb], writes=[KFb[s], pkb]))
            ua, ub = Uap(fc)
            u3 = ua.rearrange("p (c t) -> p c t", t=T)
            pt, pbuf = PSUMS[fc % 2]

            def mmC(e, fc=fc, pt=pt, u3=u3, s=s):
                last = None
                for jj in range(T):
                    o = pt[:, jj * 128:(jj + 1) * 128]
                    for j2 in range(T):
                        last = e.matmul(o, lhsT=KF[s][:, 7 + jj - j2, :], rhs=u3[:, :, j2], start=(j2 == 0), stop=False)
                    for d in range(2):
                        n = jj + 1 if d == 0 else T - jj
                        s0 = 0 if d == 0 else C + 2
                        for ri in range(2):
                            lastq = (d == 1 and ri == 1)
                            for q in range(4):
                                qq = fc * 4 + q
                                oq = pt[32 * q:32 * q + 32, jj * 128:(jj + 1) * 128]
                                last = e.matmul(oq, lhsT=VF[s][:, n, d, ri, 32 * q:32 * q + 32], rhs=SALL[:, s0:s0 + C, ri, qq], start=False, stop=lastq, tile_position=(0, 32 * q))
                return last
            def fin(mmC=mmC, s=s, ub=ub, pbuf=pbuf, fc=fc, pt=pt):
                P.op("pe", mmC, reads=[KFb[s], VFb[s], ub, SBUFS], writes=[pbuf])
                P.op("act", lambda e: e.activation(out=Z[:, fc, :].rearrange("p (c t) -> p t c", t=T), in_=pt[:].rearrange("p (t c) -> p t c", t=T),
                                                   func=AF.Gelu_apprx_tanh), reads=[pbuf], writes=[Hb[fc]])
            pend.append(fin)
        for t_ in pend:
            t_()
        if DBG and l == 0:
            for fc in range(8):
                dbg_bf(o_dZ[fc * 128:(fc + 1) * 128, :], H[:, fc, :], [Hb[fc]])
        if S5STOP == 4:
            raise _Stop()
        for t_ in range(2):
            m_ = merge("PS", KPSb[2 * t_], KPSb[2 * t_ + 1])
            PSUMS[2 + t_][1].w, PSUMS[2 + t_][1].r = m_.w, m_.r
        for bb in BIGb[0:17]:
            bb.w, bb.r = dict(SBUFS.w), dict(SBUFS.r)
        for bb in BIGb[17:22]:
            bb.w, bb.r = dict(VFb[0].w), dict(VFb[0].r)
        for i in range(2):
            LNTb[i].w, LNTb[i].r = dict(KFb[0].w), dict(KFb[0].r)
            LNTb[2 + i].w, LNTb[2 + i].r = dict(KFb[1].w), dict(KFb[1].r)
        G3 = BIG[:, 0:8 * NT].rearrange("p (k n) -> p k n", n=NT)
        for b in range(4):
            wap, wb = w_get()
            w3 = wap.rearrange("p (k n) -> p k n", n=512)
            for jj in range(2):
                jc = 2 * b + jj
                pv_, pvb = PSUMS[(jj * 2) % 4]
                pg, pgb = PSUMS[(jj * 2 + 1) % 4]
                proj_chunk(pv_, pvb, w3, wb, jj * 2, H, Hb, 8)
                proj_chunk(pg, pgb, w3, wb, jj * 2 + 1, H, Hb, 8)
                t = jj
                P.op("act", lambda e, pg=pg, t=t: e.activation(out=TS[:, t, :], in_=pg[:], func=AF.Sigmoid), reads=[pgb], writes=[TSb[t]])
                P.op("dve", lambda e, pv_=pv_, t=t, jc=jc: e.tensor_tensor(out=G3[:, jc, :], in0=pv_[:], in1=TS[:, t, :], op=ALU.mult),
                     reads=[pvb, TSb[t]], writes=[BIGb[jc]])
        outproj_ln(l, 0, G3, BIGb[0:8], 8, 2, 512, (l, 1))
        if DBG and l == 0:
            dbg_x(o_dXA)


    compute_mod(0)
    bgs[0] = record_lambda(0)
    modulate(0, 0)
    try:
        for l in range(DEPTH):
            if l >= STAGE:
                break
            if l % 2 == 0:
                s5_mixer(l)
            else:
                attention(l)
            bg = []
            if l + 1 < DEPTH and l % 2 == 1:
                compute_mod(l + 1)
                bg = record_lambda(l + 1)
            ffn(l, bg)
    except _Stop:
        pass
    for c in range(8):
        P.op("sp", lambda e, c=c: e.dma_start(out=o_yT[c * 128:(c + 1) * 128, :], in_=X[:, c, :]), reads=[Xb[c]], dma=True)
    P.finish()
    return nc


_CACHE = {}


def kernel(**inp):
    inp = {k: np.asarray(v) for k, v in inp.items()}
    f32 = np.float32
    wall = build_wall(inp)
    nblk = wall.shape[0]
    s5h = s5_host_layout(inp)
    cos, sin = rope_tables()
    rope_s = np.ascontiguousarray(np.stack([cos, sin], 1)).astype(f32)
    rope_p = np.ascontiguousarray(np.stack([np.ones_like(cos), np.zeros_like(sin)], 1)).astype(f32)
    consts = const_tables()
    bmodT = np.ascontiguousarray(inp["b_mod"].reshape(DEPTH, 48, 128).transpose(2, 0, 1).reshape(128, DEPTH * 48)).astype(f32)
    lng = inp["ln_g"].reshape(DEPTH * 2 * 8, 128).T
    lnb = inp["ln_b"].reshape(DEPTH * 2 * 8, 128).T
    lnT = np.ascontiguousarray(np.concatenate([lng, lnb], 1)).astype(f32)
    gain = np.ascontiguousarray(np.stack([inp["q_norm_g"][0], inp["q_norm_g"][1], inp["k_norm_g"][0], inp["k_norm_g"][1]], 1)).astype(f32)
    mask_s = np.zeros((128, 48), f32)
    mask_p = np.full((12, 4), -30000.0, f32)
    for kt in range(8):
        mask_p[kt, kt // 2] = 0.0
    mask_p = np.ascontiguousarray(np.broadcast_to(mask_p.reshape(1, 48), (128, 48))).astype(f32)
    in_maps = []
    for core in range(8):
        m = dict(wall=wall, bmodT=bmodT, lnT=lnT, consts=consts, gain=gain, **s5h)
        if core < 4:
            b = core
            m["xT"] = np.ascontiguousarray(inp["x_sample"][b].T)
            cvec = inp["c"][b]
            m["rope"] = rope_s
            m["maskb"] = mask_s
            m["keep"] = np.ones((128, 1), f32)
            m["ckT"] = np.ascontiguousarray(inp["cache_k"][b].transpose(0, 3, 2, 1))
            m["cv"] = np.ascontiguousarray(inp["cache_v"][b].reshape(2, 4, 128, 256).transpose(0, 2, 1, 3))
            m["h0"] = h0_layout(inp["state_s5"][b])
        else:
            s0 = (core - 4) * 4
            m["xT"] = np.ascontiguousarray(inp["x_prompt"][s0:s0 + 4].reshape(NT, D).T)
            cvec = inp["c_ctx"]
            m["rope"] = rope_p
            m["maskb"] = mask_p
            m["keep"] = np.zeros((128, 1), f32)
            m["ckT"] = np.zeros((2, 128, 2, 512), f32)
            m["cv"] = np.zeros((2, 128, 4, 256), f32)
            m["h0"] = np.zeros((128, 2, 2, 2, 32), f32)
        m["cT"] = np.ascontiguousarray(cvec.reshape(8, 128).T).astype(f32)
        in_maps.append({k: np.ascontiguousarray(v, dtype=f32) for k, v in m.items()})
    if "nc" not in _CACHE:
        _CACHE["nc"] = build_program(nblk)
    nc = _CACHE["nc"]
    _CACHE.pop("nc")
    res = run_bass_kernel_spmd(nc, in_maps, core_ids=list(range(8)))
    R = res.results
    if DBG:
        _CACHE["dbg"] = {"c%d_%s" % (ci, k): R[ci][k] for ci in (0, 4) for k in R[ci] if k.startswith("d")}
    y_sample = np.stack([R[b]["yT"].T for b in range(4)], 0).astype(f32)
    y_prompt = np.concatenate([R[4 + i]["yT"].T.reshape(4, 256, D) for i in range(4)], 0).astype(f32)
    nk = np.zeros((16, 2, 256, 2, 128), f32)
    nv = np.zeros((16, 2, 256, 2, 128), f32)
    ns = np.zeros((16, 2, 2, 2, 64, 64), f32)
    for i in range(4):
        r = R[4 + i]
        ko = r["kout"]
        vo = r["vout"]
        so = r["stout"].reshape(2, 2, 64, 4, 2, 2, 32)
        for s in range(4):
            bidx = i * 4 + s
            nk[bidx] = ko[:, :, :, s * 256:(s + 1) * 256].transpose(0, 3, 1, 2)
            nv[bidx] = vo[:, s * 256:(s + 1) * 256, :].reshape(2, 256, 2, 128)
            for d in range(2):
                k = s if d == 0 else 3 - s
                blk = so[:, :, :, k, d, :, :]
                ns[bidx, :, d] = blk.transpose(0, 3, 4, 1, 2).reshape(2, 2, 64, 64)
    return (y_prompt, y_sample, nk, nv, ns)
```

```python
import os
import numpy as np
import concourse.bass as bass
import concourse.mybir as mybir
from concourse.bass_utils import run_bass_kernel_spmd

F32 = mybir.dt.float32
BF16 = mybir.dt.bfloat16
AF = mybir.ActivationFunctionType
ALU = mybir.AluOpType

D = 1024
NT = 1024
DEPTH = 4
DFF = 2816
KFF = 22
T = 8
C = NT // T
NSLOT = 2 * (C + 1)
ALPHA = (2.0 * DEPTH) ** 0.25
LN_EPS = 1e-6
RMS_EPS = 1e-6
ATT_SCALE = 128 ** -0.5
WCOLS = 4096
NWSLOT = 3
EPOCH = 16000
NDMA = 8
MAGIC = 12582912.0
STAGE = int(os.environ.get("K_STAGE", "99"))
DBG = int(os.environ.get("K_DBG", "0"))
S5STOP = int(os.environ.get("K_S5STOP", "0"))
KVAR = int(os.environ.get("K_VAR", "0"))


class _Stop(Exception):
    pass


class Buf:
    __slots__ = ("w", "r", "name")

    def __init__(self, name=""):
        self.w = {}
        self.r = {}
        self.name = name


def merge(name, *olds):
    b = Buf(name)
    for o in olds:
        for k, v in list(o.w.items()) + list(o.r.items()):
            b.r[k] = max(b.r.get(k, 0), v)
            b.w[k] = max(b.w.get(k, 0), v)
    return b


class Prog:
    def __init__(self, nc):
        self.nc = nc
        self.eng = {"pe": nc.tensor, "act": nc.scalar, "dve": nc.vector, "pool": nc.gpsimd, "sp": nc.sync}
        self.cnt = {}
        self.semlist = {}
        self.known = {e: {} for e in self.eng}
        self.allsems = []
        self.rec = None
        for k in ["pe", "act", "dve", "pool"]:
            self.cnt[k] = 0
            self.semlist[k] = []
        for pre in ("q", "g"):
            for i in range(NDMA):
                k = "%s%d" % (pre, i)
                self.cnt[k] = 0
                self.semlist[k] = [self._newsem(k)]
        self.dma_rr = {"q": 0, "g": 0}

    def _newsem(self, name):
        cm = self.nc.semaphore("s_%s_%d" % (name, len(self.allsems)))
        s = cm.__enter__()
        self.allsems.append(s)
        return s

    def _semval(self, k, v):
        if k[0] in "qg":
            return self.semlist[k][0], v
        idx = (v - 1) // EPOCH
        while len(self.semlist[k]) <= idx:
            self.semlist[k].append(self._newsem(k))
        return self.semlist[k][idx], v - idx * EPOCH

    def op(self, e, fn, reads=(), writes=(), dma=False):
        if self.rec is not None:
            self.rec.append((e, fn, tuple(reads), tuple(writes), dma))
            return 0
        deps = {}
        for b in reads:
            for k, v in b.w.items():
                if v > deps.get(k, 0):
                    deps[k] = v
        for b in writes:
            for k, v in b.w.items():
                if v > deps.get(k, 0):
                    deps[k] = v
            for k, v in b.r.items():
                if v > deps.get(k, 0):
                    deps[k] = v
        kn = self.known[e]
        for k, v in deps.items():
            if k == e and e == "pe":
                continue
            if kn.get(k, 0) >= v:
                continue
            s, sv = self._semval(k, v)
            self.eng[e].wait_ge(s, sv)
            kn[k] = v
        inst = fn(self.eng[e])
        if dma:
            pre = "g" if e == "pool" else "q"
            key = "%s%d" % (pre, self.dma_rr[pre])
            self.dma_rr[pre] = (self.dma_rr[pre] + 1) % NDMA
            self.cnt[key] += 16
            val = self.cnt[key]
            inst.then_inc(self.semlist[key][0], 16)
        else:
            key = e
            self.cnt[e] += 1
            val = self.cnt[e]
            s, sv = self._semval(e, val)
            inst.then_inc(s, 1)
        for b in reads:
            if val > b.r.get(key, 0):
                b.r[key] = val
        for b in writes:
            b.w = {key: val}
            b.r = {}
        return val

    def finish(self):
        sp = self.eng["sp"]
        for k, v in self.cnt.items():
            if v > 0:
                s, sv = self._semval(k, v)
                sp.wait_ge(s, sv)
        for s in self.allsems:
            sp.sem_clear(s)


def _blk(W, cols):
    kin = W.shape[0]
    kc = kin // 128
    a = W[:, cols].reshape(kc, 128, len(cols)).transpose(1, 0, 2).reshape(128, kc * len(cols))
    out = np.zeros((128, WCOLS), np.float32)
    out[:, :a.shape[1]] = a
    return out


def _r(a, n):
    return np.arange(a, a + n)


def mod_blocks(w_mod_l):
    return [_blk(w_mod_l, _r(b * 512, 512)) for b in range(12)]


def ffn_blocks(w_in, w_out):
    bl = []
    for b in range(11):
        j0, j1 = 2 * b, 2 * b + 1
        cols = np.concatenate([_r(j0 * 128, 128), _r(DFF + j0 * 128, 128), _r(j1 * 128, 128), _r(DFF + j1 * 128, 128)])
        bl.append(_blk(w_in, cols))
    for c in range(8):
        bl.append(_blk(w_out, _r(c * 128, 128)))
    return bl


def s5_blocks(w_in, w_glu, w_out):
    bl = [_blk(w_in, _r(b * 512, 512)) for b in range(2)]
    for b in range(4):
        j0, j1 = 2 * b, 2 * b + 1
        cols = np.concatenate([_r(j0 * 128, 128), _r(D + j0 * 128, 128), _r(j1 * 128, 128), _r(D + j1 * 128, 128)])
        bl.append(_blk(w_glu, cols))
    bl += [_blk(w_out, _r(b * 512, 512)) for b in range(2)]
    return bl


def attn_blocks(w_qkv, w_o):
    bl = [_blk(w_qkv, _r(b * 512, 512)) for b in range(3)]
    bl += [_blk(w_o, _r(b * 512, 512)) for b in range(2)]
    return bl


def build_wall(inp):
    bl = []
    bl += mod_blocks(inp["w_mod"][0])
    for l in range(DEPTH):
        j = l // 2
        if l % 2 == 0:
            sb_ = s5_blocks(inp["w_s5_in"][j], inp["w_s5_glu"][j], inp["w_s5_out"][j])
            bl += sb_[0:2]
            if l + 1 < DEPTH:
                bl += mod_blocks(inp["w_mod"][l + 1])
            bl += sb_[2:]
        else:
            bl += attn_blocks(inp["w_qkv"][j], inp["w_o"][j])
            if l + 1 < DEPTH:
                bl += mod_blocks(inp["w_mod"][l + 1])
        bl += ffn_blocks(inp["w_ffn_in"][l], inp["w_ffn_out"][l])
    return np.ascontiguousarray(np.stack(bl, 0))


def s5_host_layout(inp):
    out = {}
    G, P, H = 64, 64, 16
    def c_lay(a):
        a5 = a.reshape(2, 2, 8, 8, P)
        a5 = np.broadcast_to(a5[:, :, :, :, None, :], (2, 2, 8, 8, H, P))
        return np.ascontiguousarray(a5.transpose(3, 4, 0, 1, 2, 5).reshape(128, 2, 2, 8, P))
    def b_lay(a):
        a5 = a.reshape(2, 2, 32, 2, P)
        return np.ascontiguousarray(a5.transpose(3, 4, 0, 1, 2).reshape(128, 2, 2, 32))
    ld = np.broadcast_to(inp["s5_log_dt"][:, :, :, None], (2, 2, G, P))
    nat = np.stack([inp["s5_a_re"], inp["s5_a_im"], ld], 0)
    out["s5n"] = np.ascontiguousarray(nat.transpose(3, 1, 0, 2, 4))
    selc = np.zeros((128, 8, 8, 16), np.float32)
    for g in range(64):
        selc[g, g // 8, g % 8, :] = 1.0
    out["selc"] = selc.reshape(128, 8, 128)
    selb = np.zeros((128, 34), np.float32)
    for g in range(64):
        selb[g, g // 2] = 1.0
        selb[g, 32 + (g % 2)] = 1.0
    out["selb"] = selb
    b = np.stack([inp["s5_b_re"], inp["s5_b_im"]], 2)
    b8 = b.reshape(2, 2, 2, 8, 4, 2, P, H)
    w0 = np.zeros((2, 8, 4, 2, H, 2, 2, 2, P), np.float32)
    bn = np.zeros((2, 8, 2, P, 2, 2, 4, 2, H), np.float32)
    for a in range(2):
        w0[:, :, :, a, :, :, :, a, :] = b8[:, :, :, :, :, a].transpose(0, 3, 4, 6, 1, 2, 5)
        bn[:, :, a, :, :, :, :, a, :] = b8[:, :, :, :, :, a].transpose(0, 3, 5, 1, 2, 4, 6)
    out["s5w0"] = np.ascontiguousarray(w0.reshape(2, 8, 128, 512))
    out["s5bn"] = np.ascontiguousarray(bn.reshape(2, 8, 128, 512))
    c = np.stack([inp["s5_c_re"], inp["s5_c_im"]], 2)
    c8 = c.reshape(2, 2, 2, 8, 4, 2, H, P)
    v0 = np.zeros((2, 8, 2, P, 2, 2, 4, 2, H), np.float32)
    for a in range(2):
        v0[:, :, a, :, :, :, :, a, :] = c8[:, :, :, :, :, a].transpose(0, 3, 6, 1, 2, 4, 5)
    out["s5v0"] = np.ascontiguousarray(v0.reshape(2, 8, 128, 512))
    out["s5d"] = np.ascontiguousarray(inp["s5_d"].reshape(2, 8, 128).transpose(2, 0, 1))
    return out


def h0_layout(st):
    s = st.reshape(2, 2, 2, 32, 2, 64)
    return np.ascontiguousarray(s.transpose(4, 5, 0, 1, 2, 3).reshape(128, 2, 2, 2, 32))


def rope_tables():
    l = np.arange(NT)
    row = (l // 64).astype(np.float32)
    col = (l % 64).astype(np.float32)
    inv = (np.float32(10000.0) ** (-np.arange(32, dtype=np.float32) / np.float32(32))).astype(np.float32)
    ar = row[None, :] * inv[:, None]
    ac = col[None, :] * inv[:, None]
    cos = np.concatenate([np.cos(ar), np.cos(ar), np.cos(ac), np.cos(ac)], 0).astype(np.float32)
    sin = np.concatenate([np.sin(ar), np.sin(ar), np.sin(ac), np.sin(ac)], 0).astype(np.float32)
    return cos, sin


def const_tables():
    ident = np.eye(128, dtype=np.float32)
    bd = np.kron(np.eye(8, dtype=np.float32), np.ones((16, 16), np.float32))
    rot = np.zeros((128, 128), np.float32)
    for base in (0, 64):
        for i in range(32):
            rot[base + 32 + i, base + i] = -1.0
            rot[base + i, base + 32 + i] = 1.0
    return np.ascontiguousarray(np.stack([ident, bd, rot], 1))


def build_program(nblk):
    nc = bass.Bass("TRN2", target_bir_lowering=False)
    P = Prog(nc)

    def din(name, shape):
        return nc.dram_tensor(name, list(shape), F32, kind="ExternalInput").ap()

    def dout(name, shape):
        return nc.dram_tensor(name, list(shape), F32, kind="ExternalOutput").ap()

    d_xT = din("xT", [D, NT])
    d_cT = din("cT", [128, 8])
    d_wall = din("wall", [nblk, 128, WCOLS])
    d_bmod = din("bmodT", [128, DEPTH * 48])
    d_ln = din("lnT", [128, 128])
    d_rope = din("rope", [128, 2, NT])
    d_mask = din("maskb", [128, 48])
    d_keep = din("keep", [128, 1])
    d_ckT = din("ckT", [2, 128, 2, 512])
    d_cv = din("cv", [2, 128, 4, 256])
    d_gain = din("gain", [128, 4])
    d_const = din("consts", [128, 3, 128])
    d_s5w0 = din("s5w0", [2, 8, 128, 512])
    d_s5bn = din("s5bn", [2, 8, 128, 512])
    d_s5v0 = din("s5v0", [2, 8, 128, 512])
    d_s5d = din("s5d", [128, 2, 8])
    d_h0 = din("h0", [128, 2, 2, 2, 32])
    d_s5n = din("s5n", [64, 2, 3, 2, 64])
    d_selc = din("selc", [128, 8, 128])
    d_selb = din("selb", [128, 34])
    o_yT = dout("yT", [D, NT])
    o_k = dout("kout", [2, 2, 128, NT])
    o_v = dout("vout", [2, NT, 256])
    o_st = dout("stout", [2, 128, 512])
    if DBG:
        o_dU = dout("dU", [D, NT]); o_dZ = dout("dZ", [D, NT]); o_dXA = dout("dXA", [D, NT]); o_dS = None
        o_dQ = dout("dQ", [D, NT]); o_dK = dout("dK", [128, 2, 1536]); o_dO = dout("dO", [D, NT]); o_dXA1 = dout("dXA1", [D, NT])

    def dbg_bf(dst, src, bufs):
        P.op("pool", lambda e: e.dma_start(out=dst, in_=src), reads=bufs, dma=True)

    def dbg_x(dst):
        for c in range(8):
            P.op("sp", lambda e, c=c: e.dma_start(out=dst[c * 128:(c + 1) * 128, :], in_=X[:, c, :]), reads=[Xb[c]], dma=True)

    def sb(name, shape, dt=F32):
        cm = nc.sbuf_tensor(name, list(shape), dt)
        return cm.__enter__()

    def ps(name, shape, dt=F32):
        cm = nc.psum_tensor(name, list(shape), dt)
        return cm.__enter__()

    X = sb("X", [128, 8, NT])
    Xb = [Buf("X%d" % i) for i in range(8)]
    H = sb("H", [128, 8, NT], BF16)
    Hb = [Buf("H%d" % i) for i in range(8)]
    BIG = sb("BIG", [128, KFF * NT], BF16)
    BIGb = [Buf("BIG%d" % i) for i in range(KFF)]
    WR = sb("WR", [128, NWSLOT, WCOLS], BF16)
    WRb = [Buf("WR%d" % i) for i in range(NWSLOT)]
    LNT = sb("LNT", [128, 4, NT], BF16)
    LNTb = [Buf("LNT%d" % i) for i in range(4)]
    TS = sb("TS", [128, 4, NT])
    TSb = [Buf("TS%d" % i) for i in range(4)]
    MOD = sb("MOD", [128, DEPTH, 48])
    MODb = [Buf("MOD%d" % i) for i in range(DEPTH)]
    BMOD = sb("BMOD", [128, DEPTH * 48]); BMODb = Buf("BMOD")
    LNP = sb("LNP", [128, 128]); LNPb = Buf("LNP")
    CT = sb("CT", [128, 8]); CTb = Buf("CT")
    CS = sb("CS", [128, 8], BF16); CSb = Buf("CS")
    ROW = sb("ROW", [1, 512]); ROWb = Buf("ROW")
    ONE1 = sb("ONE1", [1, 1]); ONE1b = Buf("ONE1")
    MASK = sb("MASK", [128, 48]); MASKb = Buf("MASK")
    KEEP = sb("KEEP", [128, 1]); KEEPb = Buf("KEEP")
    GAIN = sb("GAIN", [128, 4]); GAINb = Buf("GAIN")
    CONST = sb("CONST", [128, 3, 128]); CONSTb = Buf("CONST")
    CB = sb("CB", [128, 3, 128], BF16); CBb = Buf("CB")
    ONES = sb("ONES", [128, 128], BF16); ONESb = Buf("ONES")
    EPSC = sb("EPSC", [128, 2]); EPSb = Buf("EPSC")
    ARENA = sb("ARENA", [128, 8192])
    ARb = {"cur": [Buf("ARENA")]}

    def arena_switch(names):
        olds = ARb["cur"]
        news = [merge(n, *olds) for n in names]
        ARb["cur"] = news
        return news
    ROPE = ARENA[:, 0:2048].rearrange("p (a n) -> p a n", a=2)
    KTB = ARENA[:, 2048:3584].bitcast(BF16).rearrange("p (a n) -> p a n", a=2)
    VTB = ARENA[:, 3584:5120].bitcast(BF16).rearrange("p (a n) -> p a n", a=12)
    PR = ARENA[:, 5120:6144].bitcast(BF16).rearrange("p (a n) -> p a n", a=4)
    S5C = ARENA[:, 0:2048].rearrange("p (t d f x) -> p t d f x", t=2, d=2, f=8)
    S5K = ARENA[:, 2048:4096].rearrange("p (t d f x) -> p t d f x", t=2, d=2, f=8)
    NATQ = ARENA[0:64, 4096:4736].rearrange("p (t x) -> p t x", t=5)
    NATT = ARENA[0:64, 4736:5504].rearrange("p (t x) -> p t x", t=6)
    XPAD = ARENA[0:64, 5504:6528].rearrange("p (t x) -> p t x", t=8)
    SELC = ARENA[:, 6528:7552].rearrange("p (f x) -> p f x", f=8)
    SELB = ARENA[:, 7552:7586]
    NATQ128 = ARENA[:, 4096:4736].rearrange("p (t x) -> p t x", t=5)
    XPAD128 = ARENA[:, 5504:6528].rearrange("p (t x) -> p t x", t=8)
    VF1 = ARENA[:, 4096:4096 + 2304].bitcast(BF16).rearrange("p (n d r x) -> p n d r x", n=9, d=2, r=2)
    S5B = sb("S5B", [128, 2, 2, 32]); S5Bb = Buf("S5B")
    S5KB = sb("S5KB", [128, 2, 2, 32]); S5KBb = Buf("S5KB")
    S5TB = sb("S5TB", [128, 2, 2, 32]); S5TBb = Buf("S5TB")
    S5LT = sb("S5LT", [128, 2, 2, 32]); S5LTb = Buf("S5LT")
    S5LN = sb("S5LN", [128, 2, 32]); S5LNb = Buf("S5LN")
    S5D = sb("S5D", [128, 2, 8]); S5Db = Buf("S5D")
    H0 = sb("H0", [128, 2, 2, 2, 32]); H0b = Buf("H0")
    SELCb = Buf("SELC")
    SELBb = Buf("SELB")
    LDW = sb("LDW", [128, 2, 512]); LDWb = [Buf("LDW0"), Buf("LDW1")]
    LDN = sb("LDN", [128, 2, 512]); LDNb = [Buf("LDN0"), Buf("LDN1")]
    LDV = LDW; LDVb = LDWb
    WST = sb("WST", [128, 2, 2, 2, 128])
    WSTb = [[Buf("WST00"), Buf("WST01")], [Buf("WST10"), Buf("WST11")]]
    WTA = sb("WTA", [128, 2, 2, 2, 128])
    WTAb = [[Buf("WTA00"), Buf("WTA01")], [Buf("WTA10"), Buf("WTA11")]]
    BNB = sb("BNB", [128, 2, 2, 2, 128], BF16); BNBb = [Buf("BNB0"), Buf("BNB1")]
    KTMP = sb("KTMP", [128, 128]); KTMPb = Buf("KTMP")
    STG = sb("STG", [128, 4, 2, 2, 32]); STGb = Buf("STG")
    RST = sb("RST", [128, 2, 2, 2, 32]); RSTb = [Buf("RST0"), Buf("RST1")]
    RTM = sb("RTM", [128, 2, 2, 2, 32]); RTMb = [Buf("RTM0"), Buf("RTM1a"), Buf("RTM1b")]
    VST = sb("VST", [128, 2, 256]); VSTb = [Buf("VST0"), Buf("VST1")]

    PA = ps("PA", [128, NT]); PB = ps("PB", [128, NT]); PC = ps("PC", [128, NT]); PD = ps("PD", [128, NT])
    PAb, PBb, PCb, PDb = Buf("PA"), Buf("PB"), Buf("PC"), Buf("PD")
    PSUMS = [(PA, PAb), (PB, PBb), (PC, PCb), (PD, PDb)]

    wstate = {"issued": 0, "used": 0}

    def w_issue():
        i = wstate["issued"]
        if i >= nblk:
            return
        s = i % NWSLOT
        P.op("pool", lambda e: e.dma_start(out=WR[:, s, :], in_=d_wall[i]), writes=[WRb[s]], dma=True)
        wstate["issued"] += 1

    def w_get():
        i = wstate["used"]
        wstate["used"] += 1
        while wstate["issued"] < min(nblk, i + NWSLOT):
            w_issue()
        s = i % NWSLOT
        return WR[:, s, :], WRb[s]

    def sp_load(dst, src, buf):
        P.op("sp", lambda e: e.dma_start(out=dst, in_=src), writes=[buf], dma=True)

    for i in range(NWSLOT):
        w_issue()
    sp_load(CT[:], d_cT, CTb)
    sp_load(BMOD[:], d_bmod, BMODb)
    sp_load(LNP[:], d_ln, LNPb)
    sp_load(CONST[:], d_const, CONSTb)
    sp_load(GAIN[:], d_gain, GAINb)
    sp_load(MASK[:], d_mask, MASKb)
    sp_load(KEEP[:], d_keep, KEEPb)
    for c in range(8):
        sp_load(X[:, c, :], d_xT[c * 128:(c + 1) * 128, :], Xb[c])
    sp_load(S5D[:], d_s5d, S5Db)
    sp_load(H0[:], d_h0, H0b)
    sp_load(SELC, d_selc, SELCb)
    sp_load(SELB, d_selb, SELBb)

    P.op("dve", lambda e: e.memset(ONES[:], 1.0), writes=[ONESb])
    P.op("dve", lambda e: e.memset(ONE1[:], 1.0), writes=[ONE1b])
    P.op("dve", lambda e: e.memset(EPSC[:, 0:1], LN_EPS / (ALPHA * ALPHA)), writes=[EPSb])
    P.op("dve", lambda e: e.memset(EPSC[:, 1:2], RMS_EPS), writes=[EPSb])
    P.op("dve", lambda e: e.tensor_copy(out=CB[:], in_=CONST[:]), reads=[CONSTb], writes=[CBb])
    P.op("act", lambda e: e.activation(out=CS[:], in_=CT[:], func=AF.Silu), reads=[CTb], writes=[CSb])

    IDENT_F = CONST[:, 0, :]
    BDMASK = CONST[:, 1, :]
    ROTB = CB[:, 2, :]

    def compute_mod_gen(l):
        for b in range(12):
            wap, wb = w_get()
            w3 = wap.rearrange("p (k n) -> p k n", n=512)

            def mm(e):
                last = None
                for kc in range(8):
                    last = e.matmul(PC[0:1, 0:512], lhsT=CS[:, kc:kc + 1], rhs=w3[:, kc, :], start=(kc == 0), stop=(kc == 7))
                return last
            P.op("pe", mm, reads=[wb, CSb], writes=[PCb])
            P.op("act", lambda e: e.activation(out=ROW[:], in_=PC[0:1, 0:512], func=AF.Copy), reads=[PCb], writes=[ROWb])

            def tr(e):
                last = None
                for i in range(4):
                    col = b * 4 + i
                    last = e.matmul(PD[:, col:col + 1], lhsT=ROW[0:1, i * 128:(i + 1) * 128], rhs=ONE1[0:1, 0:1], start=True, stop=True)
                return last
            P.op("pe", tr, reads=[ROWb, ONE1b], writes=[PDb])
            yield b
        P.op("dve", lambda e: e.tensor_tensor(out=MOD[:, l, :], in0=PD[:, 0:48], in1=BMOD[:, l * 48:(l + 1) * 48], op=ALU.add),
             reads=[PDb, BMODb], writes=[MODb[l]])
        for base in (8, 32):
            P.op("dve", lambda e, base=base: e.tensor_scalar(out=MOD[:, l, base:base + 8], in0=MOD[:, l, base:base + 8], scalar1=1.0, scalar2=None, op0=ALU.add),
                 reads=[MODb[l]], writes=[MODb[l]])
        for base in (16, 40):
            P.op("dve", lambda e, base=base: e.tensor_scalar(out=MOD[:, l, base:base + 8], in0=MOD[:, l, base:base + 8], scalar1=1.0 / ALPHA, scalar2=None, op0=ALU.mult),
                 reads=[MODb[l]], writes=[MODb[l]])

    def compute_mod(l):
        for _ in compute_mod_gen(l):
            pass

    def modulate(l, which):
        so = 0 if which == 0 else 24
        for c in range(8):
            P.op("dve", lambda e, c=c: e.tensor_scalar(out=H[:, c, :], in0=X[:, c, :], scalar1=MOD[:, l, so + 8 + c:so + 9 + c],
                                                      scalar2=MOD[:, l, so + c:so + c + 1], op0=ALU.mult, op1=ALU.add),
                 reads=[Xb[c], MODb[l]], writes=[Hb[c]])

    def proj_chunk(pt, pbuf, w3, wb, oc, src, srcbufs, kcn):
        def mm(e):
            last = None
            for kc in range(kcn):
                for hf in range(2):
                    last = e.matmul(pt[:, hf * 512:(hf + 1) * 512], lhsT=w3[:, kc, oc * 128:(oc + 1) * 128],
                                    rhs=src[:, kc, hf * 512:(hf + 1) * 512], start=(kc == 0), stop=(kc == kcn - 1))
            return last
        P.op("pe", mm, reads=[wb] + list(srcbufs), writes=[pbuf])

    def outproj_ln(l, which, src, srcbufs, kcn, nblocks, cols_per_blk, next_mod):
        gcol = 16 if which == 0 else 40
        lni = (l * 2 + which) * 8
        oc_global = 0
        pend_st = None
        for b in range(nblocks):
            wap, wb = w_get()
            w3 = wap[:, 0:kcn * cols_per_blk].rearrange("p (k n) -> p k n", n=cols_per_blk)
            for oc in range(cols_per_blk // 128):
                c = oc_global
                pt, pbuf = PSUMS[c % 2]
                proj_chunk(pt, pbuf, w3, wb, oc, src, srcbufs, kcn)
                P.op("dve", lambda e, c=c, pt=pt: e.scalar_tensor_tensor(out=X[:, c, :], in0=pt[:], scalar=MOD[:, l, gcol + c:gcol + c + 1],
                                                                        in1=X[:, c, :], op0=ALU.mult, op1=ALU.add),
                     reads=[pbuf, MODb[l], Xb[c]], writes=[Xb[c]])
                s = (c % 2) * 2
                P.op("act", lambda e, c=c, s=s: e.activation(out=LNT[:, s, :], in_=X[:, c, :], func=AF.Copy), reads=[Xb[c]], writes=[LNTb[s]])
                P.op("act", lambda e, c=c, s=s: e.activation(out=LNT[:, s + 1, :], in_=X[:, c, :], func=AF.Square), reads=[Xb[c]], writes=[LNTb[s + 1]])

                def st(e, c=c, s=s):
                    last = None
                    for hf in range(2):
                        e.matmul(PC[:, hf * 512:(hf + 1) * 512], lhsT=ONES[:], rhs=LNT[:, s, hf * 512:(hf + 1) * 512], start=(c == 0), stop=(c == 7))
                        last = e.matmul(PD[:, hf * 512:(hf + 1) * 512], lhsT=ONES[:], rhs=LNT[:, s + 1, hf * 512:(hf + 1) * 512], start=(c == 0), stop=(c == 7))
                    return last
                if pend_st is not None:
                    pend_st()
                pend_st = (lambda st=st, s=s: P.op("pe", st, reads=[ONESb, LNTb[s], LNTb[s + 1]], writes=[PCb, PDb]))
                oc_global += 1
        pend_st()
        P.op("act", lambda e: e.activation(out=TS[:, 0, :], in_=PC[:], func=AF.Identity, scale=1.0 / D), reads=[PCb], writes=[TSb[0]])
        P.op("act", lambda e: e.activation(out=TS[:, 1, :], in_=TS[:, 0, :], func=AF.Square), reads=[TSb[0]], writes=[TSb[1]])
        P.op("dve", lambda e: e.scalar_tensor_tensor(out=TS[:, 1, :], in0=PD[:], scalar=1.0 / D, in1=TS[:, 1, :], op0=ALU.mult, op1=ALU.subtract),
             reads=[PDb, TSb[1]], writes=[TSb[1]])
        P.op("act", lambda e: e.activation(out=TS[:, 1, :], in_=TS[:, 1, :], func=AF.Ln, bias=EPSC[:, 0:1], scale=1.0), reads=[TSb[1], EPSb], writes=[TSb[1]])
        P.op("act", lambda e: e.activation(out=TS[:, 1, :], in_=TS[:, 1, :], func=AF.Exp, scale=-0.5), reads=[TSb[1]], writes=[TSb[1]])
        for c in range(8):
            t = 2 + (c % 2)
            P.op("pool" if c % 2 == 1 else "dve", lambda e, c=c, t=t: e.tensor_tensor(out=TS[:, t, :], in0=X[:, c, :], in1=TS[:, 0, :], op=ALU.subtract),
                 reads=[Xb[c], TSb[0]], writes=[TSb[t]])
            P.op("dve", lambda e, c=c, t=t: e.tensor_tensor(out=TS[:, t, :], in0=TS[:, t, :], in1=TS[:, 1, :], op=ALU.mult),
                 reads=[TSb[t], TSb[1]], writes=[TSb[t]])
            P.op("act", lambda e, c=c, t=t: e.activation(out=X[:, c, :], in_=TS[:, t, :], func=AF.Identity,
                                                         scale=LNP[:, lni + c:lni + c + 1], bias=LNP[:, 64 + lni + c:64 + lni + c + 1]),
                 reads=[TSb[t], LNPb], writes=[Xb[c]])
        if next_mod is not None:
            modulate(*next_mod)

    def ffn(l, bg=None):
        bg = bg if bg is not None else []
        for b in range(11):
            wap, wb = w_get()
            w3 = wap.rearrange("p (k n) -> p k n", n=512)
            for jj in range(2):
                j = 2 * b + jj
                pg, pgb = PSUMS[(jj * 2) % 4]
                pu, pub = PSUMS[(jj * 2 + 1) % 4]
                proj_chunk(pg, pgb, w3, wb, jj * 2, H, Hb, 8)
                proj_chunk(pu, pub, w3, wb, jj * 2 + 1, H, Hb, 8)
                t = jj
                P.op("act", lambda e, pg=pg, t=t: e.activation(out=TS[:, t, :], in_=pg[:], func=AF.Silu), reads=[pgb], writes=[TSb[t]])
                P.op("dve", lambda e, pu=pu, t=t, j=j: e.tensor_tensor(out=BIG[:, j * NT:(j + 1) * NT], in0=pu[:], in1=TS[:, t, :], op=ALU.mult),
                     reads=[pub, TSb[t]], writes=[BIGb[j]])
                replay(bg, 14)
        replay(bg, 10 ** 9)
        BIG3 = BIG[:].rearrange("p (k n) -> p k n", n=NT)
        nm = (l + 1, 0) if l + 1 < DEPTH else None
        outproj_ln(l, 1, BIG3, BIGb, KFF, 8, 128, nm)

    def attention(l):
        j = l // 2
        Q = BIG[:, 0:8 * NT].rearrange("p (k n) -> p k n", n=NT)
        nb = arena_switch(["ROPE", "KT0", "KT1"] + ["VT%d" % i for i in range(12)] + ["PR%d" % i for i in range(4)])
        ROPEb = nb[0]
        KTBb = nb[1:3]
        VTBb = nb[3:15]
        PRb = nb[15:19]
        sp_load(ROPE, d_rope, ROPEb)
        for kv in range(2):
            P.op("pool", lambda e, kv=kv: e.dma_start(out=KTB[:, kv, NT:NT + 512], in_=d_ckT[j, :, kv, :]), writes=[KTBb[kv]], dma=True)
        for t4 in range(4):
            P.op("pool", lambda e, t4=t4: e.dma_start(out=VTB[:, 8 + t4, :], in_=d_cv[j, :, t4, :]), writes=[VTBb[8 + t4]], dma=True)
        cur = None

        def emit_proj(hc_):
            nonlocal cur
            if hc_ % 4 == 0:
                cur = w_get()
            wap_, wb_ = cur
            w3_ = wap_.rearrange("p (k n) -> p k n", n=512)
            pt_, pbuf_ = PSUMS[hc_ % 2]
            proj_chunk(pt_, pbuf_, w3_, wb_, hc_ % 4, H, Hb, 8)
        emit_proj(0)
        for hc in range(10):
            pt, pbuf = PSUMS[hc % 2]
            if hc + 1 < 10:
                emit_proj(hc + 1)
            isk = hc >= 8
            gcol = (2 + j) if isk else j
            P.op("act", lambda e, pt=pt: e.activation(out=TS[:, 2, :], in_=pt[:], func=AF.Copy), reads=[pbuf], writes=[TSb[2]])
            P.op("act", lambda e, pt=pt: e.activation(out=LNT[:, 0, :], in_=pt[:], func=AF.Square), reads=[pbuf], writes=[LNTb[0]])

            def st(e):
                last = None
                for hf in range(2):
                    last = e.matmul(PC[:, hf * 512:(hf + 1) * 512], lhsT=ONES[:], rhs=LNT[:, 0, hf * 512:(hf + 1) * 512], start=True, stop=True)
                return last
            P.op("pe", st, reads=[ONESb, LNTb[0]], writes=[PCb])
            P.op("act", lambda e: e.activation(out=TS[:, 3, :], in_=PC[:], func=AF.Ln, bias=EPSC[:, 1:2], scale=1.0 / 128), reads=[PCb, EPSb], writes=[TSb[3]])
            P.op("act", lambda e: e.activation(out=TS[:, 3, :], in_=TS[:, 3, :], func=AF.Exp, scale=-0.5), reads=[TSb[3]], writes=[TSb[3]])
            P.op("dve", lambda e, gcol=gcol: e.scalar_tensor_tensor(out=TS[:, 2, :], in0=TS[:, 2, :], scalar=GAIN[:, gcol:gcol + 1], in1=TS[:, 3, :],
                                                                    op0=ALU.mult, op1=ALU.mult), reads=[TSb[2], TSb[3], GAINb], writes=[TSb[2]])
            P.op("act", lambda e: e.activation(out=LNT[:, 1, :], in_=TS[:, 2, :], func=AF.Copy), reads=[TSb[2]], writes=[LNTb[1]])

            def rt(e):
                last = None
                for hf in range(2):
                    last = e.matmul(PD[:, hf * 512:(hf + 1) * 512], lhsT=ROTB, rhs=LNT[:, 1, hf * 512:(hf + 1) * 512], start=True, stop=True)
                return last
            P.op("pe", rt, reads=[CBb, LNTb[1]], writes=[PDb])
            P.op("dve", lambda e: e.tensor_tensor(out=TS[:, 0, :], in0=PD[:], in1=ROPE[:, 1, :], op=ALU.mult), reads=[PDb, ROPEb], writes=[TSb[0]])
            P.op("dve", lambda e: e.tensor_tensor(out=TS[:, 2, :], in0=TS[:, 2, :], in1=ROPE[:, 0, :], op=ALU.mult), reads=[TSb[2], ROPEb], writes=[TSb[2]])
            if not isk:
                P.op("dve", lambda e, hc=hc: e.tensor_tensor(out=Q[:, hc, :], in0=TS[:, 2, :], in1=TS[:, 0, :], op=ALU.add),
                     reads=[TSb[2], TSb[0]], writes=[BIGb[hc]])
            else:
                kv = hc - 8
                P.op("dve", lambda e: e.tensor_tensor(out=TS[:, 1, :], in0=TS[:, 2, :], in1=TS[:, 0, :], op=ALU.add), reads=[TSb[2], TSb[0]], writes=[TSb[1]])
                P.op("act", lambda e, kv=kv: e.activation(out=KTB[:, kv, 0:NT], in_=TS[:, 1, :], func=AF.Copy), reads=[TSb[1]], writes=[KTBb[kv]])
                P.op("sp", lambda e, kv=kv: e.dma_start(out=o_k[j, kv], in_=TS[:, 1, :]), reads=[TSb[1]], dma=True)
        wap, wb = cur
        w3 = wap.rearrange("p (k n) -> p k n", n=512)
        for tt in range(8):
            pt, pbuf = PSUMS[tt % 2]

            def mmv(e, tt=tt, pt=pt):
                last = None
                for kc in range(8):
                    last = e.matmul(pt[:, 0:256], lhsT=H[:, kc, tt * 128:(tt + 1) * 128], rhs=w3[:, kc, 256:512], start=(kc == 0), stop=(kc == 7))
                return last
            P.op("pe", mmv, reads=[wb] + Hb, writes=[pbuf])
            s = tt % 2
            P.op("act", lambda e, pt=pt, s=s: e.activation(out=VST[:, s, :], in_=pt[:, 0:256], func=AF.Copy), reads=[pbuf], writes=[VSTb[s]])
            P.op("dve", lambda e, tt=tt, s=s: e.tensor_copy(out=VTB[:, tt, :], in_=VST[:, s, :]), reads=[VSTb[s]], writes=[VTBb[tt]])
            P.op("sp", lambda e, tt=tt, s=s: e.dma_start(out=o_v[j, tt * 128:(tt + 1) * 128, :], in_=VST[:, s, :]), reads=[VSTb[s]], dma=True)
        if DBG and l == 1:
            for h in range(8):
                dbg_bf(o_dQ[h * 128:(h + 1) * 128, :], Q[:, h, :], [BIGb[h]])
            dbg_bf(o_dK, KTB, KTBb)
        NSB = 4
        PAh = [PA[:, 0:512], PA[:, 512:1024], PD[:, 0:512], PD[:, 512:1024]]
        PAhb = [merge("PAh", PAb), merge("PAh", PAb), merge("PDh", PDb), merge("PDh", PDb)]
        pri = 0
        PRh = [[merge("PRh", PRb[i]), merge("PRh", PRb[i])] for i in range(4)]
        for h in range(8):
            kv = h // 4
            for qh in range(2):
                qs = slice(qh * 512, (qh + 1) * 512)

                def smm(e, kt, h=h, kv=kv, qs=qs):
                    return e.matmul(PAh[kt % NSB], lhsT=KTB[:, kv, kt * 128:(kt + 1) * 128], rhs=Q[:, h, qs], start=True, stop=True)
                for k0 in range(NSB - 1):
                    P.op("pe", lambda e, k0=k0: smm(e, k0), reads=[KTBb[kv], BIGb[h]], writes=[PAhb[k0]])
                for kt in range(12):
                    if kt + NSB - 1 < 12:
                        P.op("pe", lambda e, kt=kt: smm(e, kt + NSB - 1), reads=[KTBb[kv], BIGb[h]], writes=[PAhb[(kt + NSB - 1) % NSB]])
                    pslot = pri % 4
                    pri += 1
                    for g2 in range(2):
                        qg = qh * 2 + g2
                        P.op("act", lambda e, kt=kt, g2=g2, qg=qg, pslot=pslot: e.activation(
                            out=PR[:, pslot, g2 * 256:(g2 + 1) * 256], in_=PAh[kt % NSB][:, g2 * 256:(g2 + 1) * 256], func=AF.Exp,
                            bias=MASK[:, kt * 4 + qg:kt * 4 + qg + 1], scale=ATT_SCALE), reads=[PAhb[kt % NSB], MASKb], writes=[PRh[pslot][g2]])

                    def pv(e, kt=kt, kv=kv, qs=qs, pslot=pslot):
                        e.matmul(PB[:, qs], lhsT=VTB[:, kt, kv * 128:(kv + 1) * 128], rhs=PR[:, pslot, :], start=(kt == 0), stop=(kt == 11))
                        return e.matmul(PC[:, qs], lhsT=ONES[:], rhs=PR[:, pslot, :], start=(kt == 0), stop=(kt == 11))
                    P.op("pe", pv, reads=[VTBb[kt], PRh[pslot][0], PRh[pslot][1], ONESb], writes=[PBb, PCb])
                P.op("dve", lambda e, qs=qs: e.reciprocal(out=TS[:, 0, qs], in_=PC[:, qs]), reads=[PCb], writes=[TSb[0]])
                P.op("dve", lambda e, qs=qs, h=h: e.tensor_tensor(out=H[:, h, qs], in0=PB[:, qs], in1=TS[:, 0, qs], op=ALU.mult),
                     reads=[PBb, TSb[0]], writes=[Hb[h]])
        m1 = merge("PA", PAhb[0], PAhb[1])
        PAb.w, PAb.r = m1.w, m1.r
        m1 = merge("PD", PAhb[2], PAhb[3])
        PDb.w, PDb.r = m1.w, m1.r
        if DBG and l == 1:
            for h in range(8):
                dbg_bf(o_dO[h * 128:(h + 1) * 128, :], H[:, h, :], [Hb[h]])
        outproj_ln(l, 0, H, Hb, 8, 2, 512, (l, 1))
        if DBG and l == 1:
            dbg_x(o_dXA1)

    def cmul_ops(e_name, out_r, out_i, in_r, in_i, lr, li, ta, tb, reads, writes, tbufs):
        return [
            lambda: P.op(e_name, lambda e: e.tensor_tensor(out=ta, in0=in_r, in1=lr, op=ALU.mult), reads=reads, writes=[tbufs[0]]),
            lambda: P.op(e_name, lambda e: e.tensor_tensor(out=tb, in0=in_i, in1=li, op=ALU.mult), reads=reads, writes=[tbufs[1]]),
            lambda: P.op(e_name, lambda e: e.tensor_tensor(out=ta, in0=ta, in1=tb, op=ALU.subtract), reads=[tbufs[0], tbufs[1]], writes=[tbufs[0]]),
            lambda: P.op(e_name, lambda e: e.tensor_tensor(out=tb, in0=in_r, in1=li, op=ALU.mult), reads=reads, writes=[tbufs[1]]),
            lambda: P.op(e_name, lambda e: e.tensor_tensor(out=out_i, in0=in_i, in1=lr, op=ALU.mult), reads=reads + [tbufs[0]], writes=writes),
            lambda: P.op(e_name, lambda e: e.tensor_tensor(out=out_i, in0=out_i, in1=tb, op=ALU.add), reads=[tbufs[1]] + writes, writes=writes),
            lambda: P.op(e_name, lambda e: e.tensor_copy(out=out_r, in_=ta), reads=[tbufs[0]], writes=writes),
        ]

    def cmul(*a):
        for t in cmul_ops(*a):
            t()

    def interleave(lists):
        n = max(len(L) for L in lists)
        for i in range(n):
            for L in lists:
                if i < len(L):
                    L[i]()

    def lam_compute(A, Ab, Kt, Kb, Tm, Tb_, n_t):
        bufs = [Ab, Kb, Tb_]
        A0, A1, A2 = A[:, 0], A[:, 1], A[:, 2]
        K0, K1 = Kt[:, 0], Kt[:, 1]
        T0, T1, T2, T3, T4, T5 = (Tm[:, i] for i in range(6))

        def tt(o, a, b, op):
            P.op("dve", lambda e: e.tensor_tensor(out=o, in0=a, in1=b, op=op), reads=bufs, writes=bufs)

        def tsc(o, a, s1, s2=None, op0=ALU.mult, op1=ALU.add):
            if s2 is None:
                P.op("dve", lambda e: e.tensor_scalar(out=o, in0=a, scalar1=s1, scalar2=None, op0=op0), reads=bufs, writes=bufs)
            else:
                P.op("dve", lambda e: e.tensor_scalar(out=o, in0=a, scalar1=s1, scalar2=s2, op0=op0, op1=op1), reads=bufs, writes=bufs)

        def stt(o, a, sc, b, op0, op1):
            P.op("dve", lambda e: e.scalar_tensor_tensor(out=o, in0=a, scalar=sc, in1=b, op0=op0, op1=op1), reads=bufs, writes=bufs)

        def horner(t, y, divs, sign):
            tsc(t, y, sign / divs[-1], 1.0)
            for dv in reversed(divs[:-1]):
                tt(t, t, y, ALU.mult)
                tsc(t, t, sign / dv, 1.0)
        tsc(K1, A2, 0.125)
        horner(K0, K1, [1.0, 2.0, 3.0, 4.0, 5.0, 6.0, 7.0, 8.0, 9.0, 10.0, 11.0], 1.0)
        for _ in range(3):
            tt(K0, K0, K0, ALU.mult)
        tt(T0, K0, A0, ALU.mult)
        tt(T1, K0, A1, ALU.mult)
        horner(K0, T0, [2.0, 3.0, 4.0, 5.0, 6.0, 7.0], 1.0)
        tt(T2, K0, T0, ALU.mult)
        tsc(K1, T1, 1.0 / 16.0)
        tt(T3, K1, K1, ALU.mult)
        horner(K0, T3, [6.0, 20.0, 42.0, 72.0, 110.0, 156.0, 210.0], -1.0)
        tt(T4, K0, K1, ALU.mult)
        horner(K0, T3, [12.0, 30.0, 56.0, 90.0, 132.0, 182.0, 240.0], -1.0)
        stt(T5, T3, -0.5, K0, ALU.mult, ALU.mult)
        for _ in range(4):
            tt(K1, T4, T4, ALU.mult)
            stt(K0, T5, 1.0, T4, ALU.add, ALU.mult)
            tsc(T4, K0, 2.0)
            tsc(T5, K1, -2.0)
        stt(K0, T2, 1.0, T5, ALU.add, ALU.mult)
        tt(T0, K0, T2, ALU.add)
        stt(T1, T2, 1.0, T4, ALU.add, ALU.mult)
        tt(T3, A0, A0, ALU.mult)
        tt(T2, A1, A1, ALU.mult)
        tt(T3, T3, T2, ALU.add)
        P.op("dve", lambda e: e.reciprocal(out=T3, in_=T3), reads=bufs, writes=bufs)
        tt(T2, T0, A0, ALU.mult)
        tt(T4, T1, A1, ALU.mult)
        tt(T2, T2, T4, ALU.add)
        tt(K0, T2, T3, ALU.mult)
        tt(T2, T1, A0, ALU.mult)
        tt(T4, T0, A1, ALU.mult)
        tt(T2, T2, T4, ALU.subtract)
        tt(K1, T2, T3, ALU.mult)
        tsc(A0, T0, 1.0, None, op0=ALU.add)
        P.op("dve", lambda e: e.tensor_copy(out=A1, in_=T1), reads=bufs, writes=bufs)

    lamctx = {}
    bgs = {}

    def s5_lambda(l):
        j = l // 2
        nb = arena_switch(["S5C", "S5K", "S5T"])
        S5Cb, S5Kb, S5Tb = nb
        lamctx[l] = nb
        P.op("dve", lambda e: e.memset(ARENA[64:128, 4096:6528], 0.0), writes=[S5Tb])
        sp_load(NATQ[:, 0:2, :], d_s5n[:, j, 0:2].rearrange("g t d p -> g t (d p)"), S5Tb)
        sp_load(NATQ[:, 2, :], d_s5n[:, j, 2].rearrange("g d p -> g (d p)"), S5Tb)
        lam_compute(NATQ[:, 0:3, :], S5Tb, NATQ[:, 3:5, :], S5Tb, NATT, S5Tb, 0)
        for fc in range(8):
            pt, pbuf = PSUMS[2 + fc % 2]

            def mmx(e, fc=fc, pt=pt):
                e.matmul(pt[:, 0:256], lhsT=SELC[:, fc, :], rhs=NATQ128[:, 0:2, :].rearrange("p t x -> p (t x)"), start=True, stop=True)
                return e.matmul(pt[:, 256:512], lhsT=SELC[:, fc, :], rhs=NATQ128[:, 3:5, :].rearrange("p t x -> p (t x)"), start=True, stop=True)
            P.op("pe", mmx, reads=[SELCb, S5Tb], writes=[pbuf])
            P.op("act", lambda e, fc=fc, pt=pt: e.activation(out=S5C[:, :, :, fc, :], in_=pt[:, 0:256].rearrange("p (t d x) -> p t d x", t=2, d=2), func=AF.Copy),
                 reads=[pbuf], writes=[S5Cb])
            P.op("act", lambda e, fc=fc, pt=pt: e.activation(out=S5K[:, :, :, fc, :], in_=pt[:, 256:512].rearrange("p (t d x) -> p t d x", t=2, d=2), func=AF.Copy),
                 reads=[pbuf], writes=[S5Kb])
        slots = (0, 1, 3, 4)
        for ti in range(4):
            for d in range(2):
                for a in range(2):
                    P.op("dve", lambda e, ti=ti, d=d, a=a: e.tensor_scalar(out=XPAD[:, ti * 2 + d, a * 64:(a + 1) * 64], in0=NATQ[:, slots[ti], d * 64:(d + 1) * 64],
                                                                        scalar1=SELB[0:64, 32 + a:33 + a], scalar2=None, op0=ALU.mult),
                         reads=[S5Tb, SELBb], writes=[S5Tb])

        def mmb(e):
            last = None
            for k in range(8):
                last = e.matmul(PA[:, k * 32:(k + 1) * 32], lhsT=XPAD128[:, k, :], rhs=SELB[:, 0:32], start=True, stop=True)
            return last
        P.op("pe", mmb, reads=[S5Tb, SELBb], writes=[PAb])
        P.op("act", lambda e: e.activation(out=S5B[:], in_=PA[:, 0:128].rearrange("p (t d q) -> p t d q", t=2, d=2), func=AF.Copy), reads=[PAb], writes=[S5Bb])
        P.op("act", lambda e: e.activation(out=S5KB[:], in_=PA[:, 128:256].rearrange("p (t d q) -> p t d q", t=2, d=2), func=AF.Copy), reads=[PAb], writes=[S5KBb])
        P.op("dve", lambda e: e.tensor_scalar(out=S5LN[:], in0=S5B[:, 1], scalar1=-1.0, scalar2=None, op0=ALU.mult), reads=[S5Bb], writes=[S5LNb])
        P.op("dve", lambda e: e.tensor_copy(out=S5LT[:], in_=S5B[:]), reads=[S5Bb], writes=[S5LTb])
        for _ in range(3):
            cmul("dve", S5LT[:, 0], S5LT[:, 1], S5LT[:, 0], S5LT[:, 1], S5LT[:, 0], S5LT[:, 1], S5TB[:, 0], S5TB[:, 1], [S5LTb], [S5LTb], [S5TBb, S5TBb])

    def record_lambda(l):
        P.rec = []
        s5_lambda(l)
        ops = P.rec
        P.rec = None
        return ops

    def replay(ops, n):
        k = 0
        while ops and k < n:
            P.op(*ops.pop(0))
            k += 1

    def s5_mixer(l):
        j = l // 2
        SALL = BIG[:, 0:NSLOT * 64].rearrange("p (s r q) -> p s r q", r=2, q=32)
        SBUFS = merge("S", *BIGb[0:17])
        VF0b = merge("VF0", *BIGb[17:22])
        for bb in BIGb:
            bb.w, bb.r = {}, {}
        S5Cb, S5Kb, S5Tb = lamctx[l]
        VF0 = BIG[:, 17 * NT:17 * NT + 9 * 512].rearrange("p (n d r x) -> p n d r x", n=9, d=2, r=2)
        VF = [VF0, VF1]
        VFb0 = [VF0b, S5Tb]
        VFb = [[[merge("VFnd", VFb0[s_]) for _d in range(2)] for _n in range(T + 1)] for s_ in range(2)]
        VFall = [[b_ for n_ in VFb[s_] for b_ in n_] for s_ in range(2)]
        KF = [LNT[:, 0:2, :].rearrange("p a (k x) -> p (a k) x", x=128), LNT[:, 2:4, :].rearrange("p a (k x) -> p (a k) x", x=128)]
        KFb0 = [merge("KF0", LNTb[0], LNTb[1]), merge("KF1", LNTb[2], LNTb[3])]
        KFall = [[merge("KFi", KFb0[s_]) for _i in range(15)] for s_ in range(2)]
        UALL = TS[:].bitcast(BF16).rearrange("p a (h n) -> p (a h) n", n=NT)

        def Uap(fc):
            return UALL[:, fc, :], TSb[fc // 2]
        for b in range(2):
            wap, wb = w_get()
            w3 = wap.rearrange("p (k n) -> p k n", n=512)
            for oc in range(4):
                fc = b * 4 + oc
                pt, pbuf = PSUMS[fc % 2]
                proj_chunk(pt, pbuf, w3, wb, oc, H, Hb, 8)
                ua, ub = Uap(fc)
                P.op("act", lambda e, pt=pt, ua=ua: e.activation(out=ua, in_=pt[:], func=AF.Copy), reads=[pbuf], writes=[ub])
        if DBG and l == 0:
            for fc in range(8):
                dbg_bf(o_dU[fc * 128:(fc + 1) * 128, :], UALL[:, fc, :], [TSb[fc // 2]])
        replay(bgs.get(l, []), 10 ** 9)
        WF = [H[:, 0:4, :].rearrange("p k n -> p (k n)").rearrange("p (n d r x) -> p n d r x", n=8, d=2, r=2),
              H[:, 4:8, :].rearrange("p k n -> p (k n)").rearrange("p (n d r x) -> p n d r x", n=8, d=2, r=2)]
        WFb0 = [merge("WF0", *Hb[0:4]), merge("WF1", *Hb[4:8])]
        WFb = [[[merge("WFnd", WFb0[s_]) for _d in range(2)] for _n in range(T)] for s_ in range(2)]
        WFall = [[b_ for n_ in WFb[s_] for b_ in n_] for s_ in range(2)]
        if S5STOP == 1:
            raise _Stop()
        P.op("dve", lambda e: e.tensor_copy(out=SALL[:, 0], in_=H0[:, j, 0]), reads=[H0b], writes=[SBUFS])
        P.op("dve", lambda e: e.tensor_copy(out=SALL[:, 2 * C + 1], in_=H0[:, j, 1]), reads=[H0b], writes=[SBUFS])
        PSA = [PA, PB, PC, PD]
        PSAb = [PAb, PBb, PCb, PDb]
        PSAh = [[PSA[q][:, 0:512], PSA[q][:, 512:1024]] for q in range(4)]
        PSAhb = [[merge("PSAh", PSAb[q]), merge("PSAh", PSAb[q])] for q in range(4)]
        ENG = ["dve", "dve"]

        def recur_ops(eng, d, n, views, lr, li, dst, dstbuf, lbufs):
            pi, po = (n - 1) % 2, n % 2
            a_, x_ = views
            sin_ = WST[:, pi, d].rearrange("p r (a x) -> p r a x", a=a_)
            ta4 = WTA[:, d, 0].rearrange("p r (a x) -> p r a x", a=a_)
            tb4 = WTA[:, d, 1].rearrange("p r (a x) -> p r a x", a=a_)
            return [
                lambda: P.op(eng, lambda e: e.tensor_tensor(out=ta4, in0=sin_, in1=lr, op=ALU.mult), reads=[WSTb[pi][d]] + lbufs, writes=[WTAb[d][0]]),
                lambda: P.op(eng, lambda e: e.tensor_tensor(out=tb4, in0=sin_, in1=li, op=ALU.mult), reads=[WSTb[pi][d]] + lbufs, writes=[WTAb[d][1]]),
                lambda: P.op(eng, lambda e: e.tensor_tensor(out=WST[:, po, d, 0, :], in0=WTA[:, d, 0, 0, :], in1=WTA[:, d, 1, 1, :], op=ALU.subtract),
                             reads=[WTAb[d][0], WTAb[d][1]], writes=[WSTb[po][d]]),
                lambda: P.op(eng, lambda e: e.tensor_tensor(out=WST[:, po, d, 1, :], in0=WTA[:, d, 1, 0, :], in1=WTA[:, d, 0, 1, :], op=ALU.add),
                             reads=[WTAb[d][0], WTAb[d][1]], writes=[WSTb[po][d]]),
                lambda: P.op("act", lambda e: e.activation(out=dst[:, n, d], in_=WST[:, po, d], func=AF.Copy), reads=[WSTb[po][d]], writes=[dstbuf[n][d]]),
            ]

        pend = []
        sev = []
        SBUFS0 = merge("S0", SBUFS)
        for fc in range(8):
            s = fc % 2
            sp_load(LDW[:, s, :], d_s5w0[j, fc], LDWb[s])
            L4 = LDW[:, s, :].rearrange("p (d r x) -> p d r x", d=2, r=2)
            chains = []
            for d in range(2):
                eng = ENG[d]
                lr = S5C[:, 0, d, fc, :].unsqueeze(1).unsqueeze(1).broadcast_to([128, 2, 2, 64])
                li = S5C[:, 1, d, fc, :].unsqueeze(1).unsqueeze(1).broadcast_to([128, 2, 2, 64])
                kr = S5K[:, 0, d, fc, :].unsqueeze(1).broadcast_to([128, 2, 64])
                ki = S5K[:, 1, d, fc, :].unsqueeze(1).broadcast_to([128, 2, 64])
                v3 = lambda ap: ap.rearrange("p (a x) -> p a x", a=2)
                ch = cmul_ops(eng, v3(WST[:, 0, d, 0, :]), v3(WST[:, 0, d, 1, :]), v3(L4[:, d, 0, :]), v3(L4[:, d, 1, :]), kr, ki,
                              v3(WTA[:, d, 0, 0, :]), v3(WTA[:, d, 1, 0, :]), [LDWb[s], S5Kb, WSTb[0][d]], [WSTb[0][d]], [WTAb[d][0], WTAb[d][1]])
                ch.append(lambda d=d, s=s: P.op("act", lambda e: e.activation(out=WF[s][:, 0, d], in_=WST[:, 0, d], func=AF.Copy), reads=[WSTb[0][d]], writes=[WFb[s][0][d]]))
                for n in range(1, T):
                    ch += recur_ops(eng, d, n, (2, 64), lr, li, WF[s], WFb[s], [S5Cb])
                chains.append(ch)
            interleave(chains + [pend])
            ua, ub = Uap(fc)
            u3 = ua.rearrange("p (c t) -> p c t", t=T)
            hf = fc % 2

            def mmA(e, s=s, u3=u3, hf=hf):
                last = None
                for d in range(2):
                    for ri in range(2):
                        for jj in range(T):
                            n = (T - 1 - jj) if d == 0 else jj
                            for q in range(4):
                                last = e.matmul(PSAh[q][hf][:, (d * 2 + ri) * 128:(d * 2 + ri + 1) * 128], lhsT=WF[s][32 * q:32 * q + 32, n, d, ri, :],
                                                rhs=u3[32 * q:32 * q + 32, :, jj], start=(jj == 0), stop=(jj == T - 1), tile_position=(32 * q, 0))
                return last
            P.op("pe", mmA, reads=WFall[s] + [ub], writes=[PSAhb[q][hf] for q in range(4)])
            pend = []
            for q in range(4):
                qq = fc * 4 + q
                for ri in range(2):
                    src = PSAh[q][hf].rearrange("p (d r c) -> p d r c", d=2, r=2)[:, :, ri, :]
                    dst = SALL[:, 1:2 * C + 1, ri, qq].rearrange("p (d c) -> p d c", d=2)
                    eb = merge("SEV", SBUFS0)
                    sev.append(eb)
                    pend.append(lambda src=src, dst=dst, q=q, hf=hf, eb=eb: P.op("act", lambda e: e.activation(out=dst, in_=src, func=AF.Copy), reads=[PSAhb[q][hf]], writes=[eb]))
        for t_ in pend:
            t_()
        m_ = merge("S", SBUFS, *sev)
        SBUFS.w, SBUFS.r = m_.w, m_.r
        for q in range(4):
            m_ = merge("PS", PSAhb[q][0], PSAhb[q][1])
            PSAb[q].w, PSAb[q].r = m_.w, m_.r
        for s_ in range(2):
            m_ = merge("H", *WFall[s_])
            for i in range(4):
                Hb[4 * s_ + i].w, Hb[4 * s_ + i].r = dict(m_.w), dict(m_.r)
        if S5STOP == 2:
            raise _Stop()
        LTr_b = S5LT[:, 0].unsqueeze(2).broadcast_to([128, 2, 2, 32])
        LTi = S5LT[:, 1]
        P.op("dve", lambda e: e.tensor_copy(out=RST[:, 1], in_=H0[:, j]), reads=[H0b], writes=[RSTb[1]])
        modgen = compute_mod_gen(l + 1) if l + 1 < DEPTH else iter(())
        SIN = merge("SIN", SBUFS)
        SOUT = merge("SOUT", SBUFS)
        for i in range(C):
            if i % 10 == 5:
                next(modgen, None)
            pp, pc = (i + 1) % 2, i % 2
            a0 = 1 + i
            stp = 2 * C - 1 - 2 * i
            sl = slice(a0, a0 + stp + 1, stp)
            if i > 0 and i % 32 == 0:
                P.op("dve", lambda e, pp=pp: e.tensor_scalar(out=RST[:, pp], in0=RST[:, pp], scalar1=KEEP[:, 0:1], scalar2=None, op0=ALU.mult),
                     reads=[RSTb[pp], KEEPb], writes=[RSTb[pp]])
            P.op("dve", lambda e, pp=pp: e.tensor_tensor(out=RTM[:, 0], in0=RST[:, pp], in1=LTr_b, op=ALU.mult), reads=[RSTb[pp], S5LTb], writes=[RTMb[0]])
            P.op("dve", lambda e, pp=pp: e.scalar_tensor_tensor(out=RTM[:, 1, :, 0, :], in0=RST[:, pp, :, 1, :], scalar=-1.0, in1=LTi, op0=ALU.mult, op1=ALU.mult),
                 reads=[RSTb[pp], S5LTb], writes=[RTMb[1]])
            P.op("dve", lambda e, pp=pp: e.tensor_tensor(out=RTM[:, 1, :, 1, :], in0=RST[:, pp, :, 0, :], in1=LTi, op=ALU.mult), reads=[RSTb[pp], S5LTb], writes=[RTMb[2]])
            P.op("dve", lambda e: e.tensor_tensor(out=RTM[:, 0], in0=RTM[:, 0], in1=RTM[:, 1], op=ALU.add), reads=[RTMb[0], RTMb[1], RTMb[2]], writes=[RTMb[0]])
            P.op("dve", lambda e, pc=pc, sl=sl: e.tensor_tensor(out=RST[:, pc], in0=RTM[:, 0], in1=SALL[:, sl], op=ALU.add), reads=[RTMb[0], SIN], writes=[RSTb[pc]])
            P.op("act", lambda e, pc=pc, sl=sl: e.activation(out=SALL[:, sl], in_=RST[:, pc], func=AF.Copy), reads=[RSTb[pc]], writes=[SOUT])
            if i % 32 == 31:
                k = i // 32
                P.op("act", lambda e, pc=pc, k=k: e.activation(out=STG[:, k], in_=RST[:, pc], func=AF.Copy), reads=[RSTb[pc]], writes=[STGb])
        P.op("sp", lambda e: e.dma_start(out=o_st[j], in_=STG[:].rearrange("p k d r q -> p (k d r q)")), reads=[STGb], dma=True)
        for _ in modgen:
            pass
        m_ = merge("S", SIN, SOUT)
        SBUFS.w, SBUFS.r = m_.w, m_.r
        for s0 in (32, C + 1 + 32):
            P.op("dve", lambda e, s0=s0: e.tensor_scalar(out=SALL[:, s0:s0 + 65:32], in0=SALL[:, s0:s0 + 65:32], scalar1=KEEP[:, 0:1], scalar2=None, op0=ALU.mult),
                 reads=[SBUFS, KEEPb], writes=[SBUFS])
        if S5STOP == 3:
            raise _Stop()
        Z = H
        kcnt = 0
        pend = []
        KPSb = [merge("KPS", PSUMS[2 + b_ // 2][1]) for b_ in range(4)]
        for fc in range(8):
            s = fc % 2
            sp_load(LDV[:, s, :], d_s5v0[j, fc], LDVb[s])
            sp_load(LDN[:, s, :], d_s5bn[j, fc], LDNb[s])
            V4 = LDV[:, s, :].rearrange("p (d r x) -> p d r x", d=2, r=2)
            N4 = LDN[:, s, :].rearrange("p (d r x) -> p d r x", d=2, r=2)
            q0 = fc * 4
            chains = []
            for d in range(2):
                eng = ENG[d]
                krb = S5KB[:, 0, d, q0:q0 + 4].unsqueeze(2).broadcast_to([128, 4, 32])
                kib = S5KB[:, 1, d, q0:q0 + 4].unsqueeze(2).broadcast_to([128, 4, 32])
                v4 = lambda ap: ap.rearrange("p (a x) -> p a x", a=4)
                ch = cmul_ops(eng, v4(WST[:, 0, d, 0, :]), v4(WST[:, 0, d, 1, :]), v4(N4[:, d, 0, :]), v4(N4[:, d, 1, :]), krb, kib,
                              v4(WTA[:, d, 0, 0, :]), v4(WTA[:, d, 1, 0, :]), [LDNb[s], S5KBb, WSTb[0][d]], [WSTb[0][d]], [WTAb[d][0], WTAb[d][1]])
                ch.append(lambda d=d, s=s: P.op("act", lambda e: e.activation(out=BNB[:, s, d], in_=WST[:, 0, d], func=AF.Copy), reads=[WSTb[0][d]], writes=[BNBb[s]]))
                ch.append(lambda d=d, eng=eng, V4=V4, s=s: P.op(eng, lambda e: e.tensor_copy(out=WST[:, 0, d, 0, :], in_=V4[:, d, 0, :]), reads=[LDVb[s]], writes=[WSTb[0][d]]))
                ch.append(lambda d=d, eng=eng, V4=V4, s=s: P.op(eng, lambda e: e.tensor_scalar(out=WST[:, 0, d, 1, :], in0=V4[:, d, 1, :], scalar1=-1.0, scalar2=None, op0=ALU.mult), reads=[LDVb[s]], writes=[WSTb[0][d]]))
                ch.append(lambda d=d, s=s: P.op("act", lambda e: e.activation(out=VF[s][:, 0, d], in_=WST[:, 0, d], func=AF.Copy), reads=[WSTb[0][d]], writes=[VFb[s][0][d]]))
                lrb = S5B[:, 0, d, q0:q0 + 4].unsqueeze(1).unsqueeze(3).broadcast_to([128, 2, 4, 32])
                lib = S5LN[:, d, q0:q0 + 4].unsqueeze(1).unsqueeze(3).broadcast_to([128, 2, 4, 32])
                for n in range(1, T + 1):
                    ch += recur_ops(eng, d, n, (4, 32), lrb, lib, VF[s], VFb[s], [S5Bb, S5LNb])
                chains.append(ch)
            interleave(chains + [pend])
            pend = []
            klist = [(dl, d) for dl in range(T) for d in range(2) if not (dl == 0 and d == 1)]
            for g0 in range(0, len(klist), 4):
                grp = klist[g0:g0 + 4]
                bank = kcnt % 4
                kcnt += 1
                pk = PSUMS[2 + bank // 2][0]
                pkb = KPSb[bank]
                cb = (bank % 2) * 512

                def mmK(e, grp=grp, pk=pk, cb=cb, s=s):
                    last = None
                    for gi, (dl, d) in enumerate(grp):
                        col = cb + gi * 128
                        dirs = (0, 1) if dl == 0 else (d,)
                        nmm = len(dirs) * 2
                        i_ = 0
                        for dd in dirs:
                            for ri in range(2):
                                last = e.matmul(pk[:, col:col + 128], lhsT=BNB[:, s, dd, ri, :], rhs=VF[s][:, dl, dd, ri, :], start=(i_ == 0), stop=(i_ == nmm - 1))
                                i_ += 1
                    return last
                P.op("pe", mmK, reads=[BNBb[s]] + VFall[s], writes=[pkb])
                for gi, (dl, d) in enumerate(grp):
                    col = cb + gi * 128
                    idx = 7 + dl if d == 0 else 7 - dl
                    if dl == 0:
                        pend.append(lambda col=col, pk=pk, pkb=pkb: P.op("dve", lambda e: e.tensor_tensor(out=KTMP[:], in0=pk[:, col:col + 128], in1=BDMASK, op=ALU.mult), reads=[CONSTb, pkb], writes=[KTMPb]))
                        pend.append(lambda fc=fc, s=s: P.op("dve", lambda e: e.scalar_tensor_tensor(out=KF[s][:, 7, :], in0=IDENT_F, scalar=S5D[:, j, fc:fc + 1], in1=KTMP[:], op0=ALU.mult, op1=ALU.add),
                                                           reads=[CONSTb, S5Db, KTMPb], writes=[KFall[s][7]]))
                    else:
                        pend.append(lambda col=col, idx=idx, pk=pk, s=s, pkb=pkb: P.op("dve", lambda e: e.tensor_tensor(out=KF[s][:, idx, :], in0=pk[:, col:col + 128], in1=BDMASK, op=ALU.mult),
                                                                                    reads=[CONSTb, pkb], writes=[KFall[s][idx]]))
            ua, ub = Uap(fc)
            u3 = ua.rearrange("p (c t) -> p c t", t=T)
            pt, pbuf = PSUMS[fc % 2]

            def mmC(e, fc=fc, pt=pt, u3=u3, s=s):
                last = None
                for jj in range(T):
                    o = pt[:, jj * 128:(jj + 1) * 128]
                    for j2 in range(T):
                        last = e.matmul(o, lhsT=KF[s][:, 7 + jj - j2, :], rhs=u3[:, :, j2], start=(j2 == 0), stop=False)
                    for d in range(2):
                        n = jj + 1 if d == 0 else T - jj
                        s0 = 0 if d == 0 else C + 2
                        for ri in range(2):
                            lastq = (d == 1 and ri == 1)
                            for q in range(4):
                                qq = fc * 4 + q
                                oq = pt[32 * q:32 * q + 32, jj * 128:(jj + 1) * 128]
                                last = e.matmul(oq, lhsT=VF[s][:, n, d, ri, 32 * q:32 * q + 32], rhs=SALL[:, s0:s0 + C, ri, qq], start=False, stop=lastq, tile_position=(0, 32 * q))
                return last
            def fin(mmC=mmC, s=s, ub=ub, pbuf=pbuf, fc=fc, pt=pt):
                P.op("pe", mmC, reads=KFall[s] + VFall[s] + [ub, SBUFS], writes=[pbuf])
                P.op("act", lambda e: e.activation(out=Z[:, fc, :].rearrange("p (c t) -> p t c", t=T), in_=pt[:].rearrange("p (t c) -> p t c", t=T),
                                                   func=AF.Gelu_apprx_tanh), reads=[pbuf], writes=[Hb[fc]])
            pend.append(fin)
        for t_ in pend:
            t_()
        if DBG and l == 0:
            for fc in range(8):
                dbg_bf(o_dZ[fc * 128:(fc + 1) * 128, :], H[:, fc, :], [Hb[fc]])
        if S5STOP == 4:
            raise _Stop()
        for t_ in range(2):
            m_ = merge("PS", KPSb[2 * t_], KPSb[2 * t_ + 1])
            PSUMS[2 + t_][1].w, PSUMS[2 + t_][1].r = m_.w, m_.r
        for bb in BIGb[0:17]:
            bb.w, bb.r = dict(SBUFS.w), dict(SBUFS.r)
        m_ = merge("VF0", *VFall[0])
        for bb in BIGb[17:22]:
            bb.w, bb.r = dict(m_.w), dict(m_.r)
        m_ = merge("VF1", *VFall[1])
        S5Tb.w, S5Tb.r = m_.w, m_.r
        for s_ in range(2):
            m_ = merge("KF", *KFall[s_])
            for i in range(2):
                LNTb[2 * s_ + i].w, LNTb[2 * s_ + i].r = dict(m_.w), dict(m_.r)
        G3 = BIG[:, 0:8 * NT].rearrange("p (k n) -> p k n", n=NT)
        for b in range(4):
            wap, wb = w_get()
            w3 = wap.rearrange("p (k n) -> p k n", n=512)
            for jj in range(2):
                jc = 2 * b + jj
                pv_, pvb = PSUMS[(jj * 2) % 4]
                pg, pgb = PSUMS[(jj * 2 + 1) % 4]
                proj_chunk(pv_, pvb, w3, wb, jj * 2, H, Hb, 8)
                proj_chunk(pg, pgb, w3, wb, jj * 2 + 1, H, Hb, 8)
                t = jj
                P.op("act", lambda e, pg=pg, t=t: e.activation(out=TS[:, t, :], in_=pg[:], func=AF.Sigmoid), reads=[pgb], writes=[TSb[t]])
                P.op("dve", lambda e, pv_=pv_, t=t, jc=jc: e.tensor_tensor(out=G3[:, jc, :], in0=pv_[:], in1=TS[:, t, :], op=ALU.mult),
                     reads=[pvb, TSb[t]], writes=[BIGb[jc]])
        outproj_ln(l, 0, G3, BIGb[0:8], 8, 2, 512, (l, 1))
        if DBG and l == 0:
            dbg_x(o_dXA)


    compute_mod(0)
    bgs[0] = record_lambda(0)
    modulate(0, 0)
    try:
        for l in range(DEPTH):
            if l >= STAGE:
                break
            if l % 2 == 0:
                s5_mixer(l)
            else:
                attention(l)
            bg = []
            if l + 1 < DEPTH and l % 2 == 1:
                compute_mod(l + 1)
                bg = record_lambda(l + 1)
            ffn(l, bg)
    except _Stop:
        pass
    for c in range(8):
        P.op("sp", lambda e, c=c: e.dma_start(out=o_yT[c * 128:(c + 1) * 128, :], in_=X[:, c, :]), reads=[Xb[c]], dma=True)
    P.finish()
    return nc


_CACHE = {}


def kernel(**inp):
    inp = {k: np.asarray(v) for k, v in inp.items()}
    f32 = np.float32
    wall = build_wall(inp)
    nblk = wall.shape[0]
    s5h = s5_host_layout(inp)
    cos, sin = rope_tables()
    rope_s = np.ascontiguousarray(np.stack([cos, sin], 1)).astype(f32)
    rope_p = np.ascontiguousarray(np.stack([np.ones_like(cos), np.zeros_like(sin)], 1)).astype(f32)
    consts = const_tables()
    bmodT = np.ascontiguousarray(inp["b_mod"].reshape(DEPTH, 48, 128).transpose(2, 0, 1).reshape(128, DEPTH * 48)).astype(f32)
    lng = inp["ln_g"].reshape(DEPTH * 2 * 8, 128).T
    lnb = inp["ln_b"].reshape(DEPTH * 2 * 8, 128).T
    lnT = np.ascontiguousarray(np.concatenate([lng, lnb], 1)).astype(f32)
    gain = np.ascontiguousarray(np.stack([inp["q_norm_g"][0], inp["q_norm_g"][1], inp["k_norm_g"][0], inp["k_norm_g"][1]], 1)).astype(f32)
    mask_s = np.zeros((128, 48), f32)
    mask_p = np.full((12, 4), -30000.0, f32)
    for kt in range(8):
        mask_p[kt, kt // 2] = 0.0
    mask_p = np.ascontiguousarray(np.broadcast_to(mask_p.reshape(1, 48), (128, 48))).astype(f32)
    in_maps = []
    for core in range(8):
        m = dict(wall=wall, bmodT=bmodT, lnT=lnT, consts=consts, gain=gain, **s5h)
        if core < 4:
            b = core
            m["xT"] = np.ascontiguousarray(inp["x_sample"][b].T)
            cvec = inp["c"][b]
            m["rope"] = rope_s
            m["maskb"] = mask_s
            m["keep"] = np.ones((128, 1), f32)
            m["ckT"] = np.ascontiguousarray(inp["cache_k"][b].transpose(0, 3, 2, 1))
            m["cv"] = np.ascontiguousarray(inp["cache_v"][b].reshape(2, 4, 128, 256).transpose(0, 2, 1, 3))
            m["h0"] = h0_layout(inp["state_s5"][b])
        else:
            s0 = (core - 4) * 4
            m["xT"] = np.ascontiguousarray(inp["x_prompt"][s0:s0 + 4].reshape(NT, D).T)
            cvec = inp["c_ctx"]
            m["rope"] = rope_p
            m["maskb"] = mask_p
            m["keep"] = np.zeros((128, 1), f32)
            m["ckT"] = np.zeros((2, 128, 2, 512), f32)
            m["cv"] = np.zeros((2, 128, 4, 256), f32)
            m["h0"] = np.zeros((128, 2, 2, 2, 32), f32)
        m["cT"] = np.ascontiguousarray(cvec.reshape(8, 128).T).astype(f32)
        in_maps.append({k: np.ascontiguousarray(v, dtype=f32) for k, v in m.items()})
    if "nc" not in _CACHE:
        _CACHE["nc"] = build_program(nblk)
    nc = _CACHE["nc"]
    _CACHE.pop("nc")
    res = run_bass_kernel_spmd(nc, in_maps, core_ids=list(range(8)))
    R = res.results
    if DBG:
        _CACHE["dbg"] = {"c%d_%s" % (ci, k): R[ci][k] for ci in (0, 4) for k in R[ci] if k.startswith("d")}
    y_sample = np.stack([R[b]["yT"].T for b in range(4)], 0).astype(f32)
    y_prompt = np.concatenate([R[4 + i]["yT"].T.reshape(4, 256, D) for i in range(4)], 0).astype(f32)
    nk = np.zeros((16, 2, 256, 2, 128), f32)
    nv = np.zeros((16, 2, 256, 2, 128), f32)
    ns = np.zeros((16, 2, 2, 2, 64, 64), f32)
    for i in range(4):
        r = R[4 + i]
        ko = r["kout"]
        vo = r["vout"]
        so = r["stout"].reshape(2, 2, 64, 4, 2, 2, 32)
        for s in range(4):
            bidx = i * 4 + s
            nk[bidx] = ko[:, :, :, s * 256:(s + 1) * 256].transpose(0, 3, 1, 2)
            nv[bidx] = vo[:, s * 256:(s + 1) * 256, :].reshape(2, 256, 2, 128)
            for d in range(2):
                k = s if d == 0 else 3 - s
                blk = so[:, :, :, k, d, :, :]
                ns[bidx, :, d] = blk.transpose(0, 3, 4, 1, 2).reshape(2, 2, 64, 64)
    return (y_prompt, y_sample, nk, nv, ns)
```

```python
import os
import numpy as np
import concourse.bass as bass
import concourse.mybir as mybir
from concourse.bass_utils import run_bass_kernel_spmd

F32 = mybir.dt.float32
BF16 = mybir.dt.bfloat16
AF = mybir.ActivationFunctionType
ALU = mybir.AluOpType

D = 1024
NT = 1024
DEPTH = 4
DFF = 2816
KFF = 22
T = 8
C = NT // T
NSLOT = 2 * (C + 1)
ALPHA = (2.0 * DEPTH) ** 0.25
LN_EPS = 1e-6
RMS_EPS = 1e-6
ATT_SCALE = 128 ** -0.5
WCOLS = 4096
NWSLOT = 3
EPOCH = 16000
NDMA = 8
MAGIC = 12582912.0
STAGE = int(os.environ.get("K_STAGE", "99"))
DBG = int(os.environ.get("K_DBG", "0"))
S5STOP = int(os.environ.get("K_S5STOP", "0"))
KVAR = int(os.environ.get("K_VAR", "0"))


class _Stop(Exception):
    pass


class Buf:
    __slots__ = ("w", "r", "name")

    def __init__(self, name=""):
        self.w = {}
        self.r = {}
        self.name = name


def merge(name, *olds):
    b = Buf(name)
    for o in olds:
        for k, v in list(o.w.items()) + list(o.r.items()):
            b.r[k] = max(b.r.get(k, 0), v)
            b.w[k] = max(b.w.get(k, 0), v)
    return b


class Prog:
    def __init__(self, nc):
        self.nc = nc
        self.eng = {"pe": nc.tensor, "act": nc.scalar, "dve": nc.vector, "pool": nc.gpsimd, "sp": nc.sync}
        self.cnt = {}
        self.semlist = {}
        self.known = {e: {} for e in self.eng}
        self.allsems = []
        self.rec = None
        for k in ["pe", "act", "dve", "pool"]:
            self.cnt[k] = 0
            self.semlist[k] = []
        for pre in ("q", "g"):
            for i in range(NDMA):
                k = "%s%d" % (pre, i)
                self.cnt[k] = 0
                self.semlist[k] = [self._newsem(k)]
        self.dma_rr = {"q": 0, "g": 0}

    def _newsem(self, name):
        cm = self.nc.semaphore("s_%s_%d" % (name, len(self.allsems)))
        s = cm.__enter__()
        self.allsems.append(s)
        return s

    def _semval(self, k, v):
        if k[0] in "qg":
            return self.semlist[k][0], v
        idx = (v - 1) // EPOCH
        while len(self.semlist[k]) <= idx:
            self.semlist[k].append(self._newsem(k))
        return self.semlist[k][idx], v - idx * EPOCH

    def op(self, e, fn, reads=(), writes=(), dma=False):
        if self.rec is not None:
            self.rec.append((e, fn, tuple(reads), tuple(writes), dma))
            return 0
        deps = {}
        for b in reads:
            for k, v in b.w.items():
                if v > deps.get(k, 0):
                    deps[k] = v
        for b in writes:
            for k, v in b.w.items():
                if v > deps.get(k, 0):
                    deps[k] = v
            for k, v in b.r.items():
                if v > deps.get(k, 0):
                    deps[k] = v
        kn = self.known[e]
        for k, v in deps.items():
            if k == e and e == "pe":
                continue
            if kn.get(k, 0) >= v:
                continue
            s, sv = self._semval(k, v)
            self.eng[e].wait_ge(s, sv)
            kn[k] = v
        inst = fn(self.eng[e])
        if dma:
            pre = "g" if e == "pool" else "q"
            key = "%s%d" % (pre, self.dma_rr[pre])
            self.dma_rr[pre] = (self.dma_rr[pre] + 1) % NDMA
            self.cnt[key] += 16
            val = self.cnt[key]
            inst.then_inc(self.semlist[key][0], 16)
        else:
            key = e
            self.cnt[e] += 1
            val = self.cnt[e]
            s, sv = self._semval(e, val)
            inst.then_inc(s, 1)
        for b in reads:
            if val > b.r.get(key, 0):
                b.r[key] = val
        for b in writes:
            b.w = {key: val}
            b.r = {}
        return val

    def finish(self):
        sp = self.eng["sp"]
        for k, v in self.cnt.items():
            if v > 0:
                s, sv = self._semval(k, v)
                sp.wait_ge(s, sv)
        for s in self.allsems:
            sp.sem_clear(s)


def _blk(W, cols):
    kin = W.shape[0]
    kc = kin // 128
    a = W[:, cols].reshape(kc, 128, len(cols)).transpose(1, 0, 2).reshape(128, kc * len(cols))
    out = np.zeros((128, WCOLS), np.float32)
    out[:, :a.shape[1]] = a
    return out


def _r(a, n):
    return np.arange(a, a + n)


def mod_blocks(w_mod_l):
    return [_blk(w_mod_l, _r(b * 512, 512)) for b in range(12)]


def ffn_blocks(w_in, w_out):
    bl = []
    for b in range(11):
        j0, j1 = 2 * b, 2 * b + 1
        cols = np.concatenate([_r(j0 * 128, 128), _r(DFF + j0 * 128, 128), _r(j1 * 128, 128), _r(DFF + j1 * 128, 128)])
        bl.append(_blk(w_in, cols))
    for c in range(8):
        bl.append(_blk(w_out, _r(c * 128, 128)))
    return bl


def s5_blocks(w_in, w_glu, w_out):
    bl = [_blk(w_in, _r(b * 512, 512)) for b in range(2)]
    for b in range(4):
        j0, j1 = 2 * b, 2 * b + 1
        cols = np.concatenate([_r(j0 * 128, 128), _r(D + j0 * 128, 128), _r(j1 * 128, 128), _r(D + j1 * 128, 128)])
        bl.append(_blk(w_glu, cols))
    bl += [_blk(w_out, _r(b * 512, 512)) for b in range(2)]
    return bl


def attn_blocks(w_qkv, w_o):
    bl = [_blk(w_qkv, _r(b * 512, 512)) for b in range(3)]
    bl += [_blk(w_o, _r(b * 512, 512)) for b in range(2)]
    return bl


def build_wall(inp):
    bl = []
    bl += mod_blocks(inp["w_mod"][0])
    for l in range(DEPTH):
        j = l // 2
        if l % 2 == 0:
            sb_ = s5_blocks(inp["w_s5_in"][j], inp["w_s5_glu"][j], inp["w_s5_out"][j])
            bl += sb_[0:2]
            if l + 1 < DEPTH:
                bl += mod_blocks(inp["w_mod"][l + 1])
            bl += sb_[2:]
        else:
            bl += attn_blocks(inp["w_qkv"][j], inp["w_o"][j])
            if l + 1 < DEPTH:
                bl += mod_blocks(inp["w_mod"][l + 1])
        bl += ffn_blocks(inp["w_ffn_in"][l], inp["w_ffn_out"][l])
    return np.ascontiguousarray(np.stack(bl, 0))


def s5_host_layout(inp):
    out = {}
    G, P, H = 64, 64, 16
    def c_lay(a):
        a5 = a.reshape(2, 2, 8, 8, P)
        a5 = np.broadcast_to(a5[:, :, :, :, None, :], (2, 2, 8, 8, H, P))
        return np.ascontiguousarray(a5.transpose(3, 4, 0, 1, 2, 5).reshape(128, 2, 2, 8, P))
    def b_lay(a):
        a5 = a.reshape(2, 2, 32, 2, P)
        return np.ascontiguousarray(a5.transpose(3, 4, 0, 1, 2).reshape(128, 2, 2, 32))
    ld = np.broadcast_to(inp["s5_log_dt"][:, :, :, None], (2, 2, G, P))
    nat = np.stack([inp["s5_a_re"], inp["s5_a_im"], ld], 0)
    out["s5n"] = np.ascontiguousarray(nat.transpose(3, 1, 0, 2, 4))
    selc = np.zeros((128, 8, 8, 16), np.float32)
    for g in range(64):
        selc[g, g // 8, g % 8, :] = 1.0
    out["selc"] = selc.reshape(128, 8, 128)
    selb = np.zeros((128, 34), np.float32)
    for g in range(64):
        selb[g, g // 2] = 1.0
        selb[g, 32 + (g % 2)] = 1.0
    out["selb"] = selb
    b = np.stack([inp["s5_b_re"], inp["s5_b_im"]], 2)
    b8 = b.reshape(2, 2, 2, 8, 4, 2, P, H)
    w0 = np.zeros((2, 8, 4, 2, H, 2, 2, 2, P), np.float32)
    bn = np.zeros((2, 8, 2, P, 2, 2, 4, 2, H), np.float32)
    for a in range(2):
        w0[:, :, :, a, :, :, :, a, :] = b8[:, :, :, :, :, a].transpose(0, 3, 4, 6, 1, 2, 5)
        bn[:, :, a, :, :, :, :, a, :] = b8[:, :, :, :, :, a].transpose(0, 3, 5, 1, 2, 4, 6)
    out["s5w0"] = np.ascontiguousarray(w0.reshape(2, 8, 128, 512))
    out["s5bn"] = np.ascontiguousarray(bn.reshape(2, 8, 128, 512))
    c = np.stack([inp["s5_c_re"], inp["s5_c_im"]], 2)
    c8 = c.reshape(2, 2, 2, 8, 4, 2, H, P)
    v0 = np.zeros((2, 8, 2, P, 2, 2, 4, 2, H), np.float32)
    for a in range(2):
        v0[:, :, a, :, :, :, :, a, :] = c8[:, :, :, :, :, a].transpose(0, 3, 6, 1, 2, 4, 5)
    out["s5v0"] = np.ascontiguousarray(v0.reshape(2, 8, 128, 512))
    out["s5d"] = np.ascontiguousarray(inp["s5_d"].reshape(2, 8, 128).transpose(2, 0, 1))
    return out


def h0_layout(st):
    s = st.reshape(2, 2, 2, 32, 2, 64)
    return np.ascontiguousarray(s.transpose(4, 5, 0, 1, 2, 3).reshape(128, 2, 2, 2, 32))


def rope_tables():
    l = np.arange(NT)
    row = (l // 64).astype(np.float32)
    col = (l % 64).astype(np.float32)
    inv = (np.float32(10000.0) ** (-np.arange(32, dtype=np.float32) / np.float32(32))).astype(np.float32)
    ar = row[None, :] * inv[:, None]
    ac = col[None, :] * inv[:, None]
    cos = np.concatenate([np.cos(ar), np.cos(ar), np.cos(ac), np.cos(ac)], 0).astype(np.float32)
    sin = np.concatenate([np.sin(ar), np.sin(ar), np.sin(ac), np.sin(ac)], 0).astype(np.float32)
    return cos, sin


def const_tables():
    ident = np.eye(128, dtype=np.float32)
    bd = np.kron(np.eye(8, dtype=np.float32), np.ones((16, 16), np.float32))
    rot = np.zeros((128, 128), np.float32)
    for base in (0, 64):
        for i in range(32):
            rot[base + 32 + i, base + i] = -1.0
            rot[base + i, base + 32 + i] = 1.0
    return np.ascontiguousarray(np.stack([ident, bd, rot], 1))


def build_program(nblk):
    nc = bass.Bass("TRN2", target_bir_lowering=False)
    P = Prog(nc)

    def din(name, shape):
        return nc.dram_tensor(name, list(shape), F32, kind="ExternalInput").ap()

    def dout(name, shape):
        return nc.dram_tensor(name, list(shape), F32, kind="ExternalOutput").ap()

    d_xT = din("xT", [D, NT])
    d_cT = din("cT", [128, 8])
    d_wall = din("wall", [nblk, 128, WCOLS])
    d_bmod = din("bmodT", [128, DEPTH * 48])
    d_ln = din("lnT", [128, 128])
    d_rope = din("rope", [128, 2, NT])
    d_mask = din("maskb", [128, 48])
    d_keep = din("keep", [128, 1])
    d_ckT = din("ckT", [2, 128, 2, 512])
    d_cv = din("cv", [2, 128, 4, 256])
    d_gain = din("gain", [128, 4])
    d_const = din("consts", [128, 3, 128])
    d_s5w0 = din("s5w0", [2, 8, 128, 512])
    d_s5bn = din("s5bn", [2, 8, 128, 512])
    d_s5v0 = din("s5v0", [2, 8, 128, 512])
    d_s5d = din("s5d", [128, 2, 8])
    d_h0 = din("h0", [128, 2, 2, 2, 32])
    d_s5n = din("s5n", [64, 2, 3, 2, 64])
    d_selc = din("selc", [128, 8, 128])
    d_selb = din("selb", [128, 34])
    o_yT = dout("yT", [D, NT])
    o_k = dout("kout", [2, 2, 128, NT])
    o_v = dout("vout", [2, NT, 256])
    o_st = dout("stout", [2, 128, 512])
    if DBG:
        o_dU = dout("dU", [D, NT]); o_dZ = dout("dZ", [D, NT]); o_dXA = dout("dXA", [D, NT]); o_dS = None
        o_dQ = dout("dQ", [D, NT]); o_dK = dout("dK", [128, 2, 1536]); o_dO = dout("dO", [D, NT]); o_dXA1 = dout("dXA1", [D, NT])

    def dbg_bf(dst, src, bufs):
        P.op("pool", lambda e: e.dma_start(out=dst, in_=src), reads=bufs, dma=True)

    def dbg_x(dst):
        for c in range(8):
            P.op("sp", lambda e, c=c: e.dma_start(out=dst[c * 128:(c + 1) * 128, :], in_=X[:, c, :]), reads=[Xb[c]], dma=True)

    def sb(name, shape, dt=F32):
        cm = nc.sbuf_tensor(name, list(shape), dt)
        return cm.__enter__()

    def ps(name, shape, dt=F32):
        cm = nc.psum_tensor(name, list(shape), dt)
        return cm.__enter__()

    X = sb("X", [128, 8, NT])
    Xb = [Buf("X%d" % i) for i in range(8)]
    H = sb("H", [128, 8, NT], BF16)
    Hb = [Buf("H%d" % i) for i in range(8)]
    BIG = sb("BIG", [128, KFF * NT], BF16)
    BIGb = [Buf("BIG%d" % i) for i in range(KFF)]
    WR = sb("WR", [128, NWSLOT, WCOLS], BF16)
    WRb = [Buf("WR%d" % i) for i in range(NWSLOT)]
    LNT = sb("LNT", [128, 4, NT], BF16)
    LNTb = [Buf("LNT%d" % i) for i in range(4)]
    TS = sb("TS", [128, 4, NT])
    TSb = [Buf("TS%d" % i) for i in range(4)]
    MOD = sb("MOD", [128, DEPTH, 48])
    MODb = [Buf("MOD%d" % i) for i in range(DEPTH)]
    BMOD = sb("BMOD", [128, DEPTH * 48]); BMODb = Buf("BMOD")
    LNP = sb("LNP", [128, 128]); LNPb = Buf("LNP")
    CT = sb("CT", [128, 8]); CTb = Buf("CT")
    CS = sb("CS", [128, 8], BF16); CSb = Buf("CS")
    ROW = sb("ROW", [1, 512]); ROWb = Buf("ROW")
    ONE1 = sb("ONE1", [1, 1]); ONE1b = Buf("ONE1")
    MASK = sb("MASK", [128, 48]); MASKb = Buf("MASK")
    KEEP = sb("KEEP", [128, 1]); KEEPb = Buf("KEEP")
    GAIN = sb("GAIN", [128, 4]); GAINb = Buf("GAIN")
    CONST = sb("CONST", [128, 3, 128]); CONSTb = Buf("CONST")
    CB = sb("CB", [128, 3, 128], BF16); CBb = Buf("CB")
    ONES = sb("ONES", [128, 128], BF16); ONESb = Buf("ONES")
    EPSC = sb("EPSC", [128, 2]); EPSb = Buf("EPSC")
    ARENA = sb("ARENA", [128, 8192])
    ARb = {"cur": [Buf("ARENA")]}

    def arena_switch(names):
        olds = ARb["cur"]
        news = [merge(n, *olds) for n in names]
        ARb["cur"] = news
        return news
    ROPE = ARENA[:, 0:2048].rearrange("p (a n) -> p a n", a=2)
    KTB = ARENA[:, 2048:3584].bitcast(BF16).rearrange("p (a n) -> p a n", a=2)
    VTB = ARENA[:, 3584:5120].bitcast(BF16).rearrange("p (a n) -> p a n", a=12)
    PR = ARENA[:, 5120:6144].bitcast(BF16).rearrange("p (a n) -> p a n", a=4)
    S5C = ARENA[:, 0:2048].rearrange("p (t d f x) -> p t d f x", t=2, d=2, f=8)
    S5K = ARENA[:, 2048:4096].rearrange("p (t d f x) -> p t d f x", t=2, d=2, f=8)
    NATQ = ARENA[0:64, 4096:4736].rearrange("p (t x) -> p t x", t=5)
    NATT = ARENA[0:64, 4736:5504].rearrange("p (t x) -> p t x", t=6)
    XPAD = ARENA[0:64, 5504:6528].rearrange("p (t x) -> p t x", t=8)
    SELC = ARENA[:, 6528:7552].rearrange("p (f x) -> p f x", f=8)
    SELB = ARENA[:, 7552:7586]
    NATQ128 = ARENA[:, 4096:4736].rearrange("p (t x) -> p t x", t=5)
    XPAD128 = ARENA[:, 5504:6528].rearrange("p (t x) -> p t x", t=8)
    VF1 = ARENA[:, 4096:4096 + 2304].bitcast(BF16).rearrange("p (n d r x) -> p n d r x", n=9, d=2, r=2)
    S5B = sb("S5B", [128, 2, 2, 32]); S5Bb = Buf("S5B")
    S5KB = sb("S5KB", [128, 2, 2, 32]); S5KBb = Buf("S5KB")
    S5TB = sb("S5TB", [128, 2, 2, 32]); S5TBb = Buf("S5TB")
    S5LT = sb("S5LT", [128, 2, 2, 32]); S5LTb = Buf("S5LT")
    S5LN = sb("S5LN", [128, 2, 32]); S5LNb = Buf("S5LN")
    S5D = sb("S5D", [128, 2, 8]); S5Db = Buf("S5D")
    H0 = sb("H0", [128, 2, 2, 2, 32]); H0b = Buf("H0")
    SELCb = Buf("SELC")
    SELBb = Buf("SELB")
    LDW = sb("LDW", [128, 2, 512]); LDWb = [Buf("LDW0"), Buf("LDW1")]
    LDN = sb("LDN", [128, 2, 512]); LDNb = [Buf("LDN0"), Buf("LDN1")]
    LDV = LDW; LDVb = LDWb
    WST = sb("WST", [128, 2, 2, 2, 128])
    WSTb = [[Buf("WST00"), Buf("WST01")], [Buf("WST10"), Buf("WST11")]]
    WTA = sb("WTA", [128, 2, 2, 2, 128])
    WTAb = [[Buf("WTA00"), Buf("WTA01")], [Buf("WTA10"), Buf("WTA11")]]
    BNB = sb("BNB", [128, 2, 2, 2, 128], BF16); BNBb = [Buf("BNB0"), Buf("BNB1")]
    KTMP = sb("KTMP", [128, 128]); KTMPb = Buf("KTMP")
    PTMP = sb("PTMP", [128, 2, 128]); PTMPb = [Buf("PTMP0"), Buf("PTMP1")]
    STG = sb("STG", [128, 4, 2, 2, 32]); STGb = Buf("STG")
    RST = sb("RST", [128, 2, 2, 2, 32]); RSTb = [Buf("RST0"), Buf("RST1")]
    RTM = sb("RTM", [128, 2, 2, 2, 32]); RTMb = [Buf("RTM0"), Buf("RTM1a"), Buf("RTM1b")]
    VST = sb("VST", [128, 2, 256]); VSTb = [Buf("VST0"), Buf("VST1")]

    PA = ps("PA", [128, NT]); PB = ps("PB", [128, NT]); PC = ps("PC", [128, NT]); PD = ps("PD", [128, NT])
    PAb, PBb, PCb, PDb = Buf("PA"), Buf("PB"), Buf("PC"), Buf("PD")
    PSUMS = [(PA, PAb), (PB, PBb), (PC, PCb), (PD, PDb)]

    wstate = {"issued": 0, "used": 0}

    def w_issue():
        i = wstate["issued"]
        if i >= nblk:
            return
        s = i % NWSLOT
        P.op("pool", lambda e: e.dma_start(out=WR[:, s, :], in_=d_wall[i]), writes=[WRb[s]], dma=True)
        wstate["issued"] += 1

    def w_get():
        i = wstate["used"]
        wstate["used"] += 1
        while wstate["issued"] < min(nblk, i + NWSLOT):
            w_issue()
        s = i % NWSLOT
        return WR[:, s, :], WRb[s]

    def sp_load(dst, src, buf):
        P.op("sp", lambda e: e.dma_start(out=dst, in_=src), writes=[buf], dma=True)

    for i in range(NWSLOT):
        w_issue()
    sp_load(CT[:], d_cT, CTb)
    sp_load(BMOD[:], d_bmod, BMODb)
    sp_load(LNP[:], d_ln, LNPb)
    sp_load(CONST[:], d_const, CONSTb)
    sp_load(GAIN[:], d_gain, GAINb)
    sp_load(MASK[:], d_mask, MASKb)
    sp_load(KEEP[:], d_keep, KEEPb)
    for c in range(8):
        sp_load(X[:, c, :], d_xT[c * 128:(c + 1) * 128, :], Xb[c])
    sp_load(S5D[:], d_s5d, S5Db)
    sp_load(H0[:], d_h0, H0b)
    sp_load(SELC, d_selc, SELCb)
    sp_load(SELB, d_selb, SELBb)

    P.op("dve", lambda e: e.memset(ONES[:], 1.0), writes=[ONESb])
    P.op("dve", lambda e: e.memset(ONE1[:], 1.0), writes=[ONE1b])
    P.op("dve", lambda e: e.memset(EPSC[:, 0:1], LN_EPS / (ALPHA * ALPHA)), writes=[EPSb])
    P.op("dve", lambda e: e.memset(EPSC[:, 1:2], RMS_EPS), writes=[EPSb])
    P.op("dve", lambda e: e.tensor_copy(out=CB[:], in_=CONST[:]), reads=[CONSTb], writes=[CBb])
    P.op("act", lambda e: e.activation(out=CS[:], in_=CT[:], func=AF.Silu), reads=[CTb], writes=[CSb])

    IDENT_F = CONST[:, 0, :]
    BDMASK = CONST[:, 1, :]
    ROTB = CB[:, 2, :]

    def compute_mod_gen(l):
        for b in range(12):
            wap, wb = w_get()
            w3 = wap.rearrange("p (k n) -> p k n", n=512)

            def mm(e):
                last = None
                for kc in range(8):
                    last = e.matmul(PC[0:1, 0:512], lhsT=CS[:, kc:kc + 1], rhs=w3[:, kc, :], start=(kc == 0), stop=(kc == 7))
                return last
            P.op("pe", mm, reads=[wb, CSb], writes=[PCb])
            P.op("act", lambda e: e.activation(out=ROW[:], in_=PC[0:1, 0:512], func=AF.Copy), reads=[PCb], writes=[ROWb])

            def tr(e):
                last = None
                for i in range(4):
                    col = b * 4 + i
                    last = e.matmul(PD[:, col:col + 1], lhsT=ROW[0:1, i * 128:(i + 1) * 128], rhs=ONE1[0:1, 0:1], start=True, stop=True)
                return last
            P.op("pe", tr, reads=[ROWb, ONE1b], writes=[PDb])
            yield b
        P.op("dve", lambda e: e.tensor_tensor(out=MOD[:, l, :], in0=PD[:, 0:48], in1=BMOD[:, l * 48:(l + 1) * 48], op=ALU.add),
             reads=[PDb, BMODb], writes=[MODb[l]])
        for base in (8, 32):
            P.op("dve", lambda e, base=base: e.tensor_scalar(out=MOD[:, l, base:base + 8], in0=MOD[:, l, base:base + 8], scalar1=1.0, scalar2=None, op0=ALU.add),
                 reads=[MODb[l]], writes=[MODb[l]])
        for base in (16, 40):
            P.op("dve", lambda e, base=base: e.tensor_scalar(out=MOD[:, l, base:base + 8], in0=MOD[:, l, base:base + 8], scalar1=1.0 / ALPHA, scalar2=None, op0=ALU.mult),
                 reads=[MODb[l]], writes=[MODb[l]])

    def compute_mod(l):
        for _ in compute_mod_gen(l):
            pass

    def modulate(l, which):
        so = 0 if which == 0 else 24
        for c in range(8):
            P.op("dve", lambda e, c=c: e.tensor_scalar(out=H[:, c, :], in0=X[:, c, :], scalar1=MOD[:, l, so + 8 + c:so + 9 + c],
                                                      scalar2=MOD[:, l, so + c:so + c + 1], op0=ALU.mult, op1=ALU.add),
                 reads=[Xb[c], MODb[l]], writes=[Hb[c]])

    def proj_chunk(pt, pbuf, w3, wb, oc, src, srcbufs, kcn):
        def mm(e):
            last = None
            for kc in range(kcn):
                for hf in range(2):
                    last = e.matmul(pt[:, hf * 512:(hf + 1) * 512], lhsT=w3[:, kc, oc * 128:(oc + 1) * 128],
                                    rhs=src[:, kc, hf * 512:(hf + 1) * 512], start=(kc == 0), stop=(kc == kcn - 1))
            return last
        P.op("pe", mm, reads=[wb] + list(srcbufs), writes=[pbuf])

    def outproj_ln(l, which, src, srcbufs, kcn, nblocks, cols_per_blk, next_mod):
        gcol = 16 if which == 0 else 40
        lni = (l * 2 + which) * 8
        oc_global = 0
        pend_st = None
        for b in range(nblocks):
            wap, wb = w_get()
            w3 = wap[:, 0:kcn * cols_per_blk].rearrange("p (k n) -> p k n", n=cols_per_blk)
            for oc in range(cols_per_blk // 128):
                c = oc_global
                pt, pbuf = PSUMS[c % 2]
                proj_chunk(pt, pbuf, w3, wb, oc, src, srcbufs, kcn)
                P.op("dve", lambda e, c=c, pt=pt: e.scalar_tensor_tensor(out=X[:, c, :], in0=pt[:], scalar=MOD[:, l, gcol + c:gcol + c + 1],
                                                                        in1=X[:, c, :], op0=ALU.mult, op1=ALU.add),
                     reads=[pbuf, MODb[l], Xb[c]], writes=[Xb[c]])
                s = (c % 2) * 2
                P.op("act", lambda e, c=c, s=s: e.activation(out=LNT[:, s, :], in_=X[:, c, :], func=AF.Copy), reads=[Xb[c]], writes=[LNTb[s]])
                P.op("act", lambda e, c=c, s=s: e.activation(out=LNT[:, s + 1, :], in_=X[:, c, :], func=AF.Square), reads=[Xb[c]], writes=[LNTb[s + 1]])

                def st(e, c=c, s=s):
                    last = None
                    for hf in range(2):
                        e.matmul(PC[:, hf * 512:(hf + 1) * 512], lhsT=ONES[:], rhs=LNT[:, s, hf * 512:(hf + 1) * 512], start=(c == 0), stop=(c == 7))
                        last = e.matmul(PD[:, hf * 512:(hf + 1) * 512], lhsT=ONES[:], rhs=LNT[:, s + 1, hf * 512:(hf + 1) * 512], start=(c == 0), stop=(c == 7))
                    return last
                if pend_st is not None:
                    pend_st()
                pend_st = (lambda st=st, s=s: P.op("pe", st, reads=[ONESb, LNTb[s], LNTb[s + 1]], writes=[PCb, PDb]))
                oc_global += 1
        pend_st()
        P.op("act", lambda e: e.activation(out=TS[:, 0, :], in_=PC[:], func=AF.Identity, scale=1.0 / D), reads=[PCb], writes=[TSb[0]])
        P.op("act", lambda e: e.activation(out=TS[:, 1, :], in_=TS[:, 0, :], func=AF.Square), reads=[TSb[0]], writes=[TSb[1]])
        P.op("dve", lambda e: e.scalar_tensor_tensor(out=TS[:, 1, :], in0=PD[:], scalar=1.0 / D, in1=TS[:, 1, :], op0=ALU.mult, op1=ALU.subtract),
             reads=[PDb, TSb[1]], writes=[TSb[1]])
        P.op("act", lambda e: e.activation(out=TS[:, 1, :], in_=TS[:, 1, :], func=AF.Ln, bias=EPSC[:, 0:1], scale=1.0), reads=[TSb[1], EPSb], writes=[TSb[1]])
        P.op("act", lambda e: e.activation(out=TS[:, 1, :], in_=TS[:, 1, :], func=AF.Exp, scale=-0.5), reads=[TSb[1]], writes=[TSb[1]])
        for c in range(8):
            t = 2 + (c % 2)
            P.op("pool" if c % 2 == 1 else "dve", lambda e, c=c, t=t: e.tensor_tensor(out=TS[:, t, :], in0=X[:, c, :], in1=TS[:, 0, :], op=ALU.subtract),
                 reads=[Xb[c], TSb[0]], writes=[TSb[t]])
            P.op("dve", lambda e, c=c, t=t: e.tensor_tensor(out=TS[:, t, :], in0=TS[:, t, :], in1=TS[:, 1, :], op=ALU.mult),
                 reads=[TSb[t], TSb[1]], writes=[TSb[t]])
            P.op("act", lambda e, c=c, t=t: e.activation(out=X[:, c, :], in_=TS[:, t, :], func=AF.Identity,
                                                         scale=LNP[:, lni + c:lni + c + 1], bias=LNP[:, 64 + lni + c:64 + lni + c + 1]),
                 reads=[TSb[t], LNPb], writes=[Xb[c]])
        if next_mod is not None:
            modulate(*next_mod)

    def ffn(l, bg=None):
        bg = bg if bg is not None else []
        for b in range(11):
            wap, wb = w_get()
            w3 = wap.rearrange("p (k n) -> p k n", n=512)
            for jj in range(2):
                j = 2 * b + jj
                pg, pgb = PSUMS[(jj * 2) % 4]
                pu, pub = PSUMS[(jj * 2 + 1) % 4]
                proj_chunk(pg, pgb, w3, wb, jj * 2, H, Hb, 8)
                proj_chunk(pu, pub, w3, wb, jj * 2 + 1, H, Hb, 8)
                t = jj
                P.op("act", lambda e, pg=pg, t=t: e.activation(out=TS[:, t, :], in_=pg[:], func=AF.Silu), reads=[pgb], writes=[TSb[t]])
                P.op("dve", lambda e, pu=pu, t=t, j=j: e.tensor_tensor(out=BIG[:, j * NT:(j + 1) * NT], in0=pu[:], in1=TS[:, t, :], op=ALU.mult),
                     reads=[pub, TSb[t]], writes=[BIGb[j]])
                replay(bg, 14)
        replay(bg, 10 ** 9)
        BIG3 = BIG[:].rearrange("p (k n) -> p k n", n=NT)
        nm = (l + 1, 0) if l + 1 < DEPTH else None
        outproj_ln(l, 1, BIG3, BIGb, KFF, 8, 128, nm)

    def attention(l):
        j = l // 2
        Q = BIG[:, 0:8 * NT].rearrange("p (k n) -> p k n", n=NT)
        nb = arena_switch(["ROPE", "KT0", "KT1"] + ["VT%d" % i for i in range(12)] + ["PR%d" % i for i in range(4)])
        ROPEb = nb[0]
        KTBb = nb[1:3]
        VTBb = nb[3:15]
        PRb = nb[15:19]
        sp_load(ROPE, d_rope, ROPEb)
        for kv in range(2):
            P.op("pool", lambda e, kv=kv: e.dma_start(out=KTB[:, kv, NT:NT + 512], in_=d_ckT[j, :, kv, :]), writes=[KTBb[kv]], dma=True)
        for t4 in range(4):
            P.op("pool", lambda e, t4=t4: e.dma_start(out=VTB[:, 8 + t4, :], in_=d_cv[j, :, t4, :]), writes=[VTBb[8 + t4]], dma=True)
        cur = None

        def emit_proj(hc_):
            nonlocal cur
            if hc_ % 4 == 0:
                cur = w_get()
            wap_, wb_ = cur
            w3_ = wap_.rearrange("p (k n) -> p k n", n=512)
            pt_, pbuf_ = PSUMS[hc_ % 2]
            proj_chunk(pt_, pbuf_, w3_, wb_, hc_ % 4, H, Hb, 8)
        emit_proj(0)
        for hc in range(10):
            pt, pbuf = PSUMS[hc % 2]
            if hc + 1 < 10:
                emit_proj(hc + 1)
            isk = hc >= 8
            gcol = (2 + j) if isk else j
            P.op("act", lambda e, pt=pt: e.activation(out=TS[:, 2, :], in_=pt[:], func=AF.Copy), reads=[pbuf], writes=[TSb[2]])
            P.op("act", lambda e, pt=pt: e.activation(out=LNT[:, 0, :], in_=pt[:], func=AF.Square), reads=[pbuf], writes=[LNTb[0]])

            def st(e):
                last = None
                for hf in range(2):
                    last = e.matmul(PC[:, hf * 512:(hf + 1) * 512], lhsT=ONES[:], rhs=LNT[:, 0, hf * 512:(hf + 1) * 512], start=True, stop=True)
                return last
            P.op("pe", st, reads=[ONESb, LNTb[0]], writes=[PCb])
            P.op("act", lambda e: e.activation(out=TS[:, 3, :], in_=PC[:], func=AF.Ln, bias=EPSC[:, 1:2], scale=1.0 / 128), reads=[PCb, EPSb], writes=[TSb[3]])
            P.op("act", lambda e: e.activation(out=TS[:, 3, :], in_=TS[:, 3, :], func=AF.Exp, scale=-0.5), reads=[TSb[3]], writes=[TSb[3]])
            P.op("dve", lambda e, gcol=gcol: e.scalar_tensor_tensor(out=TS[:, 2, :], in0=TS[:, 2, :], scalar=GAIN[:, gcol:gcol + 1], in1=TS[:, 3, :],
                                                                    op0=ALU.mult, op1=ALU.mult), reads=[TSb[2], TSb[3], GAINb], writes=[TSb[2]])
            P.op("act", lambda e: e.activation(out=LNT[:, 1, :], in_=TS[:, 2, :], func=AF.Copy), reads=[TSb[2]], writes=[LNTb[1]])

            def rt(e):
                last = None
                for hf in range(2):
                    last = e.matmul(PD[:, hf * 512:(hf + 1) * 512], lhsT=ROTB, rhs=LNT[:, 1, hf * 512:(hf + 1) * 512], start=True, stop=True)
                return last
            P.op("pe", rt, reads=[CBb, LNTb[1]], writes=[PDb])
            P.op("dve", lambda e: e.tensor_tensor(out=TS[:, 0, :], in0=PD[:], in1=ROPE[:, 1, :], op=ALU.mult), reads=[PDb, ROPEb], writes=[TSb[0]])
            P.op("dve", lambda e: e.tensor_tensor(out=TS[:, 2, :], in0=TS[:, 2, :], in1=ROPE[:, 0, :], op=ALU.mult), reads=[TSb[2], ROPEb], writes=[TSb[2]])
            if not isk:
                P.op("dve", lambda e, hc=hc: e.tensor_tensor(out=Q[:, hc, :], in0=TS[:, 2, :], in1=TS[:, 0, :], op=ALU.add),
                     reads=[TSb[2], TSb[0]], writes=[BIGb[hc]])
            else:
                kv = hc - 8
                P.op("dve", lambda e: e.tensor_tensor(out=TS[:, 1, :], in0=TS[:, 2, :], in1=TS[:, 0, :], op=ALU.add), reads=[TSb[2], TSb[0]], writes=[TSb[1]])
                P.op("act", lambda e, kv=kv: e.activation(out=KTB[:, kv, 0:NT], in_=TS[:, 1, :], func=AF.Copy), reads=[TSb[1]], writes=[KTBb[kv]])
                P.op("sp", lambda e, kv=kv: e.dma_start(out=o_k[j, kv], in_=TS[:, 1, :]), reads=[TSb[1]], dma=True)
        wap, wb = cur
        w3 = wap.rearrange("p (k n) -> p k n", n=512)
        for tt in range(8):
            pt, pbuf = PSUMS[tt % 2]

            def mmv(e, tt=tt, pt=pt):
                last = None
                for kc in range(8):
                    last = e.matmul(pt[:, 0:256], lhsT=H[:, kc, tt * 128:(tt + 1) * 128], rhs=w3[:, kc, 256:512], start=(kc == 0), stop=(kc == 7))
                return last
            P.op("pe", mmv, reads=[wb] + Hb, writes=[pbuf])
            s = tt % 2
            P.op("act", lambda e, pt=pt, s=s: e.activation(out=VST[:, s, :], in_=pt[:, 0:256], func=AF.Copy), reads=[pbuf], writes=[VSTb[s]])
            P.op("dve", lambda e, tt=tt, s=s: e.tensor_copy(out=VTB[:, tt, :], in_=VST[:, s, :]), reads=[VSTb[s]], writes=[VTBb[tt]])
            P.op("sp", lambda e, tt=tt, s=s: e.dma_start(out=o_v[j, tt * 128:(tt + 1) * 128, :], in_=VST[:, s, :]), reads=[VSTb[s]], dma=True)
        if DBG and l == 1:
            for h in range(8):
                dbg_bf(o_dQ[h * 128:(h + 1) * 128, :], Q[:, h, :], [BIGb[h]])
            dbg_bf(o_dK, KTB, KTBb)
        NSB = 4
        PAh = [PA[:, 0:512], PA[:, 512:1024], PD[:, 0:512], PD[:, 512:1024]]
        PAhb = [merge("PAh", PAb), merge("PAh", PAb), merge("PDh", PDb), merge("PDh", PDb)]
        pri = 0
        PRh = [[merge("PRh", PRb[i]), merge("PRh", PRb[i])] for i in range(4)]
        for h in range(8):
            kv = h // 4
            for qh in range(2):
                qs = slice(qh * 512, (qh + 1) * 512)

                def smm(e, kt, h=h, kv=kv, qs=qs):
                    return e.matmul(PAh[kt % NSB], lhsT=KTB[:, kv, kt * 128:(kt + 1) * 128], rhs=Q[:, h, qs], start=True, stop=True)
                for k0 in range(NSB - 1):
                    P.op("pe", lambda e, k0=k0: smm(e, k0), reads=[KTBb[kv], BIGb[h]], writes=[PAhb[k0]])
                for kt in range(12):
                    if kt + NSB - 1 < 12:
                        P.op("pe", lambda e, kt=kt: smm(e, kt + NSB - 1), reads=[KTBb[kv], BIGb[h]], writes=[PAhb[(kt + NSB - 1) % NSB]])
                    pslot = pri % 4
                    pri += 1
                    for g2 in range(2):
                        qg = qh * 2 + g2
                        P.op("act", lambda e, kt=kt, g2=g2, qg=qg, pslot=pslot: e.activation(
                            out=PR[:, pslot, g2 * 256:(g2 + 1) * 256], in_=PAh[kt % NSB][:, g2 * 256:(g2 + 1) * 256], func=AF.Exp,
                            bias=MASK[:, kt * 4 + qg:kt * 4 + qg + 1], scale=ATT_SCALE), reads=[PAhb[kt % NSB], MASKb], writes=[PRh[pslot][g2]])

                    def pv(e, kt=kt, kv=kv, qs=qs, pslot=pslot):
                        e.matmul(PB[:, qs], lhsT=VTB[:, kt, kv * 128:(kv + 1) * 128], rhs=PR[:, pslot, :], start=(kt == 0), stop=(kt == 11))
                        return e.matmul(PC[:, qs], lhsT=ONES[:], rhs=PR[:, pslot, :], start=(kt == 0), stop=(kt == 11))
                    P.op("pe", pv, reads=[VTBb[kt], PRh[pslot][0], PRh[pslot][1], ONESb], writes=[PBb, PCb])
                P.op("dve", lambda e, qs=qs: e.reciprocal(out=TS[:, 0, qs], in_=PC[:, qs]), reads=[PCb], writes=[TSb[0]])
                P.op("dve", lambda e, qs=qs, h=h: e.tensor_tensor(out=H[:, h, qs], in0=PB[:, qs], in1=TS[:, 0, qs], op=ALU.mult),
                     reads=[PBb, TSb[0]], writes=[Hb[h]])
        m1 = merge("PA", PAhb[0], PAhb[1])
        PAb.w, PAb.r = m1.w, m1.r
        m1 = merge("PD", PAhb[2], PAhb[3])
        PDb.w, PDb.r = m1.w, m1.r
        if DBG and l == 1:
            for h in range(8):
                dbg_bf(o_dO[h * 128:(h + 1) * 128, :], H[:, h, :], [Hb[h]])
        outproj_ln(l, 0, H, Hb, 8, 2, 512, (l, 1))
        if DBG and l == 1:
            dbg_x(o_dXA1)

    def cmul_ops(e_name, out_r, out_i, in_r, in_i, lr, li, ta, tb, reads, writes, tbufs):
        return [
            lambda: P.op(e_name, lambda e: e.tensor_tensor(out=ta, in0=in_r, in1=lr, op=ALU.mult), reads=reads, writes=[tbufs[0]]),
            lambda: P.op(e_name, lambda e: e.tensor_tensor(out=tb, in0=in_i, in1=li, op=ALU.mult), reads=reads, writes=[tbufs[1]]),
            lambda: P.op(e_name, lambda e: e.tensor_tensor(out=ta, in0=ta, in1=tb, op=ALU.subtract), reads=[tbufs[0], tbufs[1]], writes=[tbufs[0]]),
            lambda: P.op(e_name, lambda e: e.tensor_tensor(out=tb, in0=in_r, in1=li, op=ALU.mult), reads=reads, writes=[tbufs[1]]),
            lambda: P.op(e_name, lambda e: e.tensor_tensor(out=out_i, in0=in_i, in1=lr, op=ALU.mult), reads=reads + [tbufs[0]], writes=writes),
            lambda: P.op(e_name, lambda e: e.tensor_tensor(out=out_i, in0=out_i, in1=tb, op=ALU.add), reads=[tbufs[1]] + writes, writes=writes),
            lambda: P.op(e_name, lambda e: e.tensor_copy(out=out_r, in_=ta), reads=[tbufs[0]], writes=writes),
        ]

    def cmul(*a):
        for t in cmul_ops(*a):
            t()

    def interleave(lists):
        n = max(len(L) for L in lists)
        for i in range(n):
            for L in lists:
                if i < len(L):
                    L[i]()

    def lam_compute(A, Ab, Kt, Kb, Tm, Tb_, n_t):
        bufs = [Ab, Kb, Tb_]
        A0, A1, A2 = A[:, 0], A[:, 1], A[:, 2]
        K0, K1 = Kt[:, 0], Kt[:, 1]
        T0, T1, T2, T3, T4, T5 = (Tm[:, i] for i in range(6))

        def tt(o, a, b, op):
            P.op("dve", lambda e: e.tensor_tensor(out=o, in0=a, in1=b, op=op), reads=bufs, writes=bufs)

        def tsc(o, a, s1, s2=None, op0=ALU.mult, op1=ALU.add):
            if s2 is None:
                P.op("dve", lambda e: e.tensor_scalar(out=o, in0=a, scalar1=s1, scalar2=None, op0=op0), reads=bufs, writes=bufs)
            else:
                P.op("dve", lambda e: e.tensor_scalar(out=o, in0=a, scalar1=s1, scalar2=s2, op0=op0, op1=op1), reads=bufs, writes=bufs)

        def stt(o, a, sc, b, op0, op1):
            P.op("dve", lambda e: e.scalar_tensor_tensor(out=o, in0=a, scalar=sc, in1=b, op0=op0, op1=op1), reads=bufs, writes=bufs)

        def horner(t, y, divs, sign):
            tsc(t, y, sign / divs[-1], 1.0)
            for dv in reversed(divs[:-1]):
                tt(t, t, y, ALU.mult)
                tsc(t, t, sign / dv, 1.0)
        tsc(K1, A2, 0.125)
        horner(K0, K1, [1.0, 2.0, 3.0, 4.0, 5.0, 6.0, 7.0, 8.0, 9.0, 10.0, 11.0], 1.0)
        for _ in range(3):
            tt(K0, K0, K0, ALU.mult)
        tt(T0, K0, A0, ALU.mult)
        tt(T1, K0, A1, ALU.mult)
        horner(K0, T0, [2.0, 3.0, 4.0, 5.0, 6.0, 7.0], 1.0)
        tt(T2, K0, T0, ALU.mult)
        tsc(K1, T1, 1.0 / 16.0)
        tt(T3, K1, K1, ALU.mult)
        horner(K0, T3, [6.0, 20.0, 42.0, 72.0, 110.0, 156.0, 210.0], -1.0)
        tt(T4, K0, K1, ALU.mult)
        horner(K0, T3, [12.0, 30.0, 56.0, 90.0, 132.0, 182.0, 240.0], -1.0)
        stt(T5, T3, -0.5, K0, ALU.mult, ALU.mult)
        for _ in range(4):
            tt(K1, T4, T4, ALU.mult)
            stt(K0, T5, 1.0, T4, ALU.add, ALU.mult)
            tsc(T4, K0, 2.0)
            tsc(T5, K1, -2.0)
        stt(K0, T2, 1.0, T5, ALU.add, ALU.mult)
        tt(T0, K0, T2, ALU.add)
        stt(T1, T2, 1.0, T4, ALU.add, ALU.mult)
        tt(T3, A0, A0, ALU.mult)
        tt(T2, A1, A1, ALU.mult)
        tt(T3, T3, T2, ALU.add)
        P.op("dve", lambda e: e.reciprocal(out=T3, in_=T3), reads=bufs, writes=bufs)
        tt(T2, T0, A0, ALU.mult)
        tt(T4, T1, A1, ALU.mult)
        tt(T2, T2, T4, ALU.add)
        tt(K0, T2, T3, ALU.mult)
        tt(T2, T1, A0, ALU.mult)
        tt(T4, T0, A1, ALU.mult)
        tt(T2, T2, T4, ALU.subtract)
        tt(K1, T2, T3, ALU.mult)
        tsc(A0, T0, 1.0, None, op0=ALU.add)
        P.op("dve", lambda e: e.tensor_copy(out=A1, in_=T1), reads=bufs, writes=bufs)

    lamctx = {}
    bgs = {}

    def s5_lambda(l):
        j = l // 2
        nb = arena_switch(["S5C", "S5K", "S5T"])
        S5Cb, S5Kb, S5Tb = nb
        lamctx[l] = nb
        P.op("dve", lambda e: e.memset(ARENA[64:128, 4096:6528], 0.0), writes=[S5Tb])
        sp_load(NATQ[:, 0:2, :], d_s5n[:, j, 0:2].rearrange("g t d p -> g t (d p)"), S5Tb)
        sp_load(NATQ[:, 2, :], d_s5n[:, j, 2].rearrange("g d p -> g (d p)"), S5Tb)
        lam_compute(NATQ[:, 0:3, :], S5Tb, NATQ[:, 3:5, :], S5Tb, NATT, S5Tb, 0)
        for fc in range(8):
            pt, pbuf = PSUMS[2 + fc % 2]

            def mmx(e, fc=fc, pt=pt):
                e.matmul(pt[:, 0:256], lhsT=SELC[:, fc, :], rhs=NATQ128[:, 0:2, :].rearrange("p t x -> p (t x)"), start=True, stop=True)
                return e.matmul(pt[:, 256:512], lhsT=SELC[:, fc, :], rhs=NATQ128[:, 3:5, :].rearrange("p t x -> p (t x)"), start=True, stop=True)
            P.op("pe", mmx, reads=[SELCb, S5Tb], writes=[pbuf])
            P.op("act", lambda e, fc=fc, pt=pt: e.activation(out=S5C[:, :, :, fc, :], in_=pt[:, 0:256].rearrange("p (t d x) -> p t d x", t=2, d=2), func=AF.Copy),
                 reads=[pbuf], writes=[S5Cb])
            P.op("act", lambda e, fc=fc, pt=pt: e.activation(out=S5K[:, :, :, fc, :], in_=pt[:, 256:512].rearrange("p (t d x) -> p t d x", t=2, d=2), func=AF.Copy),
                 reads=[pbuf], writes=[S5Kb])
        slots = (0, 1, 3, 4)
        for ti in range(4):
            for d in range(2):
                for a in range(2):
                    P.op("dve", lambda e, ti=ti, d=d, a=a: e.tensor_scalar(out=XPAD[:, ti * 2 + d, a * 64:(a + 1) * 64], in0=NATQ[:, slots[ti], d * 64:(d + 1) * 64],
                                                                        scalar1=SELB[0:64, 32 + a:33 + a], scalar2=None, op0=ALU.mult),
                         reads=[S5Tb, SELBb], writes=[S5Tb])

        def mmb(e):
            last = None
            for k in range(8):
                last = e.matmul(PA[:, k * 32:(k + 1) * 32], lhsT=XPAD128[:, k, :], rhs=SELB[:, 0:32], start=True, stop=True)
            return last
        P.op("pe", mmb, reads=[S5Tb, SELBb], writes=[PAb])
        P.op("act", lambda e: e.activation(out=S5B[:], in_=PA[:, 0:128].rearrange("p (t d q) -> p t d q", t=2, d=2), func=AF.Copy), reads=[PAb], writes=[S5Bb])
        P.op("act", lambda e: e.activation(out=S5KB[:], in_=PA[:, 128:256].rearrange("p (t d q) -> p t d q", t=2, d=2), func=AF.Copy), reads=[PAb], writes=[S5KBb])
        P.op("dve", lambda e: e.tensor_scalar(out=S5LN[:], in0=S5B[:, 1], scalar1=-1.0, scalar2=None, op0=ALU.mult), reads=[S5Bb], writes=[S5LNb])
        P.op("dve", lambda e: e.tensor_copy(out=S5LT[:], in_=S5B[:]), reads=[S5Bb], writes=[S5LTb])
        for _ in range(3):
            cmul("dve", S5LT[:, 0], S5LT[:, 1], S5LT[:, 0], S5LT[:, 1], S5LT[:, 0], S5LT[:, 1], S5TB[:, 0], S5TB[:, 1], [S5LTb], [S5LTb], [S5TBb, S5TBb])

    def record_lambda(l):
        P.rec = []
        s5_lambda(l)
        ops = P.rec
        P.rec = None
        return ops

    def replay(ops, n):
        k = 0
        while ops and k < n:
            P.op(*ops.pop(0))
            k += 1

    def s5_mixer(l):
        j = l // 2
        SALL = BIG[:, 0:NSLOT * 64].rearrange("p (s r q) -> p s r q", r=2, q=32)
        SBUFS = merge("S", *BIGb[0:17])
        VF0b = merge("VF0", *BIGb[17:22])
        for bb in BIGb:
            bb.w, bb.r = {}, {}
        S5Cb, S5Kb, S5Tb = lamctx[l]
        VF0 = BIG[:, 17 * NT:17 * NT + 9 * 512].rearrange("p (n d r x) -> p n d r x", n=9, d=2, r=2)
        VF = [VF0, VF1]
        VFb0 = [VF0b, S5Tb]
        VFb = [[[merge("VFnd", VFb0[s_]) for _d in range(2)] for _n in range(T + 1)] for s_ in range(2)]
        VFall = [[b_ for n_ in VFb[s_] for b_ in n_] for s_ in range(2)]
        KF = [LNT[:, 0:2, :].rearrange("p a (k x) -> p (a k) x", x=128), LNT[:, 2:4, :].rearrange("p a (k x) -> p (a k) x", x=128)]
        KFb0 = [merge("KF0", LNTb[0], LNTb[1]), merge("KF1", LNTb[2], LNTb[3])]
        KFall = [[merge("KFi", KFb0[s_]) for _i in range(15)] for s_ in range(2)]
        UALL = TS[:].bitcast(BF16).rearrange("p a (h n) -> p (a h) n", n=NT)

        def Uap(fc):
            return UALL[:, fc, :], TSb[fc // 2]
        for b in range(2):
            wap, wb = w_get()
            w3 = wap.rearrange("p (k n) -> p k n", n=512)
            for oc in range(4):
                fc = b * 4 + oc
                pt, pbuf = PSUMS[fc % 2]
                proj_chunk(pt, pbuf, w3, wb, oc, H, Hb, 8)
                ua, ub = Uap(fc)
                P.op("act", lambda e, pt=pt, ua=ua: e.activation(out=ua, in_=pt[:], func=AF.Copy), reads=[pbuf], writes=[ub])
        if DBG and l == 0:
            for fc in range(8):
                dbg_bf(o_dU[fc * 128:(fc + 1) * 128, :], UALL[:, fc, :], [TSb[fc // 2]])
        replay(bgs.get(l, []), 10 ** 9)
        WF = [H[:, 0:4, :].rearrange("p k n -> p (k n)").rearrange("p (n d r x) -> p n d r x", n=8, d=2, r=2),
              H[:, 4:8, :].rearrange("p k n -> p (k n)").rearrange("p (n d r x) -> p n d r x", n=8, d=2, r=2)]
        WFb0 = [merge("WF0", *Hb[0:4]), merge("WF1", *Hb[4:8])]
        WFb = [[[merge("WFnd", WFb0[s_]) for _d in range(2)] for _n in range(T)] for s_ in range(2)]
        WFall = [[b_ for n_ in WFb[s_] for b_ in n_] for s_ in range(2)]
        if S5STOP == 1:
            raise _Stop()
        P.op("dve", lambda e: e.tensor_copy(out=SALL[:, 0], in_=H0[:, j, 0]), reads=[H0b], writes=[SBUFS])
        P.op("dve", lambda e: e.tensor_copy(out=SALL[:, 2 * C + 1], in_=H0[:, j, 1]), reads=[H0b], writes=[SBUFS])
        PSA = [PA, PB, PC, PD]
        PSAb = [PAb, PBb, PCb, PDb]
        PSAh = [[PSA[q][:, 0:512], PSA[q][:, 512:1024]] for q in range(4)]
        PSAhb = [[merge("PSAh", PSAb[q]), merge("PSAh", PSAb[q])] for q in range(4)]
        ENG = ["dve", "dve"]

        def recur_ops(eng, d, n, views, lr, li, dst, dstbuf, lbufs, src=None, srcbuf=None):
            pi, po = (n - 1) % 2, n % 2
            a_, x_ = views
            sin_ = (src if src is not None else WST[:, pi, d]).rearrange("p r (a x) -> p r a x", a=a_)
            if srcbuf is not None:
                lbufs = lbufs + [srcbuf]
            ta4 = WTA[:, d, 0].rearrange("p r (a x) -> p r a x", a=a_)
            tb4 = WTA[:, d, 1].rearrange("p r (a x) -> p r a x", a=a_)
            return [
                lambda: P.op(eng, lambda e: e.tensor_tensor(out=ta4, in0=sin_, in1=lr, op=ALU.mult), reads=[WSTb[pi][d]] + lbufs, writes=[WTAb[d][0]]),
                lambda: P.op(eng, lambda e: e.tensor_tensor(out=tb4, in0=sin_, in1=li, op=ALU.mult), reads=[WSTb[pi][d]] + lbufs, writes=[WTAb[d][1]]),
                lambda: P.op(eng, lambda e: e.tensor_tensor(out=WST[:, po, d, 0, :], in0=WTA[:, d, 0, 0, :], in1=WTA[:, d, 1, 1, :], op=ALU.subtract),
                             reads=[WTAb[d][0], WTAb[d][1]], writes=[WSTb[po][d]]),
                lambda: P.op(eng, lambda e: e.tensor_tensor(out=WST[:, po, d, 1, :], in0=WTA[:, d, 1, 0, :], in1=WTA[:, d, 0, 1, :], op=ALU.add),
                             reads=[WTAb[d][0], WTAb[d][1]], writes=[WSTb[po][d]]),
                lambda: P.op("act", lambda e: e.activation(out=dst[:, n, d], in_=WST[:, po, d], func=AF.Copy), reads=[WSTb[po][d]], writes=[dstbuf[n][d]]),
            ]

        pend = []
        sev = []
        SBUFS0 = merge("S0", SBUFS)
        for fc in range(8):
            s = fc % 2
            sp_load(LDW[:, s, :], d_s5w0[j, fc], LDWb[s])
            L4 = LDW[:, s, :].rearrange("p (d r x) -> p d r x", d=2, r=2)
            chains = []
            for d in range(2):
                eng = ENG[d]
                lr = S5C[:, 0, d, fc, :].unsqueeze(1).unsqueeze(1).broadcast_to([128, 2, 2, 64])
                li = S5C[:, 1, d, fc, :].unsqueeze(1).unsqueeze(1).broadcast_to([128, 2, 2, 64])
                kr = S5K[:, 0, d, fc, :].unsqueeze(1).broadcast_to([128, 2, 64])
                ki = S5K[:, 1, d, fc, :].unsqueeze(1).broadcast_to([128, 2, 64])
                v3 = lambda ap: ap.rearrange("p (a x) -> p a x", a=2)
                cmul("pool", v3(L4[:, d, 0, :]), v3(L4[:, d, 1, :]), v3(L4[:, d, 0, :]), v3(L4[:, d, 1, :]), kr, ki,
                     v3(PTMP[:, 0, :]), v3(PTMP[:, 1, :]), [LDWb[s], S5Kb], [LDWb[s]], [PTMPb[0], PTMPb[1]])
                ch = [lambda d=d, s=s, L4=L4: P.op("act", lambda e: e.activation(out=WF[s][:, 0, d], in_=L4[:, d], func=AF.Copy), reads=[LDWb[s]], writes=[WFb[s][0][d]])]
                for n in range(1, T):
                    if n == 1:
                        ch += recur_ops(eng, d, n, (2, 64), lr, li, WF[s], WFb[s], [S5Cb], src=L4[:, d], srcbuf=LDWb[s])
                    else:
                        ch += recur_ops(eng, d, n, (2, 64), lr, li, WF[s], WFb[s], [S5Cb])
                chains.append(ch)
            interleave(chains + [pend])
            ua, ub = Uap(fc)
            u3 = ua.rearrange("p (c t) -> p c t", t=T)
            hf = fc % 2

            def mmA(e, s=s, u3=u3, hf=hf):
                last = None
                for d in range(2):
                    for ri in range(2):
                        for jj in range(T):
                            n = (T - 1 - jj) if d == 0 else jj
                            for q in range(4):
                                last = e.matmul(PSAh[q][hf][:, (d * 2 + ri) * 128:(d * 2 + ri + 1) * 128], lhsT=WF[s][32 * q:32 * q + 32, n, d, ri, :],
                                                rhs=u3[32 * q:32 * q + 32, :, jj], start=(jj == 0), stop=(jj == T - 1), tile_position=(32 * q, 0))
                return last
            P.op("pe", mmA, reads=WFall[s] + [ub], writes=[PSAhb[q][hf] for q in range(4)])
            pend = []
            for q in range(4):
                qq = fc * 4 + q
                for ri in range(2):
                    src = PSAh[q][hf].rearrange("p (d r c) -> p d r c", d=2, r=2)[:, :, ri, :]
                    dst = SALL[:, 1:2 * C + 1, ri, qq].rearrange("p (d c) -> p d c", d=2)
                    eb = merge("SEV", SBUFS0)
                    sev.append(eb)
                    pend.append(lambda src=src, dst=dst, q=q, hf=hf, eb=eb: P.op("act", lambda e: e.activation(out=dst, in_=src, func=AF.Copy), reads=[PSAhb[q][hf]], writes=[eb]))
        for t_ in pend:
            t_()
        m_ = merge("S", SBUFS, *sev)
        SBUFS.w, SBUFS.r = m_.w, m_.r
        for q in range(4):
            m_ = merge("PS", PSAhb[q][0], PSAhb[q][1])
            PSAb[q].w, PSAb[q].r = m_.w, m_.r
        for s_ in range(2):
            m_ = merge("H", *WFall[s_])
            for i in range(4):
                Hb[4 * s_ + i].w, Hb[4 * s_ + i].r = dict(m_.w), dict(m_.r)
        if S5STOP == 2:
            raise _Stop()
        LTr_b = S5LT[:, 0].unsqueeze(2).broadcast_to([128, 2, 2, 32])
        LTi = S5LT[:, 1]
        P.op("dve", lambda e: e.tensor_copy(out=RST[:, 1], in_=H0[:, j]), reads=[H0b], writes=[RSTb[1]])
        modgen = compute_mod_gen(l + 1) if l + 1 < DEPTH else iter(())
        SIN = merge("SIN", SBUFS)
        SOUT = merge("SOUT", SBUFS)
        for i in range(C):
            if i % 10 == 5:
                next(modgen, None)
            pp, pc = (i + 1) % 2, i % 2
            a0 = 1 + i
            stp = 2 * C - 1 - 2 * i
            sl = slice(a0, a0 + stp + 1, stp)
            if i > 0 and i % 32 == 0:
                P.op("dve", lambda e, pp=pp: e.tensor_scalar(out=RST[:, pp], in0=RST[:, pp], scalar1=KEEP[:, 0:1], scalar2=None, op0=ALU.mult),
                     reads=[RSTb[pp], KEEPb], writes=[RSTb[pp]])
            P.op("dve", lambda e, pp=pp: e.tensor_tensor(out=RTM[:, 0], in0=RST[:, pp], in1=LTr_b, op=ALU.mult), reads=[RSTb[pp], S5LTb], writes=[RTMb[0]])
            P.op("dve", lambda e, pp=pp: e.scalar_tensor_tensor(out=RTM[:, 1, :, 0, :], in0=RST[:, pp, :, 1, :], scalar=-1.0, in1=LTi, op0=ALU.mult, op1=ALU.mult),
                 reads=[RSTb[pp], S5LTb], writes=[RTMb[1]])
            P.op("dve", lambda e, pp=pp: e.tensor_tensor(out=RTM[:, 1, :, 1, :], in0=RST[:, pp, :, 0, :], in1=LTi, op=ALU.mult), reads=[RSTb[pp], S5LTb], writes=[RTMb[2]])
            P.op("dve", lambda e: e.tensor_tensor(out=RTM[:, 0], in0=RTM[:, 0], in1=RTM[:, 1], op=ALU.add), reads=[RTMb[0], RTMb[1], RTMb[2]], writes=[RTMb[0]])
            P.op("dve", lambda e, pc=pc, sl=sl: e.tensor_tensor(out=RST[:, pc], in0=RTM[:, 0], in1=SALL[:, sl], op=ALU.add), reads=[RTMb[0], SIN], writes=[RSTb[pc]])
            P.op("act", lambda e, pc=pc, sl=sl: e.activation(out=SALL[:, sl], in_=RST[:, pc], func=AF.Copy), reads=[RSTb[pc]], writes=[SOUT])
            if i % 32 == 31:
                k = i // 32
                P.op("act", lambda e, pc=pc, k=k: e.activation(out=STG[:, k], in_=RST[:, pc], func=AF.Copy), reads=[RSTb[pc]], writes=[STGb])
        P.op("sp", lambda e: e.dma_start(out=o_st[j], in_=STG[:].rearrange("p k d r q -> p (k d r q)")), reads=[STGb], dma=True)
        for _ in modgen:
            pass
        m_ = merge("S", SIN, SOUT)
        SBUFS.w, SBUFS.r = m_.w, m_.r
        for s0 in (32, C + 1 + 32):
            P.op("dve", lambda e, s0=s0: e.tensor_scalar(out=SALL[:, s0:s0 + 65:32], in0=SALL[:, s0:s0 + 65:32], scalar1=KEEP[:, 0:1], scalar2=None, op0=ALU.mult),
                 reads=[SBUFS, KEEPb], writes=[SBUFS])
        if S5STOP == 3:
            raise _Stop()
        Z = H
        kcnt = 0
        pend = []
        KPSb = [merge("KPS", PSUMS[2 + b_ // 2][1]) for b_ in range(4)]
        for fc in range(8):
            s = fc % 2
            sp_load(LDV[:, s, :], d_s5v0[j, fc], LDVb[s])
            sp_load(LDN[:, s, :], d_s5bn[j, fc], LDNb[s])
            V4 = LDV[:, s, :].rearrange("p (d r x) -> p d r x", d=2, r=2)
            N4 = LDN[:, s, :].rearrange("p (d r x) -> p d r x", d=2, r=2)
            q0 = fc * 4
            chains = []
            for d in range(2):
                eng = ENG[d]
                krb = S5KB[:, 0, d, q0:q0 + 4].unsqueeze(2).broadcast_to([128, 4, 32])
                kib = S5KB[:, 1, d, q0:q0 + 4].unsqueeze(2).broadcast_to([128, 4, 32])
                v4 = lambda ap: ap.rearrange("p (a x) -> p a x", a=4)
                ch = cmul_ops(eng, v4(WST[:, 0, d, 0, :]), v4(WST[:, 0, d, 1, :]), v4(N4[:, d, 0, :]), v4(N4[:, d, 1, :]), krb, kib,
                              v4(WTA[:, d, 0, 0, :]), v4(WTA[:, d, 1, 0, :]), [LDNb[s], S5KBb, WSTb[0][d]], [WSTb[0][d]], [WTAb[d][0], WTAb[d][1]])
                ch.append(lambda d=d, s=s: P.op("act", lambda e: e.activation(out=BNB[:, s, d], in_=WST[:, 0, d], func=AF.Copy), reads=[WSTb[0][d]], writes=[BNBb[s]]))
                ch.append(lambda d=d, eng=eng, V4=V4, s=s: P.op(eng, lambda e: e.tensor_copy(out=WST[:, 0, d, 0, :], in_=V4[:, d, 0, :]), reads=[LDVb[s]], writes=[WSTb[0][d]]))
                ch.append(lambda d=d, eng=eng, V4=V4, s=s: P.op(eng, lambda e: e.tensor_scalar(out=WST[:, 0, d, 1, :], in0=V4[:, d, 1, :], scalar1=-1.0, scalar2=None, op0=ALU.mult), reads=[LDVb[s]], writes=[WSTb[0][d]]))
                ch.append(lambda d=d, s=s: P.op("act", lambda e: e.activation(out=VF[s][:, 0, d], in_=WST[:, 0, d], func=AF.Copy), reads=[WSTb[0][d]], writes=[VFb[s][0][d]]))
                lrb = S5B[:, 0, d, q0:q0 + 4].unsqueeze(1).unsqueeze(3).broadcast_to([128, 2, 4, 32])
                lib = S5LN[:, d, q0:q0 + 4].unsqueeze(1).unsqueeze(3).broadcast_to([128, 2, 4, 32])
                for n in range(1, T + 1):
                    ch += recur_ops(eng, d, n, (4, 32), lrb, lib, VF[s], VFb[s], [S5Bb, S5LNb])
                chains.append(ch)
            interleave(chains + [pend])
            pend = []
            klist = [(dl, d) for dl in range(T) for d in range(2) if not (dl == 0 and d == 1)]
            for g0 in range(0, len(klist), 4):
                grp = klist[g0:g0 + 4]
                bank = kcnt % 4
                kcnt += 1
                pk = PSUMS[2 + bank // 2][0]
                pkb = KPSb[bank]
                cb = (bank % 2) * 512

                def mmK(e, grp=grp, pk=pk, cb=cb, s=s):
                    last = None
                    for gi, (dl, d) in enumerate(grp):
                        col = cb + gi * 128
                        dirs = (0, 1) if dl == 0 else (d,)
                        nmm = len(dirs) * 2
                        i_ = 0
                        for dd in dirs:
                            for ri in range(2):
                                last = e.matmul(pk[:, col:col + 128], lhsT=BNB[:, s, dd, ri, :], rhs=VF[s][:, dl, dd, ri, :], start=(i_ == 0), stop=(i_ == nmm - 1))
                                i_ += 1
                    return last
                P.op("pe", mmK, reads=[BNBb[s]] + VFall[s], writes=[pkb])
                for gi, (dl, d) in enumerate(grp):
                    col = cb + gi * 128
                    idx = 7 + dl if d == 0 else 7 - dl
                    if dl == 0:
                        pend.append(lambda col=col, pk=pk, pkb=pkb: P.op("dve", lambda e: e.tensor_tensor(out=KTMP[:], in0=pk[:, col:col + 128], in1=BDMASK, op=ALU.mult), reads=[CONSTb, pkb], writes=[KTMPb]))
                        pend.append(lambda fc=fc, s=s: P.op("dve", lambda e: e.scalar_tensor_tensor(out=KF[s][:, 7, :], in0=IDENT_F, scalar=S5D[:, j, fc:fc + 1], in1=KTMP[:], op0=ALU.mult, op1=ALU.add),
                                                           reads=[CONSTb, S5Db, KTMPb], writes=[KFall[s][7]]))
                    else:
                        pend.append(lambda col=col, idx=idx, pk=pk, s=s, pkb=pkb: P.op("dve", lambda e: e.tensor_tensor(out=KF[s][:, idx, :], in0=pk[:, col:col + 128], in1=BDMASK, op=ALU.mult),
                                                                                    reads=[CONSTb, pkb], writes=[KFall[s][idx]]))
            ua, ub = Uap(fc)
            u3 = ua.rearrange("p (c t) -> p c t", t=T)
            pt, pbuf = PSUMS[fc % 2]

            def mmC(e, fc=fc, pt=pt, u3=u3, s=s):
                last = None
                for jj in range(T):
                    o = pt[:, jj * 128:(jj + 1) * 128]
                    for j2 in range(T):
                        last = e.matmul(o, lhsT=KF[s][:, 7 + jj - j2, :], rhs=u3[:, :, j2], start=(j2 == 0), stop=False)
                    for d in range(2):
                        n = jj + 1 if d == 0 else T - jj
                        s0 = 0 if d == 0 else C + 2
                        for ri in range(2):
                            lastq = (d == 1 and ri == 1)
                            for q in range(4):
                                qq = fc * 4 + q
                                oq = pt[32 * q:32 * q + 32, jj * 128:(jj + 1) * 128]
                                last = e.matmul(oq, lhsT=VF[s][:, n, d, ri, 32 * q:32 * q + 32], rhs=SALL[:, s0:s0 + C, ri, qq], start=False, stop=lastq, tile_position=(0, 32 * q))
                return last
            def fin(mmC=mmC, s=s, ub=ub, pbuf=pbuf, fc=fc, pt=pt):
                P.op("pe", mmC, reads=KFall[s] + VFall[s] + [ub, SBUFS], writes=[pbuf])
                P.op("act", lambda e: e.activation(out=Z[:, fc, :].rearrange("p (c t) -> p t c", t=T), in_=pt[:].rearrange("p (t c) -> p t c", t=T),
                                                   func=AF.Gelu_apprx_tanh), reads=[pbuf], writes=[Hb[fc]])
            pend.append(fin)
        for t_ in pend:
            t_()
        if DBG and l == 0:
            for fc in range(8):
                dbg_bf(o_dZ[fc * 128:(fc + 1) * 128, :], H[:, fc, :], [Hb[fc]])
        if S5STOP == 4:
            raise _Stop()
        for t_ in range(2):
            m_ = merge("PS", KPSb[2 * t_], KPSb[2 * t_ + 1])
            PSUMS[2 + t_][1].w, PSUMS[2 + t_][1].r = m_.w, m_.r
        for bb in BIGb[0:17]:
            bb.w, bb.r = dict(SBUFS.w), dict(SBUFS.r)
        m_ = merge("VF0", *VFall[0])
        for bb in BIGb[17:22]:
            bb.w, bb.r = dict(m_.w), dict(m_.r)
        m_ = merge("VF1", *VFall[1])
        S5Tb.w, S5Tb.r = m_.w, m_.r
        for s_ in range(2):
            m_ = merge("KF", *KFall[s_])
            for i in range(2):
                LNTb[2 * s_ + i].w, LNTb[2 * s_ + i].r = dict(m_.w), dict(m_.r)
        G3 = BIG[:, 0:8 * NT].rearrange("p (k n) -> p k n", n=NT)
        for b in range(4):
            wap, wb = w_get()
            w3 = wap.rearrange("p (k n) -> p k n", n=512)
            for jj in range(2):
                jc = 2 * b + jj
                pv_, pvb = PSUMS[(jj * 2) % 4]
                pg, pgb = PSUMS[(jj * 2 + 1) % 4]
                proj_chunk(pv_, pvb, w3, wb, jj * 2, H, Hb, 8)
                proj_chunk(pg, pgb, w3, wb, jj * 2 + 1, H, Hb, 8)
                t = jj
                P.op("act", lambda e, pg=pg, t=t: e.activation(out=TS[:, t, :], in_=pg[:], func=AF.Sigmoid), reads=[pgb], writes=[TSb[t]])
                P.op("dve", lambda e, pv_=pv_, t=t, jc=jc: e.tensor_tensor(out=G3[:, jc, :], in0=pv_[:], in1=TS[:, t, :], op=ALU.mult),
                     reads=[pvb, TSb[t]], writes=[BIGb[jc]])
        outproj_ln(l, 0, G3, BIGb[0:8], 8, 2, 512, (l, 1))
        if DBG and l == 0:
            dbg_x(o_dXA)


    compute_mod(0)
    bgs[0] = record_lambda(0)
    modulate(0, 0)
    try:
        for l in range(DEPTH):
            if l >= STAGE:
                break
            if l % 2 == 0:
                s5_mixer(l)
            else:
                attention(l)
            bg = []
            if l + 1 < DEPTH and l % 2 == 1:
                compute_mod(l + 1)
                bg = record_lambda(l + 1)
            ffn(l, bg)
    except _Stop:
        pass
    for c in range(8):
        P.op("sp", lambda e, c=c: e.dma_start(out=o_yT[c * 128:(c + 1) * 128, :], in_=X[:, c, :]), reads=[Xb[c]], dma=True)
    P.finish()
    return nc


_CACHE = {}


def kernel(**inp):
    inp = {k: np.asarray(v) for k, v in inp.items()}
    f32 = np.float32
    wall = build_wall(inp)
    nblk = wall.shape[0]
    s5h = s5_host_layout(inp)
    cos, sin = rope_tables()
    rope_s = np.ascontiguousarray(np.stack([cos, sin], 1)).astype(f32)
    rope_p = np.ascontiguousarray(np.stack([np.ones_like(cos), np.zeros_like(sin)], 1)).astype(f32)
    consts = const_tables()
    bmodT = np.ascontiguousarray(inp["b_mod"].reshape(DEPTH, 48, 128).transpose(2, 0, 1).reshape(128, DEPTH * 48)).astype(f32)
    lng = inp["ln_g"].reshape(DEPTH * 2 * 8, 128).T
    lnb = inp["ln_b"].reshape(DEPTH * 2 * 8, 128).T
    lnT = np.ascontiguousarray(np.concatenate([lng, lnb], 1)).astype(f32)
    gain = np.ascontiguousarray(np.stack([inp["q_norm_g"][0], inp["q_norm_g"][1], inp["k_norm_g"][0], inp["k_norm_g"][1]], 1)).astype(f32)
    mask_s = np.zeros((128, 48), f32)
    mask_p = np.full((12, 4), -30000.0, f32)
    for kt in range(8):
        mask_p[kt, kt // 2] = 0.0
    mask_p = np.ascontiguousarray(np.broadcast_to(mask_p.reshape(1, 48), (128, 48))).astype(f32)
    in_maps = []
    for core in range(8):
        m = dict(wall=wall, bmodT=bmodT, lnT=lnT, consts=consts, gain=gain, **s5h)
        if core < 4:
            b = core
            m["xT"] = np.ascontiguousarray(inp["x_sample"][b].T)
            cvec = inp["c"][b]
            m["rope"] = rope_s
            m["maskb"] = mask_s
            m["keep"] = np.ones((128, 1), f32)
            m["ckT"] = np.ascontiguousarray(inp["cache_k"][b].transpose(0, 3, 2, 1))
            m["cv"] = np.ascontiguousarray(inp["cache_v"][b].reshape(2, 4, 128, 256).transpose(0, 2, 1, 3))
            m["h0"] = h0_layout(inp["state_s5"][b])
        else:
            s0 = (core - 4) * 4
            m["xT"] = np.ascontiguousarray(inp["x_prompt"][s0:s0 + 4].reshape(NT, D).T)
            cvec = inp["c_ctx"]
            m["rope"] = rope_p
            m["maskb"] = mask_p
            m["keep"] = np.zeros((128, 1), f32)
            m["ckT"] = np.zeros((2, 128, 2, 512), f32)
            m["cv"] = np.zeros((2, 128, 4, 256), f32)
            m["h0"] = np.zeros((128, 2, 2, 2, 32), f32)
        m["cT"] = np.ascontiguousarray(cvec.reshape(8, 128).T).astype(f32)
        in_maps.append({k: np.ascontiguousarray(v, dtype=f32) for k, v in m.items()})
    if "nc" not in _CACHE:
        _CACHE["nc"] = build_program(nblk)
    nc = _CACHE["nc"]
    _CACHE.pop("nc")
    res = run_bass_kernel_spmd(nc, in_maps, core_ids=list(range(8)))
    R = res.results
    if DBG:
        _CACHE["dbg"] = {"c%d_%s" % (ci, k): R[ci][k] for ci in (0, 4) for k in R[ci] if k.startswith("d")}
    y_sample = np.stack([R[b]["yT"].T for b in range(4)], 0).astype(f32)
    y_prompt = np.concatenate([R[4 + i]["yT"].T.reshape(4, 256, D) for i in range(4)], 0).astype(f32)
    nk = np.zeros((16, 2, 256, 2, 128), f32)
    nv = np.zeros((16, 2, 256, 2, 128), f32)
    ns = np.zeros((16, 2, 2, 2, 64, 64), f32)
    for i in range(4):
        r = R[4 + i]
        ko = r["kout"]
        vo = r["vout"]
        so = r["stout"].reshape(2, 2, 64, 4, 2, 2, 32)
        for s in range(4):
            bidx = i * 4 + s
            nk[bidx] = ko[:, :, :, s * 256:(s + 1) * 256].transpose(0, 3, 1, 2)
            nv[bidx] = vo[:, s * 256:(s + 1) * 256, :].reshape(2, 256, 2, 128)
            for d in range(2):
                k = s if d == 0 else 3 - s
                blk = so[:, :, :, k, d, :, :]
                ns[bidx, :, d] = blk.transpose(0, 3, 4, 1, 2).reshape(2, 2, 64, 64)
    return (y_prompt, y_sample, nk, nv, ns)
```
